# Optimizing a Trainium2 kernel written in Bass

```python
import math
import jax
import jax.numpy as jnp
from jax import lax
import numpy as np

D_MODEL = 2048
BATCH = 4
SEQ = 2048
DEPTH = 4

CHUNK = 64
N_PREV_CHUNKS = 8
BAND_CHUNKS = N_PREV_CHUNKS + 1
Q_BLOCK = 128
ROPE_THETA = 500000.0
PLE_DIM = 256
D_FF = 5632
NORM_EPS = 1e-6
NEG_INF = -1e30

A_HEADS = 8
A_HEAD_DIM = 128
A_WIDTH = A_HEADS * A_HEAD_DIM
REL_FUTURE = CHUNK - 1
REL_PAST_CLIP = 128
REL_TABLE = REL_FUTURE + REL_PAST_CLIP + 1

B_HEADS = 8
B_NOPE_DIM = 128
B_ROPE_DIM = 64
B_V_DIM = 128
B_Q_LORA = 512
B_KV_LORA = 256
B_WIDTH = B_HEADS * B_V_DIM

AB_IN = 3 * A_WIDTH + B_Q_LORA + B_KV_LORA + B_ROPE_DIM
AB_OUT = A_WIDTH + B_WIDTH

C_HEADS = 8
C_HEAD_DIM = 128
C_V_DIM = 2 * C_HEAD_DIM
C_ROT_DIM = C_HEAD_DIM // 4
C_IN = C_HEADS * (4 * C_HEAD_DIM + C_V_DIM)
C_OUT = C_HEADS * C_V_DIM

N_EVEN = (DEPTH + 1) // 2
N_ODD = DEPTH // 2

kernel_name = 'hybrid_streaming_encoder_block'


def rmsnorm(x, g):
    xf = x.astype(jnp.float32)
    y = xf * lax.rsqrt(jnp.mean(xf * xf, axis=-1, keepdims=True) + NORM_EPS)
    return (y * g.astype(jnp.float32)).astype(x.dtype)


def swiglu(h, w_in, w_out):
    a, b = jnp.split(h @ w_in, 2, axis=-1)
    return (jax.nn.silu(a) * b) @ w_out


def rope_tables(seq, rot_dim, dtype):
    inv = ROPE_THETA ** (-jnp.arange(0, rot_dim, 2, dtype=jnp.float32) / rot_dim)
    ang = jnp.arange(seq, dtype=jnp.float32)[:, None] * inv[None, :]
    return jnp.cos(ang).astype(dtype), jnp.sin(ang).astype(dtype)


def apply_rope(x, cos, sin):
    c = cos[None, :, None, :]
    s = sin[None, :, None, :]
    x1, x2 = jnp.split(x, 2, axis=-1)
    return jnp.concatenate([x1 * c - x2 * s, x2 * c + x1 * s], axis=-1)


def partial_rope(x, cos, sin, rot):
    return jnp.concatenate([apply_rope(x[..., :rot], cos, sin), x[..., rot:]], axis=-1)


def chunk_mask(qs, qe, ke):
    qc = jnp.arange(qs, qe) // CHUNK
    kc = jnp.arange(ke) // CHUNK
    return kc[None, :] <= qc[:, None]


def chunked_relpos_attention(q, k, v, rel_table):
    b, s, h, d = q.shape
    nc = s // CHUNK
    band = BAND_CHUNKS * CHUNK
    qc = q.reshape(b, nc, CHUNK, h, d)
    pad = ((0, 0), (N_PREV_CHUNKS, 0), (0, 0), (0, 0), (0, 0))
    kp = jnp.pad(k.reshape(b, nc, CHUNK, h, d), pad)
    vp = jnp.pad(v.reshape(b, nc, CHUNK, h, d), pad)
    idx = jnp.arange(nc)[:, None] + jnp.arange(BAND_CHUNKS)[None, :]
    kb = kp[:, idx].reshape(b, nc, band, h, d)
    vb = vp[:, idx].reshape(b, nc, band, h, d)
    scores = jnp.einsum('bnqhd,bnkhd->bhnqk', qc, kb).astype(jnp.float32) * (d ** -0.5)
    dist = jnp.arange(CHUNK)[:, None] + N_PREV_CHUNKS * CHUNK - jnp.arange(band)[None, :]
    bias = rel_table[:, jnp.clip(dist, -REL_FUTURE, REL_PAST_CLIP) + REL_FUTURE]
    scores = scores + bias[None, :, None].astype(jnp.float32)
    valid = (jnp.arange(nc)[:, None] - N_PREV_CHUNKS + jnp.arange(band)[None, :] // CHUNK) >= 0
    scores = jnp.where(valid[None, None, :, None, :], scores, NEG_INF)
    probs = jax.nn.softmax(scores, axis=-1).astype(v.dtype)
    out = jnp.einsum('bhnqk,bnkhd->bnqhd', probs, vb)
    return out.reshape(b, s, h * d)


def mla_attention(cq, ckv, kr, g_q, w_qup, g_kv, w_kvup, cos, sin):
    b, s, _ = cq.shape
    q = (rmsnorm(cq, g_q) @ w_qup).reshape(b, s, B_HEADS, B_NOPE_DIM + B_ROPE_DIM)
    q_nope = q[..., :B_NOPE_DIM]
    q_rope = apply_rope(q[..., B_NOPE_DIM:], cos, sin)
    kv = (rmsnorm(ckv, g_kv) @ w_kvup).reshape(b, s, B_HEADS, B_NOPE_DIM + B_V_DIM)
    k_nope = kv[..., :B_NOPE_DIM]
    v = kv[..., B_NOPE_DIM:]
    k_rope = apply_rope(kr[:, :, None, :], cos, sin)[:, :, 0, :]
    scale = (B_NOPE_DIM + B_ROPE_DIM) ** -0.5
    outs = []
    for qs in range(0, s, Q_BLOCK):
        qe = qs + Q_BLOCK
        sc = (jnp.einsum('bqhd,bkhd->bhqk', q_nope[:, qs:qe], k_nope[:, :qe])
              + jnp.einsum('bqhr,bkr->bhqk', q_rope[:, qs:qe], k_rope[:, :qe])).astype(jnp.float32) * scale
        sc = jnp.where(chunk_mask(qs, qe, qe)[None, None], sc, NEG_INF)
        pr = jax.nn.softmax(sc, axis=-1).astype(v.dtype)
        outs.append(jnp.einsum('bhqk,bkhd->bqhd', pr, v[:, :qe]))
    return jnp.concatenate(outs, axis=1).reshape(b, s, B_WIDTH)


def diff_attention(u, lq1, lk1, lq2, lk2, g_sub, lambda_init, cos, sin):
    b, s, _ = u.shape
    u = u.reshape(b, s, C_HEADS, 4 * C_HEAD_DIM + C_V_DIM)
    q1, q2, k1, k2 = [partial_rope(u[..., i * C_HEAD_DIM:(i + 1) * C_HEAD_DIM], cos, sin, C_ROT_DIM)
                      for i in range(4)]
    v = u[..., 4 * C_HEAD_DIM:]
    lam = (jnp.exp(jnp.sum(lq1.astype(jnp.float32) * lk1.astype(jnp.float32)))
           - jnp.exp(jnp.sum(lq2.astype(jnp.float32) * lk2.astype(jnp.float32))) + lambda_init)
    scale = C_HEAD_DIM ** -0.5
    outs = []
    for qs in range(0, s, Q_BLOCK):
        qe = qs + Q_BLOCK
        mask = chunk_mask(qs, qe, qe)[None, None]
        s1 = jnp.einsum('bqhd,bkhd->bhqk', q1[:, qs:qe], k1[:, :qe]).astype(jnp.float32) * scale
        s2 = jnp.einsum('bqhd,bkhd->bhqk', q2[:, qs:qe], k2[:, :qe]).astype(jnp.float32) * scale
        w = (jax.nn.softmax(jnp.where(mask, s1, NEG_INF), axis=-1)
             - lam * jax.nn.softmax(jnp.where(mask, s2, NEG_INF), axis=-1))
        outs.append(jnp.einsum('bhqk,bkhd->bqhd', w.astype(v.dtype), v[:, :qe]))
    o = jnp.concatenate(outs, axis=1)
    o = rmsnorm(o, g_sub) * (1.0 - lambda_init)
    return o.reshape(b, s, C_OUT)


def setup_inputs(seed: int = 0) -> dict:
    key = jax.random.key(seed)
    ks = iter(jax.random.split(key, 40))
    f32 = jnp.float32

    def w(shape, fan_in):
        return jax.random.normal(next(ks), shape, f32) * (fan_in ** -0.5)

    def gain(shape):
        return 1.0 + 0.05 * jax.random.normal(next(ks), shape, f32)

    def small(shape, sd):
        return sd * jax.random.normal(next(ks), shape, f32)

    return {
        'x': jax.random.normal(next(ks), (BATCH, SEQ, D_MODEL), f32),
        'p': jax.random.normal(next(ks), (DEPTH, BATCH, SEQ, PLE_DIM), f32),
        'ffn1_g_pre': gain((DEPTH, D_MODEL)),
        'ffn1_w_in': w((DEPTH, D_MODEL, 2 * D_FF), D_MODEL),
        'ffn1_w_out': w((DEPTH, D_FF, D_MODEL), D_FF),
        'ffn1_g_post': gain((DEPTH, D_MODEL)),
        'mix_g_pre': gain((DEPTH, D_MODEL)),
        'mix_g_post': gain((DEPTH, D_MODEL)),
        'ab_w_in': w((N_EVEN, D_MODEL, AB_IN), D_MODEL),
        'a_rel_bias': small((N_EVEN, A_HEADS, REL_TABLE), 0.2),
        'b_g_q': gain((N_EVEN, B_Q_LORA)),
        'b_w_qup': w((N_EVEN, B_Q_LORA, B_HEADS * (B_NOPE_DIM + B_ROPE_DIM)), B_Q_LORA),
        'b_g_kv': gain((N_EVEN, B_KV_LORA)),
        'b_w_kvup': w((N_EVEN, B_KV_LORA, B_HEADS * (B_NOPE_DIM + B_V_DIM)), B_KV_LORA),
        'ab_w_out': w((N_EVEN, AB_OUT, D_MODEL), AB_OUT),
        'c_w_in': w((N_ODD, D_MODEL, C_IN), D_MODEL),
        'c_lq1': small((N_ODD, C_HEAD_DIM), 0.1),
        'c_lk1': small((N_ODD, C_HEAD_DIM), 0.1),
        'c_lq2': small((N_ODD, C_HEAD_DIM), 0.1),
        'c_lk2': small((N_ODD, C_HEAD_DIM), 0.1),
        'c_g_sub': gain((N_ODD, C_V_DIM)),
        'c_w_out': w((N_ODD, C_OUT, D_MODEL), C_OUT),
        'ffn2_g_pre': gain((DEPTH, D_MODEL)),
        'ffn2_w_in': w((DEPTH, D_MODEL, 2 * D_FF), D_MODEL),
        'ffn2_w_out': w((DEPTH, D_FF, D_MODEL), D_FF),
        'ffn2_g_post': gain((DEPTH, D_MODEL)),
        'ple_g_pre': gain((DEPTH, D_MODEL)),
        'ple_w_gate': w((DEPTH, D_MODEL, D_MODEL), D_MODEL),
        'ple_w_proj': w((DEPTH, PLE_DIM, D_MODEL), PLE_DIM),
        'ple_g_post': gain((DEPTH, D_MODEL)),
    }


def reference(x, p, ffn1_g_pre, ffn1_w_in, ffn1_w_out, ffn1_g_post, mix_g_pre, mix_g_post,
              ab_w_in, a_rel_bias, b_g_q, b_w_qup, b_g_kv, b_w_kvup, ab_w_out,
              c_w_in, c_lq1, c_lk1, c_lq2, c_lk2, c_g_sub, c_w_out,
              ffn2_g_pre, ffn2_w_in, ffn2_w_out, ffn2_g_post,
              ple_g_pre, ple_w_gate, ple_w_proj, ple_g_post):
    b, s, _ = x.shape
    cos_b, sin_b = rope_tables(s, B_ROPE_DIM, x.dtype)
    cos_c, sin_c = rope_tables(s, C_ROT_DIM, x.dtype)
    split_at = [A_WIDTH, 2 * A_WIDTH, 3 * A_WIDTH, 3 * A_WIDTH + B_Q_LORA,
                3 * A_WIDTH + B_Q_LORA + B_KV_LORA]
    h = x
    for i in range(DEPTH):
        j = i // 2
        h = h + 0.5 * rmsnorm(swiglu(rmsnorm(h, ffn1_g_pre[i]), ffn1_w_in[i], ffn1_w_out[i]), ffn1_g_post[i])
        hn = rmsnorm(h, mix_g_pre[i])
        if i % 2 == 0:
            u = hn @ ab_w_in[j]
            qa, ka, va, cq, ckv, kr = jnp.split(u, split_at, axis=-1)
            oa = chunked_relpos_attention(qa.reshape(b, s, A_HEADS, A_HEAD_DIM),
                                          ka.reshape(b, s, A_HEADS, A_HEAD_DIM),
                                          va.reshape(b, s, A_HEADS, A_HEAD_DIM), a_rel_bias[j])
            ob = mla_attention(cq, ckv, kr, b_g_q[j], b_w_qup[j], b_g_kv[j], b_w_kvup[j], cos_b, sin_b)
            mix = jnp.concatenate([oa, ob], axis=-1) @ ab_w_out[j]
        else:
            lambda_init = 0.8 - 0.6 * math.exp(-0.3 * i)
            oc = diff_attention(hn @ c_w_in[j], c_lq1[j], c_lk1[j], c_lq2[j], c_lk2[j], c_g_sub[j],
                                lambda_init, cos_c, sin_c)
            mix = oc @ c_w_out[j]
        h = h + rmsnorm(mix, mix_g_post[i])
        h = h + 0.5 * rmsnorm(swiglu(rmsnorm(h, ffn2_g_pre[i]), ffn2_w_in[i], ffn2_w_out[i]), ffn2_g_post[i])
        gate = jax.nn.sigmoid(rmsnorm(h, ple_g_pre[i]) @ ple_w_gate[i])
        h = h + rmsnorm(gate * (p[i] @ ple_w_proj[i]), ple_g_post[i])
    return h
```

```python
import contextlib
import math
import numpy as np
import concourse.bass as bass
import concourse.mybir as mybir
from concourse.bass_utils import run_bass_kernel_spmd

F32 = mybir.dt.float32
BF16 = mybir.dt.bfloat16
AF = mybir.ActivationFunctionType
ALU = mybir.AluOpType

D = 2048
NT = 1024
TH = 512
KT = D // 128
DFF = 5632
FC = DFF // 128
EPS = 1e-6
SEM_LIMIT = 20000


class T:
    __slots__ = ("w", "r", "name")

    def __init__(self, name=""):
        self.w = None
        self.r = {}
        self.name = name


class DmaSem:
    def __init__(self, K, name):
        self.sem = K.newsem(name)
        self.group = "dma_" + name
        self.n = 0


class E:
    def __init__(self, K, name, eng, is_pe=False):
        self.K = K
        self.name = name
        self.e = eng
        self.is_pe = is_pe
        self.epoch = 0
        self.cnt = 0
        self.sem = K.newsem(f"{name}_e0")
        self.seen = {}

    def _need(self, toks):
        for tok in toks:
            if tok is None:
                continue
            group, epoch, val, sem = tok
            if self.is_pe and group == self.name:
                continue
            s = self.seen.get(group)
            if s is not None and s >= (epoch, val):
                continue
            self.e.wait_ge(sem, val)
            self.seen[group] = (epoch, val)

    def deps(self, reads, writes):
        toks = []
        for b in reads:
            toks.append(b.w)
        for b in writes:
            toks.append(b.w)
            toks.extend(b.r.values())
        self._need(toks)

    def _record(self, tok, reads, writes):
        for b in reads:
            old = b.r.get(tok[0])
            if old is None or (old[1], old[2]) < (tok[1], tok[2]):
                b.r[tok[0]] = tok
        for b in writes:
            b.w = tok
            b.r = {}

    def op(self, fn, reads=(), writes=(), inc=True):
        self.deps(reads, writes)
        ins = fn()
        if inc:
            ins.then_inc(self.sem, 1)
            self.cnt += 1
            tok = (self.name, self.epoch, self.cnt, self.sem)
            self._record(tok, reads, writes)
            if self.cnt >= SEM_LIMIT:
                self.epoch += 1
                self.cnt = 0
                self.sem = self.K.newsem(f"{self.name}_e{self.epoch}")
        else:
            tok = (self.name, self.epoch, self.cnt + 1, self.sem)
            self._record(tok, reads, writes)
        return ins

    def dma(self, out, in_, ds, reads=(), writes=(), **kw):
        self.deps(reads, writes)
        ins = self.e.dma_start(out=out, in_=in_, **kw)
        ds.n += 16
        ins.then_inc(ds.sem, 16)
        tok = (ds.group, 0, ds.n, ds.sem)
        self._record(tok, reads, writes)
        return ins


class K:
    def __init__(self):
        self.nc = bass.Bass("TRN2", target_bir_lowering=False)
        self.stack = contextlib.ExitStack()
        self.nsem = 0
        nc = self.nc
        self.pe = E(self, "pe", nc.tensor, is_pe=True)
        self.act = E(self, "act", nc.scalar)
        self.dve = E(self, "dve", nc.vector)
        self.pool = E(self, "pool", nc.gpsimd)
        self.sp = E(self, "sp", nc.sync)
        self.outsems = []

    def newsem(self, name):
        self.nsem += 1
        return self.stack.enter_context(self.nc.semaphore(f"s{self.nsem}_{name}"))

    def sb(self, name, shape, dt):
        return self.stack.enter_context(self.nc.sbuf_tensor(name, shape, dt))

    def ps(self, name, shape, dt=F32):
        return self.stack.enter_context(self.nc.psum_tensor(name, shape, dt))

    def din(self, name, shape, dt=F32):
        return self.nc.dram_tensor(name, list(shape), dt, kind="ExternalInput").ap()

    def dout(self, name, shape, dt=F32):
        return self.nc.dram_tensor(name, list(shape), dt, kind="ExternalOutput").ap()

    def dint(self, name, shape, dt=F32):
        return self.nc.dram_tensor(name, list(shape), dt, kind="Internal").ap()

    def finish(self):
        for ds in self.outsems:
            self.sp.e.wait_ge(ds.sem, ds.n)
        self.stack.close()
        return self.nc


class State:
    def __init__(self, k: K):
        self.k = k
        self.h = k.sb("h", [128, KT, NT], F32)
        self.h_t = [[T(f"h{c}_{t}") for t in range(2)] for c in range(KT)]
        self.bank = [k.ps(f"bank{i}", [128, 512], F32) for i in range(8)]
        self.bank_t = [T(f"bank{i}") for i in range(8)]
        self.ones_bf = k.sb("ones_bf", [128, 128], BF16)
        self.ones_f = k.sb("ones_f", [128, 128], F32)
        self.ones_t = T("ones")
        k.dve.op(lambda: k.nc.vector.memset(self.ones_bf[:], 1.0), writes=[self.ones_t])
        k.dve.op(lambda: k.nc.vector.memset(self.ones_f[:], 1.0), writes=[self.ones_t])
        self.big = k.sb("big", [128, 96 * 1024], mybir.dt.uint8)
        self.sq = [k.sb(f"sq{i}", [128, TH], F32) for i in range(2)]
        self.sq_t = [T(f"sq{i}") for i in range(2)]
        self.sq_i = 0
        self.rstd = k.sb("rstd", [128, TH], F32)
        self.rstd_t = T("rstd")
        self.gains = k.sb("gains", [128, 10, KT], F32)
        self.gains_t = T("gains")
        self.misc_ds = DmaSem(k, "misc")


def rms_stats_accum(k, st, src_ap, src_t, bank_i, first, last, src_is_psum=False):
    i = st.sq_i
    st.sq_i ^= 1
    sq, sq_t = st.sq[i], st.sq_t[i]
    k.act.op(lambda: k.nc.scalar.activation(out=sq[:], in_=src_ap, func=AF.Square),
             reads=[src_t], writes=[sq_t])
    k.pe.op(lambda: k.nc.tensor.matmul(st.bank[bank_i][:], st.ones_f[:], sq[:], start=first, stop=last),
            reads=[sq_t, st.ones_t], writes=[st.bank_t[bank_i]], inc=True)


def rstd_from_bank(k, st, bank_i, nfeat):
    k.dve.op(lambda: k.nc.vector.tensor_scalar(out=st.rstd[:], in0=st.bank[bank_i][:], scalar1=1.0 / nfeat,
                                              scalar2=EPS, op0=ALU.mult, op1=ALU.add),
             reads=[st.bank_t[bank_i]], writes=[st.rstd_t])
    k.act.op(lambda: k.nc.scalar.activation(out=st.rstd[:], in_=st.rstd[:], func=AF.Sqrt),
             reads=[st.rstd_t], writes=[st.rstd_t])
    k.dve.op(lambda: k.nc.vector.reciprocal(out=st.rstd[:], in_=st.rstd[:]),
             reads=[st.rstd_t], writes=[st.rstd_t])


class WPool:
    def __init__(self, k, name, nslots, elems):
        self.k = k
        self.slots = [k.sb(f"{name}{i}", [128, elems], BF16) for i in range(nslots)]
        self.ts = [T(f"{name}{i}") for i in range(nslots)]
        self.ds = [DmaSem(k, f"{name}{i}") for i in range(nslots)]
        self.i = 0

    def next(self):
        i = self.i
        self.i = (self.i + 1) % len(self.slots)
        return self.slots[i], self.ts[i], self.ds[i]


def carve(st, off_bytes, shape, dt):
    esz = 2 if dt == BF16 else 4
    n = 1
    for s_ in shape[1:]:
        n *= s_
    ap = st.big[:, off_bytes:off_bytes + n * esz].bitcast(dt)
    if len(shape) == 3:
        ap = ap.rearrange("p (a b) -> p a b", a=shape[1])
    return ap


def load_gains(k, st, vecs, scales):
    nc = k.nc
    for i, v in enumerate(vecs):
        k.sp.dma(st.gains[:, i, :], v.rearrange("(kt p) -> p kt", p=128), st.misc_ds,
                 writes=[st.gains_t], allow_slow_non_contiguous=True)
    for i, s_ in enumerate(scales):
        if s_ != 1.0:
            k.dve.op(lambda i=i, s_=s_: nc.vector.tensor_scalar(
                out=st.gains[:, i, :], in0=st.gains[:, i, :], scalar1=s_, scalar2=None, op0=ALU.mult),
                reads=[st.gains_t], writes=[st.gains_t])


def all_h(st):
    return [st.h_t[c][t] for c in range(KT) for t in range(2)]


def load_h(k, st, hT):
    v = hT.rearrange("(kt p) t -> p kt t", p=128)
    ds = DmaSem(k, "hload")
    for q in range(4):
        k.sp.dma(st.h[:, q * 4:(q + 1) * 4, :], v[:, q * 4:(q + 1) * 4, :], ds, writes=all_h(st))


def store_h(k, st, hT_out):
    v = hT_out.rearrange("(kt p) t -> p kt t", p=128)
    ds = DmaSem(k, "hstore")
    k.outsems.append(ds)
    for q in range(4):
        k.sp.dma(v[:, q * 4:(q + 1) * 4, :], st.h[:, q * 4:(q + 1) * 4, :], ds, reads=all_h(st))


def prenorm_half(k, st, th, gi, xn, xn_t):
    nc = k.nc
    tsl = slice(th * TH, (th + 1) * TH)
    for kt in range(KT):
        rms_stats_accum(k, st, st.h[:, kt, tsl], st.h_t[kt][th], 6, kt == 0, kt == KT - 1)
    rstd_from_bank(k, st, 6, D)
    for kt in range(KT):
        k.dve.op(lambda kt=kt: nc.vector.scalar_tensor_tensor(
            out=xn[:, kt, :], in0=st.h[:, kt, tsl], scalar=st.gains[:, gi, kt:kt + 1], in1=st.rstd[:],
            op0=ALU.mult, op1=ALU.mult),
            reads=[st.h_t[kt][th], st.rstd_t, st.gains_t], writes=[xn_t[kt]])


def tail_half(k, st, th, gi, y, y_t):
    nc = k.nc
    tsl = slice(th * TH, (th + 1) * TH)
    rstd_from_bank(k, st, 7, D)
    for dc in range(KT):
        k.dve.op(lambda dc=dc: nc.vector.scalar_tensor_tensor(
            out=y[:, dc, :], in0=y[:, dc, :], scalar=st.gains[:, gi, dc:dc + 1], in1=st.rstd[:],
            op0=ALU.mult, op1=ALU.mult),
            reads=[y_t[dc], st.rstd_t, st.gains_t], writes=[y_t[dc]])
        k.dve.op(lambda dc=dc: nc.vector.tensor_tensor(
            out=st.h[:, dc, tsl], in0=st.h[:, dc, tsl], in1=y[:, dc, :], op=ALU.add),
            reads=[y_t[dc], st.h_t[dc][th]], writes=[st.h_t[dc][th]])


def outproj_half(k, st, wp, w_dram, nkc, rhs_of, rhs_t, y, y_t):
    nc = k.nc
    w_v = w_dram.rearrange("(kc p) n -> p kc n", p=128)
    for dc in range(KT):
        wb, wb_t, wb_ds = wp.next()
        wv = wb[:, 0:nkc * 128].rearrange("p (kc n) -> p kc n", kc=nkc)
        k.pool.dma(wv, w_v[:, :, dc * 128:(dc + 1) * 128], wb_ds, writes=[wb_t])
        bnk = 4 + (dc % 2)
        for kc in range(nkc):
            k.pe.op(lambda kc=kc, bnk=bnk, wv=wv: nc.tensor.matmul(
                st.bank[bnk][:], wv[:, kc, :], rhs_of(kc), start=(kc == 0), stop=(kc == nkc - 1)),
                reads=[wb_t, rhs_t[kc]], writes=[st.bank_t[bnk]], inc=(kc == nkc - 1))
        k.act.op(lambda dc=dc, bnk=bnk: nc.scalar.copy(out=y[:, dc, :], in_=st.bank[bnk][:]),
                 reads=[st.bank_t[bnk]], writes=[y_t[dc]])
        rms_stats_accum(k, st, st.bank[bnk][:], st.bank_t[bnk], 7, dc == 0, dc == KT - 1)


def ffn_half(k, st, wp, th, w_in, w_out, gpre_i, gpost_i):
    nc = k.nc
    xn = carve(st, 0, [128, KT, TH], BF16)
    g = carve(st, 16 * 1024, [128, FC, TH], BF16)
    y = carve(st, 60 * 1024, [128, KT, TH], F32)
    xn_t = [T() for _ in range(KT)]
    g_t = [T() for _ in range(FC)]
    y_t = [T() for _ in range(KT)]
    for t_ in xn_t + g_t + y_t:
        t_.w = st.big_t.w
        t_.r = dict(st.big_t.r)
    prenorm_half(k, st, th, gpre_i, xn, xn_t)
    w_in_v = w_in.rearrange("(kt p) n -> p kt n", p=128)
    BC = 2
    BW = BC * 128
    pb = 0
    for j in range(FC // BC):
        wb, wb_t, wb_ds = wp.next()
        wv = wb[:, 0:KT * 2 * BW].rearrange("p (kt two n) -> p kt two n", kt=KT, two=2)
        k.pool.dma(wv[:, :, 0, :], w_in_v[:, :, j * BW:(j + 1) * BW], wb_ds, writes=[wb_t])
        k.pool.dma(wv[:, :, 1, :], w_in_v[:, :, DFF + j * BW:DFF + (j + 1) * BW], wb_ds, writes=[wb_t])
        for c in range(BC):
            m = j * BC + c
            ba, bb = pb, pb + 1
            pb = (pb + 2) % 4
            for half, bnk in ((0, ba), (1, bb)):
                for kt in range(KT):
                    k.pe.op(lambda kt=kt, half=half, bnk=bnk, c=c, wv=wv: nc.tensor.matmul(
                        st.bank[bnk][:], wv[:, kt, half, c * 128:(c + 1) * 128], xn[:, kt, :],
                        start=(kt == 0), stop=(kt == KT - 1)),
                        reads=[wb_t, xn_t[kt]], writes=[st.bank_t[bnk]], inc=(kt == KT - 1))
            sl, sl_t = st.silu[m % 2], st.silu_t[m % 2]
            k.act.op(lambda ba=ba, sl=sl: nc.scalar.activation(out=sl[:], in_=st.bank[ba][:], func=AF.Silu),
                     reads=[st.bank_t[ba]], writes=[sl_t])
            k.dve.op(lambda bb=bb, sl=sl, m=m: nc.vector.tensor_tensor(
                out=g[:, m, :], in0=st.bank[bb][:], in1=sl[:], op=ALU.mult),
                reads=[st.bank_t[bb], sl_t], writes=[g_t[m]])
    outproj_half(k, st, wp, w_out, FC, lambda kc: g[:, kc, :], g_t, y, y_t)
    tail_half(k, st, th, gpost_i, y, y_t)
    merge_big(st, xn_t + g_t + y_t)


def merge_big(st, ts):
    r = {}
    w = None
    for t_ in ts:
        for tok in list(t_.r.values()) + ([t_.w] if t_.w is not None else []):
            old = r.get(tok[0])
            if old is None or (old[1], old[2]) < (tok[1], tok[2]):
                r[tok[0]] = tok
    st.big_t.w = None
    st.big_t.r = r


def ple_half(k, st, wp, th, w_gate, w_proj, pT_dram, gpre_i, gpost_i):
    nc = k.nc
    xn = carve(st, 0, [128, KT, TH], BF16)
    pt = carve(st, 16 * 1024, [128, 2, TH], BF16)
    sg = carve(st, 20 * 1024, [128, 2, TH], F32)
    y = carve(st, 60 * 1024, [128, KT, TH], F32)
    xn_t = [T() for _ in range(KT)]
    y_t = [T() for _ in range(KT)]
    pt_t = T()
    sg_t = [T(), T()]
    for t_ in xn_t + y_t + [pt_t] + sg_t:
        t_.w = st.big_t.w
        t_.r = dict(st.big_t.r)
    tsl = slice(th * TH, (th + 1) * TH)
    ds = DmaSem(k, f"pt{st.uid()}")
    k.pool.dma(pt, pT_dram.rearrange("(kc p) t -> p kc t", p=128)[:, :, tsl], ds, writes=[pt_t])
    prenorm_half(k, st, th, gpre_i, xn, xn_t)
    wg_v = w_gate.rearrange("(kt p) n -> p kt n", p=128)
    wp_v = w_proj.rearrange("(kt p) n -> p kt n", p=128)
    for blk in range(8):
        wb, wb_t, wb_ds = wp.next()
        wv = wb[:, 0:(KT + 2) * 256].rearrange("p (kt n) -> p kt n", kt=KT + 2)
        k.pool.dma(wv[:, 0:KT, :], wg_v[:, :, blk * 256:(blk + 1) * 256], wb_ds, writes=[wb_t])
        k.pool.dma(wv[:, KT:KT + 2, :], wp_v[:, :, blk * 256:(blk + 1) * 256], wb_ds, writes=[wb_t])
        for c in range(2):
            dc = blk * 2 + c
            bg, bp = (0, 1) if dc % 2 == 0 else (2, 3)
            for kt in range(KT):
                k.pe.op(lambda kt=kt, bg=bg, c=c, wv=wv: nc.tensor.matmul(
                    st.bank[bg][:], wv[:, kt, c * 128:(c + 1) * 128], xn[:, kt, :],
                    start=(kt == 0), stop=(kt == KT - 1)),
                    reads=[wb_t, xn_t[kt]], writes=[st.bank_t[bg]], inc=(kt == KT - 1))
            for kc in range(2):
                k.pe.op(lambda kc=kc, bp=bp, c=c, wv=wv: nc.tensor.matmul(
                    st.bank[bp][:], wv[:, KT + kc, c * 128:(c + 1) * 128], pt[:, kc, :],
                    start=(kc == 0), stop=(kc == 1)),
                    reads=[wb_t, pt_t], writes=[st.bank_t[bp]], inc=(kc == 1))
            s_, s_t = sg[:, dc % 2, :], sg_t[dc % 2]
            k.act.op(lambda bg=bg, s_=s_: nc.scalar.activation(out=s_, in_=st.bank[bg][:], func=AF.Sigmoid),
                     reads=[st.bank_t[bg]], writes=[s_t])
            k.dve.op(lambda bp=bp, s_=s_, dc=dc: nc.vector.tensor_tensor(
                out=y[:, dc, :], in0=st.bank[bp][:], in1=s_, op=ALU.mult),
                reads=[st.bank_t[bp], s_t], writes=[y_t[dc]])
            rms_stats_accum(k, st, y[:, dc, :], y_t[dc], 7, dc == 0, dc == KT - 1)
    tail_half(k, st, th, gpost_i, y, y_t)
    merge_big(st, xn_t + y_t + [pt_t] + sg_t)


class Stager:
    def __init__(self, k, st, n=2):
        self.k = k
        self.tiles = [k.sb(f"stg{i}", [128, NT], BF16) for i in range(n)]
        self.ts = [T(f"stg{i}") for i in range(n)]
        self.ds = [DmaSem(k, f"stg{i}") for i in range(n)]
        self.i = 0
        self.stores = []

    def next(self):
        i = self.i
        self.i = (self.i + 1) % len(self.tiles)
        return self.tiles[i], self.ts[i], self.ds[i]

    def store(self, dst, src, t_, ds):
        self.k.sp.dma(dst, src, ds, reads=[t_])
        self.stores.append((ds.group, 0, ds.n, ds.sem))

    def barrier(self, eng):
        eng._need(self.stores)
        self.stores = []


def load_w_slot(k, wp, view_ap, nk, ncols, pieces):
    wb, wb_t, wb_ds = wp.next()
    wv = wb[:, 0:nk * ncols].rearrange("p (kt n) -> p kt n", kt=nk)
    for c0, src in pieces:
        w = src.shape[-1]
        k.pool.dma(wv[:, :, c0:c0 + w], src, wb_ds, writes=[wb_t])
    return wv, wb_t


def mm_group(k, st, bank_i, lhs_list, rhs_list, reads, M=128, N=TH):
    nc = k.nc
    n = len(lhs_list)
    for i in range(n):
        k.pe.op(lambda i=i: nc.tensor.matmul(st.bank[bank_i][0:M, 0:N], lhs_list[i], rhs_list[i],
                                            start=(i == 0), stop=(i == n - 1)),
                reads=reads[i], writes=[st.bank_t[bank_i]], inc=(i == n - 1))


def hn_all(k, st, gi):
    hn = carve(st, 0, [128, KT, NT], BF16)
    hn_t = [T() for _ in range(KT)]
    for t_ in hn_t:
        t_.w = st.big_t.w
        t_.r = dict(st.big_t.r)
    for th in range(2):
        prenorm_half(k, st, th, gi, hn[:, :, th * TH:(th + 1) * TH], hn_t)
    return hn, hn_t


def fm_plain(k, st, sg, wv, wb_t, nk, c0, M, rhs_of, rhs_t, dst_rows, pbank):
    nc = k.nc
    stg, stg_t, stg_ds = sg.next()
    for th in range(2):
        b = pbank[0]
        pbank[0] = (pbank[0] + 1) % 4
        mm_group(k, st, b, [wv[:, kt, c0:c0 + M] for kt in range(nk)], [rhs_of(kt, th) for kt in range(nk)],
                 [[wb_t, rhs_t[kt]] for kt in range(nk)], M=M)
        k.act.op(lambda b=b, th=th: nc.scalar.copy(out=stg[0:M, th * TH:(th + 1) * TH], in_=st.bank[b][0:M, :]),
                 reads=[st.bank_t[b]], writes=[stg_t])
    sg.store(dst_rows, stg[0:M, :], stg_t, stg_ds)


def fm_rope(k, st, sg, wv, wb_t, nk, c0, M, wvr, wbr_t, cr0, R, rhs_of, rhs_t, dst_rows, pbank, cc, ss, tmp1, tmp2, tmp_t):
    nc = k.nc
    stg, stg_t, stg_ds = sg.next()
    for th in range(2):
        tsl = slice(th * TH, (th + 1) * TH)
        b = pbank[0]
        b2 = (b + 1) % 4
        pbank[0] = (pbank[0] + 2) % 4
        mm_group(k, st, b, [wv[:, kt, c0:c0 + M] for kt in range(nk)], [rhs_of(kt, th) for kt in range(nk)],
                 [[wb_t, rhs_t[kt]] for kt in range(nk)], M=M)
        mm_group(k, st, b2, [wvr[:, kt, cr0:cr0 + R] for kt in range(nk)], [rhs_of(kt, th) for kt in range(nk)],
                 [[wbr_t, rhs_t[kt]] for kt in range(nk)], M=R)
        k.dve.op(lambda b=b, tsl=tsl: nc.vector.tensor_tensor(out=tmp1[0:R, :], in0=st.bank[b][0:R, :], in1=cc[0:R, tsl], op=ALU.mult),
                 reads=[st.bank_t[b], st.rope_t], writes=[tmp_t[0]])
        k.dve.op(lambda b2=b2, tsl=tsl: nc.vector.tensor_tensor(out=tmp2[0:R, :], in0=st.bank[b2][0:R, :], in1=ss[0:R, tsl], op=ALU.mult),
                 reads=[st.bank_t[b2], st.rope_t], writes=[tmp_t[1]])
        k.dve.op(lambda tsl=tsl: nc.vector.tensor_tensor(out=stg[0:R, tsl], in0=tmp1[0:R, :], in1=tmp2[0:R, :], op=ALU.add),
                 reads=[tmp_t[0], tmp_t[1]], writes=[stg_t])
        if M > R:
            for (p0, p1) in ((32, 64), (64, 128)):
                k.act.op(lambda b=b, tsl=tsl, p0=p0, p1=p1: nc.scalar.copy(out=stg[p0:p1, tsl], in_=st.bank[b][p0:p1, :]),
                         reads=[st.bank_t[b]], writes=[stg_t])
    sg.store(dst_rows, stg[0:M, :], stg_t, stg_ds)


def tm_proj(k, st, sg, wv, wb_t, nk, ncols, lhs_of, lhs_t, dstV, pbank, col_of=None):
    nc = k.nc
    for tt in range(NT // 128):
        b = pbank[0]
        pbank[0] = (pbank[0] + 1) % 4
        stg, stg_t, stg_ds = sg.next()
        if col_of is None:
            mm_group(k, st, b, [lhs_of(kt, tt) for kt in range(nk)], [wv[:, kt, 0:ncols] for kt in range(nk)],
                     [[wb_t, lhs_t[kt]] for kt in range(nk)], M=128, N=ncols)
        else:
            for (o0, c0, w) in col_of:
                k_last = (o0, c0, w) == col_of[-1]
                for kt in range(nk):
                    k.pe.op(lambda kt=kt, o0=o0, c0=c0, w=w: nc.tensor.matmul(
                        st.bank[b][:, o0:o0 + w], lhs_of(kt, tt), wv[:, kt, c0:c0 + w], start=(kt == 0), stop=(kt == nk - 1)),
                        reads=[wb_t, lhs_t[kt]], writes=[st.bank_t[b]], inc=(k_last and kt == nk - 1))
        k.act.op(lambda b=b: nc.scalar.copy(out=stg[:, 0:ncols], in_=st.bank[b][:, 0:ncols]),
                 reads=[st.bank_t[b]], writes=[stg_t])
        sg.store(dstV[tt * 128:(tt + 1) * 128, :], stg[:, 0:ncols], stg_t, stg_ds)


def wview(w2d, nk):
    return w2d.rearrange("(kt p) n -> p kt n", p=128)


def load_rope(k, st, rope_dram, R):
    cc = carve(st, 88 * 1024, [128, NT], F32)
    ss = carve(st, 92 * 1024, [128, NT], F32)
    st.rope_t = T("rope")
    st.rope_t.w = st.big_t.w
    st.rope_t.r = dict(st.big_t.r)
    ds = DmaSem(k, f"rope{st.uid()}")
    k.sp.dma(cc[0:R, :], rope_dram[0], ds, writes=[st.rope_t])
    k.sp.dma(ss[0:R, :], rope_dram[1], ds, writes=[st.rope_t])
    return cc, ss


def proj_even(k, st, wp, sg, W, j, gi, rope_b, QT, KT_, V):
    nc = k.nc
    hn, hn_t = hn_all(k, st, gi)
    cc, ss = load_rope(k, st, rope_b, 64)
    cq = carve(st, 32 * 1024, [128, 4, NT], F32)
    ckv = carve(st, 48 * 1024, [128, 2, NT], F32)
    cqn = carve(st, 56 * 1024, [128, 4, NT], BF16)
    ckvn = carve(st, 64 * 1024, [128, 2, NT], BF16)
    tmp1 = carve(st, 68 * 1024, [128, TH], F32)
    tmp2 = carve(st, 70 * 1024, [128, TH], F32)
    tmp_t = [T(), T()]
    cq_t = [T() for _ in range(4)]
    ckv_t = [T() for _ in range(2)]
    cqn_t = [T() for _ in range(4)]
    ckvn_t = [T() for _ in range(2)]
    for t_ in tmp_t + cq_t + ckv_t + cqn_t + ckvn_t:
        t_.w = st.big_t.w
        t_.r = dict(st.big_t.r)
    pbank = [0]
    w_in = wview(W("ab_w_in", j), KT)
    rhs_of = lambda kt, th: hn[:, kt, th * TH:(th + 1) * TH]
    for grp, dst in ((0, QT), (1, KT_)):
        for blk in range(2):
            c0 = grp * 1024 + blk * 512
            wv, wb_t = load_w_slot(k, wp, None, KT, 512, [(0, w_in[:, :, c0:c0 + 512])])
            for c in range(4):
                r0 = blk * 512 + c * 128
                fm_plain(k, st, sg, wv, wb_t, KT, c * 128, 128, rhs_of, hn_t, dst[r0:r0 + 128, :], pbank)
    for blk in range(2):
        c0 = 2048 + blk * 512
        wv, wb_t = load_w_slot(k, wp, None, KT, 512, [(0, w_in[:, :, c0:c0 + 512])])
        tm_proj(k, st, sg, wv, wb_t, KT, 512, lambda kt, tt: hn[:, kt, tt * 128:(tt + 1) * 128], hn_t,
                V[:, blk * 512:(blk + 1) * 512], pbank)
    wv, wb_t = load_w_slot(k, wp, None, KT, 512, [(0, w_in[:, :, 3072:3584])])
    for c in range(4):
        for th in range(2):
            b = pbank[0]
            pbank[0] = (pbank[0] + 1) % 4
            mm_group(k, st, b, [wv[:, kt, c * 128:(c + 1) * 128] for kt in range(KT)], [rhs_of(kt, th) for kt in range(KT)],
                     [[wb_t, hn_t[kt]] for kt in range(KT)])
            k.act.op(lambda b=b, c=c, th=th: nc.scalar.copy(out=cq[:, c, th * TH:(th + 1) * TH], in_=st.bank[b][:]),
                     reads=[st.bank_t[b]], writes=[cq_t[c]])
    krp = wview(W("ab_w_in_krp", j), KT)
    wv, wb_t = load_w_slot(k, wp, None, KT, 512, [(0, w_in[:, :, 3584:3904]), (320, krp)])
    for c in range(2):
        for th in range(2):
            b = pbank[0]
            pbank[0] = (pbank[0] + 1) % 4
            mm_group(k, st, b, [wv[:, kt, c * 128:(c + 1) * 128] for kt in range(KT)], [rhs_of(kt, th) for kt in range(KT)],
                     [[wb_t, hn_t[kt]] for kt in range(KT)])
            k.act.op(lambda b=b, c=c, th=th: nc.scalar.copy(out=ckv[:, c, th * TH:(th + 1) * TH], in_=st.bank[b][:]),
                     reads=[st.bank_t[b]], writes=[ckv_t[c]])
    fm_rope(k, st, sg, wv, wb_t, KT, 256, 64, wv, wb_t, 320, 64, rhs_of, hn_t, KT_[2048:2112, :], pbank, cc, ss, tmp1, tmp2, tmp_t)
    for (src, src_t, dstn, dstn_t, n, gidx, nfeat) in ((cq, cq_t, cqn, cqn_t, 4, 8, 512), (ckv, ckv_t, ckvn, ckvn_t, 2, 9, 256)):
        for th in range(2):
            tsl = slice(th * TH, (th + 1) * TH)
            for c in range(n):
                rms_stats_accum(k, st, src[:, c, tsl], src_t[c], 6, c == 0, c == n - 1)
            rstd_from_bank(k, st, 6, nfeat)
            for c in range(n):
                k.dve.op(lambda c=c, tsl=tsl, src=src, dstn=dstn, gidx=gidx: nc.vector.scalar_tensor_tensor(
                    out=dstn[:, c, tsl], in0=src[:, c, tsl], scalar=st.gains[:, gidx, c:c + 1], in1=st.rstd[:],
                    op0=ALU.mult, op1=ALU.mult),
                    reads=[src_t[c], st.rstd_t, st.gains_t], writes=[dstn_t[c]])
    qup = wview(W("b_w_qup", j), 4)
    qrp = wview(W("b_w_qup_rp", j), 4)
    wvq, wbq_t = load_w_slot(k, wp, None, 4, 2048, [(0, qup), (1536, qrp)])
    rhs_q = lambda kt, th: cqn[:, kt, th * TH:(th + 1) * TH]
    for hh in range(8):
        fm_plain(k, st, sg, wvq, wbq_t, 4, hh * 192, 128, rhs_q, cqn_t, QT[1024 + hh * 128:1024 + (hh + 1) * 128, :], pbank)
        fm_rope(k, st, sg, wvq, wbq_t, 4, hh * 192 + 128, 64, wvq, wbq_t, 1536 + hh * 64, 64, rhs_q, cqn_t,
                QT[2048 + hh * 64:2048 + (hh + 1) * 64, :], pbank, cc, ss, tmp1, tmp2, tmp_t)
    kvup = wview(W("b_w_kvup", j), 2)
    wvk, wbk_t = load_w_slot(k, wp, None, 2, 2048, [(0, kvup)])
    rhs_k = lambda kt, th: ckvn[:, kt, th * TH:(th + 1) * TH]
    for hh in range(8):
        fm_plain(k, st, sg, wvk, wbk_t, 2, hh * 256, 128, rhs_k, ckvn_t, KT_[1024 + hh * 128:1024 + (hh + 1) * 128, :], pbank)
    for half in range(2):
        tm_proj(k, st, sg, wvk, wbk_t, 2, 512, lambda kt, tt: ckvn[:, kt, tt * 128:(tt + 1) * 128], ckvn_t,
                V[:, 1024 + half * 512:1024 + (half + 1) * 512], pbank,
                col_of=[(i * 128, (half * 4 + i) * 256 + 128, 128) for i in range(4)])
    merge_big(st, hn_t + tmp_t + cq_t + ckv_t + cqn_t + ckvn_t + [st.rope_t])


def proj_odd(k, st, wp, sg, W, j, gi, rope_c, QT, KT_, V):
    nc = k.nc
    hn, hn_t = hn_all(k, st, gi)
    cc, ss = load_rope(k, st, rope_c, 32)
    tmp1 = carve(st, 68 * 1024, [128, TH], F32)
    tmp2 = carve(st, 70 * 1024, [128, TH], F32)
    tmp_t = [T(), T()]
    for t_ in tmp_t:
        t_.w = st.big_t.w
        t_.r = dict(st.big_t.r)
    pbank = [0]
    w_in = wview(W("c_w_in", j), KT)
    w_rp = wview(W("c_w_in_rp", j), KT)
    rhs_of = lambda kt, th: hn[:, kt, th * TH:(th + 1) * TH]
    for hh in range(8):
        wv, wb_t = load_w_slot(k, wp, None, KT, 512, [(0, w_in[:, :, hh * 768:hh * 768 + 512])])
        wvr, wbr_t = load_w_slot(k, wp, None, KT, 512, [(0, w_rp[:, :, hh * 128:(hh + 1) * 128]),
                                                        (128, w_in[:, :, hh * 768 + 512:hh * 768 + 768])])
        for which in range(4):
            dst = (QT if which < 2 else KT_)[hh * 256 + (which % 2) * 128: hh * 256 + (which % 2) * 128 + 128, :]
            fm_rope(k, st, sg, wv, wb_t, KT, which * 128, 128, wvr, wbr_t, which * 32, 32, rhs_of, hn_t, dst, pbank,
                    cc, ss, tmp1, tmp2, tmp_t)
        tm_proj(k, st, sg, wvr, wbr_t, KT, 256, lambda kt, tt: hn[:, kt, tt * 128:(tt + 1) * 128], hn_t,
                V[:, hh * 256:(hh + 1) * 256], pbank, col_of=[(0, 128, 256)])
    merge_big(st, hn_t + tmp_t + [st.rope_t])


class RowChunks:
    def __init__(self, chunks):
        self.chunks = chunks

    def __getitem__(self, key):
        rs, cs = key
        for (r0, n, ap) in self.chunks:
            if r0 <= rs.start and rs.stop <= r0 + n:
                return ap[rs.start - r0:rs.stop - r0, cs]
        raise IndexError(f"rows {rs} straddle chunks")


class VView:
    def __init__(self, chunk_aps, c0=0, c1=2048):
        self.v = [ap.rearrange("(t a) c -> t (a c)", a=2) for ap in chunk_aps]
        self.c0, self.c1 = c0, c1
        self.raw = chunk_aps

    def __getitem__(self, key):
        rs, cs = key
        if rs == slice(None):
            nv = VView(self.raw, self.c0 + cs.start, self.c0 + cs.stop)
            return nv
        ch = rs.start // 512
        assert (rs.stop - 1) // 512 == ch
        cc0 = self.c0 + (cs.start or 0) if cs != slice(None) else self.c0
        cc1 = self.c0 + cs.stop if cs != slice(None) else self.c1
        return self.v[ch][rs.start - ch * 512:rs.stop - ch * 512, cc0:cc1]

    def half(self, hf):
        return self.v[hf][:, self.c0:self.c1]


class AttnCtx:
    def __init__(self, k, st):
        self.k = k
        self.st = st
        self.ao = carve(st, 0, [128, KT, NT], BF16)
        self.ao_t = [T() for _ in range(KT)]
        self.sets = []
        for s_ in range(2):
            base = 32 * 1024 + s_ * 24 * 1024
            d = dict(
                q=[carve(st, base + i * 2048, [128, NT], BF16) for i in range(2)],
                ko=[carve(st, base + 4096 + i * 2048, [128, NT], BF16) for i in range(2)],
                kr=[carve(st, base + 8192 + i * 2048, [128, NT], BF16) for i in range(2)],
                vo=carve(st, base + 12288, [128, 8, 256], BF16),
                vr=carve(st, base + 16384, [128, 8, 256], BF16),
                t=T(), ds=DmaSem(k, f"hs{st.uid()}"))
            self.sets.append(d)
        self.pt = [carve(st, 80 * 1024 + i * 256, [128, 128], BF16) for i in range(8)]
        self.pt_t = [T() for _ in range(8)]
        self.pt_i = 0
        self.dg_i = 0
        self.bias = [carve(st, 82 * 1024 + i * 1536, [128, 3, 128], F32) for i in range(2)]
        self.bias_t = [T(), T()]
        self.bias_ds = [DmaSem(k, f"bs{st.uid()}") for _ in range(2)]
        self.rinv = [carve(st, 86 * 1024 + i * 512, [128, 128], F32) for i in range(2)]
        self.rinv_t = [T(), T()]
        self.o1n = carve(st, 87 * 1024, [128, 256], F32)
        self.o2n = carve(st, 88 * 1024, [128, 256], F32)
        self.dd = carve(st, 89 * 1024, [128, 256], F32)
        self.sq2 = carve(st, 90 * 1024, [128, 256], F32)
        self.stmp = carve(st, 91 * 1024, [128, 128], F32)
        self.tmp_t = [T() for _ in range(5)]
        self.sslot_t = [T() for _ in range(8)]
        self.ss_i = 0
        every = self.ao_t + [d["t"] for d in self.sets] + self.pt_t + self.bias_t + self.rinv_t + self.tmp_t
        for t_ in every:
            t_.w = st.big_t.w
            t_.r = dict(st.big_t.r)
        self.every = every
        for i in (6, 7):
            k.dve.op(lambda i=i: k.nc.vector.memset(self.pt[i][:], 0.0), writes=[self.pt_t[i]])

    def done(self):
        merge_big(self.st, self.every)


def attn_tile(k, st, ax, hs, i, streams, blocks, dv, finalize):
    nc = k.nc
    qsl = slice(i * 128, (i + 1) * 128)
    nb = len(blocks)
    for bi, (is_rem, j, kind, bias_ap, bt) in enumerate(blocks):
        ksl = slice(j * 128, (j + 1) * 128)
        for si, (parts, scale) in enumerate(streams):
            s_i = ax.ss_i
            ax.ss_i = (ax.ss_i + 1) % 8
            sb_, so = s_i // 4, (s_i % 4) * 128
            S = st.bank[sb_][:, so:so + 128]
            S_t = ax.sslot_t[s_i]
            for pi, (qi, ki, Kp) in enumerate(parts):
                kt_ = (hs["kr"] if is_rem else hs["ko"])[ki]
                k.pe.op(lambda kt_=kt_, qi=qi, Kp=Kp, S=S, pi=pi: nc.tensor.matmul(
                    S, kt_[0:Kp, ksl], hs["q"][qi][0:Kp, qsl], start=(pi == 0), stop=(pi == len(parts) - 1)),
                    reads=[hs["t"]], writes=[S_t], inc=(pi == len(parts) - 1))
            if kind == "diag":
                p_i = 6 + ax.dg_i
                ax.dg_i ^= 1
            else:
                p_i = ax.pt_i
                ax.pt_i = (ax.pt_i + 1) % 6
            PT, PT_t = ax.pt[p_i], ax.pt_t[p_i]
            if kind == "tile":
                k.dve.op(lambda S=S, bt=bt, scale=scale: nc.vector.scalar_tensor_tensor(
                    out=ax.stmp[:], in0=S, scalar=scale, in1=ax.cur_bias[:, bt, :], op0=ALU.mult, op1=ALU.add),
                    reads=[S_t, ax.cur_bias_t], writes=[ax.tmp_t[4]])
                k.act.op(lambda PT=PT, bias_ap=bias_ap: nc.scalar.activation(out=PT[:], in_=ax.stmp[:], func=AF.Exp, bias=bias_ap, scale=1.0),
                         reads=[ax.tmp_t[4], st.cst_t], writes=[PT_t])
            elif kind == "diag":
                k.act.op(lambda PT=PT, S=S, scale=scale, bias_ap=bias_ap: nc.scalar.activation(
                    out=PT[0:64, :], in_=S[0:64, :], func=AF.Exp, bias=bias_ap[0:64, :], scale=scale),
                    reads=[S_t, st.cst_t], writes=[PT_t])
                k.act.op(lambda PT=PT, S=S, scale=scale, bias_ap=bias_ap: nc.scalar.activation(
                    out=PT[64:128, 64:128], in_=S[64:128, 64:128], func=AF.Exp, bias=bias_ap[64:128, :], scale=scale),
                    reads=[S_t, st.cst_t], writes=[PT_t])
            else:
                k.act.op(lambda PT=PT, S=S, scale=scale, bias_ap=bias_ap: nc.scalar.activation(
                    out=PT[:], in_=S, func=AF.Exp, bias=bias_ap, scale=scale),
                    reads=[S_t, st.cst_t], writes=[PT_t])
            vt = hs["vr"] if is_rem else hs["vo"]
            ob, sb2 = 2 + 2 * si, 3 + 2 * si
            for c in range(dv // 128):
                k.pe.op(lambda vt=vt, c=c, PT=PT, ob=ob: nc.tensor.matmul(
                    st.bank[ob][:, c * 128:(c + 1) * 128], vt[:, j, c * 128:(c + 1) * 128], PT[:],
                    start=(bi == 0 and c == 0), stop=(bi == nb - 1)),
                    reads=[hs["t"], PT_t], writes=[st.bank_t[ob]], inc=False)
            k.pe.op(lambda PT=PT, sb2=sb2: nc.tensor.matmul(
                st.bank[sb2][:, 0:128], st.ones_bf[:], PT[:], start=(bi == 0), stop=(bi == nb - 1)),
                reads=[PT_t, st.ones_t], writes=[st.bank_t[sb2]], inc=True)
    finalize(i)


def load_head(k, ax, hs, qsrc, kosrc, krsrc, vosrc, vrsrc, dv):
    for idx, (ap, R) in enumerate(qsrc):
        k.sp.dma(hs["q"][idx][0:R, :], ap, hs["ds"], writes=[hs["t"]])
    for idx, (ap, R) in enumerate(kosrc):
        k.sp.dma(hs["ko"][idx][0:R, :], ap, hs["ds"], writes=[hs["t"]])
    for idx, (ap, R) in enumerate(krsrc):
        k.sp.dma(hs["kr"][idx][0:R, :], ap, hs["ds"], writes=[hs["t"]])
    for hf in range(2):
        k.sp.dma(hs["vo"][:, hf * 4:(hf + 1) * 4, 0:dv], vosrc.half(hf).rearrange("(j p) d -> p j d", p=128), hs["ds"], writes=[hs["t"]])
        k.sp.dma(hs["vr"][:, hf * 4:(hf + 1) * 4, 0:dv], vrsrc.half(hf).rearrange("(j p) d -> p j d", p=128), hs["ds"], writes=[hs["t"]])


def fin_simple(k, st, ax, chunk):
    nc = k.nc

    def f(i):
        qsl = slice(i * 128, (i + 1) * 128)
        r, r_t = ax.rinv[0], ax.rinv_t[0]
        k.dve.op(lambda: nc.vector.reciprocal(out=r[:], in_=st.bank[3][:, 0:128]), reads=[st.bank_t[3]], writes=[r_t])
        k.dve.op(lambda: nc.vector.tensor_tensor(out=ax.ao[:, chunk, qsl], in0=st.bank[2][:, 0:128], in1=r[:], op=ALU.mult),
                 reads=[st.bank_t[2], r_t], writes=[ax.ao_t[chunk]])
    return f


def attn_even(k, st, sg, j, QT, KTo, Vo, KTr, Vr, abias, acv):
    nc = k.nc
    ax = AttnCtx(k, st)
    cb = st.cb
    ds = DmaSem(k, f"cb{st.uid()}")
    k.sp.dma(cb[:, 0:8], acv[j].partition_broadcast(128), ds, writes=[st.cst_t])
    k.dve.op(lambda: nc.vector.tensor_scalar(out=cb[:, 8:16], in0=cb[:, 0:8], scalar1=st.rb[:, 0:1], scalar2=None, op0=ALU.add),
             reads=[st.cst_t], writes=[st.cst_t])
    sA = 128 ** -0.5
    for hh in range(8):
        hs = ax.sets[hh % 2]
        load_head(k, ax, hs, [(QT[hh * 128:(hh + 1) * 128, :], 128)], [(KTo[hh * 128:(hh + 1) * 128, :], 128)],
                  [(KTr[hh * 128:(hh + 1) * 128, :], 128)], Vo[:, hh * 128:(hh + 1) * 128], Vr[:, hh * 128:(hh + 1) * 128], 128)
        bt, bt_t, bt_ds = ax.bias[hh % 2], ax.bias_t[hh % 2], ax.bias_ds[hh % 2]
        k.sp.dma(bt, abias[j, hh].rearrange("t k q -> k t q"), bt_ds, writes=[bt_t])
        ax.cur_bias, ax.cur_bias_t = bt, bt_t
        for i in range(8):
            blocks = []
            for d_ in range(4, -1, -1):
                jg = i - d_
                rem = jg < 0
                jj = jg + 8 if rem else jg
                if d_ in (2, 3):
                    blocks.append((rem, jj, "plain", cb[:, (8 if rem else 0) + hh:(8 if rem else 0) + hh + 1], None))
                else:
                    tix = {4: 0, 1: 1, 0: 2}[d_]
                    blocks.append((rem, jj, "tile", (st.rb if rem else st.zb)[:, 0:1], tix))
            attn_tile(k, st, ax, hs, i, [([(0, 0, 128)], sA)], blocks, 128, fin_simple(k, st, ax, hh))
    sB = 192 ** -0.5
    for hh in range(8):
        hs = ax.sets[hh % 2]
        load_head(k, ax, hs,
                  [(QT[1024 + hh * 128:1024 + (hh + 1) * 128, :], 128), (QT[2048 + hh * 64:2048 + (hh + 1) * 64, :], 64)],
                  [(KTo[1024 + hh * 128:1024 + (hh + 1) * 128, :], 128), (KTo[2048:2112, :], 64)],
                  [(KTr[1024 + hh * 128:1024 + (hh + 1) * 128, :], 128), (KTr[2048:2112, :], 64)],
                  Vo[:, 1024 + hh * 128:1024 + (hh + 1) * 128], Vr[:, 1024 + hh * 128:1024 + (hh + 1) * 128], 128)
        for i in range(8):
            blocks = [(True, jj, "plain", st.rb[:, 0:1], None) for jj in range(8)]
            blocks += [(False, jj, "diag" if jj == i else "plain", st.zb[:, 0:1], None) for jj in range(i + 1)]
            attn_tile(k, st, ax, hs, i, [([(0, 0, 128), (1, 1, 64)], sB)], blocks, 128, fin_simple(k, st, ax, 8 + hh))
    return ax


def attn_odd(k, st, sg, j, layer, QT, KTo, Vo, KTr, Vr, lvec, gsub):
    nc = k.nc
    ax = AttnCtx(k, st)
    lam_init = 0.8 - 0.6 * math.exp(-0.3 * layer)
    lv = st.lv
    ds = DmaSem(k, f"lv{st.uid()}")
    for q in range(4):
        k.sp.dma(lv[:, q:q + 1], lvec[q][j].rearrange("(p o) -> p o", o=1), ds, writes=[st.cst_t])
    k.sp.dma(lv[:, 8:10], gsub[j].rearrange("(c p) -> p c", p=128), ds, writes=[st.cst_t], allow_slow_non_contiguous=True)
    k.dve.op(lambda: nc.vector.tensor_tensor(out=lv[:, 4:5], in0=lv[:, 0:1], in1=lv[:, 1:2], op=ALU.mult), reads=[st.cst_t], writes=[st.cst_t])
    k.dve.op(lambda: nc.vector.tensor_tensor(out=lv[:, 5:6], in0=lv[:, 2:3], in1=lv[:, 3:4], op=ALU.mult), reads=[st.cst_t], writes=[st.cst_t])
    k.pe.op(lambda: nc.tensor.matmul(st.bank[6][:, 0:2], st.ones_f[:], lv[:, 4:6], start=True, stop=True),
            reads=[st.cst_t, st.ones_t], writes=[st.bank_t[6]], inc=True)
    k.act.op(lambda: nc.scalar.activation(out=lv[:, 6:8], in_=st.bank[6][:, 0:2], func=AF.Exp), reads=[st.bank_t[6]], writes=[st.cst_t])
    k.dve.op(lambda: nc.vector.tensor_tensor(out=lv[:, 4:5], in0=lv[:, 7:8], in1=lv[:, 6:7], op=ALU.subtract), reads=[st.cst_t], writes=[st.cst_t])
    k.dve.op(lambda: nc.vector.tensor_scalar(out=lv[:, 4:5], in0=lv[:, 4:5], scalar1=-lam_init, scalar2=None, op0=ALU.add),
             reads=[st.cst_t], writes=[st.cst_t])
    k.dve.op(lambda: nc.vector.tensor_scalar(out=lv[:, 8:10], in0=lv[:, 8:10], scalar1=1.0 - lam_init, scalar2=None, op0=ALU.mult),
             reads=[st.cst_t], writes=[st.cst_t])
    sC = 128 ** -0.5

    def fin(hh):
        def f(i):
            qsl = slice(i * 128, (i + 1) * 128)
            for si, (dst, dst_t) in enumerate(((ax.o1n, ax.tmp_t[0]), (ax.o2n, ax.tmp_t[1]))):
                r, r_t = ax.rinv[si], ax.rinv_t[si]
                k.dve.op(lambda r=r, si=si: nc.vector.reciprocal(out=r[:], in_=st.bank[3 + 2 * si][:, 0:128]),
                         reads=[st.bank_t[3 + 2 * si]], writes=[r_t])
                for c in range(2):
                    k.dve.op(lambda r=r, si=si, c=c, dst=dst: nc.vector.tensor_tensor(
                        out=dst[:, c * 128:(c + 1) * 128], in0=st.bank[2 + 2 * si][:, c * 128:(c + 1) * 128], in1=r[:], op=ALU.mult),
                        reads=[st.bank_t[2 + 2 * si], r_t], writes=[dst_t])
            k.dve.op(lambda: nc.vector.scalar_tensor_tensor(out=ax.dd[:], in0=ax.o2n[:], scalar=lv[:, 4:5], in1=ax.o1n[:],
                                                           op0=ALU.mult, op1=ALU.add),
                     reads=[ax.tmp_t[0], ax.tmp_t[1], st.cst_t], writes=[ax.tmp_t[2]])
            k.act.op(lambda: nc.scalar.activation(out=ax.sq2[:], in_=ax.dd[:], func=AF.Square), reads=[ax.tmp_t[2]], writes=[ax.tmp_t[3]])
            for c in range(2):
                k.pe.op(lambda c=c: nc.tensor.matmul(st.bank[6][:, 0:128], st.ones_f[:], ax.sq2[:, c * 128:(c + 1) * 128],
                                                     start=(c == 0), stop=(c == 1)),
                        reads=[ax.tmp_t[3], st.ones_t], writes=[st.bank_t[6]], inc=(c == 1))
            k.dve.op(lambda: nc.vector.tensor_scalar(out=ax.stmp[:], in0=st.bank[6][:, 0:128], scalar1=1.0 / 256, scalar2=EPS,
                                                    op0=ALU.mult, op1=ALU.add), reads=[st.bank_t[6]], writes=[ax.tmp_t[4]])
            k.act.op(lambda: nc.scalar.activation(out=ax.stmp[:], in_=ax.stmp[:], func=AF.Sqrt), reads=[ax.tmp_t[4]], writes=[ax.tmp_t[4]])
            k.dve.op(lambda: nc.vector.reciprocal(out=ax.stmp[:], in_=ax.stmp[:]), reads=[ax.tmp_t[4]], writes=[ax.tmp_t[4]])
            for c in range(2):
                k.dve.op(lambda c=c: nc.vector.scalar_tensor_tensor(
                    out=ax.ao[:, hh * 2 + c, qsl], in0=ax.dd[:, c * 128:(c + 1) * 128], scalar=lv[:, 8 + c:9 + c], in1=ax.stmp[:],
                    op0=ALU.mult, op1=ALU.mult),
                    reads=[ax.tmp_t[2], ax.tmp_t[4], st.cst_t], writes=[ax.ao_t[hh * 2 + c]])
        return f

    for hh in range(8):
        hs = ax.sets[hh % 2]
        r0 = hh * 256
        load_head(k, ax, hs, [(QT[r0:r0 + 128, :], 128), (QT[r0 + 128:r0 + 256, :], 128)],
                  [(KTo[r0:r0 + 128, :], 128), (KTo[r0 + 128:r0 + 256, :], 128)],
                  [(KTr[r0:r0 + 128, :], 128), (KTr[r0 + 128:r0 + 256, :], 128)],
                  Vo[:, r0:r0 + 256], Vr[:, r0:r0 + 256], 256)
        for i in range(8):
            blocks = [(True, jj, "plain", st.rb[:, 0:1], None) for jj in range(8)]
            blocks += [(False, jj, "diag" if jj == i else "plain", st.zb[:, 0:1], None) for jj in range(i + 1)]
            attn_tile(k, st, ax, hs, i, [([(0, 0, 128)], sC), ([(1, 1, 128)], sC)], blocks, 256, fin(hh))
    return ax


def mix_out(k, st, wp, ax, w_out, gi):
    for th in range(2):
        y = carve(st, 60 * 1024, [128, KT, TH], F32)
        y_t = [T() for _ in range(KT)]
        for t_ in y_t:
            t_.w = st.big_t.w
            t_.r = dict(st.big_t.r)
            for e_ in ax.every:
                for tok in list(e_.r.values()) + ([e_.w] if e_.w else []):
                    old = t_.r.get(tok[0])
                    if old is None or (old[1], old[2]) < (tok[1], tok[2]):
                        t_.r[tok[0]] = tok
        outproj_half(k, st, wp, w_out, KT, lambda kc, th=th: ax.ao[:, kc, th * TH:(th + 1) * TH], ax.ao_t, y, y_t)
        tail_half(k, st, th, gi, y, y_t)
        ax.every = ax.every + y_t
    ax.done()


NCORES = 8
QROWS = 2560
KROWS = 2112
KVROWS = KROWS + 2048


def make_state(k):
    st = State(k)
    st.big_t = T("big")
    st._uid = [0]

    def uid():
        st._uid[0] += 1
        return st._uid[0]
    st.uid = uid
    st.silu = [k.sb(f"silu{i}", [128, TH], BF16) for i in range(2)]
    st.silu_t = [T(f"silu{i}") for i in range(2)]
    st.cb = k.sb("cb", [128, 16], F32)
    st.rb = k.sb("rb", [128, 1], F32)
    st.zb = k.sb("zb", [128, 1], F32)
    st.lv = k.sb("lv", [128, 16], F32)
    st.cst_t = T("cst")
    k.dve.op(lambda: k.nc.vector.memset(st.zb[:], 0.0), writes=[st.cst_t])
    return st


class Weights:
    def __init__(self, k, specs, gather=True):
        self.k = k
        self.full = {}
        self.t = {}
        nc = k.nc
        for name, (R, C) in specs.items():
            if not gather:
                self.full[name] = k.din("w_" + name, [R, C])
                self.t[name] = T()
                continue
            sh = k.din("w_" + name, [R // NCORES, C])
            bounce = nc.dram_tensor("wb_" + name, [R // NCORES, C], F32)
            full = nc.dram_tensor("wf_" + name, [R, C], F32)
            ds = DmaSem(k, "wb_" + name)
            rows = R // NCORES
            step = max(1, (1 << 18) // C)
            for r0 in range(0, rows, step):
                r1 = min(rows, r0 + step)
                k.pool.dma(bounce.ap()[r0:r1, :], sh[r0:r1, :], ds)
            k.pool.e.wait_ge(ds.sem, ds.n)
            sem = k.newsem("cc_" + name)
            nc.gpsimd.collective_compute("AllGather", ALU.bypass, replica_groups=[list(range(NCORES))],
                                         ins=[bounce.ap().opt()], outs=[full.ap().opt()]).then_inc(sem)
            t_ = T()
            t_.w = ("cc_" + name, 0, 1, sem)
            self.full[name] = full.ap()
            self.t[name] = t_

    def get(self, name):
        self.k.pool._need([self.t[name].w])
        return self.full[name]


def build_part1(layer, gather=False):
    even = layer % 2 == 0
    k = K()
    hT = k.din("hT", [D, NT])
    gv = k.din("gvec", [3, D])
    out = k.dout("hT_out", [D, NT])
    qt = k.dout("qt_out", [QROWS, NT], BF16)
    kv = k.dout("kv_out", [KVROWS, NT], BF16)
    specs = {"ffn_w_in": (D, 2 * DFF), "ffn_w_out": (DFF, D)}
    if even:
        specs.update({"ab_w_in": (D, 3904), "ab_w_in_krp": (D, 64), "b_w_qup": (512, 1536), "b_w_qup_rp": (512, 512),
                      "b_w_kvup": (256, 2048)})
        rope = k.din("rope", [2, 64, NT])
        gq = k.din("g_q", [512])
        gkv = k.din("g_kv", [256])
    else:
        specs.update({"c_w_in": (D, 6144), "c_w_in_rp": (D, 1024)})
        rope = k.din("rope", [2, 32, NT])
    st = make_state(k)
    W = Weights(k, specs, gather)
    wp = WPool(k, "w", 2, 8192)
    sg = Stager(k, st)
    load_gains(k, st, [gv[0], gv[1], gv[2]], [1.0, 0.5, 1.0])
    if even:
        k.sp.dma(st.gains[:, 8, 0:4], gq.rearrange("(c p) -> p c", p=128), st.misc_ds, writes=[st.gains_t], allow_slow_non_contiguous=True)
        k.sp.dma(st.gains[:, 9, 0:2], gkv.rearrange("(c p) -> p c", p=128), st.misc_ds, writes=[st.gains_t], allow_slow_non_contiguous=True)
    load_h(k, st, hT)
    for th in range(2):
        ffn_half(k, st, wp, th, W.get("ffn_w_in"), W.get("ffn_w_out"), 0, 1)
    KT_ = RowChunks([(0, KROWS, kv[0:KROWS, :])])
    V = VView([kv[KROWS:KROWS + 1024, :], kv[KROWS + 1024:KVROWS, :]])
    Wf = lambda name, j: W.get(name)
    if even:
        proj_even(k, st, wp, sg, Wf, 0, 2, rope, qt, KT_, V)
    else:
        proj_odd(k, st, wp, sg, Wf, 0, 2, rope, qt, KT_, V)
    store_h(k, st, out)
    for ds in sg.ds:
        k.outsems.append(ds)
    return k.finish()


def build_part2(layer, gather=False):
    even = layer % 2 == 0
    k = K()
    hT = k.din("hT", [D, NT])
    gv = k.din("gvec", [5, D])
    qt = k.din("qt", [QROWS, NT], BF16)
    kvo = k.din("kv_own", [KVROWS, NT], BF16)
    kvr = k.din("kv_rem", [KVROWS, NT], BF16)
    rbias = k.din("rbias", [128, 1])
    pT = k.din("pT", [256, NT])
    out = k.dout("hT_out", [D, NT])
    specs = {"mix_w_out": (D, D), "ffn_w_in": (D, 2 * DFF), "ffn_w_out": (DFF, D), "ple_w_gate": (D, D), "ple_w_proj": (256, D)}
    if even:
        abias = k.din("abias", [1, 8, 3, 128, 128])
        acv = k.din("acv", [1, 8])
    else:
        lvec = [k.din(f"lvec{q}", [1, 128]) for q in range(4)]
        gsub = k.din("gsub", [1, 256])
    st = make_state(k)
    W = Weights(k, specs, gather)
    wp = WPool(k, "w", 2, 8192)
    k.sp.dma(st.rb[:], rbias, st.misc_ds, writes=[st.cst_t])
    load_gains(k, st, [gv[0], gv[1], gv[2], gv[3], gv[4]], [1.0, 1.0, 0.5, 1.0, 1.0])
    load_h(k, st, hT)
    KTo, KTr = RowChunks([(0, KROWS, kvo[0:KROWS, :])]), RowChunks([(0, KROWS, kvr[0:KROWS, :])])
    Vo = VView([kvo[KROWS:KROWS + 1024, :], kvo[KROWS + 1024:KVROWS, :]])
    Vr = VView([kvr[KROWS:KROWS + 1024, :], kvr[KROWS + 1024:KVROWS, :]])
    if even:
        ax = attn_even(k, st, None, 0, qt, KTo, Vo, KTr, Vr, abias, acv)
    else:
        ax = attn_odd(k, st, None, 0, layer, qt, KTo, Vo, KTr, Vr, lvec, gsub)
    mix_out(k, st, wp, ax, W.get("mix_w_out"), 0)
    for th in range(2):
        ffn_half(k, st, wp, th, W.get("ffn_w_in"), W.get("ffn_w_out"), 1, 2)
    for th in range(2):
        ple_half(k, st, wp, th, W.get("ple_w_gate"), W.get("ple_w_proj"), pT, 3, 4)
    store_h(k, st, out)
    return k.finish()


W_SHAPES = {
    "ffn1_w_in": (4, D, 2 * DFF), "ffn1_w_out": (4, DFF, D), "ffn2_w_in": (4, D, 2 * DFF), "ffn2_w_out": (4, DFF, D),
    "ab_w_in": (2, D, 3904), "ab_w_in_krp": (2, D, 64), "b_w_qup": (2, 512, 1536), "b_w_qup_rp": (2, 512, 512),
    "b_w_kvup": (2, 256, 2048), "ab_w_out": (2, D, D), "c_w_in": (2, D, 6144), "c_w_in_rp": (2, D, 1024),
    "c_w_out": (2, D, D), "ple_w_gate": (4, D, D), "ple_w_proj": (4, 256, D),
}


def build_fused(nlayers=4, ncores=NCORES):
    k = K()
    nc = k.nc
    hT = k.din("hT", [D, NT])
    out = k.dout("hT_out", [D, NT])
    gv = k.din("gvec", [4, 8, D])
    gq = k.din("g_q", [2, 512])
    gkv = k.din("g_kv", [2, 256])
    rope_b = k.din("rope_b", [2, 64, NT])
    rope_c = k.din("rope_c", [2, 32, NT])
    rbias = k.din("rbias", [128, 1])
    pT = k.din("pT", [4, 256, NT])
    abias = k.din("abias", [2, 8, 3, 128, 128])
    acv = k.din("acv", [2, 8])
    lvec = [k.din(f"lvec{q}", [2, 128]) for q in range(4)]
    gsub = k.din("gsub", [2, 256])
    Wd = {n: k.din("w_" + n, list(shp)) for n, shp in W_SHAPES.items()}
    W = lambda name, j: Wd[name][j]
    st = make_state(k)
    wp = WPool(k, "w", 2, 8192)
    sg = Stager(k, st)
    k.sp.dma(st.rb[:], rbias, DmaSem(k, "rb"), writes=[st.cst_t])
    load_h(k, st, hT)
    for layer in range(nlayers):
        even = layer % 2 == 0
        j = layer // 2
        gds = DmaSem(k, f"g{layer}")
        for i in range(8):
            k.sp.dma(st.gains[:, i, :], gv[layer, i].rearrange("(kt p) -> p kt", p=128), gds,
                     writes=[st.gains_t], allow_slow_non_contiguous=True)
        if even:
            k.sp.dma(st.gains[:, 8, 0:4], gq[j].rearrange("(c p) -> p c", p=128), gds, writes=[st.gains_t], allow_slow_non_contiguous=True)
            k.sp.dma(st.gains[:, 9, 0:2], gkv[j].rearrange("(c p) -> p c", p=128), gds, writes=[st.gains_t], allow_slow_non_contiguous=True)
        for i in (1, 5):
            k.dve.op(lambda i=i: nc.vector.tensor_scalar(out=st.gains[:, i, :], in0=st.gains[:, i, :], scalar1=0.5, scalar2=None,
                                                         op0=ALU.mult), reads=[st.gains_t], writes=[st.gains_t])
        for th in range(2):
            ffn_half(k, st, wp, th, W("ffn1_w_in", layer), W("ffn1_w_out", layer), 0, 1)
        qts = k.dint(f"qt{layer}", [QROWS, NT], BF16)
        csz = [1024, 1024, 1024, 1024] + ([64] if even else [])
        own_c = [k.dint(f"kvown{layer}_{ci}", [n_, NT], BF16) for ci, n_ in enumerate(csz)]
        pair_c = [k.dint(f"kvpair{layer}_{ci}", [2 * n_, NT], BF16) for ci, n_ in enumerate(csz)]
        kt_chunks = [(0, 1024, own_c[0]), (1024, 1024, own_c[1])] + ([(2048, 64, own_c[4])] if even else [])
        KTo = RowChunks(kt_chunks)
        Vo = VView([own_c[2], own_c[3]])
        if even:
            proj_even(k, st, wp, sg, W, j, 2, rope_b, qts, KTo, Vo)
        else:
            proj_odd(k, st, wp, sg, W, j, 2, rope_c, qts, KTo, Vo)
        toks = list(sg.stores)
        sg.stores = []
        k.pool._need(toks)
        k.sp._need(toks)
        cctoks = []
        for ci in range(len(csz)):
            ccsem = k.newsem(f"cc{layer}_{ci}")
            nc.gpsimd.collective_compute("AllGather", ALU.bypass, replica_groups=[[2 * i_, 2 * i_ + 1] for i_ in range(ncores // 2)],
                                         ins=[own_c[ci].opt()], outs=[pair_c[ci].opt()]).then_inc(ccsem)
            cctoks.append((f"cc{layer}_{ci}", 0, 1, ccsem))
        k.sp._need(cctoks)
        KTr = RowChunks([(0, 1024, pair_c[0][0:1024, :]), (1024, 1024, pair_c[1][0:1024, :])]
                        + ([(2048, 64, pair_c[4][0:64, :])] if even else []))
        Vr = VView([pair_c[2][0:1024, :], pair_c[3][0:1024, :]])
        if even:
            ax = attn_even(k, st, None, j, qts, KTo, Vo, KTr, Vr, abias, acv)
            mix_out(k, st, wp, ax, W("ab_w_out", j), 3)
        else:
            ax = attn_odd(k, st, None, j, layer, qts, KTo, Vo, KTr, Vr, lvec, gsub)
            mix_out(k, st, wp, ax, W("c_w_out", j), 3)
        for th in range(2):
            ffn_half(k, st, wp, th, W("ffn2_w_in", layer), W("ffn2_w_out", layer), 4, 5)
        for th in range(2):
            ple_half(k, st, wp, th, W("ple_w_gate", layer), W("ple_w_proj", layer), pT[layer], 6, 7)
    store_h(k, st, out)
    return k.finish()


def kernel_fused(I, nlayers=4):
    x, p = I["x"], I["p"]
    shared = {"gvec": np.ascontiguousarray(np.stack([
        np.stack([I["ffn1_g_pre"][l], I["ffn1_g_post"][l], I["mix_g_pre"][l], I["mix_g_post"][l],
                  I["ffn2_g_pre"][l], I["ffn2_g_post"][l], I["ple_g_pre"][l], I["ple_g_post"][l]]) for l in range(4)])),
        "g_q": I["b_g_q"], "g_kv": I["b_g_kv"], "gsub": I["c_g_sub"]}
    tiles = [_abias_tiles(I["a_rel_bias"][j]) for j in range(2)]
    shared["abias"] = np.ascontiguousarray(np.stack([t[0] for t in tiles]))
    shared["acv"] = np.ascontiguousarray(np.concatenate([t[1] for t in tiles], 0))
    for q_, n_ in enumerate(("c_lq1", "c_lk1", "c_lq2", "c_lk2")):
        shared[f"lvec{q_}"] = I[n_]
    for n_ in ("ffn1_w_in", "ffn1_w_out", "ffn2_w_in", "ffn2_w_out", "ab_w_in", "b_w_qup", "b_w_kvup", "ab_w_out",
               "c_w_in", "c_w_out", "ple_w_gate", "ple_w_proj"):
        shared["w_" + n_] = I[n_]
    ab = I["ab_w_in"]
    shared["w_ab_w_in_krp"] = np.ascontiguousarray(ab[:, :, 3840 + _swap_halves(64)])
    idx = np.concatenate([h_ * 192 + 128 + _swap_halves(64) for h_ in range(8)])
    shared["w_b_w_qup_rp"] = np.ascontiguousarray(I["b_w_qup"][:, :, idx])
    idx = np.concatenate([h_ * 768 + w_ * 128 + _swap_halves(32) for h_ in range(8) for w_ in range(4)])
    shared["w_c_w_in_rp"] = np.ascontiguousarray(I["c_w_in"][:, :, idx])
    in_maps = []
    for c in range(NCORES):
        m = dict(shared)
        b, hf = c // 2, c % 2
        m["hT"] = np.ascontiguousarray(x[b, hf * NT:(hf + 1) * NT, :].T)
        m["rope_b"] = _rope_tab(64, hf * NT)
        m["rope_c"] = _rope_tab(32, hf * NT)
        m["rbias"] = np.full((128, 1), 0.0 if hf == 1 else -30000.0, np.float32)
        m["pT"] = np.ascontiguousarray(np.transpose(p[:, b, hf * NT:(hf + 1) * NT, :], (0, 2, 1)))
        in_maps.append(m)
    nc = build_fused(nlayers, _NRUN)
    res = _run(nc, in_maps)
    hT = [res[c]["hT_out"] for c in range(NCORES)]
    _dbg(f"h_ple_{nlayers - 1}", hT)
    out = np.empty((4, 2 * NT, D), np.float32)
    for c in range(NCORES):
        out[c // 2, (c % 2) * NT:(c % 2 + 1) * NT, :] = hT[c].T
    return out


def _shard_rows(w):
    w = np.ascontiguousarray(w)
    return [w] * NCORES


def _rope_tab(rot, pos0):
    inv = (500000.0 ** (-np.arange(0, rot, 2, dtype=np.float32) / np.float32(rot))).astype(np.float32)
    ang = (np.arange(pos0, pos0 + NT, dtype=np.float32)[:, None] * inv[None, :]).astype(np.float32)
    c = np.cos(ang).astype(np.float32).T
    s = np.sin(ang).astype(np.float32).T
    return np.ascontiguousarray(np.stack([np.concatenate([c, c], 0), np.concatenate([-s, s], 0)], 0))


def _swap_halves(n):
    return np.concatenate([np.arange(n // 2, n), np.arange(0, n // 2)])


def _abias_tiles(table):
    kk = np.arange(128)[:, None]
    qq = np.arange(128)[None, :]
    out = np.empty((8, 3, 128, 128), np.float32)
    far = table[:, 191]
    t4 = np.broadcast_to(far[:, None, None], (8, 128, 128)).copy()
    t4[:, (kk < 64) & (qq >= 64)] = -30000.0
    out[:, 0] = t4
    d1 = np.clip(128 + qq - kk, -63, 128) + 63
    out[:, 1] = table[:, d1]
    d0 = np.clip(qq - kk, -63, 128) + 63
    t0 = table[:, d0].copy()
    t0[:, (kk >= 64) & (qq < 64)] = -30000.0
    out[:, 2] = t0
    return out, np.ascontiguousarray(far[None, :])


_NRUN = NCORES


def _run(nc, in_maps):
    res = run_bass_kernel_spmd(nc, in_maps[:_NRUN], core_ids=list(range(_NRUN)))
    r = list(res.results)
    while len(r) < NCORES:
        r.append(r[len(r) % _NRUN])
    return r


_DEBUG = None
_KSTOP = 0


def _dbg(name, hT):
    if _DEBUG is None:
        return
    ref = _DEBUG[name][0]
    for c in range(2):
        o = hT[c].T.astype(np.float32)
        r = ref[c * NT:(c + 1) * NT]
        print("DBG", name, "core", c, "relerr", float(np.sqrt(((o - r) ** 2).mean() / (r ** 2).mean())),
              "finite", bool(np.isfinite(o).all()), flush=True)
        print("   per-tile", [round(float(np.sqrt(((o[t * 128:(t + 1) * 128] - r[t * 128:(t + 1) * 128]) ** 2).mean() / (r ** 2).mean())), 4) for t in range(8)], flush=True)


_FUSED = True
_NLAYERS = 4


def kernel(**I):
    import ml_dtypes
    I = {k_: np.asarray(v) for k_, v in I.items()}
    if _FUSED:
        return kernel_fused(I, _NLAYERS)
    x, p = I["x"], I["p"]
    hT = [np.ascontiguousarray(x[c // 2, (c % 2) * NT:(c % 2 + 1) * NT, :].T) for c in range(NCORES)]
    rbias = [np.full((128, 1), 0.0 if c % 2 == 1 else -30000.0, np.float32) for c in range(NCORES)]
    for layer in range(4):
        even = layer % 2 == 0
        j = layer // 2
        shared = {"gvec": np.stack([I["ffn1_g_pre"][layer], I["ffn1_g_post"][layer], I["mix_g_pre"][layer]])}
        wsh = {"ffn_w_in": _shard_rows(I["ffn1_w_in"][layer]), "ffn_w_out": _shard_rows(I["ffn1_w_out"][layer])}
        if even:
            w_in = I["ab_w_in"][j]
            wsh["ab_w_in"] = _shard_rows(w_in)
            wsh["ab_w_in_krp"] = _shard_rows(w_in[:, 3840 + _swap_halves(64)])
            qup = I["b_w_qup"][j]
            wsh["b_w_qup"] = _shard_rows(qup)
            idx = np.concatenate([h_ * 192 + 128 + _swap_halves(64) for h_ in range(8)])
            wsh["b_w_qup_rp"] = _shard_rows(qup[:, idx])
            wsh["b_w_kvup"] = _shard_rows(I["b_w_kvup"][j])
            shared["g_q"] = I["b_g_q"][j]
            shared["g_kv"] = I["b_g_kv"][j]
            rope = [_rope_tab(64, (c % 2) * NT) for c in range(NCORES)]
        else:
            w_in = I["c_w_in"][j]
            wsh["c_w_in"] = _shard_rows(w_in)
            idx = np.concatenate([h_ * 768 + w_ * 128 + _swap_halves(32) for h_ in range(8) for w_ in range(4)])
            wsh["c_w_in_rp"] = _shard_rows(w_in[:, idx])
            rope = [_rope_tab(32, (c % 2) * NT) for c in range(NCORES)]
        nc = build_part1(layer)
        in_maps = []
        for c in range(NCORES):
            m = dict(shared)
            m["hT"] = hT[c]
            m["rope"] = rope[c]
            for n_, sh in wsh.items():
                m["w_" + n_] = sh[c]
            in_maps.append(m)
        res = _run(nc, in_maps)
        hT = [res[c]["hT_out"] for c in range(NCORES)]
        _dbg(f"h_ffn1_{layer}", hT)
        qt = [res[c]["qt_out"] for c in range(NCORES)]
        kv = [res[c]["kv_out"] for c in range(NCORES)]
        if _KSTOP == 10 + layer:
            return None
        del wsh, in_maps
        shared = {"gvec": np.stack([I["mix_g_post"][layer], I["ffn2_g_pre"][layer], I["ffn2_g_post"][layer],
                                    I["ple_g_pre"][layer], I["ple_g_post"][layer]])}
        wsh = {"mix_w_out": _shard_rows((I["ab_w_out"] if even else I["c_w_out"])[j]),
               "ffn_w_in": _shard_rows(I["ffn2_w_in"][layer]), "ffn_w_out": _shard_rows(I["ffn2_w_out"][layer]),
               "ple_w_gate": _shard_rows(I["ple_w_gate"][layer]), "ple_w_proj": _shard_rows(I["ple_w_proj"][layer])}
        if even:
            tiles, far = _abias_tiles(I["a_rel_bias"][j])
            shared["abias"] = tiles[None]
            shared["acv"] = far
        else:
            for q_, n_ in enumerate(("c_lq1", "c_lk1", "c_lq2", "c_lk2")):
                shared[f"lvec{q_}"] = I[n_][j][None]
            shared["gsub"] = I["c_g_sub"][j][None]
        nc = build_part2(layer)
        in_maps = []
        for c in range(NCORES):
            m = dict(shared)
            m["hT"] = hT[c]
            m["qt"] = qt[c]
            m["kv_own"] = kv[c]
            m["kv_rem"] = kv[c - 1] if c % 2 == 1 else kv[c]
            m["rbias"] = rbias[c]
            m["pT"] = np.ascontiguousarray(p[layer, c // 2, (c % 2) * NT:(c % 2 + 1) * NT, :].T)
            for n_, sh in wsh.items():
                m["w_" + n_] = sh[c]
            in_maps.append(m)
        res = _run(nc, in_maps)
        hT = [res[c]["hT_out"] for c in range(NCORES)]
        _dbg(f"h_ple_{layer}", hT)
        if _KSTOP == 20 + layer:
            return None
        del wsh, in_maps
    out = np.empty((4, 2 * NT, D), np.float32)
    for c in range(NCORES):
        out[c // 2, (c % 2) * NT:(c % 2 + 1) * NT, :] = hT[c].T
    return out
```

```python
import contextlib
import math
import numpy as np
import concourse.bass as bass
import concourse.mybir as mybir
from concourse.bass_utils import run_bass_kernel_spmd

F32 = mybir.dt.float32
BF16 = mybir.dt.bfloat16
AF = mybir.ActivationFunctionType
ALU = mybir.AluOpType

D = 2048
NT = 1024
TH = 512
KT = D // 128
DFF = 5632
FC = DFF // 128
EPS = 1e-6
SEM_LIMIT = 20000


class T:
    __slots__ = ("w", "r", "name")

    def __init__(self, name=""):
        self.w = None
        self.r = {}
        self.name = name


class DmaSem:
    def __init__(self, K, name):
        self.sem = K.newsem(name)
        self.group = "dma_" + name
        self.n = 0


class E:
    def __init__(self, K, name, eng, is_pe=False):
        self.K = K
        self.name = name
        self.e = eng
        self.is_pe = is_pe
        self.epoch = 0
        self.cnt = 0
        self.sem = K.newsem(f"{name}_e0")
        self.seen = {}

    def _need(self, toks):
        for tok in toks:
            if tok is None:
                continue
            group, epoch, val, sem = tok
            if self.is_pe and group == self.name:
                continue
            s = self.seen.get(group)
            if s is not None and s >= (epoch, val):
                continue
            self.e.wait_ge(sem, val)
            self.seen[group] = (epoch, val)

    def deps(self, reads, writes):
        toks = []
        for b in reads:
            toks.append(b.w)
        for b in writes:
            toks.append(b.w)
            toks.extend(b.r.values())
        self._need(toks)

    def _record(self, tok, reads, writes):
        for b in reads:
            old = b.r.get(tok[0])
            if old is None or (old[1], old[2]) < (tok[1], tok[2]):
                b.r[tok[0]] = tok
        for b in writes:
            b.w = tok
            b.r = {}

    def op(self, fn, reads=(), writes=(), inc=True):
        self.deps(reads, writes)
        ins = fn()
        if inc:
            ins.then_inc(self.sem, 1)
            self.cnt += 1
            tok = (self.name, self.epoch, self.cnt, self.sem)
            self._record(tok, reads, writes)
            if self.cnt >= SEM_LIMIT:
                self.epoch += 1
                self.cnt = 0
                self.sem = self.K.newsem(f"{self.name}_e{self.epoch}")
        else:
            tok = (self.name, self.epoch, self.cnt + 1, self.sem)
            self._record(tok, reads, writes)
        return ins

    def dma(self, out, in_, ds, reads=(), writes=(), **kw):
        self.deps(reads, writes)
        ins = self.e.dma_start(out=out, in_=in_, **kw)
        ds.n += 16
        ins.then_inc(ds.sem, 16)
        tok = (ds.group, 0, ds.n, ds.sem)
        self._record(tok, reads, writes)
        return ins


class K:
    def __init__(self):
        self.nc = bass.Bass("TRN2", target_bir_lowering=False)
        self.stack = contextlib.ExitStack()
        self.nsem = 0
        nc = self.nc
        self.pe = E(self, "pe", nc.tensor, is_pe=True)
        self.act = E(self, "act", nc.scalar)
        self.dve = E(self, "dve", nc.vector)
        self.pool = E(self, "pool", nc.gpsimd)
        self.sp = E(self, "sp", nc.sync)
        self.outsems = []

    def newsem(self, name):
        self.nsem += 1
        return self.stack.enter_context(self.nc.semaphore(f"s{self.nsem}_{name}"))

    def sb(self, name, shape, dt):
        return self.stack.enter_context(self.nc.sbuf_tensor(name, shape, dt))

    def ps(self, name, shape, dt=F32):
        return self.stack.enter_context(self.nc.psum_tensor(name, shape, dt))

    def din(self, name, shape, dt=F32):
        return self.nc.dram_tensor(name, list(shape), dt, kind="ExternalInput").ap()

    def dout(self, name, shape, dt=F32):
        return self.nc.dram_tensor(name, list(shape), dt, kind="ExternalOutput").ap()

    def dint(self, name, shape, dt=F32):
        return self.nc.dram_tensor(name, list(shape), dt, kind="Internal").ap()

    def finish(self):
        for ds in self.outsems:
            self.sp.e.wait_ge(ds.sem, ds.n)
        self.stack.close()
        return self.nc


class State:
    def __init__(self, k: K):
        self.k = k
        self.h = k.sb("h", [128, KT, NT], F32)
        self.h_t = [[T(f"h{c}_{t}") for t in range(2)] for c in range(KT)]
        self.bank = [k.ps(f"bank{i}", [128, 512], F32) for i in range(8)]
        self.bank_t = [T(f"bank{i}") for i in range(8)]
        self.ones_bf = k.sb("ones_bf", [128, 128], BF16)
        self.ones_f = k.sb("ones_f", [128, 128], F32)
        self.ones_t = T("ones")
        k.dve.op(lambda: k.nc.vector.memset(self.ones_bf[:], 1.0), writes=[self.ones_t])
        k.dve.op(lambda: k.nc.vector.memset(self.ones_f[:], 1.0), writes=[self.ones_t])
        self.big = k.sb("big", [128, 136 * 1024], mybir.dt.uint8)
        self.sq = [k.sb(f"sq{i}", [128, TH], F32) for i in range(1)]
        self.sq_t = [T(f"sq{i}") for i in range(1)]
        self.sq_i = 0
        self.rstd = k.sb("rstd", [128, TH], F32)
        self.rstd_t = T("rstd")
        self.gains = k.sb("gains", [128, 10, KT], F32)
        self.gains_t = T("gains")
        self.misc_ds = DmaSem(k, "misc")


def rms_stats_accum(k, st, src_ap, src_t, bank_i, first, last, src_is_psum=False):
    i = 0
    sq, sq_t = st.sq[i], st.sq_t[i]
    k.act.op(lambda: k.nc.scalar.activation(out=sq[:], in_=src_ap, func=AF.Square),
             reads=[src_t], writes=[sq_t])
    k.pe.op(lambda: k.nc.tensor.matmul(st.bank[bank_i][:], st.ones_f[:], sq[:], start=first, stop=last),
            reads=[sq_t, st.ones_t], writes=[st.bank_t[bank_i]], inc=True)


def rstd_from_bank(k, st, bank_i, nfeat):
    k.dve.op(lambda: k.nc.vector.tensor_scalar(out=st.rstd[:], in0=st.bank[bank_i][:], scalar1=1.0 / nfeat,
                                              scalar2=EPS, op0=ALU.mult, op1=ALU.add),
             reads=[st.bank_t[bank_i]], writes=[st.rstd_t])
    k.act.op(lambda: k.nc.scalar.activation(out=st.rstd[:], in_=st.rstd[:], func=AF.Sqrt),
             reads=[st.rstd_t], writes=[st.rstd_t])
    k.dve.op(lambda: k.nc.vector.reciprocal(out=st.rstd[:], in_=st.rstd[:]),
             reads=[st.rstd_t], writes=[st.rstd_t])


class WPool:
    def __init__(self, k, st):
        self.k = k
        self.st = st
        self.ds = [DmaSem(k, f"w{i}") for i in range(4)]
        self.slots = []
        self.ts = []
        self.i = 0
        self.cfg = None

    def config(self, offs_kb, kb):
        cfg = (tuple(offs_kb), kb)
        if cfg == self.cfg:
            return
        st = self.st
        u = {}
        for t_ in self.ts + [st.big_t]:
            for tok in list(t_.r.values()) + ([t_.w] if t_.w is not None else []):
                old = u.get(tok[0])
                if old is None or (old[1], old[2]) < (tok[1], tok[2]):
                    u[tok[0]] = tok
        st.big_t.w = None
        st.big_t.r = dict(u)
        self.slots = [carve(st, o * 1024, [128, kb * 512], BF16) for o in offs_kb]
        self.ts = [T() for _ in offs_kb]
        for t_ in self.ts:
            t_.r = dict(u)
        self.i = 0
        self.cfg = cfg

    def big16(self):
        self.config([104, 120], 16)

    def small4(self):
        self.config([120, 124, 128, 132], 4)

    def next(self):
        i = self.i
        self.i = (self.i + 1) % len(self.slots)
        return self.slots[i], self.ts[i], self.ds[i]


def carve(st, off_bytes, shape, dt):
    esz = 2 if dt == BF16 else 4
    n = 1
    for s_ in shape[1:]:
        n *= s_
    ap = st.big[:, off_bytes:off_bytes + n * esz].bitcast(dt)
    if len(shape) == 3:
        ap = ap.rearrange("p (a b) -> p a b", a=shape[1])
    return ap


def load_gains(k, st, vecs, scales):
    nc = k.nc
    for i, v in enumerate(vecs):
        k.sp.dma(st.gains[:, i, :], v.rearrange("(kt p) -> p kt", p=128), st.misc_ds,
                 writes=[st.gains_t], allow_slow_non_contiguous=True)
    for i, s_ in enumerate(scales):
        if s_ != 1.0:
            k.dve.op(lambda i=i, s_=s_: nc.vector.tensor_scalar(
                out=st.gains[:, i, :], in0=st.gains[:, i, :], scalar1=s_, scalar2=None, op0=ALU.mult),
                reads=[st.gains_t], writes=[st.gains_t])


def all_h(st):
    return [st.h_t[c][t] for c in range(KT) for t in range(2)]


def load_h(k, st, hT):
    v = hT.rearrange("(kt p) t -> p kt t", p=128)
    ds = DmaSem(k, "hload")
    for q in range(4):
        k.sp.dma(st.h[:, q * 4:(q + 1) * 4, :], v[:, q * 4:(q + 1) * 4, :], ds, writes=all_h(st))


def store_h(k, st, hT_out):
    v = hT_out.rearrange("(kt p) t -> p kt t", p=128)
    ds = DmaSem(k, "hstore")
    k.outsems.append(ds)
    for q in range(4):
        k.sp.dma(v[:, q * 4:(q + 1) * 4, :], st.h[:, q * 4:(q + 1) * 4, :], ds, reads=all_h(st))


def prenorm_half(k, st, th, gi, xn, xn_t):
    nc = k.nc
    tsl = slice(th * TH, (th + 1) * TH)
    for kt in range(KT):
        rms_stats_accum(k, st, st.h[:, kt, tsl], st.h_t[kt][th], 6, kt == 0, kt == KT - 1)
    rstd_from_bank(k, st, 6, D)
    for kt in range(KT):
        k.dve.op(lambda kt=kt: nc.vector.scalar_tensor_tensor(
            out=xn[:, kt, :], in0=st.h[:, kt, tsl], scalar=st.gains[:, gi, kt:kt + 1], in1=st.rstd[:],
            op0=ALU.mult, op1=ALU.mult),
            reads=[st.h_t[kt][th], st.rstd_t, st.gains_t], writes=[xn_t[kt]])


def tail_half(k, st, th, gi, y, y_t):
    nc = k.nc
    tsl = slice(th * TH, (th + 1) * TH)
    rstd_from_bank(k, st, 7, D)
    for dc in range(KT):
        k.dve.op(lambda dc=dc: nc.vector.scalar_tensor_tensor(
            out=y[:, dc, :], in0=y[:, dc, :], scalar=st.gains[:, gi, dc:dc + 1], in1=st.rstd[:],
            op0=ALU.mult, op1=ALU.mult),
            reads=[y_t[dc], st.rstd_t, st.gains_t], writes=[y_t[dc]])
        k.dve.op(lambda dc=dc: nc.vector.tensor_tensor(
            out=st.h[:, dc, tsl], in0=st.h[:, dc, tsl], in1=y[:, dc, :], op=ALU.add),
            reads=[y_t[dc], st.h_t[dc][th]], writes=[st.h_t[dc][th]])


def outproj_half(k, st, wp, w_dram, nkc, rhs_of, rhs_t, y, y_t, kblk=None):
    nc = k.nc
    kblk = kblk or nkc
    w_v = w_dram.rearrange("(kc p) n -> p kc n", p=128)
    for dc in range(KT):
        bnk = 4 + (dc % 2)
        for k0 in range(0, nkc, kblk):
            wb, wb_t, wb_ds = wp.next()
            wv = wb[:, 0:kblk * 128].rearrange("p (kc n) -> p kc n", kc=kblk)
            k.pool.dma(wv, w_v[:, k0:k0 + kblk, dc * 128:(dc + 1) * 128], wb_ds, writes=[wb_t])
            for kc in range(k0, k0 + kblk):
                k.pe.op(lambda kc=kc, bnk=bnk, wv=wv, k0=k0: nc.tensor.matmul(
                    st.bank[bnk][:], wv[:, kc - k0, :], rhs_of(kc), start=(kc == 0), stop=(kc == nkc - 1)),
                    reads=[wb_t, rhs_t[kc]], writes=[st.bank_t[bnk]], inc=(kc == k0 + kblk - 1))
        k.act.op(lambda dc=dc, bnk=bnk: nc.scalar.copy(out=y[:, dc, :], in_=st.bank[bnk][:]),
                 reads=[st.bank_t[bnk]], writes=[y_t[dc]])
        rms_stats_accum(k, st, st.bank[bnk][:], st.bank_t[bnk], 7, dc == 0, dc == KT - 1)


def ffn_full(k, st, wp, w_in, w_out, gpre_i, gpost_i):
    nc = k.nc
    wp.small4()
    xn = carve(st, 0, [128, KT, NT], BF16)
    g = carve(st, 32 * 1024, [128, FC, NT], BF16)
    y = carve(st, 0, [128, KT, TH], F32)
    xn_t = [T() for _ in range(KT)]
    g_t = [[T() for _ in range(2)] for _ in range(FC)]
    for t_ in xn_t + [t2 for gg in g_t for t2 in gg]:
        t_.w = st.big_t.w
        t_.r = dict(st.big_t.r)
    for th in range(2):
        prenorm_half(k, st, th, gpre_i, xn[:, :, th * TH:(th + 1) * TH], xn_t)
    w_in_v = w_in.rearrange("(kt p) n -> p kt n", p=128)
    for m in range(FC):
        wvs = []
        for half in range(2):
            wb, wb_t, wb_ds = wp.next()
            wv = wb[:, 0:KT * 128].rearrange("p (kt n) -> p kt n", kt=KT)
            k.pool.dma(wv, w_in_v[:, :, half * DFF + m * 128:half * DFF + (m + 1) * 128], wb_ds, writes=[wb_t])
            wvs.append((wv, wb_t))
        for th in range(2):
            tsl = slice(th * TH, (th + 1) * TH)
            ba, bb = 2 * th, 2 * th + 1
            for half, bnk in ((0, ba), (1, bb)):
                wv, wb_t = wvs[half]
                for kt in range(KT):
                    k.pe.op(lambda kt=kt, bnk=bnk, wv=wv, tsl=tsl: nc.tensor.matmul(
                        st.bank[bnk][:], wv[:, kt, :], xn[:, kt, tsl], start=(kt == 0), stop=(kt == KT - 1)),
                        reads=[wb_t, xn_t[kt]], writes=[st.bank_t[bnk]], inc=(kt == KT - 1))
            k.act.op(lambda ba=ba, m=m, tsl=tsl: nc.scalar.activation(out=g[:, m, tsl], in_=st.bank[ba][:], func=AF.Silu),
                     reads=[st.bank_t[ba]], writes=[g_t[m][th]])
            k.dve.op(lambda bb=bb, m=m, tsl=tsl: nc.vector.tensor_tensor(
                out=g[:, m, tsl], in0=st.bank[bb][:], in1=g[:, m, tsl], op=ALU.mult),
                reads=[st.bank_t[bb], g_t[m][th]], writes=[g_t[m][th]])
    y_t = [T() for _ in range(KT)]
    for t_ in y_t:
        for x_ in xn_t:
            for tok in list(x_.r.values()) + ([x_.w] if x_.w is not None else []):
                old = t_.r.get(tok[0])
                if old is None or (old[1], old[2]) < (tok[1], tok[2]):
                    t_.r[tok[0]] = tok
    for th in range(2):
        tsl = slice(th * TH, (th + 1) * TH)
        outproj_half(k, st, wp, w_out, FC, lambda kc, tsl=tsl: g[:, kc, tsl], [g_t[kc][th] for kc in range(FC)], y, y_t, kblk=11)
        tail_half(k, st, th, gpost_i, y, y_t)
    merge_big(st, xn_t + [t2 for gg in g_t for t2 in gg] + y_t)


def merge_big(st, ts):
    r = {}
    w = None
    for t_ in ts:
        for tok in list(t_.r.values()) + ([t_.w] if t_.w is not None else []):
            old = r.get(tok[0])
            if old is None or (old[1], old[2]) < (tok[1], tok[2]):
                r[tok[0]] = tok
    st.big_t.w = None
    st.big_t.r = r


def ple_half(k, st, wp, th, w_gate, w_proj, pT_dram, gpre_i, gpost_i):
    nc = k.nc
    xn = carve(st, 0, [128, KT, TH], BF16)
    pt = carve(st, 16 * 1024, [128, 2, TH], BF16)
    sg = carve(st, 20 * 1024, [128, 2, TH], F32)
    y = carve(st, 60 * 1024, [128, KT, TH], F32)
    xn_t = [T() for _ in range(KT)]
    y_t = [T() for _ in range(KT)]
    pt_t = T()
    sg_t = [T(), T()]
    for t_ in xn_t + y_t + [pt_t] + sg_t:
        t_.w = st.big_t.w
        t_.r = dict(st.big_t.r)
    tsl = slice(th * TH, (th + 1) * TH)
    ds = DmaSem(k, f"pt{st.uid()}")
    k.pool.dma(pt, pT_dram.rearrange("(kc p) t -> p kc t", p=128)[:, :, tsl], ds, writes=[pt_t])
    prenorm_half(k, st, th, gpre_i, xn, xn_t)
    wg_v = w_gate.rearrange("(kt p) n -> p kt n", p=128)
    wp_v = w_proj.rearrange("(kt p) n -> p kt n", p=128)
    for blk in range(8):
        wb, wb_t, wb_ds = wp.next()
        wv = wb[:, 0:(KT + 2) * 256].rearrange("p (kt n) -> p kt n", kt=KT + 2)
        k.pool.dma(wv[:, 0:KT, :], wg_v[:, :, blk * 256:(blk + 1) * 256], wb_ds, writes=[wb_t])
        k.pool.dma(wv[:, KT:KT + 2, :], wp_v[:, :, blk * 256:(blk + 1) * 256], wb_ds, writes=[wb_t])
        for c in range(2):
            dc = blk * 2 + c
            bg, bp = (0, 1) if dc % 2 == 0 else (2, 3)
            for kt in range(KT):
                k.pe.op(lambda kt=kt, bg=bg, c=c, wv=wv: nc.tensor.matmul(
                    st.bank[bg][:], wv[:, kt, c * 128:(c + 1) * 128], xn[:, kt, :],
                    start=(kt == 0), stop=(kt == KT - 1)),
                    reads=[wb_t, xn_t[kt]], writes=[st.bank_t[bg]], inc=(kt == KT - 1))
            for kc in range(2):
                k.pe.op(lambda kc=kc, bp=bp, c=c, wv=wv: nc.tensor.matmul(
                    st.bank[bp][:], wv[:, KT + kc, c * 128:(c + 1) * 128], pt[:, kc, :],
                    start=(kc == 0), stop=(kc == 1)),
                    reads=[wb_t, pt_t], writes=[st.bank_t[bp]], inc=(kc == 1))
            s_, s_t = sg[:, dc % 2, :], sg_t[dc % 2]
            k.act.op(lambda bg=bg, s_=s_: nc.scalar.activation(out=s_, in_=st.bank[bg][:], func=AF.Sigmoid),
                     reads=[st.bank_t[bg]], writes=[s_t])
            k.dve.op(lambda bp=bp, s_=s_, dc=dc: nc.vector.tensor_tensor(
                out=y[:, dc, :], in0=st.bank[bp][:], in1=s_, op=ALU.mult),
                reads=[st.bank_t[bp], s_t], writes=[y_t[dc]])
            rms_stats_accum(k, st, y[:, dc, :], y_t[dc], 7, dc == 0, dc == KT - 1)
    tail_half(k, st, th, gpost_i, y, y_t)
    merge_big(st, xn_t + y_t + [pt_t] + sg_t)


class Stager:
    def __init__(self, k, st, n=2):
        self.k = k
        self.tiles = [carve(st, (96 + 2 * i) * 1024, [128, NT], BF16) for i in range(n)]
        self.ts = [T(f"stg{i}") for i in range(n)]
        self.ds = [DmaSem(k, f"stg{i}") for i in range(n)]
        self.i = 0
        self.stores = []

    def next(self):
        i = self.i
        self.i = (self.i + 1) % len(self.tiles)
        return self.tiles[i], self.ts[i], self.ds[i]

    def store(self, dst, src, t_, ds):
        self.k.sp.dma(dst, src, ds, reads=[t_])
        self.stores.append((ds.group, 0, ds.n, ds.sem))

    def sync(self, st):
        for t_ in self.ts:
            t_.w = st.big_t.w
            t_.r = dict(st.big_t.r)

    def barrier(self, eng):
        eng._need(self.stores)
        self.stores = []


def load_w_slot(k, wp, view_ap, nk, ncols, pieces):
    wb, wb_t, wb_ds = wp.next()
    wv = wb[:, 0:nk * ncols].rearrange("p (kt n) -> p kt n", kt=nk)
    for c0, src in pieces:
        w = src.shape[-1]
        k.pool.dma(wv[:, :, c0:c0 + w], src, wb_ds, writes=[wb_t])
    return wv, wb_t


def mm_group(k, st, bank_i, lhs_list, rhs_list, reads, M=128, N=TH):
    nc = k.nc
    n = len(lhs_list)
    for i in range(n):
        k.pe.op(lambda i=i: nc.tensor.matmul(st.bank[bank_i][0:M, 0:N], lhs_list[i], rhs_list[i],
                                            start=(i == 0), stop=(i == n - 1)),
                reads=reads[i], writes=[st.bank_t[bank_i]], inc=(i == n - 1))


def hn_all(k, st, gi):
    hn = carve(st, 0, [128, KT, NT], BF16)
    hn_t = [T() for _ in range(KT)]
    for t_ in hn_t:
        t_.w = st.big_t.w
        t_.r = dict(st.big_t.r)
    for th in range(2):
        prenorm_half(k, st, th, gi, hn[:, :, th * TH:(th + 1) * TH], hn_t)
    return hn, hn_t


def fm_plain(k, st, sg, wv, wb_t, nk, c0, M, rhs_of, rhs_t, dst_rows, pbank):
    nc = k.nc
    stg, stg_t, stg_ds = sg.next()
    for th in range(2):
        b = pbank[0]
        pbank[0] = (pbank[0] + 1) % 4
        mm_group(k, st, b, [wv[:, kt, c0:c0 + M] for kt in range(nk)], [rhs_of(kt, th) for kt in range(nk)],
                 [[wb_t, rhs_t[kt]] for kt in range(nk)], M=M)
        k.act.op(lambda b=b, th=th: nc.scalar.copy(out=stg[0:M, th * TH:(th + 1) * TH], in_=st.bank[b][0:M, :]),
                 reads=[st.bank_t[b]], writes=[stg_t])
    sg.store(dst_rows, stg[0:M, :], stg_t, stg_ds)


def fm_rope(k, st, sg, wv, wb_t, nk, c0, M, wvr, wbr_t, cr0, R, rhs_of, rhs_t, dst_rows, pbank, cc, ss, tmp1, tmp2, tmp_t):
    nc = k.nc
    stg, stg_t, stg_ds = sg.next()
    for th in range(2):
        tsl = slice(th * TH, (th + 1) * TH)
        b = pbank[0]
        b2 = (b + 1) % 4
        pbank[0] = (pbank[0] + 2) % 4
        mm_group(k, st, b, [wv[:, kt, c0:c0 + M] for kt in range(nk)], [rhs_of(kt, th) for kt in range(nk)],
                 [[wb_t, rhs_t[kt]] for kt in range(nk)], M=M)
        mm_group(k, st, b2, [wvr[:, kt, cr0:cr0 + R] for kt in range(nk)], [rhs_of(kt, th) for kt in range(nk)],
                 [[wbr_t, rhs_t[kt]] for kt in range(nk)], M=R)
        k.dve.op(lambda b=b, tsl=tsl: nc.vector.tensor_tensor(out=tmp1[0:R, :], in0=st.bank[b][0:R, :], in1=cc[0:R, tsl], op=ALU.mult),
                 reads=[st.bank_t[b], st.rope_t], writes=[tmp_t[0]])
        k.dve.op(lambda b2=b2, tsl=tsl: nc.vector.tensor_tensor(out=tmp2[0:R, :], in0=st.bank[b2][0:R, :], in1=ss[0:R, tsl], op=ALU.mult),
                 reads=[st.bank_t[b2], st.rope_t], writes=[tmp_t[1]])
        k.dve.op(lambda tsl=tsl: nc.vector.tensor_tensor(out=stg[0:R, tsl], in0=tmp1[0:R, :], in1=tmp2[0:R, :], op=ALU.add),
                 reads=[tmp_t[0], tmp_t[1]], writes=[stg_t])
        if M > R:
            for (p0, p1) in ((32, 64), (64, 128)):
                k.act.op(lambda b=b, tsl=tsl, p0=p0, p1=p1: nc.scalar.copy(out=stg[p0:p1, tsl], in_=st.bank[b][p0:p1, :]),
                         reads=[st.bank_t[b]], writes=[stg_t])
    sg.store(dst_rows, stg[0:M, :], stg_t, stg_ds)


def tm_proj(k, st, sg, wv, wb_t, nk, ncols, lhs_of, lhs_t, dstV, pbank, col_of=None):
    nc = k.nc
    for tt in range(NT // 128):
        b = pbank[0]
        pbank[0] = (pbank[0] + 1) % 4
        stg, stg_t, stg_ds = sg.next()
        if col_of is None:
            mm_group(k, st, b, [lhs_of(kt, tt) for kt in range(nk)], [wv[:, kt, 0:ncols] for kt in range(nk)],
                     [[wb_t, lhs_t[kt]] for kt in range(nk)], M=128, N=ncols)
        else:
            for (o0, c0, w) in col_of:
                k_last = (o0, c0, w) == col_of[-1]
                for kt in range(nk):
                    k.pe.op(lambda kt=kt, o0=o0, c0=c0, w=w: nc.tensor.matmul(
                        st.bank[b][:, o0:o0 + w], lhs_of(kt, tt), wv[:, kt, c0:c0 + w], start=(kt == 0), stop=(kt == nk - 1)),
                        reads=[wb_t, lhs_t[kt]], writes=[st.bank_t[b]], inc=(k_last and kt == nk - 1))
        k.act.op(lambda b=b: nc.scalar.copy(out=stg[:, 0:ncols], in_=st.bank[b][:, 0:ncols]),
                 reads=[st.bank_t[b]], writes=[stg_t])
        sg.store(dstV[tt * 128:(tt + 1) * 128, :], stg[:, 0:ncols], stg_t, stg_ds)


def wview(w2d, nk):
    return w2d.rearrange("(kt p) n -> p kt n", p=128)


def load_rope(k, st, rope_dram, R):
    cc = carve(st, 88 * 1024, [128, NT], F32)
    ss = carve(st, 92 * 1024, [128, NT], F32)
    st.rope_t = T("rope")
    st.rope_t.w = st.big_t.w
    st.rope_t.r = dict(st.big_t.r)
    ds = DmaSem(k, f"rope{st.uid()}")
    k.sp.dma(cc[0:R, :], rope_dram[0], ds, writes=[st.rope_t])
    k.sp.dma(ss[0:R, :], rope_dram[1], ds, writes=[st.rope_t])
    return cc, ss


def proj_even(k, st, wp, sg, W, j, gi, rope_b, QT, KT_, V):
    nc = k.nc
    sg.sync(st)
    hn, hn_t = hn_all(k, st, gi)
    cc, ss = load_rope(k, st, rope_b, 64)
    cq = carve(st, 32 * 1024, [128, 4, NT], F32)
    ckv = carve(st, 48 * 1024, [128, 2, NT], F32)
    cqn = carve(st, 56 * 1024, [128, 4, NT], BF16)
    ckvn = carve(st, 64 * 1024, [128, 2, NT], BF16)
    tmp1 = carve(st, 68 * 1024, [128, TH], F32)
    tmp2 = carve(st, 70 * 1024, [128, TH], F32)
    tmp_t = [T(), T()]
    cq_t = [T() for _ in range(4)]
    ckv_t = [T() for _ in range(2)]
    cqn_t = [T() for _ in range(4)]
    ckvn_t = [T() for _ in range(2)]
    for t_ in tmp_t + cq_t + ckv_t + cqn_t + ckvn_t:
        t_.w = st.big_t.w
        t_.r = dict(st.big_t.r)
    pbank = [0]
    w_in = wview(W("ab_w_in", j), KT)
    rhs_of = lambda kt, th: hn[:, kt, th * TH:(th + 1) * TH]
    for grp, dst in ((0, QT), (1, KT_)):
        for blk in range(2):
            c0 = grp * 1024 + blk * 512
            wv, wb_t = load_w_slot(k, wp, None, KT, 512, [(0, w_in[:, :, c0:c0 + 512])])
            for c in range(4):
                r0 = blk * 512 + c * 128
                fm_plain(k, st, sg, wv, wb_t, KT, c * 128, 128, rhs_of, hn_t, dst[r0:r0 + 128, :], pbank)
    for blk in range(2):
        c0 = 2048 + blk * 512
        wv, wb_t = load_w_slot(k, wp, None, KT, 512, [(0, w_in[:, :, c0:c0 + 512])])
        tm_proj(k, st, sg, wv, wb_t, KT, 512, lambda kt, tt: hn[:, kt, tt * 128:(tt + 1) * 128], hn_t,
                V[:, blk * 512:(blk + 1) * 512], pbank)
    wv, wb_t = load_w_slot(k, wp, None, KT, 512, [(0, w_in[:, :, 3072:3584])])
    for c in range(4):
        for th in range(2):
            b = pbank[0]
            pbank[0] = (pbank[0] + 1) % 4
            mm_group(k, st, b, [wv[:, kt, c * 128:(c + 1) * 128] for kt in range(KT)], [rhs_of(kt, th) for kt in range(KT)],
                     [[wb_t, hn_t[kt]] for kt in range(KT)])
            k.act.op(lambda b=b, c=c, th=th: nc.scalar.copy(out=cq[:, c, th * TH:(th + 1) * TH], in_=st.bank[b][:]),
                     reads=[st.bank_t[b]], writes=[cq_t[c]])
    krp = wview(W("ab_w_in_krp", j), KT)
    wv, wb_t = load_w_slot(k, wp, None, KT, 512, [(0, w_in[:, :, 3584:3904]), (320, krp)])
    for c in range(2):
        for th in range(2):
            b = pbank[0]
            pbank[0] = (pbank[0] + 1) % 4
            mm_group(k, st, b, [wv[:, kt, c * 128:(c + 1) * 128] for kt in range(KT)], [rhs_of(kt, th) for kt in range(KT)],
                     [[wb_t, hn_t[kt]] for kt in range(KT)])
            k.act.op(lambda b=b, c=c, th=th: nc.scalar.copy(out=ckv[:, c, th * TH:(th + 1) * TH], in_=st.bank[b][:]),
                     reads=[st.bank_t[b]], writes=[ckv_t[c]])
    fm_rope(k, st, sg, wv, wb_t, KT, 256, 64, wv, wb_t, 320, 64, rhs_of, hn_t, KT_[2048:2112, :], pbank, cc, ss, tmp1, tmp2, tmp_t)
    for (src, src_t, dstn, dstn_t, n, gidx, nfeat) in ((cq, cq_t, cqn, cqn_t, 4, 8, 512), (ckv, ckv_t, ckvn, ckvn_t, 2, 9, 256)):
        for th in range(2):
            tsl = slice(th * TH, (th + 1) * TH)
            for c in range(n):
                rms_stats_accum(k, st, src[:, c, tsl], src_t[c], 6, c == 0, c == n - 1)
            rstd_from_bank(k, st, 6, nfeat)
            for c in range(n):
                k.dve.op(lambda c=c, tsl=tsl, src=src, dstn=dstn, gidx=gidx: nc.vector.scalar_tensor_tensor(
                    out=dstn[:, c, tsl], in0=src[:, c, tsl], scalar=st.gains[:, gidx, c:c + 1], in1=st.rstd[:],
                    op0=ALU.mult, op1=ALU.mult),
                    reads=[src_t[c], st.rstd_t, st.gains_t], writes=[dstn_t[c]])
    qup = wview(W("b_w_qup", j), 4)
    qrp = wview(W("b_w_qup_rp", j), 4)
    wvq, wbq_t = load_w_slot(k, wp, None, 4, 2048, [(0, qup), (1536, qrp)])
    rhs_q = lambda kt, th: cqn[:, kt, th * TH:(th + 1) * TH]
    for hh in range(8):
        fm_plain(k, st, sg, wvq, wbq_t, 4, hh * 192, 128, rhs_q, cqn_t, QT[1024 + hh * 128:1024 + (hh + 1) * 128, :], pbank)
        fm_rope(k, st, sg, wvq, wbq_t, 4, hh * 192 + 128, 64, wvq, wbq_t, 1536 + hh * 64, 64, rhs_q, cqn_t,
                QT[2048 + hh * 64:2048 + (hh + 1) * 64, :], pbank, cc, ss, tmp1, tmp2, tmp_t)
    kvup = wview(W("b_w_kvup", j), 2)
    wvk, wbk_t = load_w_slot(k, wp, None, 2, 2048, [(0, kvup)])
    rhs_k = lambda kt, th: ckvn[:, kt, th * TH:(th + 1) * TH]
    for hh in range(8):
        fm_plain(k, st, sg, wvk, wbk_t, 2, hh * 256, 128, rhs_k, ckvn_t, KT_[1024 + hh * 128:1024 + (hh + 1) * 128, :], pbank)
    for half in range(2):
        tm_proj(k, st, sg, wvk, wbk_t, 2, 512, lambda kt, tt: ckvn[:, kt, tt * 128:(tt + 1) * 128], ckvn_t,
                V[:, 1024 + half * 512:1024 + (half + 1) * 512], pbank,
                col_of=[(i * 128, (half * 4 + i) * 256 + 128, 128) for i in range(4)])
    merge_big(st, hn_t + tmp_t + cq_t + ckv_t + cqn_t + ckvn_t + [st.rope_t] + sg.ts)


def proj_odd(k, st, wp, sg, W, j, gi, rope_c, QT, KT_, V):
    nc = k.nc
    sg.sync(st)
    hn, hn_t = hn_all(k, st, gi)
    cc, ss = load_rope(k, st, rope_c, 32)
    tmp1 = carve(st, 68 * 1024, [128, TH], F32)
    tmp2 = carve(st, 70 * 1024, [128, TH], F32)
    tmp_t = [T(), T()]
    for t_ in tmp_t:
        t_.w = st.big_t.w
        t_.r = dict(st.big_t.r)
    pbank = [0]
    w_in = wview(W("c_w_in", j), KT)
    w_rp = wview(W("c_w_in_rp", j), KT)
    rhs_of = lambda kt, th: hn[:, kt, th * TH:(th + 1) * TH]
    for hh in range(8):
        wv, wb_t = load_w_slot(k, wp, None, KT, 512, [(0, w_in[:, :, hh * 768:hh * 768 + 512])])
        wvr, wbr_t = load_w_slot(k, wp, None, KT, 512, [(0, w_rp[:, :, hh * 128:(hh + 1) * 128]),
                                                        (128, w_in[:, :, hh * 768 + 512:hh * 768 + 768])])
        for which in range(4):
            dst = (QT if which < 2 else KT_)[hh * 256 + (which % 2) * 128: hh * 256 + (which % 2) * 128 + 128, :]
            fm_rope(k, st, sg, wv, wb_t, KT, which * 128, 128, wvr, wbr_t, which * 32, 32, rhs_of, hn_t, dst, pbank,
                    cc, ss, tmp1, tmp2, tmp_t)
        tm_proj(k, st, sg, wvr, wbr_t, KT, 256, lambda kt, tt: hn[:, kt, tt * 128:(tt + 1) * 128], hn_t,
                V[:, hh * 256:(hh + 1) * 256], pbank, col_of=[(0, 128, 256)])
    merge_big(st, hn_t + tmp_t + [st.rope_t] + sg.ts)


class RowChunks:
    def __init__(self, chunks):
        self.chunks = chunks

    def __getitem__(self, key):
        rs, cs = key
        for (r0, n, ap) in self.chunks:
            if r0 <= rs.start and rs.stop <= r0 + n:
                return ap[rs.start - r0:rs.stop - r0, cs]
        raise IndexError(f"rows {rs} straddle chunks")


class VView:
    def __init__(self, chunk_aps, c0=0, c1=2048):
        self.v = [ap.rearrange("(t a) c -> t (a c)", a=2) for ap in chunk_aps]
        self.c0, self.c1 = c0, c1
        self.raw = chunk_aps

    def __getitem__(self, key):
        rs, cs = key
        if rs == slice(None):
            nv = VView(self.raw, self.c0 + cs.start, self.c0 + cs.stop)
            return nv
        ch = rs.start // 512
        assert (rs.stop - 1) // 512 == ch
        cc0 = self.c0 + (cs.start or 0) if cs != slice(None) else self.c0
        cc1 = self.c0 + cs.stop if cs != slice(None) else self.c1
        return self.v[ch][rs.start - ch * 512:rs.stop - ch * 512, cc0:cc1]

    def half(self, hf):
        return self.v[hf][:, self.c0:self.c1]


class AttnCtx:
    def __init__(self, k, st):
        self.k = k
        self.st = st
        self.ao = carve(st, 0, [128, KT, NT], BF16)
        self.ao_t = [T() for _ in range(KT)]
        self.sets = []
        for s_ in range(2):
            base = 32 * 1024 + s_ * 24 * 1024
            d = dict(
                q=[carve(st, base + i * 2048, [128, NT], BF16) for i in range(2)],
                ko=[carve(st, base + 4096 + i * 2048, [128, NT], BF16) for i in range(2)],
                kr=[carve(st, base + 8192 + i * 2048, [128, NT], BF16) for i in range(2)],
                vo=carve(st, base + 12288, [128, 8, 256], BF16),
                vr=carve(st, base + 16384, [128, 8, 256], BF16),
                t=T(), ds=DmaSem(k, f"hs{st.uid()}"))
            self.sets.append(d)
        self.pt = [carve(st, 80 * 1024 + i * 256, [128, 128], BF16) for i in range(8)]
        self.pt_t = [T() for _ in range(8)]
        self.pt_i = 0
        self.dg_i = 0
        self.bias = [carve(st, 82 * 1024 + i * 1536, [128, 3, 128], F32) for i in range(2)]
        self.bias_t = [T(), T()]
        self.bias_ds = [DmaSem(k, f"bs{st.uid()}") for _ in range(2)]
        self.rinv = [carve(st, 86 * 1024 + i * 512, [128, 128], F32) for i in range(2)]
        self.rinv_t = [T(), T()]
        self.o1n = carve(st, 87 * 1024, [128, 256], F32)
        self.o2n = carve(st, 88 * 1024, [128, 256], F32)
        self.dd = carve(st, 89 * 1024, [128, 256], F32)
        self.sq2 = carve(st, 90 * 1024, [128, 256], F32)
        self.stmp = carve(st, 91 * 1024, [128, 128], F32)
        self.tmp_t = [T() for _ in range(5)]
        self.sslot_t = [T() for _ in range(8)]
        self.ss_i = 0
        every = self.ao_t + [d["t"] for d in self.sets] + self.pt_t + self.bias_t + self.rinv_t + self.tmp_t
        for t_ in every:
            t_.w = st.big_t.w
            t_.r = dict(st.big_t.r)
        self.every = every
        for i in (6, 7):
            k.dve.op(lambda i=i: k.nc.vector.memset(self.pt[i][:], 0.0), writes=[self.pt_t[i]])

    def done(self):
        merge_big(self.st, self.every)


def attn_tile(k, st, ax, hs, i, streams, blocks, dv, finalize):
    nc = k.nc
    qsl = slice(i * 128, (i + 1) * 128)
    nb = len(blocks)
    for bi, (is_rem, j, kind, bias_ap, bt) in enumerate(blocks):
        ksl = slice(j * 128, (j + 1) * 128)
        for si, (parts, scale) in enumerate(streams):
            s_i = ax.ss_i
            ax.ss_i = (ax.ss_i + 1) % 8
            sb_, so = s_i // 4, (s_i % 4) * 128
            S = st.bank[sb_][:, so:so + 128]
            S_t = ax.sslot_t[s_i]
            for pi, (qi, ki, Kp) in enumerate(parts):
                kt_ = (hs["kr"] if is_rem else hs["ko"])[ki]
                k.pe.op(lambda kt_=kt_, qi=qi, Kp=Kp, S=S, pi=pi: nc.tensor.matmul(
                    S, kt_[0:Kp, ksl], hs["q"][qi][0:Kp, qsl], start=(pi == 0), stop=(pi == len(parts) - 1)),
                    reads=[hs["t"]], writes=[S_t], inc=(pi == len(parts) - 1))
            if kind == "diag":
                p_i = 6 + ax.dg_i
                ax.dg_i ^= 1
            else:
                p_i = ax.pt_i
                ax.pt_i = (ax.pt_i + 1) % 6
            PT, PT_t = ax.pt[p_i], ax.pt_t[p_i]
            if kind == "tile":
                k.dve.op(lambda S=S, bt=bt, scale=scale: nc.vector.scalar_tensor_tensor(
                    out=ax.stmp[:], in0=S, scalar=scale, in1=ax.cur_bias[:, bt, :], op0=ALU.mult, op1=ALU.add),
                    reads=[S_t, ax.cur_bias_t], writes=[ax.tmp_t[4]])
                k.act.op(lambda PT=PT, bias_ap=bias_ap: nc.scalar.activation(out=PT[:], in_=ax.stmp[:], func=AF.Exp, bias=bias_ap, scale=1.0),
                         reads=[ax.tmp_t[4], st.cst_t], writes=[PT_t])
            elif kind == "diag":
                k.act.op(lambda PT=PT, S=S, scale=scale, bias_ap=bias_ap: nc.scalar.activation(
                    out=PT[0:64, :], in_=S[0:64, :], func=AF.Exp, bias=bias_ap[0:64, :], scale=scale),
                    reads=[S_t, st.cst_t], writes=[PT_t])
                k.act.op(lambda PT=PT, S=S, scale=scale, bias_ap=bias_ap: nc.scalar.activation(
                    out=PT[64:128, 64:128], in_=S[64:128, 64:128], func=AF.Exp, bias=bias_ap[64:128, :], scale=scale),
                    reads=[S_t, st.cst_t], writes=[PT_t])
            else:
                k.act.op(lambda PT=PT, S=S, scale=scale, bias_ap=bias_ap: nc.scalar.activation(
                    out=PT[:], in_=S, func=AF.Exp, bias=bias_ap, scale=scale),
                    reads=[S_t, st.cst_t], writes=[PT_t])
            vt = hs["vr"] if is_rem else hs["vo"]
            ob, sb2 = 2 + 2 * si, 3 + 2 * si
            for c in range(dv // 128):
                k.pe.op(lambda vt=vt, c=c, PT=PT, ob=ob: nc.tensor.matmul(
                    st.bank[ob][:, c * 128:(c + 1) * 128], vt[:, j, c * 128:(c + 1) * 128], PT[:],
                    start=(bi == 0 and c == 0), stop=(bi == nb - 1)),
                    reads=[hs["t"], PT_t], writes=[st.bank_t[ob]], inc=False)
            k.pe.op(lambda PT=PT, sb2=sb2: nc.tensor.matmul(
                st.bank[sb2][:, 0:128], st.ones_bf[:], PT[:], start=(bi == 0), stop=(bi == nb - 1)),
                reads=[PT_t, st.ones_t], writes=[st.bank_t[sb2]], inc=True)
    finalize(i)


def load_head(k, ax, hs, qsrc, kosrc, krsrc, vosrc, vrsrc, dv):
    for idx, (ap, R) in enumerate(qsrc):
        k.sp.dma(hs["q"][idx][0:R, :], ap, hs["ds"], writes=[hs["t"]])
    for idx, (ap, R) in enumerate(kosrc):
        k.sp.dma(hs["ko"][idx][0:R, :], ap, hs["ds"], writes=[hs["t"]])
    for idx, (ap, R) in enumerate(krsrc):
        k.sp.dma(hs["kr"][idx][0:R, :], ap, hs["ds"], writes=[hs["t"]])
    for hf in range(2):
        k.sp.dma(hs["vo"][:, hf * 4:(hf + 1) * 4, 0:dv], vosrc.half(hf).rearrange("(j p) d -> p j d", p=128), hs["ds"], writes=[hs["t"]])
        k.sp.dma(hs["vr"][:, hf * 4:(hf + 1) * 4, 0:dv], vrsrc.half(hf).rearrange("(j p) d -> p j d", p=128), hs["ds"], writes=[hs["t"]])


def fin_simple(k, st, ax, chunk):
    nc = k.nc

    def f(i):
        qsl = slice(i * 128, (i + 1) * 128)
        r, r_t = ax.rinv[0], ax.rinv_t[0]
        k.dve.op(lambda: nc.vector.reciprocal(out=r[:], in_=st.bank[3][:, 0:128]), reads=[st.bank_t[3]], writes=[r_t])
        k.dve.op(lambda: nc.vector.tensor_tensor(out=ax.ao[:, chunk, qsl], in0=st.bank[2][:, 0:128], in1=r[:], op=ALU.mult),
                 reads=[st.bank_t[2], r_t], writes=[ax.ao_t[chunk]])
    return f


def attn_even(k, st, sg, j, QT, KTo, Vo, KTr, Vr, abias, acv):
    nc = k.nc
    ax = AttnCtx(k, st)
    cb = st.cb
    ds = DmaSem(k, f"cb{st.uid()}")
    k.sp.dma(cb[:, 0:8], acv[j].partition_broadcast(128), ds, writes=[st.cst_t])
    k.dve.op(lambda: nc.vector.tensor_scalar(out=cb[:, 8:16], in0=cb[:, 0:8], scalar1=st.rb[:, 0:1], scalar2=None, op0=ALU.add),
             reads=[st.cst_t], writes=[st.cst_t])
    sA = 128 ** -0.5
    for hh in range(8):
        hs = ax.sets[hh % 2]
        load_head(k, ax, hs, [(QT[hh * 128:(hh + 1) * 128, :], 128)], [(KTo[hh * 128:(hh + 1) * 128, :], 128)],
                  [(KTr[hh * 128:(hh + 1) * 128, :], 128)], Vo[:, hh * 128:(hh + 1) * 128], Vr[:, hh * 128:(hh + 1) * 128], 128)
        bt, bt_t, bt_ds = ax.bias[hh % 2], ax.bias_t[hh % 2], ax.bias_ds[hh % 2]
        k.sp.dma(bt, abias[j, hh].rearrange("t k q -> k t q"), bt_ds, writes=[bt_t])
        ax.cur_bias, ax.cur_bias_t = bt, bt_t
        for i in range(8):
            blocks = []
            for d_ in range(4, -1, -1):
                jg = i - d_
                rem = jg < 0
                jj = jg + 8 if rem else jg
                if d_ in (2, 3):
                    blocks.append((rem, jj, "plain", cb[:, (8 if rem else 0) + hh:(8 if rem else 0) + hh + 1], None))
                else:
                    tix = {4: 0, 1: 1, 0: 2}[d_]
                    blocks.append((rem, jj, "tile", (st.rb if rem else st.zb)[:, 0:1], tix))
            attn_tile(k, st, ax, hs, i, [([(0, 0, 128)], sA)], blocks, 128, fin_simple(k, st, ax, hh))
    sB = 192 ** -0.5
    for hh in range(8):
        hs = ax.sets[hh % 2]
        load_head(k, ax, hs,
                  [(QT[1024 + hh * 128:1024 + (hh + 1) * 128, :], 128), (QT[2048 + hh * 64:2048 + (hh + 1) * 64, :], 64)],
                  [(KTo[1024 + hh * 128:1024 + (hh + 1) * 128, :], 128), (KTo[2048:2112, :], 64)],
                  [(KTr[1024 + hh * 128:1024 + (hh + 1) * 128, :], 128), (KTr[2048:2112, :], 64)],
                  Vo[:, 1024 + hh * 128:1024 + (hh + 1) * 128], Vr[:, 1024 + hh * 128:1024 + (hh + 1) * 128], 128)
        for i in range(8):
            blocks = [(True, jj, "plain", st.rb[:, 0:1], None) for jj in range(8)]
            blocks += [(False, jj, "diag" if jj == i else "plain", st.zb[:, 0:1], None) for jj in range(i + 1)]
            attn_tile(k, st, ax, hs, i, [([(0, 0, 128), (1, 1, 64)], sB)], blocks, 128, fin_simple(k, st, ax, 8 + hh))
    return ax


def attn_odd(k, st, sg, j, layer, QT, KTo, Vo, KTr, Vr, lvec, gsub):
    nc = k.nc
    ax = AttnCtx(k, st)
    lam_init = 0.8 - 0.6 * math.exp(-0.3 * layer)
    lv = st.lv
    ds = DmaSem(k, f"lv{st.uid()}")
    for q in range(4):
        k.sp.dma(lv[:, q:q + 1], lvec[q][j].rearrange("(p o) -> p o", o=1), ds, writes=[st.cst_t])
    k.sp.dma(lv[:, 8:10], gsub[j].rearrange("(c p) -> p c", p=128), ds, writes=[st.cst_t], allow_slow_non_contiguous=True)
    k.dve.op(lambda: nc.vector.tensor_tensor(out=lv[:, 4:5], in0=lv[:, 0:1], in1=lv[:, 1:2], op=ALU.mult), reads=[st.cst_t], writes=[st.cst_t])
    k.dve.op(lambda: nc.vector.tensor_tensor(out=lv[:, 5:6], in0=lv[:, 2:3], in1=lv[:, 3:4], op=ALU.mult), reads=[st.cst_t], writes=[st.cst_t])
    k.pe.op(lambda: nc.tensor.matmul(st.bank[6][:, 0:2], st.ones_f[:], lv[:, 4:6], start=True, stop=True),
            reads=[st.cst_t, st.ones_t], writes=[st.bank_t[6]], inc=True)
    k.act.op(lambda: nc.scalar.activation(out=lv[:, 6:8], in_=st.bank[6][:, 0:2], func=AF.Exp), reads=[st.bank_t[6]], writes=[st.cst_t])
    k.dve.op(lambda: nc.vector.tensor_tensor(out=lv[:, 4:5], in0=lv[:, 7:8], in1=lv[:, 6:7], op=ALU.subtract), reads=[st.cst_t], writes=[st.cst_t])
    k.dve.op(lambda: nc.vector.tensor_scalar(out=lv[:, 4:5], in0=lv[:, 4:5], scalar1=-lam_init, scalar2=None, op0=ALU.add),
             reads=[st.cst_t], writes=[st.cst_t])
    k.dve.op(lambda: nc.vector.tensor_scalar(out=lv[:, 8:10], in0=lv[:, 8:10], scalar1=1.0 - lam_init, scalar2=None, op0=ALU.mult),
             reads=[st.cst_t], writes=[st.cst_t])
    sC = 128 ** -0.5

    def fin(hh):
        def f(i):
            qsl = slice(i * 128, (i + 1) * 128)
            for si, (dst, dst_t) in enumerate(((ax.o1n, ax.tmp_t[0]), (ax.o2n, ax.tmp_t[1]))):
                r, r_t = ax.rinv[si], ax.rinv_t[si]
                k.dve.op(lambda r=r, si=si: nc.vector.reciprocal(out=r[:], in_=st.bank[3 + 2 * si][:, 0:128]),
                         reads=[st.bank_t[3 + 2 * si]], writes=[r_t])
                for c in range(2):
                    k.dve.op(lambda r=r, si=si, c=c, dst=dst: nc.vector.tensor_tensor(
                        out=dst[:, c * 128:(c + 1) * 128], in0=st.bank[2 + 2 * si][:, c * 128:(c + 1) * 128], in1=r[:], op=ALU.mult),
                        reads=[st.bank_t[2 + 2 * si], r_t], writes=[dst_t])
            k.dve.op(lambda: nc.vector.scalar_tensor_tensor(out=ax.dd[:], in0=ax.o2n[:], scalar=lv[:, 4:5], in1=ax.o1n[:],
                                                           op0=ALU.mult, op1=ALU.add),
                     reads=[ax.tmp_t[0], ax.tmp_t[1], st.cst_t], writes=[ax.tmp_t[2]])
            k.act.op(lambda: nc.scalar.activation(out=ax.sq2[:], in_=ax.dd[:], func=AF.Square), reads=[ax.tmp_t[2]], writes=[ax.tmp_t[3]])
            for c in range(2):
                k.pe.op(lambda c=c: nc.tensor.matmul(st.bank[6][:, 0:128], st.ones_f[:], ax.sq2[:, c * 128:(c + 1) * 128],
                                                     start=(c == 0), stop=(c == 1)),
                        reads=[ax.tmp_t[3], st.ones_t], writes=[st.bank_t[6]], inc=(c == 1))
            k.dve.op(lambda: nc.vector.tensor_scalar(out=ax.stmp[:], in0=st.bank[6][:, 0:128], scalar1=1.0 / 256, scalar2=EPS,
                                                    op0=ALU.mult, op1=ALU.add), reads=[st.bank_t[6]], writes=[ax.tmp_t[4]])
            k.act.op(lambda: nc.scalar.activation(out=ax.stmp[:], in_=ax.stmp[:], func=AF.Sqrt), reads=[ax.tmp_t[4]], writes=[ax.tmp_t[4]])
            k.dve.op(lambda: nc.vector.reciprocal(out=ax.stmp[:], in_=ax.stmp[:]), reads=[ax.tmp_t[4]], writes=[ax.tmp_t[4]])
            for c in range(2):
                k.dve.op(lambda c=c: nc.vector.scalar_tensor_tensor(
                    out=ax.ao[:, hh * 2 + c, qsl], in0=ax.dd[:, c * 128:(c + 1) * 128], scalar=lv[:, 8 + c:9 + c], in1=ax.stmp[:],
                    op0=ALU.mult, op1=ALU.mult),
                    reads=[ax.tmp_t[2], ax.tmp_t[4], st.cst_t], writes=[ax.ao_t[hh * 2 + c]])
        return f

    for hh in range(8):
        hs = ax.sets[hh % 2]
        r0 = hh * 256
        load_head(k, ax, hs, [(QT[r0:r0 + 128, :], 128), (QT[r0 + 128:r0 + 256, :], 128)],
                  [(KTo[r0:r0 + 128, :], 128), (KTo[r0 + 128:r0 + 256, :], 128)],
                  [(KTr[r0:r0 + 128, :], 128), (KTr[r0 + 128:r0 + 256, :], 128)],
                  Vo[:, r0:r0 + 256], Vr[:, r0:r0 + 256], 256)
        for i in range(8):
            blocks = [(True, jj, "plain", st.rb[:, 0:1], None) for jj in range(8)]
            blocks += [(False, jj, "diag" if jj == i else "plain", st.zb[:, 0:1], None) for jj in range(i + 1)]
            attn_tile(k, st, ax, hs, i, [([(0, 0, 128)], sC), ([(1, 1, 128)], sC)], blocks, 256, fin(hh))
    return ax


def mix_out(k, st, wp, ax, w_out, gi):
    for th in range(2):
        y = carve(st, 60 * 1024, [128, KT, TH], F32)
        y_t = [T() for _ in range(KT)]
        for t_ in y_t:
            t_.w = st.big_t.w
            t_.r = dict(st.big_t.r)
            for e_ in ax.every:
                for tok in list(e_.r.values()) + ([e_.w] if e_.w else []):
                    old = t_.r.get(tok[0])
                    if old is None or (old[1], old[2]) < (tok[1], tok[2]):
                        t_.r[tok[0]] = tok
        outproj_half(k, st, wp, w_out, KT, lambda kc, th=th: ax.ao[:, kc, th * TH:(th + 1) * TH], ax.ao_t, y, y_t)
        tail_half(k, st, th, gi, y, y_t)
        ax.every = ax.every + y_t
    ax.done()


NCORES = 8
QROWS = 2560
KROWS = 2112
KVROWS = KROWS + 2048


def make_state(k):
    st = State(k)
    st.big_t = T("big")
    st._uid = [0]

    def uid():
        st._uid[0] += 1
        return st._uid[0]
    st.uid = uid
    st.cb = k.sb("cb", [128, 16], F32)
    st.rb = k.sb("rb", [128, 1], F32)
    st.zb = k.sb("zb", [128, 1], F32)
    st.lv = k.sb("lv", [128, 16], F32)
    st.cst_t = T("cst")
    k.dve.op(lambda: k.nc.vector.memset(st.zb[:], 0.0), writes=[st.cst_t])
    return st


class Weights:
    def __init__(self, k, specs, gather=True):
        self.k = k
        self.full = {}
        self.t = {}
        nc = k.nc
        for name, (R, C) in specs.items():
            if not gather:
                self.full[name] = k.din("w_" + name, [R, C])
                self.t[name] = T()
                continue
            sh = k.din("w_" + name, [R // NCORES, C])
            bounce = nc.dram_tensor("wb_" + name, [R // NCORES, C], F32)
            full = nc.dram_tensor("wf_" + name, [R, C], F32)
            ds = DmaSem(k, "wb_" + name)
            rows = R // NCORES
            step = max(1, (1 << 18) // C)
            for r0 in range(0, rows, step):
                r1 = min(rows, r0 + step)
                k.pool.dma(bounce.ap()[r0:r1, :], sh[r0:r1, :], ds)
            k.pool.e.wait_ge(ds.sem, ds.n)
            sem = k.newsem("cc_" + name)
            nc.gpsimd.collective_compute("AllGather", ALU.bypass, replica_groups=[list(range(NCORES))],
                                         ins=[bounce.ap().opt()], outs=[full.ap().opt()]).then_inc(sem)
            t_ = T()
            t_.w = ("cc_" + name, 0, 1, sem)
            self.full[name] = full.ap()
            self.t[name] = t_

    def get(self, name):
        self.k.pool._need([self.t[name].w])
        return self.full[name]


def build_part1(layer, gather=False):
    even = layer % 2 == 0
    k = K()
    hT = k.din("hT", [D, NT])
    gv = k.din("gvec", [3, D])
    out = k.dout("hT_out", [D, NT])
    qt = k.dout("qt_out", [QROWS, NT], BF16)
    kv = k.dout("kv_out", [KVROWS, NT], BF16)
    specs = {"ffn_w_in": (D, 2 * DFF), "ffn_w_out": (DFF, D)}
    if even:
        specs.update({"ab_w_in": (D, 3904), "ab_w_in_krp": (D, 64), "b_w_qup": (512, 1536), "b_w_qup_rp": (512, 512),
                      "b_w_kvup": (256, 2048)})
        rope = k.din("rope", [2, 64, NT])
        gq = k.din("g_q", [512])
        gkv = k.din("g_kv", [256])
    else:
        specs.update({"c_w_in": (D, 6144), "c_w_in_rp": (D, 1024)})
        rope = k.din("rope", [2, 32, NT])
    st = make_state(k)
    W = Weights(k, specs, gather)
    wp = WPool(k, st)
    wp.big16()
    sg = Stager(k, st)
    load_gains(k, st, [gv[0], gv[1], gv[2]], [1.0, 0.5, 1.0])
    if even:
        k.sp.dma(st.gains[:, 8, 0:4], gq.rearrange("(c p) -> p c", p=128), st.misc_ds, writes=[st.gains_t], allow_slow_non_contiguous=True)
        k.sp.dma(st.gains[:, 9, 0:2], gkv.rearrange("(c p) -> p c", p=128), st.misc_ds, writes=[st.gains_t], allow_slow_non_contiguous=True)
    load_h(k, st, hT)
    ffn_full(k, st, wp, W.get("ffn_w_in"), W.get("ffn_w_out"), 0, 1)
    wp.big16()
    KT_ = RowChunks([(0, KROWS, kv[0:KROWS, :])])
    V = VView([kv[KROWS:KROWS + 1024, :], kv[KROWS + 1024:KVROWS, :]])
    Wf = lambda name, j: W.get(name)
    if even:
        proj_even(k, st, wp, sg, Wf, 0, 2, rope, qt, KT_, V)
    else:
        proj_odd(k, st, wp, sg, Wf, 0, 2, rope, qt, KT_, V)
    store_h(k, st, out)
    for ds in sg.ds:
        k.outsems.append(ds)
    return k.finish()


def build_part2(layer, gather=False):
    even = layer % 2 == 0
    k = K()
    hT = k.din("hT", [D, NT])
    gv = k.din("gvec", [5, D])
    qt = k.din("qt", [QROWS, NT], BF16)
    kvo = k.din("kv_own", [KVROWS, NT], BF16)
    kvr = k.din("kv_rem", [KVROWS, NT], BF16)
    rbias = k.din("rbias", [128, 1])
    pT = k.din("pT", [256, NT])
    out = k.dout("hT_out", [D, NT])
    specs = {"mix_w_out": (D, D), "ffn_w_in": (D, 2 * DFF), "ffn_w_out": (DFF, D), "ple_w_gate": (D, D), "ple_w_proj": (256, D)}
    if even:
        abias = k.din("abias", [1, 8, 3, 128, 128])
        acv = k.din("acv", [1, 8])
    else:
        lvec = [k.din(f"lvec{q}", [1, 128]) for q in range(4)]
        gsub = k.din("gsub", [1, 256])
    st = make_state(k)
    W = Weights(k, specs, gather)
    wp = WPool(k, st)
    wp.big16()
    k.sp.dma(st.rb[:], rbias, st.misc_ds, writes=[st.cst_t])
    load_gains(k, st, [gv[0], gv[1], gv[2], gv[3], gv[4]], [1.0, 1.0, 0.5, 1.0, 1.0])
    load_h(k, st, hT)
    KTo, KTr = RowChunks([(0, KROWS, kvo[0:KROWS, :])]), RowChunks([(0, KROWS, kvr[0:KROWS, :])])
    Vo = VView([kvo[KROWS:KROWS + 1024, :], kvo[KROWS + 1024:KVROWS, :]])
    Vr = VView([kvr[KROWS:KROWS + 1024, :], kvr[KROWS + 1024:KVROWS, :]])
    if even:
        ax = attn_even(k, st, None, 0, qt, KTo, Vo, KTr, Vr, abias, acv)
    else:
        ax = attn_odd(k, st, None, 0, layer, qt, KTo, Vo, KTr, Vr, lvec, gsub)
    mix_out(k, st, wp, ax, W.get("mix_w_out"), 0)
    ffn_full(k, st, wp, W.get("ffn_w_in"), W.get("ffn_w_out"), 1, 2)
    wp.big16()
    for th in range(2):
        ple_half(k, st, wp, th, W.get("ple_w_gate"), W.get("ple_w_proj"), pT, 3, 4)
    store_h(k, st, out)
    return k.finish()


W_SHAPES = {
    "ffn1_w_in": (4, D, 2 * DFF), "ffn1_w_out": (4, DFF, D), "ffn2_w_in": (4, D, 2 * DFF), "ffn2_w_out": (4, DFF, D),
    "ab_w_in": (2, D, 3904), "ab_w_in_krp": (2, D, 64), "b_w_qup": (2, 512, 1536), "b_w_qup_rp": (2, 512, 512),
    "b_w_kvup": (2, 256, 2048), "ab_w_out": (2, D, D), "c_w_in": (2, D, 6144), "c_w_in_rp": (2, D, 1024),
    "c_w_out": (2, D, D), "ple_w_gate": (4, D, D), "ple_w_proj": (4, 256, D),
}


def build_fused(nlayers=4, ncores=NCORES):
    k = K()
    nc = k.nc
    hT = k.din("hT", [D, NT])
    out = k.dout("hT_out", [D, NT])
    gv = k.din("gvec", [4, 8, D])
    gq = k.din("g_q", [2, 512])
    gkv = k.din("g_kv", [2, 256])
    rope_b = k.din("rope_b", [2, 64, NT])
    rope_c = k.din("rope_c", [2, 32, NT])
    rbias = k.din("rbias", [128, 1])
    pT = k.din("pT", [4, 256, NT])
    abias = k.din("abias", [2, 8, 3, 128, 128])
    acv = k.din("acv", [2, 8])
    lvec = [k.din(f"lvec{q}", [2, 128]) for q in range(4)]
    gsub = k.din("gsub", [2, 256])
    Wd = {n: k.din("w_" + n, list(shp)) for n, shp in W_SHAPES.items()}
    W = lambda name, j: Wd[name][j]
    st = make_state(k)
    wp = WPool(k, st)
    wp.big16()
    sg = Stager(k, st)
    k.sp.dma(st.rb[:], rbias, DmaSem(k, "rb"), writes=[st.cst_t])
    load_h(k, st, hT)
    for layer in range(nlayers):
        even = layer % 2 == 0
        j = layer // 2
        gds = DmaSem(k, f"g{layer}")
        for i in range(8):
            k.sp.dma(st.gains[:, i, :], gv[layer, i].rearrange("(kt p) -> p kt", p=128), gds,
                     writes=[st.gains_t], allow_slow_non_contiguous=True)
        if even:
            k.sp.dma(st.gains[:, 8, 0:4], gq[j].rearrange("(c p) -> p c", p=128), gds, writes=[st.gains_t], allow_slow_non_contiguous=True)
            k.sp.dma(st.gains[:, 9, 0:2], gkv[j].rearrange("(c p) -> p c", p=128), gds, writes=[st.gains_t], allow_slow_non_contiguous=True)
        for i in (1, 5):
            k.dve.op(lambda i=i: nc.vector.tensor_scalar(out=st.gains[:, i, :], in0=st.gains[:, i, :], scalar1=0.5, scalar2=None,
                                                         op0=ALU.mult), reads=[st.gains_t], writes=[st.gains_t])
        ffn_full(k, st, wp, W("ffn1_w_in", layer), W("ffn1_w_out", layer), 0, 1)
        wp.big16()
        qts = k.dint(f"qt{layer}", [QROWS, NT], BF16)
        csz = [1024, 1024, 1024, 1024] + ([64] if even else [])
        own_c = [k.dint(f"kvown{layer}_{ci}", [n_, NT], BF16) for ci, n_ in enumerate(csz)]
        pair_c = [k.dint(f"kvpair{layer}_{ci}", [2 * n_, NT], BF16) for ci, n_ in enumerate(csz)]
        kt_chunks = [(0, 1024, own_c[0]), (1024, 1024, own_c[1])] + ([(2048, 64, own_c[4])] if even else [])
        KTo = RowChunks(kt_chunks)
        Vo = VView([own_c[2], own_c[3]])
        if even:
            proj_even(k, st, wp, sg, W, j, 2, rope_b, qts, KTo, Vo)
        else:
            proj_odd(k, st, wp, sg, W, j, 2, rope_c, qts, KTo, Vo)
        toks = list(sg.stores)
        sg.stores = []
        k.pool._need(toks)
        k.sp._need(toks)
        cctoks = []
        for ci in range(len(csz)):
            ccsem = k.newsem(f"cc{layer}_{ci}")
            nc.gpsimd.collective_compute("AllGather", ALU.bypass, replica_groups=[[2 * i_, 2 * i_ + 1] for i_ in range(ncores // 2)],
                                         ins=[own_c[ci].opt()], outs=[pair_c[ci].opt()]).then_inc(ccsem)
            cctoks.append((f"cc{layer}_{ci}", 0, 1, ccsem))
        k.sp._need(cctoks)
        KTr = RowChunks([(0, 1024, pair_c[0][0:1024, :]), (1024, 1024, pair_c[1][0:1024, :])]
                        + ([(2048, 64, pair_c[4][0:64, :])] if even else []))
        Vr = VView([pair_c[2][0:1024, :], pair_c[3][0:1024, :]])
        if even:
            ax = attn_even(k, st, None, j, qts, KTo, Vo, KTr, Vr, abias, acv)
            mix_out(k, st, wp, ax, W("ab_w_out", j), 3)
        else:
            ax = attn_odd(k, st, None, j, layer, qts, KTo, Vo, KTr, Vr, lvec, gsub)
            mix_out(k, st, wp, ax, W("c_w_out", j), 3)
        ffn_full(k, st, wp, W("ffn2_w_in", layer), W("ffn2_w_out", layer), 4, 5)
        wp.big16()
        for th in range(2):
            ple_half(k, st, wp, th, W("ple_w_gate", layer), W("ple_w_proj", layer), pT[layer], 6, 7)
    store_h(k, st, out)
    return k.finish()


def kernel_fused(I, nlayers=4):
    x, p = I["x"], I["p"]
    shared = {"gvec": np.ascontiguousarray(np.stack([
        np.stack([I["ffn1_g_pre"][l], I["ffn1_g_post"][l], I["mix_g_pre"][l], I["mix_g_post"][l],
                  I["ffn2_g_pre"][l], I["ffn2_g_post"][l], I["ple_g_pre"][l], I["ple_g_post"][l]]) for l in range(4)])),
        "g_q": I["b_g_q"], "g_kv": I["b_g_kv"], "gsub": I["c_g_sub"]}
    tiles = [_abias_tiles(I["a_rel_bias"][j]) for j in range(2)]
    shared["abias"] = np.ascontiguousarray(np.stack([t[0] for t in tiles]))
    shared["acv"] = np.ascontiguousarray(np.concatenate([t[1] for t in tiles], 0))
    for q_, n_ in enumerate(("c_lq1", "c_lk1", "c_lq2", "c_lk2")):
        shared[f"lvec{q_}"] = I[n_]
    for n_ in ("ffn1_w_in", "ffn1_w_out", "ffn2_w_in", "ffn2_w_out", "ab_w_in", "b_w_qup", "b_w_kvup", "ab_w_out",
               "c_w_in", "c_w_out", "ple_w_gate", "ple_w_proj"):
        shared["w_" + n_] = I[n_]
    ab = I["ab_w_in"]
    shared["w_ab_w_in_krp"] = np.ascontiguousarray(ab[:, :, 3840 + _swap_halves(64)])
    idx = np.concatenate([h_ * 192 + 128 + _swap_halves(64) for h_ in range(8)])
    shared["w_b_w_qup_rp"] = np.ascontiguousarray(I["b_w_qup"][:, :, idx])
    idx = np.concatenate([h_ * 768 + w_ * 128 + _swap_halves(32) for h_ in range(8) for w_ in range(4)])
    shared["w_c_w_in_rp"] = np.ascontiguousarray(I["c_w_in"][:, :, idx])
    in_maps = []
    for c in range(NCORES):
        m = dict(shared)
        b, hf = c // 2, c % 2
        m["hT"] = np.ascontiguousarray(x[b, hf * NT:(hf + 1) * NT, :].T)
        m["rope_b"] = _rope_tab(64, hf * NT)
        m["rope_c"] = _rope_tab(32, hf * NT)
        m["rbias"] = np.full((128, 1), 0.0 if hf == 1 else -30000.0, np.float32)
        m["pT"] = np.ascontiguousarray(np.transpose(p[:, b, hf * NT:(hf + 1) * NT, :], (0, 2, 1)))
        in_maps.append(m)
    nc = build_fused(nlayers, _NRUN)
    res = _run(nc, in_maps)
    hT = [res[c]["hT_out"] for c in range(NCORES)]
    _dbg(f"h_ple_{nlayers - 1}", hT)
    out = np.empty((4, 2 * NT, D), np.float32)
    for c in range(NCORES):
        out[c // 2, (c % 2) * NT:(c % 2 + 1) * NT, :] = hT[c].T
    return out


def _shard_rows(w):
    w = np.ascontiguousarray(w)
    return [w] * NCORES


def _rope_tab(rot, pos0):
    inv = (500000.0 ** (-np.arange(0, rot, 2, dtype=np.float32) / np.float32(rot))).astype(np.float32)
    ang = (np.arange(pos0, pos0 + NT, dtype=np.float32)[:, None] * inv[None, :]).astype(np.float32)
    c = np.cos(ang).astype(np.float32).T
    s = np.sin(ang).astype(np.float32).T
    return np.ascontiguousarray(np.stack([np.concatenate([c, c], 0), np.concatenate([-s, s], 0)], 0))


def _swap_halves(n):
    return np.concatenate([np.arange(n // 2, n), np.arange(0, n // 2)])


def _abias_tiles(table):
    kk = np.arange(128)[:, None]
    qq = np.arange(128)[None, :]
    out = np.empty((8, 3, 128, 128), np.float32)
    far = table[:, 191]
    t4 = np.broadcast_to(far[:, None, None], (8, 128, 128)).copy()
    t4[:, (kk < 64) & (qq >= 64)] = -30000.0
    out[:, 0] = t4
    d1 = np.clip(128 + qq - kk, -63, 128) + 63
    out[:, 1] = table[:, d1]
    d0 = np.clip(qq - kk, -63, 128) + 63
    t0 = table[:, d0].copy()
    t0[:, (kk >= 64) & (qq < 64)] = -30000.0
    out[:, 2] = t0
    return out, np.ascontiguousarray(far[None, :])


_NRUN = NCORES


def _run(nc, in_maps):
    res = run_bass_kernel_spmd(nc, in_maps[:_NRUN], core_ids=list(range(_NRUN)))
    r = list(res.results)
    while len(r) < NCORES:
        r.append(r[len(r) % _NRUN])
    return r


_DEBUG = None
_KSTOP = 0


def _dbg(name, hT):
    if _DEBUG is None:
        return
    ref = _DEBUG[name][0]
    for c in range(2):
        o = hT[c].T.astype(np.float32)
        r = ref[c * NT:(c + 1) * NT]
        print("DBG", name, "core", c, "relerr", float(np.sqrt(((o - r) ** 2).mean() / (r ** 2).mean())),
              "finite", bool(np.isfinite(o).all()), flush=True)
        print("   per-tile", [round(float(np.sqrt(((o[t * 128:(t + 1) * 128] - r[t * 128:(t + 1) * 128]) ** 2).mean() / (r ** 2).mean())), 4) for t in range(8)], flush=True)


_FUSED = True
_NLAYERS = 4


def kernel(**I):
    import ml_dtypes
    I = {k_: np.asarray(v) for k_, v in I.items()}
    if _FUSED:
        return kernel_fused(I, _NLAYERS)
    x, p = I["x"], I["p"]
    hT = [np.ascontiguousarray(x[c // 2, (c % 2) * NT:(c % 2 + 1) * NT, :].T) for c in range(NCORES)]
    rbias = [np.full((128, 1), 0.0 if c % 2 == 1 else -30000.0, np.float32) for c in range(NCORES)]
    for layer in range(4):
        even = layer % 2 == 0
        j = layer // 2
        shared = {"gvec": np.stack([I["ffn1_g_pre"][layer], I["ffn1_g_post"][layer], I["mix_g_pre"][layer]])}
        wsh = {"ffn_w_in": _shard_rows(I["ffn1_w_in"][layer]), "ffn_w_out": _shard_rows(I["ffn1_w_out"][layer])}
        if even:
            w_in = I["ab_w_in"][j]
            wsh["ab_w_in"] = _shard_rows(w_in)
            wsh["ab_w_in_krp"] = _shard_rows(w_in[:, 3840 + _swap_halves(64)])
            qup = I["b_w_qup"][j]
            wsh["b_w_qup"] = _shard_rows(qup)
            idx = np.concatenate([h_ * 192 + 128 + _swap_halves(64) for h_ in range(8)])
            wsh["b_w_qup_rp"] = _shard_rows(qup[:, idx])
            wsh["b_w_kvup"] = _shard_rows(I["b_w_kvup"][j])
            shared["g_q"] = I["b_g_q"][j]
            shared["g_kv"] = I["b_g_kv"][j]
            rope = [_rope_tab(64, (c % 2) * NT) for c in range(NCORES)]
        else:
            w_in = I["c_w_in"][j]
            wsh["c_w_in"] = _shard_rows(w_in)
            idx = np.concatenate([h_ * 768 + w_ * 128 + _swap_halves(32) for h_ in range(8) for w_ in range(4)])
            wsh["c_w_in_rp"] = _shard_rows(w_in[:, idx])
            rope = [_rope_tab(32, (c % 2) * NT) for c in range(NCORES)]
        nc = build_part1(layer)
        in_maps = []
        for c in range(NCORES):
            m = dict(shared)
            m["hT"] = hT[c]
            m["rope"] = rope[c]
            for n_, sh in wsh.items():
                m["w_" + n_] = sh[c]
            in_maps.append(m)
        res = _run(nc, in_maps)
        hT = [res[c]["hT_out"] for c in range(NCORES)]
        _dbg(f"h_ffn1_{layer}", hT)
        qt = [res[c]["qt_out"] for c in range(NCORES)]
        kv = [res[c]["kv_out"] for c in range(NCORES)]
        if _KSTOP == 10 + layer:
            return None
        del wsh, in_maps
        shared = {"gvec": np.stack([I["mix_g_post"][layer], I["ffn2_g_pre"][layer], I["ffn2_g_post"][layer],
                                    I["ple_g_pre"][layer], I["ple_g_post"][layer]])}
        wsh = {"mix_w_out": _shard_rows((I["ab_w_out"] if even else I["c_w_out"])[j]),
               "ffn_w_in": _shard_rows(I["ffn2_w_in"][layer]), "ffn_w_out": _shard_rows(I["ffn2_w_out"][layer]),
               "ple_w_gate": _shard_rows(I["ple_w_gate"][layer]), "ple_w_proj": _shard_rows(I["ple_w_proj"][layer])}
        if even:
            tiles, far = _abias_tiles(I["a_rel_bias"][j])
            shared["abias"] = tiles[None]
            shared["acv"] = far
        else:
            for q_, n_ in enumerate(("c_lq1", "c_lk1", "c_lq2", "c_lk2")):
                shared[f"lvec{q_}"] = I[n_][j][None]
            shared["gsub"] = I["c_g_sub"][j][None]
        nc = build_part2(layer)
        in_maps = []
        for c in range(NCORES):
            m = dict(shared)
            m["hT"] = hT[c]
            m["qt"] = qt[c]
            m["kv_own"] = kv[c]
            m["kv_rem"] = kv[c - 1] if c % 2 == 1 else kv[c]
            m["rbias"] = rbias[c]
            m["pT"] = np.ascontiguousarray(p[layer, c // 2, (c % 2) * NT:(c % 2 + 1) * NT, :].T)
            for n_, sh in wsh.items():
                m["w_" + n_] = sh[c]
            in_maps.append(m)
        res = _run(nc, in_maps)
        hT = [res[c]["hT_out"] for c in range(NCORES)]
        _dbg(f"h_ple_{layer}", hT)
        if _KSTOP == 20 + layer:
            return None
        del wsh, in_maps
    out = np.empty((4, 2 * NT, D), np.float32)
    for c in range(NCORES):
        out[c // 2, (c % 2) * NT:(c % 2 + 1) * NT, :] = hT[c].T
    return out
```

```python
import contextlib
import math
import numpy as np
import concourse.bass as bass
import concourse.mybir as mybir
from concourse.bass_utils import run_bass_kernel_spmd

F32 = mybir.dt.float32
BF16 = mybir.dt.bfloat16
AF = mybir.ActivationFunctionType
ALU = mybir.AluOpType

D = 2048
NT = 1024
TH = 512
KT = D // 128
DFF = 5632
FC = DFF // 128
EPS = 1e-6
SEM_LIMIT = 20000


class T:
    __slots__ = ("w", "r", "name")

    def __init__(self, name=""):
        self.w = None
        self.r = {}
        self.name = name


class DmaSem:
    def __init__(self, K, name):
        self.sem = K.newsem(name)
        self.group = "dma_" + name
        self.n = 0


class E:
    def __init__(self, K, name, eng, is_pe=False):
        self.K = K
        self.name = name
        self.e = eng
        self.is_pe = is_pe
        self.epoch = 0
        self.cnt = 0
        self.sem = K.newsem(f"{name}_e0")
        self.seen = {}

    def _need(self, toks):
        for tok in toks:
            if tok is None:
                continue
            group, epoch, val, sem = tok
            if self.is_pe and group == self.name:
                continue
            s = self.seen.get(group)
            if s is not None and s >= (epoch, val):
                continue
            self.e.wait_ge(sem, val)
            self.seen[group] = (epoch, val)

    def deps(self, reads, writes):
        toks = []
        for b in reads:
            toks.append(b.w)
        for b in writes:
            toks.append(b.w)
            toks.extend(b.r.values())
        self._need(toks)

    def _record(self, tok, reads, writes):
        for b in reads:
            old = b.r.get(tok[0])
            if old is None or (old[1], old[2]) < (tok[1], tok[2]):
                b.r[tok[0]] = tok
        for b in writes:
            b.w = tok
            b.r = {}

    def op(self, fn, reads=(), writes=(), inc=True):
        self.deps(reads, writes)
        ins = fn()
        if inc:
            ins.then_inc(self.sem, 1)
            self.cnt += 1
            tok = (self.name, self.epoch, self.cnt, self.sem)
            self._record(tok, reads, writes)
            if self.cnt >= SEM_LIMIT:
                self.epoch += 1
                self.cnt = 0
                self.sem = self.K.newsem(f"{self.name}_e{self.epoch}")
        else:
            tok = (self.name, self.epoch, self.cnt + 1, self.sem)
            self._record(tok, reads, writes)
        return ins

    def dma(self, out, in_, ds, reads=(), writes=(), **kw):
        self.deps(reads, writes)
        ins = self.e.dma_start(out=out, in_=in_, **kw)
        ds.n += 16
        ins.then_inc(ds.sem, 16)
        tok = (ds.group, 0, ds.n, ds.sem)
        self._record(tok, reads, writes)
        return ins


class K:
    def __init__(self):
        self.nc = bass.Bass("TRN2", target_bir_lowering=False)
        self.stack = contextlib.ExitStack()
        self.nsem = 0
        nc = self.nc
        self.pe = E(self, "pe", nc.tensor, is_pe=True)
        self.act = E(self, "act", nc.scalar)
        self.dve = E(self, "dve", nc.vector)
        self.pool = E(self, "pool", nc.gpsimd)
        self.sp = E(self, "sp", nc.sync)
        self.outsems = []

    def newsem(self, name):
        self.nsem += 1
        return self.stack.enter_context(self.nc.semaphore(f"s{self.nsem}_{name}"))

    def sb(self, name, shape, dt):
        return self.stack.enter_context(self.nc.sbuf_tensor(name, shape, dt))

    def ps(self, name, shape, dt=F32):
        return self.stack.enter_context(self.nc.psum_tensor(name, shape, dt))

    def din(self, name, shape, dt=F32):
        return self.nc.dram_tensor(name, list(shape), dt, kind="ExternalInput").ap()

    def dout(self, name, shape, dt=F32):
        return self.nc.dram_tensor(name, list(shape), dt, kind="ExternalOutput").ap()

    def dint(self, name, shape, dt=F32):
        return self.nc.dram_tensor(name, list(shape), dt, kind="Internal").ap()

    def finish(self):
        for ds in self.outsems:
            self.sp.e.wait_ge(ds.sem, ds.n)
        self.stack.close()
        return self.nc


class State:
    def __init__(self, k: K):
        self.k = k
        self.h = k.sb("h", [128, KT, NT], F32)
        self.h_t = [[T(f"h{c}_{t}") for t in range(2)] for c in range(KT)]
        self.bank = [k.ps(f"bank{i}", [128, 512], F32) for i in range(8)]
        self.bank_t = [T(f"bank{i}") for i in range(8)]
        self.ones_bf = k.sb("ones_bf", [128, 128], BF16)
        self.ones_f = k.sb("ones_f", [128, 128], F32)
        self.ones_t = T("ones")
        k.dve.op(lambda: k.nc.vector.memset(self.ones_bf[:], 1.0), writes=[self.ones_t])
        k.dve.op(lambda: k.nc.vector.memset(self.ones_f[:], 1.0), writes=[self.ones_t])
        self.big = k.sb("big", [128, 136 * 1024], mybir.dt.uint8)
        self.sq = [k.sb(f"sq{i}", [128, TH], F32) for i in range(1)]
        self.sq_t = [T(f"sq{i}") for i in range(1)]
        self.sq_i = 0
        self.rstd = k.sb("rstd", [128, TH], F32)
        self.rstd_t = T("rstd")
        self.gains = k.sb("gains", [128, 10, KT], F32)
        self.gains_t = T("gains")
        self.misc_ds = DmaSem(k, "misc")


def rms_stats_accum(k, st, src_ap, src_t, bank_i, first, last, src_is_psum=False):
    i = 0
    sq, sq_t = st.sq[i], st.sq_t[i]
    k.act.op(lambda: k.nc.scalar.activation(out=sq[:], in_=src_ap, func=AF.Square),
             reads=[src_t], writes=[sq_t])
    k.pe.op(lambda: k.nc.tensor.matmul(st.bank[bank_i][:], st.ones_f[:], sq[:], start=first, stop=last),
            reads=[sq_t, st.ones_t], writes=[st.bank_t[bank_i]], inc=True)


def rstd_from_bank(k, st, bank_i, nfeat):
    k.dve.op(lambda: k.nc.vector.tensor_scalar(out=st.rstd[:], in0=st.bank[bank_i][:], scalar1=1.0 / nfeat,
                                              scalar2=EPS, op0=ALU.mult, op1=ALU.add),
             reads=[st.bank_t[bank_i]], writes=[st.rstd_t])
    k.act.op(lambda: k.nc.scalar.activation(out=st.rstd[:], in_=st.rstd[:], func=AF.Sqrt),
             reads=[st.rstd_t], writes=[st.rstd_t])
    k.dve.op(lambda: k.nc.vector.reciprocal(out=st.rstd[:], in_=st.rstd[:]),
             reads=[st.rstd_t], writes=[st.rstd_t])


class WPool:
    def __init__(self, k, st):
        self.k = k
        self.st = st
        self.ds = [DmaSem(k, f"w{i}") for i in range(4)]
        self.ds_x = [DmaSem(k, f"wx{i}") for i in range(4)]
        self.slots = []
        self.ts = []
        self.i = 0
        self.cfg = None

    def config(self, offs_kb, kb):
        cfg = (tuple(offs_kb), kb)
        if cfg == self.cfg:
            return
        st = self.st
        u = {}
        for t_ in self.ts + [st.big_t]:
            for tok in list(t_.r.values()) + ([t_.w] if t_.w is not None else []):
                old = u.get(tok[0])
                if old is None or (old[1], old[2]) < (tok[1], tok[2]):
                    u[tok[0]] = tok
        st.big_t.w = None
        st.big_t.r = dict(u)
        self.slots = [carve(st, o * 1024, [128, kb * 512], BF16) for o in offs_kb]
        self.ts = [T() for _ in offs_kb]
        for t_ in self.ts:
            t_.r = dict(u)
        self.i = 0
        self.cfg = cfg

    def big16(self):
        self.config([104, 120], 16)

    def small4(self):
        self.config([120, 124, 128, 132], 4)

    def next(self):
        i = self.i
        self.i = (self.i + 1) % len(self.slots)
        return self.slots[i], self.ts[i], self.ds[i]


def carve(st, off_bytes, shape, dt):
    esz = 2 if dt == BF16 else 4
    n = 1
    for s_ in shape[1:]:
        n *= s_
    ap = st.big[:, off_bytes:off_bytes + n * esz].bitcast(dt)
    if len(shape) == 3:
        ap = ap.rearrange("p (a b) -> p a b", a=shape[1])
    return ap


def load_gains(k, st, vecs, scales):
    nc = k.nc
    for i, v in enumerate(vecs):
        k.sp.dma(st.gains[:, i, :], v.rearrange("(kt p) -> p kt", p=128), st.misc_ds,
                 writes=[st.gains_t], allow_slow_non_contiguous=True)
    for i, s_ in enumerate(scales):
        if s_ != 1.0:
            k.dve.op(lambda i=i, s_=s_: nc.vector.tensor_scalar(
                out=st.gains[:, i, :], in0=st.gains[:, i, :], scalar1=s_, scalar2=None, op0=ALU.mult),
                reads=[st.gains_t], writes=[st.gains_t])


def all_h(st):
    return [st.h_t[c][t] for c in range(KT) for t in range(2)]


def load_h(k, st, hT):
    v = hT.rearrange("(kt p) t -> p kt t", p=128)
    ds = DmaSem(k, "hload")
    for q in range(4):
        k.sp.dma(st.h[:, q * 4:(q + 1) * 4, :], v[:, q * 4:(q + 1) * 4, :], ds, writes=all_h(st))


def store_h(k, st, hT_out):
    v = hT_out.rearrange("(kt p) t -> p kt t", p=128)
    ds = DmaSem(k, "hstore")
    k.outsems.append(ds)
    for q in range(4):
        k.sp.dma(v[:, q * 4:(q + 1) * 4, :], st.h[:, q * 4:(q + 1) * 4, :], ds, reads=all_h(st))


def prenorm_half(k, st, th, gi, xn, xn_t):
    nc = k.nc
    tsl = slice(th * TH, (th + 1) * TH)
    for kt in range(KT):
        rms_stats_accum(k, st, st.h[:, kt, tsl], st.h_t[kt][th], 6, kt == 0, kt == KT - 1)
    rstd_from_bank(k, st, 6, D)
    for kt in range(KT):
        k.dve.op(lambda kt=kt: nc.vector.scalar_tensor_tensor(
            out=xn[:, kt, :], in0=st.h[:, kt, tsl], scalar=st.gains[:, gi, kt:kt + 1], in1=st.rstd[:],
            op0=ALU.mult, op1=ALU.mult),
            reads=[st.h_t[kt][th], st.rstd_t, st.gains_t], writes=[xn_t[kt]])


def tail_half(k, st, th, gi, y, y_t):
    nc = k.nc
    tsl = slice(th * TH, (th + 1) * TH)
    rstd_from_bank(k, st, 7, D)
    for dc in range(KT):
        k.dve.op(lambda dc=dc: nc.vector.scalar_tensor_tensor(
            out=y[:, dc, :], in0=y[:, dc, :], scalar=st.gains[:, gi, dc:dc + 1], in1=st.rstd[:],
            op0=ALU.mult, op1=ALU.mult),
            reads=[y_t[dc], st.rstd_t, st.gains_t], writes=[y_t[dc]])
        k.dve.op(lambda dc=dc: nc.vector.tensor_tensor(
            out=st.h[:, dc, tsl], in0=st.h[:, dc, tsl], in1=y[:, dc, :], op=ALU.add),
            reads=[y_t[dc], st.h_t[dc][th]], writes=[st.h_t[dc][th]])


def outproj_half(k, st, wp, w_dram, nkc, rhs_of, rhs_t, y, y_t, kblk=None):
    nc = k.nc
    kblk = kblk or nkc
    w_v = w_dram.rearrange("(kc p) n -> p kc n", p=128)
    for dc in range(KT):
        bnk = 4 + (dc % 2)
        for k0 in range(0, nkc, kblk):
            wb, wb_t, wb_ds = wp.next()
            wv = wb[:, 0:kblk * 128].rearrange("p (kc n) -> p kc n", kc=kblk)
            k.pool.dma(wv, w_v[:, k0:k0 + kblk, dc * 128:(dc + 1) * 128], wb_ds, writes=[wb_t])
            for kc in range(k0, k0 + kblk):
                k.pe.op(lambda kc=kc, bnk=bnk, wv=wv, k0=k0: nc.tensor.matmul(
                    st.bank[bnk][:], wv[:, kc - k0, :], rhs_of(kc), start=(kc == 0), stop=(kc == nkc - 1)),
                    reads=[wb_t, rhs_t[kc]], writes=[st.bank_t[bnk]], inc=(kc == k0 + kblk - 1))
        k.act.op(lambda dc=dc, bnk=bnk: nc.scalar.copy(out=y[:, dc, :], in_=st.bank[bnk][:]),
                 reads=[st.bank_t[bnk]], writes=[y_t[dc]])
        rms_stats_accum(k, st, st.bank[bnk][:], st.bank_t[bnk], 7, dc == 0, dc == KT - 1)


def ffn_full(k, st, wp, w_in, w_out, gpre_i, gpost_i):
    nc = k.nc
    wp.small4()
    xn = carve(st, 0, [128, KT, NT], BF16)
    g = carve(st, 32 * 1024, [128, FC, NT], BF16)
    y = carve(st, 0, [128, KT, TH], F32)
    xn_t = [T() for _ in range(KT)]
    g_t = [[T() for _ in range(2)] for _ in range(FC)]
    for t_ in xn_t + [t2 for gg in g_t for t2 in gg]:
        t_.w = st.big_t.w
        t_.r = dict(st.big_t.r)
    for th in range(2):
        prenorm_half(k, st, th, gpre_i, xn[:, :, th * TH:(th + 1) * TH], xn_t)
    w_in_v = w_in.rearrange("(kt p) n -> p kt n", p=128)
    XM0 = 36
    x_slots = [carve(st, 32 * 1024 + (XM0 + 2 * e) * 2048, [128, 2048], BF16) for e in range(4)]
    x_ts = [T() for _ in range(4)]
    for t_ in x_ts:
        t_.w = st.big_t.w
        t_.r = dict(st.big_t.r)
    rot = [0]

    def next_slot(m):
        if m < 32:
            i_ = rot[0] % 8
            rot[0] += 1
            if i_ >= 4:
                return x_slots[i_ - 4], x_ts[i_ - 4], wp.ds_x[i_ - 4]
            return wp.slots[i_], wp.ts[i_], wp.ds[i_]
        i_ = rot[0] % 4
        rot[0] += 1
        return wp.slots[i_], wp.ts[i_], wp.ds[i_]

    for m in range(FC):
        wvs = []
        for half in range(2):
            wb, wb_t, wb_ds = next_slot(m)
            wv = wb[:, 0:KT * 128].rearrange("p (kt n) -> p kt n", kt=KT)
            k.pool.dma(wv, w_in_v[:, :, half * DFF + m * 128:half * DFF + (m + 1) * 128], wb_ds, writes=[wb_t])
            wvs.append((wv, wb_t))
        for th in range(2):
            tsl = slice(th * TH, (th + 1) * TH)
            ba, bb = 2 * th, 2 * th + 1
            for half, bnk in ((0, ba), (1, bb)):
                wv, wb_t = wvs[half]
                for kt in range(KT):
                    k.pe.op(lambda kt=kt, bnk=bnk, wv=wv, tsl=tsl: nc.tensor.matmul(
                        st.bank[bnk][:], wv[:, kt, :], xn[:, kt, tsl], start=(kt == 0), stop=(kt == KT - 1)),
                        reads=[wb_t, xn_t[kt]], writes=[st.bank_t[bnk]], inc=(kt == KT - 1))
            k.act.op(lambda ba=ba, m=m, tsl=tsl: nc.scalar.activation(out=g[:, m, tsl], in_=st.bank[ba][:], func=AF.Silu),
                     reads=[st.bank_t[ba]], writes=[g_t[m][th]] + ([x_ts[(m - XM0) // 2]] if m >= XM0 else []))
            k.dve.op(lambda bb=bb, m=m, tsl=tsl: nc.vector.tensor_tensor(
                out=g[:, m, tsl], in0=st.bank[bb][:], in1=g[:, m, tsl], op=ALU.mult),
                reads=[st.bank_t[bb], g_t[m][th]], writes=[g_t[m][th]])
    y_t = [T() for _ in range(KT)]
    for t_ in y_t:
        for x_ in xn_t:
            for tok in list(x_.r.values()) + ([x_.w] if x_.w is not None else []):
                old = t_.r.get(tok[0])
                if old is None or (old[1], old[2]) < (tok[1], tok[2]):
                    t_.r[tok[0]] = tok
    for th in range(2):
        tsl = slice(th * TH, (th + 1) * TH)
        outproj_half(k, st, wp, w_out, FC, lambda kc, tsl=tsl: g[:, kc, tsl], [g_t[kc][th] for kc in range(FC)], y, y_t, kblk=11)
        tail_half(k, st, th, gpost_i, y, y_t)
    wp.i = 0
    merge_big(st, xn_t + [t2 for gg in g_t for t2 in gg] + y_t + x_ts)


def merge_big(st, ts):
    r = {}
    w = None
    for t_ in ts:
        for tok in list(t_.r.values()) + ([t_.w] if t_.w is not None else []):
            old = r.get(tok[0])
            if old is None or (old[1], old[2]) < (tok[1], tok[2]):
                r[tok[0]] = tok
    st.big_t.w = None
    st.big_t.r = r


def ple_half(k, st, wp, th, w_gate, w_proj, pT_dram, gpre_i, gpost_i):
    nc = k.nc
    xn = carve(st, 0, [128, KT, TH], BF16)
    pt = carve(st, 16 * 1024, [128, 2, TH], BF16)
    sg = carve(st, 20 * 1024, [128, 2, TH], F32)
    y = carve(st, 60 * 1024, [128, KT, TH], F32)
    xn_t = [T() for _ in range(KT)]
    y_t = [T() for _ in range(KT)]
    pt_t = T()
    sg_t = [T(), T()]
    for t_ in xn_t + y_t + [pt_t] + sg_t:
        t_.w = st.big_t.w
        t_.r = dict(st.big_t.r)
    tsl = slice(th * TH, (th + 1) * TH)
    ds = DmaSem(k, f"pt{st.uid()}")
    k.pool.dma(pt, pT_dram.rearrange("(kc p) t -> p kc t", p=128)[:, :, tsl], ds, writes=[pt_t])
    prenorm_half(k, st, th, gpre_i, xn, xn_t)
    wg_v = w_gate.rearrange("(kt p) n -> p kt n", p=128)
    wp_v = w_proj.rearrange("(kt p) n -> p kt n", p=128)
    for blk in range(8):
        wb, wb_t, wb_ds = wp.next()
        wv = wb[:, 0:(KT + 2) * 256].rearrange("p (kt n) -> p kt n", kt=KT + 2)
        k.pool.dma(wv[:, 0:KT, :], wg_v[:, :, blk * 256:(blk + 1) * 256], wb_ds, writes=[wb_t])
        k.pool.dma(wv[:, KT:KT + 2, :], wp_v[:, :, blk * 256:(blk + 1) * 256], wb_ds, writes=[wb_t])
        for c in range(2):
            dc = blk * 2 + c
            bg, bp = (0, 1) if dc % 2 == 0 else (2, 3)
            for kt in range(KT):
                k.pe.op(lambda kt=kt, bg=bg, c=c, wv=wv: nc.tensor.matmul(
                    st.bank[bg][:], wv[:, kt, c * 128:(c + 1) * 128], xn[:, kt, :],
                    start=(kt == 0), stop=(kt == KT - 1)),
                    reads=[wb_t, xn_t[kt]], writes=[st.bank_t[bg]], inc=(kt == KT - 1))
            for kc in range(2):
                k.pe.op(lambda kc=kc, bp=bp, c=c, wv=wv: nc.tensor.matmul(
                    st.bank[bp][:], wv[:, KT + kc, c * 128:(c + 1) * 128], pt[:, kc, :],
                    start=(kc == 0), stop=(kc == 1)),
                    reads=[wb_t, pt_t], writes=[st.bank_t[bp]], inc=(kc == 1))
            s_, s_t = sg[:, dc % 2, :], sg_t[dc % 2]
            k.act.op(lambda bg=bg, s_=s_: nc.scalar.activation(out=s_, in_=st.bank[bg][:], func=AF.Sigmoid),
                     reads=[st.bank_t[bg]], writes=[s_t])
            k.dve.op(lambda bp=bp, s_=s_, dc=dc: nc.vector.tensor_tensor(
                out=y[:, dc, :], in0=st.bank[bp][:], in1=s_, op=ALU.mult),
                reads=[st.bank_t[bp], s_t], writes=[y_t[dc]])
            rms_stats_accum(k, st, y[:, dc, :], y_t[dc], 7, dc == 0, dc == KT - 1)
    tail_half(k, st, th, gpost_i, y, y_t)
    merge_big(st, xn_t + y_t + [pt_t] + sg_t)


class Stager:
    def __init__(self, k, st, n=2):
        self.k = k
        self.tiles = [carve(st, (96 + 2 * i) * 1024, [128, NT], BF16) for i in range(n)]
        self.ts = [T(f"stg{i}") for i in range(n)]
        self.ds = [DmaSem(k, f"stg{i}") for i in range(n)]
        self.i = 0
        self.stores = []

    def next(self):
        i = self.i
        self.i = (self.i + 1) % len(self.tiles)
        return self.tiles[i], self.ts[i], self.ds[i]

    def store(self, dst, src, t_, ds):
        self.k.sp.dma(dst, src, ds, reads=[t_])
        self.stores.append((ds.group, 0, ds.n, ds.sem))

    def sync(self, st):
        for t_ in self.ts:
            t_.w = st.big_t.w
            t_.r = dict(st.big_t.r)

    def barrier(self, eng):
        eng._need(self.stores)
        self.stores = []


def load_w_slot(k, wp, view_ap, nk, ncols, pieces):
    wb, wb_t, wb_ds = wp.next()
    wv = wb[:, 0:nk * ncols].rearrange("p (kt n) -> p kt n", kt=nk)
    for c0, src in pieces:
        w = src.shape[-1]
        k.pool.dma(wv[:, :, c0:c0 + w], src, wb_ds, writes=[wb_t])
    return wv, wb_t


def mm_group(k, st, bank_i, lhs_list, rhs_list, reads, M=128, N=TH):
    nc = k.nc
    n = len(lhs_list)
    for i in range(n):
        k.pe.op(lambda i=i: nc.tensor.matmul(st.bank[bank_i][0:M, 0:N], lhs_list[i], rhs_list[i],
                                            start=(i == 0), stop=(i == n - 1)),
                reads=reads[i], writes=[st.bank_t[bank_i]], inc=(i == n - 1))


def hn_all(k, st, gi):
    hn = carve(st, 0, [128, KT, NT], BF16)
    hn_t = [T() for _ in range(KT)]
    for t_ in hn_t:
        t_.w = st.big_t.w
        t_.r = dict(st.big_t.r)
    for th in range(2):
        prenorm_half(k, st, th, gi, hn[:, :, th * TH:(th + 1) * TH], hn_t)
    return hn, hn_t


def fm_plain(k, st, sg, wv, wb_t, nk, c0, M, rhs_of, rhs_t, dst_rows, pbank):
    nc = k.nc
    stg, stg_t, stg_ds = sg.next()
    for th in range(2):
        b = pbank[0]
        pbank[0] = (pbank[0] + 1) % 4
        mm_group(k, st, b, [wv[:, kt, c0:c0 + M] for kt in range(nk)], [rhs_of(kt, th) for kt in range(nk)],
                 [[wb_t, rhs_t[kt]] for kt in range(nk)], M=M)
        k.act.op(lambda b=b, th=th: nc.scalar.copy(out=stg[0:M, th * TH:(th + 1) * TH], in_=st.bank[b][0:M, :]),
                 reads=[st.bank_t[b]], writes=[stg_t])
    sg.store(dst_rows, stg[0:M, :], stg_t, stg_ds)


def fm_rope(k, st, sg, wv, wb_t, nk, c0, M, wvr, wbr_t, cr0, R, rhs_of, rhs_t, dst_rows, pbank, cc, ss, tmp1, tmp2, tmp_t):
    nc = k.nc
    stg, stg_t, stg_ds = sg.next()
    for th in range(2):
        tsl = slice(th * TH, (th + 1) * TH)
        b = pbank[0]
        b2 = (b + 1) % 4
        pbank[0] = (pbank[0] + 2) % 4
        mm_group(k, st, b, [wv[:, kt, c0:c0 + M] for kt in range(nk)], [rhs_of(kt, th) for kt in range(nk)],
                 [[wb_t, rhs_t[kt]] for kt in range(nk)], M=M)
        mm_group(k, st, b2, [wvr[:, kt, cr0:cr0 + R] for kt in range(nk)], [rhs_of(kt, th) for kt in range(nk)],
                 [[wbr_t, rhs_t[kt]] for kt in range(nk)], M=R)
        k.dve.op(lambda b=b, tsl=tsl: nc.vector.tensor_tensor(out=tmp1[0:R, :], in0=st.bank[b][0:R, :], in1=cc[0:R, tsl], op=ALU.mult),
                 reads=[st.bank_t[b], st.rope_t], writes=[tmp_t[0]])
        k.dve.op(lambda b2=b2, tsl=tsl: nc.vector.tensor_tensor(out=tmp2[0:R, :], in0=st.bank[b2][0:R, :], in1=ss[0:R, tsl], op=ALU.mult),
                 reads=[st.bank_t[b2], st.rope_t], writes=[tmp_t[1]])
        k.dve.op(lambda tsl=tsl: nc.vector.tensor_tensor(out=stg[0:R, tsl], in0=tmp1[0:R, :], in1=tmp2[0:R, :], op=ALU.add),
                 reads=[tmp_t[0], tmp_t[1]], writes=[stg_t])
        if M > R:
            for (p0, p1) in ((32, 64), (64, 128)):
                k.act.op(lambda b=b, tsl=tsl, p0=p0, p1=p1: nc.scalar.copy(out=stg[p0:p1, tsl], in_=st.bank[b][p0:p1, :]),
                         reads=[st.bank_t[b]], writes=[stg_t])
    sg.store(dst_rows, stg[0:M, :], stg_t, stg_ds)


def tm_proj(k, st, sg, wv, wb_t, nk, ncols, lhs_of, lhs_t, dstV, pbank, col_of=None):
    nc = k.nc
    for tt in range(NT // 128):
        b = pbank[0]
        pbank[0] = (pbank[0] + 1) % 4
        stg, stg_t, stg_ds = sg.next()
        if col_of is None:
            mm_group(k, st, b, [lhs_of(kt, tt) for kt in range(nk)], [wv[:, kt, 0:ncols] for kt in range(nk)],
                     [[wb_t, lhs_t[kt]] for kt in range(nk)], M=128, N=ncols)
        else:
            for (o0, c0, w) in col_of:
                k_last = (o0, c0, w) == col_of[-1]
                for kt in range(nk):
                    k.pe.op(lambda kt=kt, o0=o0, c0=c0, w=w: nc.tensor.matmul(
                        st.bank[b][:, o0:o0 + w], lhs_of(kt, tt), wv[:, kt, c0:c0 + w], start=(kt == 0), stop=(kt == nk - 1)),
                        reads=[wb_t, lhs_t[kt]], writes=[st.bank_t[b]], inc=(k_last and kt == nk - 1))
        k.act.op(lambda b=b: nc.scalar.copy(out=stg[:, 0:ncols], in_=st.bank[b][:, 0:ncols]),
                 reads=[st.bank_t[b]], writes=[stg_t])
        sg.store(dstV[tt * 128:(tt + 1) * 128, :], stg[:, 0:ncols], stg_t, stg_ds)


def wview(w2d, nk):
    return w2d.rearrange("(kt p) n -> p kt n", p=128)


def load_rope(k, st, rope_dram, R):
    cc = carve(st, 88 * 1024, [128, NT], F32)
    ss = carve(st, 92 * 1024, [128, NT], F32)
    st.rope_t = T("rope")
    st.rope_t.w = st.big_t.w
    st.rope_t.r = dict(st.big_t.r)
    ds = DmaSem(k, f"rope{st.uid()}")
    k.sp.dma(cc[0:R, :], rope_dram[0], ds, writes=[st.rope_t])
    k.sp.dma(ss[0:R, :], rope_dram[1], ds, writes=[st.rope_t])
    return cc, ss


def proj_even(k, st, wp, sg, W, j, gi, rope_b, QT, KT_, V):
    nc = k.nc
    sg.sync(st)
    hn, hn_t = hn_all(k, st, gi)
    cc, ss = load_rope(k, st, rope_b, 64)
    cq = carve(st, 32 * 1024, [128, 4, NT], F32)
    ckv = carve(st, 48 * 1024, [128, 2, NT], F32)
    cqn = carve(st, 56 * 1024, [128, 4, NT], BF16)
    ckvn = carve(st, 64 * 1024, [128, 2, NT], BF16)
    tmp1 = carve(st, 68 * 1024, [128, TH], F32)
    tmp2 = carve(st, 70 * 1024, [128, TH], F32)
    tmp_t = [T(), T()]
    cq_t = [T() for _ in range(4)]
    ckv_t = [T() for _ in range(2)]
    cqn_t = [T() for _ in range(4)]
    ckvn_t = [T() for _ in range(2)]
    for t_ in tmp_t + cq_t + ckv_t + cqn_t + ckvn_t:
        t_.w = st.big_t.w
        t_.r = dict(st.big_t.r)
    pbank = [0]
    w_in = wview(W("ab_w_in", j), KT)
    rhs_of = lambda kt, th: hn[:, kt, th * TH:(th + 1) * TH]
    for grp, dst in ((0, QT), (1, KT_)):
        for blk in range(2):
            c0 = grp * 1024 + blk * 512
            wv, wb_t = load_w_slot(k, wp, None, KT, 512, [(0, w_in[:, :, c0:c0 + 512])])
            for c in range(4):
                r0 = blk * 512 + c * 128
                fm_plain(k, st, sg, wv, wb_t, KT, c * 128, 128, rhs_of, hn_t, dst[r0:r0 + 128, :], pbank)
    for blk in range(2):
        c0 = 2048 + blk * 512
        wv, wb_t = load_w_slot(k, wp, None, KT, 512, [(0, w_in[:, :, c0:c0 + 512])])
        tm_proj(k, st, sg, wv, wb_t, KT, 512, lambda kt, tt: hn[:, kt, tt * 128:(tt + 1) * 128], hn_t,
                V[:, blk * 512:(blk + 1) * 512], pbank)
    wv, wb_t = load_w_slot(k, wp, None, KT, 512, [(0, w_in[:, :, 3072:3584])])
    for c in range(4):
        for th in range(2):
            b = pbank[0]
            pbank[0] = (pbank[0] + 1) % 4
            mm_group(k, st, b, [wv[:, kt, c * 128:(c + 1) * 128] for kt in range(KT)], [rhs_of(kt, th) for kt in range(KT)],
                     [[wb_t, hn_t[kt]] for kt in range(KT)])
            k.act.op(lambda b=b, c=c, th=th: nc.scalar.copy(out=cq[:, c, th * TH:(th + 1) * TH], in_=st.bank[b][:]),
                     reads=[st.bank_t[b]], writes=[cq_t[c]])
    krp = wview(W("ab_w_in_krp", j), KT)
    wv, wb_t = load_w_slot(k, wp, None, KT, 512, [(0, w_in[:, :, 3584:3904]), (320, krp)])
    for c in range(2):
        for th in range(2):
            b = pbank[0]
            pbank[0] = (pbank[0] + 1) % 4
            mm_group(k, st, b, [wv[:, kt, c * 128:(c + 1) * 128] for kt in range(KT)], [rhs_of(kt, th) for kt in range(KT)],
                     [[wb_t, hn_t[kt]] for kt in range(KT)])
            k.act.op(lambda b=b, c=c, th=th: nc.scalar.copy(out=ckv[:, c, th * TH:(th + 1) * TH], in_=st.bank[b][:]),
                     reads=[st.bank_t[b]], writes=[ckv_t[c]])
    fm_rope(k, st, sg, wv, wb_t, KT, 256, 64, wv, wb_t, 320, 64, rhs_of, hn_t, KT_[2048:2112, :], pbank, cc, ss, tmp1, tmp2, tmp_t)
    for (src, src_t, dstn, dstn_t, n, gidx, nfeat) in ((cq, cq_t, cqn, cqn_t, 4, 8, 512), (ckv, ckv_t, ckvn, ckvn_t, 2, 9, 256)):
        for th in range(2):
            tsl = slice(th * TH, (th + 1) * TH)
            for c in range(n):
                rms_stats_accum(k, st, src[:, c, tsl], src_t[c], 6, c == 0, c == n - 1)
            rstd_from_bank(k, st, 6, nfeat)
            for c in range(n):
                k.dve.op(lambda c=c, tsl=tsl, src=src, dstn=dstn, gidx=gidx: nc.vector.scalar_tensor_tensor(
                    out=dstn[:, c, tsl], in0=src[:, c, tsl], scalar=st.gains[:, gidx, c:c + 1], in1=st.rstd[:],
                    op0=ALU.mult, op1=ALU.mult),
                    reads=[src_t[c], st.rstd_t, st.gains_t], writes=[dstn_t[c]])
    qup = wview(W("b_w_qup", j), 4)
    qrp = wview(W("b_w_qup_rp", j), 4)
    wvq, wbq_t = load_w_slot(k, wp, None, 4, 2048, [(0, qup), (1536, qrp)])
    rhs_q = lambda kt, th: cqn[:, kt, th * TH:(th + 1) * TH]
    for hh in range(8):
        fm_plain(k, st, sg, wvq, wbq_t, 4, hh * 192, 128, rhs_q, cqn_t, QT[1024 + hh * 128:1024 + (hh + 1) * 128, :], pbank)
        fm_rope(k, st, sg, wvq, wbq_t, 4, hh * 192 + 128, 64, wvq, wbq_t, 1536 + hh * 64, 64, rhs_q, cqn_t,
                QT[2048 + hh * 64:2048 + (hh + 1) * 64, :], pbank, cc, ss, tmp1, tmp2, tmp_t)
    kvup = wview(W("b_w_kvup", j), 2)
    wvk, wbk_t = load_w_slot(k, wp, None, 2, 2048, [(0, kvup)])
    rhs_k = lambda kt, th: ckvn[:, kt, th * TH:(th + 1) * TH]
    for hh in range(8):
        fm_plain(k, st, sg, wvk, wbk_t, 2, hh * 256, 128, rhs_k, ckvn_t, KT_[1024 + hh * 128:1024 + (hh + 1) * 128, :], pbank)
    for half in range(2):
        tm_proj(k, st, sg, wvk, wbk_t, 2, 512, lambda kt, tt: ckvn[:, kt, tt * 128:(tt + 1) * 128], ckvn_t,
                V[:, 1024 + half * 512:1024 + (half + 1) * 512], pbank,
                col_of=[(i * 128, (half * 4 + i) * 256 + 128, 128) for i in range(4)])
    merge_big(st, hn_t + tmp_t + cq_t + ckv_t + cqn_t + ckvn_t + [st.rope_t] + sg.ts)


def proj_odd(k, st, wp, sg, W, j, gi, rope_c, QT, KT_, V):
    nc = k.nc
    sg.sync(st)
    hn, hn_t = hn_all(k, st, gi)
    cc, ss = load_rope(k, st, rope_c, 32)
    tmp1 = carve(st, 68 * 1024, [128, TH], F32)
    tmp2 = carve(st, 70 * 1024, [128, TH], F32)
    tmp_t = [T(), T()]
    for t_ in tmp_t:
        t_.w = st.big_t.w
        t_.r = dict(st.big_t.r)
    pbank = [0]
    w_in = wview(W("c_w_in", j), KT)
    w_rp = wview(W("c_w_in_rp", j), KT)
    rhs_of = lambda kt, th: hn[:, kt, th * TH:(th + 1) * TH]
    for hh in range(8):
        wv, wb_t = load_w_slot(k, wp, None, KT, 512, [(0, w_in[:, :, hh * 768:hh * 768 + 512])])
        wvr, wbr_t = load_w_slot(k, wp, None, KT, 512, [(0, w_rp[:, :, hh * 128:(hh + 1) * 128]),
                                                        (128, w_in[:, :, hh * 768 + 512:hh * 768 + 768])])
        for which in range(4):
            dst = (QT if which < 2 else KT_)[hh * 256 + (which % 2) * 128: hh * 256 + (which % 2) * 128 + 128, :]
            fm_rope(k, st, sg, wv, wb_t, KT, which * 128, 128, wvr, wbr_t, which * 32, 32, rhs_of, hn_t, dst, pbank,
                    cc, ss, tmp1, tmp2, tmp_t)
        tm_proj(k, st, sg, wvr, wbr_t, KT, 256, lambda kt, tt: hn[:, kt, tt * 128:(tt + 1) * 128], hn_t,
                V[:, hh * 256:(hh + 1) * 256], pbank, col_of=[(0, 128, 256)])
    merge_big(st, hn_t + tmp_t + [st.rope_t] + sg.ts)


class RowChunks:
    def __init__(self, chunks):
        self.chunks = chunks

    def __getitem__(self, key):
        rs, cs = key
        for (r0, n, ap) in self.chunks:
            if r0 <= rs.start and rs.stop <= r0 + n:
                return ap[rs.start - r0:rs.stop - r0, cs]
        raise IndexError(f"rows {rs} straddle chunks")


class VView:
    def __init__(self, chunk_aps, c0=0, c1=2048):
        self.v = [ap.rearrange("(t a) c -> t (a c)", a=2) for ap in chunk_aps]
        self.c0, self.c1 = c0, c1
        self.raw = chunk_aps

    def __getitem__(self, key):
        rs, cs = key
        if rs == slice(None):
            nv = VView(self.raw, self.c0 + cs.start, self.c0 + cs.stop)
            return nv
        ch = rs.start // 512
        assert (rs.stop - 1) // 512 == ch
        cc0 = self.c0 + (cs.start or 0) if cs != slice(None) else self.c0
        cc1 = self.c0 + cs.stop if cs != slice(None) else self.c1
        return self.v[ch][rs.start - ch * 512:rs.stop - ch * 512, cc0:cc1]

    def half(self, hf):
        return self.v[hf][:, self.c0:self.c1]


class AttnCtx:
    def __init__(self, k, st):
        self.k = k
        self.st = st
        self.ao = carve(st, 0, [128, KT, NT], BF16)
        self.ao_t = [T() for _ in range(KT)]
        self.sets = []
        for s_ in range(2):
            base = 32 * 1024 + s_ * 24 * 1024
            d = dict(
                q=[carve(st, base + i * 2048, [128, NT], BF16) for i in range(2)],
                ko=[carve(st, base + 4096 + i * 2048, [128, NT], BF16) for i in range(2)],
                kr=[carve(st, base + 8192 + i * 2048, [128, NT], BF16) for i in range(2)],
                vo=carve(st, base + 12288, [128, 8, 256], BF16),
                vr=carve(st, base + 16384, [128, 8, 256], BF16),
                t=T(), ds=DmaSem(k, f"hs{st.uid()}"))
            self.sets.append(d)
        self.pt = [carve(st, 80 * 1024 + i * 256, [128, 128], BF16) for i in range(8)]
        self.pt_t = [T() for _ in range(8)]
        self.pt_i = 0
        self.dg_i = 0
        self.bias = [carve(st, 82 * 1024 + i * 1536, [128, 3, 128], F32) for i in range(2)]
        self.bias_t = [T(), T()]
        self.bias_ds = [DmaSem(k, f"bs{st.uid()}") for _ in range(2)]
        self.rinv = [carve(st, 86 * 1024 + i * 512, [128, 128], F32) for i in range(2)]
        self.rinv_t = [T(), T()]
        self.o1n = carve(st, 87 * 1024, [128, 256], F32)
        self.o2n = carve(st, 88 * 1024, [128, 256], F32)
        self.dd = carve(st, 89 * 1024, [128, 256], F32)
        self.sq2 = carve(st, 90 * 1024, [128, 256], F32)
        self.stmp = carve(st, 91 * 1024, [128, 128], F32)
        self.tmp_t = [T() for _ in range(5)]
        self.sslot_t = [T() for _ in range(8)]
        self.ss_i = 0
        every = self.ao_t + [d["t"] for d in self.sets] + self.pt_t + self.bias_t + self.rinv_t + self.tmp_t
        for t_ in every:
            t_.w = st.big_t.w
            t_.r = dict(st.big_t.r)
        self.every = every
        for i in (6, 7):
            k.dve.op(lambda i=i: k.nc.vector.memset(self.pt[i][:], 0.0), writes=[self.pt_t[i]])

    def done(self):
        merge_big(self.st, self.every)


def attn_tile(k, st, ax, hs, i, streams, blocks, dv, finalize):
    nc = k.nc
    qsl = slice(i * 128, (i + 1) * 128)
    nb = len(blocks)
    for bi, (is_rem, j, kind, bias_ap, bt) in enumerate(blocks):
        ksl = slice(j * 128, (j + 1) * 128)
        for si, (parts, scale) in enumerate(streams):
            s_i = ax.ss_i
            ax.ss_i = (ax.ss_i + 1) % 8
            sb_, so = s_i // 4, (s_i % 4) * 128
            S = st.bank[sb_][:, so:so + 128]
            S_t = ax.sslot_t[s_i]
            for pi, (qi, ki, Kp) in enumerate(parts):
                kt_ = (hs["kr"] if is_rem else hs["ko"])[ki]
                k.pe.op(lambda kt_=kt_, qi=qi, Kp=Kp, S=S, pi=pi: nc.tensor.matmul(
                    S, kt_[0:Kp, ksl], hs["q"][qi][0:Kp, qsl], start=(pi == 0), stop=(pi == len(parts) - 1)),
                    reads=[hs["t"]], writes=[S_t], inc=(pi == len(parts) - 1))
            if kind == "diag":
                p_i = 6 + ax.dg_i
                ax.dg_i ^= 1
            else:
                p_i = ax.pt_i
                ax.pt_i = (ax.pt_i + 1) % 6
            PT, PT_t = ax.pt[p_i], ax.pt_t[p_i]
            if kind == "tile":
                k.dve.op(lambda S=S, bt=bt, scale=scale: nc.vector.scalar_tensor_tensor(
                    out=ax.stmp[:], in0=S, scalar=scale, in1=ax.cur_bias[:, bt, :], op0=ALU.mult, op1=ALU.add),
                    reads=[S_t, ax.cur_bias_t], writes=[ax.tmp_t[4]])
                k.act.op(lambda PT=PT, bias_ap=bias_ap: nc.scalar.activation(out=PT[:], in_=ax.stmp[:], func=AF.Exp, bias=bias_ap, scale=1.0),
                         reads=[ax.tmp_t[4], st.cst_t], writes=[PT_t])
            elif kind == "diag":
                k.act.op(lambda PT=PT, S=S, scale=scale, bias_ap=bias_ap: nc.scalar.activation(
                    out=PT[0:64, :], in_=S[0:64, :], func=AF.Exp, bias=bias_ap[0:64, :], scale=scale),
                    reads=[S_t, st.cst_t], writes=[PT_t])
                k.act.op(lambda PT=PT, S=S, scale=scale, bias_ap=bias_ap: nc.scalar.activation(
                    out=PT[64:128, 64:128], in_=S[64:128, 64:128], func=AF.Exp, bias=bias_ap[64:128, :], scale=scale),
                    reads=[S_t, st.cst_t], writes=[PT_t])
            else:
                k.act.op(lambda PT=PT, S=S, scale=scale, bias_ap=bias_ap: nc.scalar.activation(
                    out=PT[:], in_=S, func=AF.Exp, bias=bias_ap, scale=scale),
                    reads=[S_t, st.cst_t], writes=[PT_t])
            vt = hs["vr"] if is_rem else hs["vo"]
            ob, sb2 = 2 + 2 * si, 3 + 2 * si
            for c in range(dv // 128):
                k.pe.op(lambda vt=vt, c=c, PT=PT, ob=ob: nc.tensor.matmul(
                    st.bank[ob][:, c * 128:(c + 1) * 128], vt[:, j, c * 128:(c + 1) * 128], PT[:],
                    start=(bi == 0 and c == 0), stop=(bi == nb - 1)),
                    reads=[hs["t"], PT_t], writes=[st.bank_t[ob]], inc=False)
            k.pe.op(lambda PT=PT, sb2=sb2: nc.tensor.matmul(
                st.bank[sb2][:, 0:128], st.ones_bf[:], PT[:], start=(bi == 0), stop=(bi == nb - 1)),
                reads=[PT_t, st.ones_t], writes=[st.bank_t[sb2]], inc=True)
    finalize(i)


def load_head(k, ax, hs, qsrc, kosrc, krsrc, vosrc, vrsrc, dv):
    for idx, (ap, R) in enumerate(qsrc):
        k.sp.dma(hs["q"][idx][0:R, :], ap, hs["ds"], writes=[hs["t"]])
    for idx, (ap, R) in enumerate(kosrc):
        k.sp.dma(hs["ko"][idx][0:R, :], ap, hs["ds"], writes=[hs["t"]])
    for idx, (ap, R) in enumerate(krsrc):
        k.sp.dma(hs["kr"][idx][0:R, :], ap, hs["ds"], writes=[hs["t"]])
    for hf in range(2):
        k.sp.dma(hs["vo"][:, hf * 4:(hf + 1) * 4, 0:dv], vosrc.half(hf).rearrange("(j p) d -> p j d", p=128), hs["ds"], writes=[hs["t"]])
        k.sp.dma(hs["vr"][:, hf * 4:(hf + 1) * 4, 0:dv], vrsrc.half(hf).rearrange("(j p) d -> p j d", p=128), hs["ds"], writes=[hs["t"]])


def fin_simple(k, st, ax, chunk):
    nc = k.nc

    def f(i):
        qsl = slice(i * 128, (i + 1) * 128)
        r, r_t = ax.rinv[0], ax.rinv_t[0]
        k.dve.op(lambda: nc.vector.reciprocal(out=r[:], in_=st.bank[3][:, 0:128]), reads=[st.bank_t[3]], writes=[r_t])
        k.dve.op(lambda: nc.vector.tensor_tensor(out=ax.ao[:, chunk, qsl], in0=st.bank[2][:, 0:128], in1=r[:], op=ALU.mult),
                 reads=[st.bank_t[2], r_t], writes=[ax.ao_t[chunk]])
    return f


def attn_even(k, st, sg, j, QT, KTo, Vo, KTr, Vr, abias, acv):
    nc = k.nc
    ax = AttnCtx(k, st)
    cb = st.cb
    ds = DmaSem(k, f"cb{st.uid()}")
    k.sp.dma(cb[:, 0:8], acv[j].partition_broadcast(128), ds, writes=[st.cst_t])
    k.dve.op(lambda: nc.vector.tensor_scalar(out=cb[:, 8:16], in0=cb[:, 0:8], scalar1=st.rb[:, 0:1], scalar2=None, op0=ALU.add),
             reads=[st.cst_t], writes=[st.cst_t])
    sA = 128 ** -0.5
    for hh in range(8):
        hs = ax.sets[hh % 2]
        load_head(k, ax, hs, [(QT[hh * 128:(hh + 1) * 128, :], 128)], [(KTo[hh * 128:(hh + 1) * 128, :], 128)],
                  [(KTr[hh * 128:(hh + 1) * 128, :], 128)], Vo[:, hh * 128:(hh + 1) * 128], Vr[:, hh * 128:(hh + 1) * 128], 128)
        bt, bt_t, bt_ds = ax.bias[hh % 2], ax.bias_t[hh % 2], ax.bias_ds[hh % 2]
        k.sp.dma(bt, abias[j, hh].rearrange("t k q -> k t q"), bt_ds, writes=[bt_t])
        ax.cur_bias, ax.cur_bias_t = bt, bt_t
        for i in range(8):
            blocks = []
            for d_ in range(4, -1, -1):
                jg = i - d_
                rem = jg < 0
                jj = jg + 8 if rem else jg
                if d_ in (2, 3):
                    blocks.append((rem, jj, "plain", cb[:, (8 if rem else 0) + hh:(8 if rem else 0) + hh + 1], None))
                else:
                    tix = {4: 0, 1: 1, 0: 2}[d_]
                    blocks.append((rem, jj, "tile", (st.rb if rem else st.zb)[:, 0:1], tix))
            attn_tile(k, st, ax, hs, i, [([(0, 0, 128)], sA)], blocks, 128, fin_simple(k, st, ax, hh))
    sB = 192 ** -0.5
    for hh in range(8):
        hs = ax.sets[hh % 2]
        load_head(k, ax, hs,
                  [(QT[1024 + hh * 128:1024 + (hh + 1) * 128, :], 128), (QT[2048 + hh * 64:2048 + (hh + 1) * 64, :], 64)],
                  [(KTo[1024 + hh * 128:1024 + (hh + 1) * 128, :], 128), (KTo[2048:2112, :], 64)],
                  [(KTr[1024 + hh * 128:1024 + (hh + 1) * 128, :], 128), (KTr[2048:2112, :], 64)],
                  Vo[:, 1024 + hh * 128:1024 + (hh + 1) * 128], Vr[:, 1024 + hh * 128:1024 + (hh + 1) * 128], 128)
        for i in range(8):
            blocks = [(True, jj, "plain", st.rb[:, 0:1], None) for jj in range(8)]
            blocks += [(False, jj, "diag" if jj == i else "plain", st.zb[:, 0:1], None) for jj in range(i + 1)]
            attn_tile(k, st, ax, hs, i, [([(0, 0, 128), (1, 1, 64)], sB)], blocks, 128, fin_simple(k, st, ax, 8 + hh))
    return ax


def attn_odd(k, st, sg, j, layer, QT, KTo, Vo, KTr, Vr, lvec, gsub):
    nc = k.nc
    ax = AttnCtx(k, st)
    lam_init = 0.8 - 0.6 * math.exp(-0.3 * layer)
    lv = st.lv
    ds = DmaSem(k, f"lv{st.uid()}")
    for q in range(4):
        k.sp.dma(lv[:, q:q + 1], lvec[q][j].rearrange("(p o) -> p o", o=1), ds, writes=[st.cst_t])
    k.sp.dma(lv[:, 8:10], gsub[j].rearrange("(c p) -> p c", p=128), ds, writes=[st.cst_t], allow_slow_non_contiguous=True)
    k.dve.op(lambda: nc.vector.tensor_tensor(out=lv[:, 4:5], in0=lv[:, 0:1], in1=lv[:, 1:2], op=ALU.mult), reads=[st.cst_t], writes=[st.cst_t])
    k.dve.op(lambda: nc.vector.tensor_tensor(out=lv[:, 5:6], in0=lv[:, 2:3], in1=lv[:, 3:4], op=ALU.mult), reads=[st.cst_t], writes=[st.cst_t])
    k.pe.op(lambda: nc.tensor.matmul(st.bank[6][:, 0:2], st.ones_f[:], lv[:, 4:6], start=True, stop=True),
            reads=[st.cst_t, st.ones_t], writes=[st.bank_t[6]], inc=True)
    k.act.op(lambda: nc.scalar.activation(out=lv[:, 6:8], in_=st.bank[6][:, 0:2], func=AF.Exp), reads=[st.bank_t[6]], writes=[st.cst_t])
    k.dve.op(lambda: nc.vector.tensor_tensor(out=lv[:, 4:5], in0=lv[:, 7:8], in1=lv[:, 6:7], op=ALU.subtract), reads=[st.cst_t], writes=[st.cst_t])
    k.dve.op(lambda: nc.vector.tensor_scalar(out=lv[:, 4:5], in0=lv[:, 4:5], scalar1=-lam_init, scalar2=None, op0=ALU.add),
             reads=[st.cst_t], writes=[st.cst_t])
    k.dve.op(lambda: nc.vector.tensor_scalar(out=lv[:, 8:10], in0=lv[:, 8:10], scalar1=1.0 - lam_init, scalar2=None, op0=ALU.mult),
             reads=[st.cst_t], writes=[st.cst_t])
    sC = 128 ** -0.5

    def fin(hh):
        def f(i):
            qsl = slice(i * 128, (i + 1) * 128)
            for si, (dst, dst_t) in enumerate(((ax.o1n, ax.tmp_t[0]), (ax.o2n, ax.tmp_t[1]))):
                r, r_t = ax.rinv[si], ax.rinv_t[si]
                k.dve.op(lambda r=r, si=si: nc.vector.reciprocal(out=r[:], in_=st.bank[3 + 2 * si][:, 0:128]),
                         reads=[st.bank_t[3 + 2 * si]], writes=[r_t])
                for c in range(2):
                    k.dve.op(lambda r=r, si=si, c=c, dst=dst: nc.vector.tensor_tensor(
                        out=dst[:, c * 128:(c + 1) * 128], in0=st.bank[2 + 2 * si][:, c * 128:(c + 1) * 128], in1=r[:], op=ALU.mult),
                        reads=[st.bank_t[2 + 2 * si], r_t], writes=[dst_t])
            k.dve.op(lambda: nc.vector.scalar_tensor_tensor(out=ax.dd[:], in0=ax.o2n[:], scalar=lv[:, 4:5], in1=ax.o1n[:],
                                                           op0=ALU.mult, op1=ALU.add),
                     reads=[ax.tmp_t[0], ax.tmp_t[1], st.cst_t], writes=[ax.tmp_t[2]])
            k.act.op(lambda: nc.scalar.activation(out=ax.sq2[:], in_=ax.dd[:], func=AF.Square), reads=[ax.tmp_t[2]], writes=[ax.tmp_t[3]])
            for c in range(2):
                k.pe.op(lambda c=c: nc.tensor.matmul(st.bank[6][:, 0:128], st.ones_f[:], ax.sq2[:, c * 128:(c + 1) * 128],
                                                     start=(c == 0), stop=(c == 1)),
                        reads=[ax.tmp_t[3], st.ones_t], writes=[st.bank_t[6]], inc=(c == 1))
            k.dve.op(lambda: nc.vector.tensor_scalar(out=ax.stmp[:], in0=st.bank[6][:, 0:128], scalar1=1.0 / 256, scalar2=EPS,
                                                    op0=ALU.mult, op1=ALU.add), reads=[st.bank_t[6]], writes=[ax.tmp_t[4]])
            k.act.op(lambda: nc.scalar.activation(out=ax.stmp[:], in_=ax.stmp[:], func=AF.Sqrt), reads=[ax.tmp_t[4]], writes=[ax.tmp_t[4]])
            k.dve.op(lambda: nc.vector.reciprocal(out=ax.stmp[:], in_=ax.stmp[:]), reads=[ax.tmp_t[4]], writes=[ax.tmp_t[4]])
            for c in range(2):
                k.dve.op(lambda c=c: nc.vector.scalar_tensor_tensor(
                    out=ax.ao[:, hh * 2 + c, qsl], in0=ax.dd[:, c * 128:(c + 1) * 128], scalar=lv[:, 8 + c:9 + c], in1=ax.stmp[:],
                    op0=ALU.mult, op1=ALU.mult),
                    reads=[ax.tmp_t[2], ax.tmp_t[4], st.cst_t], writes=[ax.ao_t[hh * 2 + c]])
        return f

    for hh in range(8):
        hs = ax.sets[hh % 2]
        r0 = hh * 256
        load_head(k, ax, hs, [(QT[r0:r0 + 128, :], 128), (QT[r0 + 128:r0 + 256, :], 128)],
                  [(KTo[r0:r0 + 128, :], 128), (KTo[r0 + 128:r0 + 256, :], 128)],
                  [(KTr[r0:r0 + 128, :], 128), (KTr[r0 + 128:r0 + 256, :], 128)],
                  Vo[:, r0:r0 + 256], Vr[:, r0:r0 + 256], 256)
        for i in range(8):
            blocks = [(True, jj, "plain", st.rb[:, 0:1], None) for jj in range(8)]
            blocks += [(False, jj, "diag" if jj == i else "plain", st.zb[:, 0:1], None) for jj in range(i + 1)]
            attn_tile(k, st, ax, hs, i, [([(0, 0, 128)], sC), ([(1, 1, 128)], sC)], blocks, 256, fin(hh))
    return ax


def mix_out(k, st, wp, ax, w_out, gi):
    for th in range(2):
        y = carve(st, 60 * 1024, [128, KT, TH], F32)
        y_t = [T() for _ in range(KT)]
        for t_ in y_t:
            t_.w = st.big_t.w
            t_.r = dict(st.big_t.r)
            for e_ in ax.every:
                for tok in list(e_.r.values()) + ([e_.w] if e_.w else []):
                    old = t_.r.get(tok[0])
                    if old is None or (old[1], old[2]) < (tok[1], tok[2]):
                        t_.r[tok[0]] = tok
        outproj_half(k, st, wp, w_out, KT, lambda kc, th=th: ax.ao[:, kc, th * TH:(th + 1) * TH], ax.ao_t, y, y_t)
        tail_half(k, st, th, gi, y, y_t)
        ax.every = ax.every + y_t
    ax.done()


NCORES = 8
QROWS = 2560
KROWS = 2112
KVROWS = KROWS + 2048


def make_state(k):
    st = State(k)
    st.big_t = T("big")
    st._uid = [0]

    def uid():
        st._uid[0] += 1
        return st._uid[0]
    st.uid = uid
    st.cb = k.sb("cb", [128, 16], F32)
    st.rb = k.sb("rb", [128, 1], F32)
    st.zb = k.sb("zb", [128, 1], F32)
    st.lv = k.sb("lv", [128, 16], F32)
    st.cst_t = T("cst")
    k.dve.op(lambda: k.nc.vector.memset(st.zb[:], 0.0), writes=[st.cst_t])
    return st


class Weights:
    def __init__(self, k, specs, gather=True):
        self.k = k
        self.full = {}
        self.t = {}
        nc = k.nc
        for name, (R, C) in specs.items():
            if not gather:
                self.full[name] = k.din("w_" + name, [R, C])
                self.t[name] = T()
                continue
            sh = k.din("w_" + name, [R // NCORES, C])
            bounce = nc.dram_tensor("wb_" + name, [R // NCORES, C], F32)
            full = nc.dram_tensor("wf_" + name, [R, C], F32)
            ds = DmaSem(k, "wb_" + name)
            rows = R // NCORES
            step = max(1, (1 << 18) // C)
            for r0 in range(0, rows, step):
                r1 = min(rows, r0 + step)
                k.pool.dma(bounce.ap()[r0:r1, :], sh[r0:r1, :], ds)
            k.pool.e.wait_ge(ds.sem, ds.n)
            sem = k.newsem("cc_" + name)
            nc.gpsimd.collective_compute("AllGather", ALU.bypass, replica_groups=[list(range(NCORES))],
                                         ins=[bounce.ap().opt()], outs=[full.ap().opt()]).then_inc(sem)
            t_ = T()
            t_.w = ("cc_" + name, 0, 1, sem)
            self.full[name] = full.ap()
            self.t[name] = t_

    def get(self, name):
        self.k.pool._need([self.t[name].w])
        return self.full[name]


def build_part1(layer, gather=False):
    even = layer % 2 == 0
    k = K()
    hT = k.din("hT", [D, NT])
    gv = k.din("gvec", [3, D])
    out = k.dout("hT_out", [D, NT])
    qt = k.dout("qt_out", [QROWS, NT], BF16)
    kv = k.dout("kv_out", [KVROWS, NT], BF16)
    specs = {"ffn_w_in": (D, 2 * DFF), "ffn_w_out": (DFF, D)}
    if even:
        specs.update({"ab_w_in": (D, 3904), "ab_w_in_krp": (D, 64), "b_w_qup": (512, 1536), "b_w_qup_rp": (512, 512),
                      "b_w_kvup": (256, 2048)})
        rope = k.din("rope", [2, 64, NT])
        gq = k.din("g_q", [512])
        gkv = k.din("g_kv", [256])
    else:
        specs.update({"c_w_in": (D, 6144), "c_w_in_rp": (D, 1024)})
        rope = k.din("rope", [2, 32, NT])
    st = make_state(k)
    W = Weights(k, specs, gather)
    wp = WPool(k, st)
    wp.big16()
    sg = Stager(k, st)
    load_gains(k, st, [gv[0], gv[1], gv[2]], [1.0, 0.5, 1.0])
    if even:
        k.sp.dma(st.gains[:, 8, 0:4], gq.rearrange("(c p) -> p c", p=128), st.misc_ds, writes=[st.gains_t], allow_slow_non_contiguous=True)
        k.sp.dma(st.gains[:, 9, 0:2], gkv.rearrange("(c p) -> p c", p=128), st.misc_ds, writes=[st.gains_t], allow_slow_non_contiguous=True)
    load_h(k, st, hT)
    ffn_full(k, st, wp, W.get("ffn_w_in"), W.get("ffn_w_out"), 0, 1)
    wp.big16()
    KT_ = RowChunks([(0, KROWS, kv[0:KROWS, :])])
    V = VView([kv[KROWS:KROWS + 1024, :], kv[KROWS + 1024:KVROWS, :]])
    Wf = lambda name, j: W.get(name)
    if even:
        proj_even(k, st, wp, sg, Wf, 0, 2, rope, qt, KT_, V)
    else:
        proj_odd(k, st, wp, sg, Wf, 0, 2, rope, qt, KT_, V)
    store_h(k, st, out)
    for ds in sg.ds:
        k.outsems.append(ds)
    return k.finish()


def build_part2(layer, gather=False):
    even = layer % 2 == 0
    k = K()
    hT = k.din("hT", [D, NT])
    gv = k.din("gvec", [5, D])
    qt = k.din("qt", [QROWS, NT], BF16)
    kvo = k.din("kv_own", [KVROWS, NT], BF16)
    kvr = k.din("kv_rem", [KVROWS, NT], BF16)
    rbias = k.din("rbias", [128, 1])
    pT = k.din("pT", [256, NT])
    out = k.dout("hT_out", [D, NT])
    specs = {"mix_w_out": (D, D), "ffn_w_in": (D, 2 * DFF), "ffn_w_out": (DFF, D), "ple_w_gate": (D, D), "ple_w_proj": (256, D)}
    if even:
        abias = k.din("abias", [1, 8, 3, 128, 128])
        acv = k.din("acv", [1, 8])
    else:
        lvec = [k.din(f"lvec{q}", [1, 128]) for q in range(4)]
        gsub = k.din("gsub", [1, 256])
    st = make_state(k)
    W = Weights(k, specs, gather)
    wp = WPool(k, st)
    wp.big16()
    k.sp.dma(st.rb[:], rbias, st.misc_ds, writes=[st.cst_t])
    load_gains(k, st, [gv[0], gv[1], gv[2], gv[3], gv[4]], [1.0, 1.0, 0.5, 1.0, 1.0])
    load_h(k, st, hT)
    KTo, KTr = RowChunks([(0, KROWS, kvo[0:KROWS, :])]), RowChunks([(0, KROWS, kvr[0:KROWS, :])])
    Vo = VView([kvo[KROWS:KROWS + 1024, :], kvo[KROWS + 1024:KVROWS, :]])
    Vr = VView([kvr[KROWS:KROWS + 1024, :], kvr[KROWS + 1024:KVROWS, :]])
    if even:
        ax = attn_even(k, st, None, 0, qt, KTo, Vo, KTr, Vr, abias, acv)
    else:
        ax = attn_odd(k, st, None, 0, layer, qt, KTo, Vo, KTr, Vr, lvec, gsub)
    mix_out(k, st, wp, ax, W.get("mix_w_out"), 0)
    ffn_full(k, st, wp, W.get("ffn_w_in"), W.get("ffn_w_out"), 1, 2)
    wp.big16()
    for th in range(2):
        ple_half(k, st, wp, th, W.get("ple_w_gate"), W.get("ple_w_proj"), pT, 3, 4)
    store_h(k, st, out)
    return k.finish()


W_SHAPES = {
    "ffn1_w_in": (4, D, 2 * DFF), "ffn1_w_out": (4, DFF, D), "ffn2_w_in": (4, D, 2 * DFF), "ffn2_w_out": (4, DFF, D),
    "ab_w_in": (2, D, 3904), "ab_w_in_krp": (2, D, 64), "b_w_qup": (2, 512, 1536), "b_w_qup_rp": (2, 512, 512),
    "b_w_kvup": (2, 256, 2048), "ab_w_out": (2, D, D), "c_w_in": (2, D, 6144), "c_w_in_rp": (2, D, 1024),
    "c_w_out": (2, D, D), "ple_w_gate": (4, D, D), "ple_w_proj": (4, 256, D),
}


def build_fused(nlayers=4, ncores=NCORES):
    k = K()
    nc = k.nc
    hT = k.din("hT", [D, NT])
    out = k.dout("hT_out", [D, NT])
    gv = k.din("gvec", [4, 8, D])
    gq = k.din("g_q", [2, 512])
    gkv = k.din("g_kv", [2, 256])
    rope_b = k.din("rope_b", [2, 64, NT])
    rope_c = k.din("rope_c", [2, 32, NT])
    rbias = k.din("rbias", [128, 1])
    pT = k.din("pT", [4, 256, NT])
    abias = k.din("abias", [2, 8, 3, 128, 128])
    acv = k.din("acv", [2, 8])
    lvec = [k.din(f"lvec{q}", [2, 128]) for q in range(4)]
    gsub = k.din("gsub", [2, 256])
    Wd = {n: k.din("w_" + n, list(shp)) for n, shp in W_SHAPES.items()}
    W = lambda name, j: Wd[name][j]
    st = make_state(k)
    wp = WPool(k, st)
    wp.big16()
    sg = Stager(k, st)
    k.sp.dma(st.rb[:], rbias, DmaSem(k, "rb"), writes=[st.cst_t])
    load_h(k, st, hT)
    for layer in range(nlayers):
        even = layer % 2 == 0
        j = layer // 2
        gds = DmaSem(k, f"g{layer}")
        for i in range(8):
            k.sp.dma(st.gains[:, i, :], gv[layer, i].rearrange("(kt p) -> p kt", p=128), gds,
                     writes=[st.gains_t], allow_slow_non_contiguous=True)
        if even:
            k.sp.dma(st.gains[:, 8, 0:4], gq[j].rearrange("(c p) -> p c", p=128), gds, writes=[st.gains_t], allow_slow_non_contiguous=True)
            k.sp.dma(st.gains[:, 9, 0:2], gkv[j].rearrange("(c p) -> p c", p=128), gds, writes=[st.gains_t], allow_slow_non_contiguous=True)
        for i in (1, 5):
            k.dve.op(lambda i=i: nc.vector.tensor_scalar(out=st.gains[:, i, :], in0=st.gains[:, i, :], scalar1=0.5, scalar2=None,
                                                         op0=ALU.mult), reads=[st.gains_t], writes=[st.gains_t])
        ffn_full(k, st, wp, W("ffn1_w_in", layer), W("ffn1_w_out", layer), 0, 1)
        wp.big16()
        qts = k.dint(f"qt{layer}", [QROWS, NT], BF16)
        csz = [1024, 1024, 1024, 1024] + ([64] if even else [])
        own_c = [k.dint(f"kvown{layer}_{ci}", [n_, NT], BF16) for ci, n_ in enumerate(csz)]
        pair_c = [k.dint(f"kvpair{layer}_{ci}", [2 * n_, NT], BF16) for ci, n_ in enumerate(csz)]
        kt_chunks = [(0, 1024, own_c[0]), (1024, 1024, own_c[1])] + ([(2048, 64, own_c[4])] if even else [])
        KTo = RowChunks(kt_chunks)
        Vo = VView([own_c[2], own_c[3]])
        if even:
            proj_even(k, st, wp, sg, W, j, 2, rope_b, qts, KTo, Vo)
        else:
            proj_odd(k, st, wp, sg, W, j, 2, rope_c, qts, KTo, Vo)
        toks = list(sg.stores)
        sg.stores = []
        k.pool._need(toks)
        k.sp._need(toks)
        cctoks = []
        for ci in range(len(csz)):
            ccsem = k.newsem(f"cc{layer}_{ci}")
            nc.gpsimd.collective_compute("AllGather", ALU.bypass, replica_groups=[[2 * i_, 2 * i_ + 1] for i_ in range(ncores // 2)],
                                         ins=[own_c[ci].opt()], outs=[pair_c[ci].opt()]).then_inc(ccsem)
            cctoks.append((f"cc{layer}_{ci}", 0, 1, ccsem))
        k.sp._need(cctoks)
        KTr = RowChunks([(0, 1024, pair_c[0][0:1024, :]), (1024, 1024, pair_c[1][0:1024, :])]
                        + ([(2048, 64, pair_c[4][0:64, :])] if even else []))
        Vr = VView([pair_c[2][0:1024, :], pair_c[3][0:1024, :]])
        if even:
            ax = attn_even(k, st, None, j, qts, KTo, Vo, KTr, Vr, abias, acv)
            mix_out(k, st, wp, ax, W("ab_w_out", j), 3)
        else:
            ax = attn_odd(k, st, None, j, layer, qts, KTo, Vo, KTr, Vr, lvec, gsub)
            mix_out(k, st, wp, ax, W("c_w_out", j), 3)
        ffn_full(k, st, wp, W("ffn2_w_in", layer), W("ffn2_w_out", layer), 4, 5)
        wp.big16()
        for th in range(2):
            ple_half(k, st, wp, th, W("ple_w_gate", layer), W("ple_w_proj", layer), pT[layer], 6, 7)
    store_h(k, st, out)
    return k.finish()


def kernel_fused(I, nlayers=4):
    x, p = I["x"], I["p"]
    shared = {"gvec": np.ascontiguousarray(np.stack([
        np.stack([I["ffn1_g_pre"][l], I["ffn1_g_post"][l], I["mix_g_pre"][l], I["mix_g_post"][l],
                  I["ffn2_g_pre"][l], I["ffn2_g_post"][l], I["ple_g_pre"][l], I["ple_g_post"][l]]) for l in range(4)])),
        "g_q": I["b_g_q"], "g_kv": I["b_g_kv"], "gsub": I["c_g_sub"]}
    tiles = [_abias_tiles(I["a_rel_bias"][j]) for j in range(2)]
    shared["abias"] = np.ascontiguousarray(np.stack([t[0] for t in tiles]))
    shared["acv"] = np.ascontiguousarray(np.concatenate([t[1] for t in tiles], 0))
    for q_, n_ in enumerate(("c_lq1", "c_lk1", "c_lq2", "c_lk2")):
        shared[f"lvec{q_}"] = I[n_]
    for n_ in ("ffn1_w_in", "ffn1_w_out", "ffn2_w_in", "ffn2_w_out", "ab_w_in", "b_w_qup", "b_w_kvup", "ab_w_out",
               "c_w_in", "c_w_out", "ple_w_gate", "ple_w_proj"):
        shared["w_" + n_] = I[n_]
    ab = I["ab_w_in"]
    shared["w_ab_w_in_krp"] = np.ascontiguousarray(ab[:, :, 3840 + _swap_halves(64)])
    idx = np.concatenate([h_ * 192 + 128 + _swap_halves(64) for h_ in range(8)])
    shared["w_b_w_qup_rp"] = np.ascontiguousarray(I["b_w_qup"][:, :, idx])
    idx = np.concatenate([h_ * 768 + w_ * 128 + _swap_halves(32) for h_ in range(8) for w_ in range(4)])
    shared["w_c_w_in_rp"] = np.ascontiguousarray(I["c_w_in"][:, :, idx])
    in_maps = []
    for c in range(NCORES):
        m = dict(shared)
        b, hf = c // 2, c % 2
        m["hT"] = np.ascontiguousarray(x[b, hf * NT:(hf + 1) * NT, :].T)
        m["rope_b"] = _rope_tab(64, hf * NT)
        m["rope_c"] = _rope_tab(32, hf * NT)
        m["rbias"] = np.full((128, 1), 0.0 if hf == 1 else -30000.0, np.float32)
        m["pT"] = np.ascontiguousarray(np.transpose(p[:, b, hf * NT:(hf + 1) * NT, :], (0, 2, 1)))
        in_maps.append(m)
    nc = build_fused(nlayers, _NRUN)
    res = _run(nc, in_maps)
    hT = [res[c]["hT_out"] for c in range(NCORES)]
    _dbg(f"h_ple_{nlayers - 1}", hT)
    out = np.empty((4, 2 * NT, D), np.float32)
    for c in range(NCORES):
        out[c // 2, (c % 2) * NT:(c % 2 + 1) * NT, :] = hT[c].T
    return out


def _shard_rows(w):
    w = np.ascontiguousarray(w)
    return [w] * NCORES


def _rope_tab(rot, pos0):
    inv = (500000.0 ** (-np.arange(0, rot, 2, dtype=np.float32) / np.float32(rot))).astype(np.float32)
    ang = (np.arange(pos0, pos0 + NT, dtype=np.float32)[:, None] * inv[None, :]).astype(np.float32)
    c = np.cos(ang).astype(np.float32).T
    s = np.sin(ang).astype(np.float32).T
    return np.ascontiguousarray(np.stack([np.concatenate([c, c], 0), np.concatenate([-s, s], 0)], 0))


def _swap_halves(n):
    return np.concatenate([np.arange(n // 2, n), np.arange(0, n // 2)])


def _abias_tiles(table):
    kk = np.arange(128)[:, None]
    qq = np.arange(128)[None, :]
    out = np.empty((8, 3, 128, 128), np.float32)
    far = table[:, 191]
    t4 = np.broadcast_to(far[:, None, None], (8, 128, 128)).copy()
    t4[:, (kk < 64) & (qq >= 64)] = -30000.0
    out[:, 0] = t4
    d1 = np.clip(128 + qq - kk, -63, 128) + 63
    out[:, 1] = table[:, d1]
    d0 = np.clip(qq - kk, -63, 128) + 63
    t0 = table[:, d0].copy()
    t0[:, (kk >= 64) & (qq < 64)] = -30000.0
    out[:, 2] = t0
    return out, np.ascontiguousarray(far[None, :])


_NRUN = NCORES


def _run(nc, in_maps):
    res = run_bass_kernel_spmd(nc, in_maps[:_NRUN], core_ids=list(range(_NRUN)))
    r = list(res.results)
    while len(r) < NCORES:
        r.append(r[len(r) % _NRUN])
    return r


_DEBUG = None
_KSTOP = 0


def _dbg(name, hT):
    if _DEBUG is None:
        return
    ref = _DEBUG[name][0]
    for c in range(2):
        o = hT[c].T.astype(np.float32)
        r = ref[c * NT:(c + 1) * NT]
        print("DBG", name, "core", c, "relerr", float(np.sqrt(((o - r) ** 2).mean() / (r ** 2).mean())),
              "finite", bool(np.isfinite(o).all()), flush=True)
        print("   per-tile", [round(float(np.sqrt(((o[t * 128:(t + 1) * 128] - r[t * 128:(t + 1) * 128]) ** 2).mean() / (r ** 2).mean())), 4) for t in range(8)], flush=True)


_FUSED = True
_NLAYERS = 4


def kernel(**I):
    import ml_dtypes
    I = {k_: np.asarray(v) for k_, v in I.items()}
    if _FUSED:
        return kernel_fused(I, _NLAYERS)
    x, p = I["x"], I["p"]
    hT = [np.ascontiguousarray(x[c // 2, (c % 2) * NT:(c % 2 + 1) * NT, :].T) for c in range(NCORES)]
    rbias = [np.full((128, 1), 0.0 if c % 2 == 1 else -30000.0, np.float32) for c in range(NCORES)]
    for layer in range(4):
        even = layer % 2 == 0
        j = layer // 2
        shared = {"gvec": np.stack([I["ffn1_g_pre"][layer], I["ffn1_g_post"][layer], I["mix_g_pre"][layer]])}
        wsh = {"ffn_w_in": _shard_rows(I["ffn1_w_in"][layer]), "ffn_w_out": _shard_rows(I["ffn1_w_out"][layer])}
        if even:
            w_in = I["ab_w_in"][j]
            wsh["ab_w_in"] = _shard_rows(w_in)
            wsh["ab_w_in_krp"] = _shard_rows(w_in[:, 3840 + _swap_halves(64)])
            qup = I["b_w_qup"][j]
            wsh["b_w_qup"] = _shard_rows(qup)
            idx = np.concatenate([h_ * 192 + 128 + _swap_halves(64) for h_ in range(8)])
            wsh["b_w_qup_rp"] = _shard_rows(qup[:, idx])
            wsh["b_w_kvup"] = _shard_rows(I["b_w_kvup"][j])
            shared["g_q"] = I["b_g_q"][j]
            shared["g_kv"] = I["b_g_kv"][j]
            rope = [_rope_tab(64, (c % 2) * NT) for c in range(NCORES)]
        else:
            w_in = I["c_w_in"][j]
            wsh["c_w_in"] = _shard_rows(w_in)
            idx = np.concatenate([h_ * 768 + w_ * 128 + _swap_halves(32) for h_ in range(8) for w_ in range(4)])
            wsh["c_w_in_rp"] = _shard_rows(w_in[:, idx])
            rope = [_rope_tab(32, (c % 2) * NT) for c in range(NCORES)]
        nc = build_part1(layer)
        in_maps = []
        for c in range(NCORES):
            m = dict(shared)
            m["hT"] = hT[c]
            m["rope"] = rope[c]
            for n_, sh in wsh.items():
                m["w_" + n_] = sh[c]
            in_maps.append(m)
        res = _run(nc, in_maps)
        hT = [res[c]["hT_out"] for c in range(NCORES)]
        _dbg(f"h_ffn1_{layer}", hT)
        qt = [res[c]["qt_out"] for c in range(NCORES)]
        kv = [res[c]["kv_out"] for c in range(NCORES)]
        if _KSTOP == 10 + layer:
            return None
        del wsh, in_maps
        shared = {"gvec": np.stack([I["mix_g_post"][layer], I["ffn2_g_pre"][layer], I["ffn2_g_post"][layer],
                                    I["ple_g_pre"][layer], I["ple_g_post"][layer]])}
        wsh = {"mix_w_out": _shard_rows((I["ab_w_out"] if even else I["c_w_out"])[j]),
               "ffn_w_in": _shard_rows(I["ffn2_w_in"][layer]), "ffn_w_out": _shard_rows(I["ffn2_w_out"][layer]),
               "ple_w_gate": _shard_rows(I["ple_w_gate"][layer]), "ple_w_proj": _shard_rows(I["ple_w_proj"][layer])}
        if even:
            tiles, far = _abias_tiles(I["a_rel_bias"][j])
            shared["abias"] = tiles[None]
            shared["acv"] = far
        else:
            for q_, n_ in enumerate(("c_lq1", "c_lk1", "c_lq2", "c_lk2")):
                shared[f"lvec{q_}"] = I[n_][j][None]
            shared["gsub"] = I["c_g_sub"][j][None]
        nc = build_part2(layer)
        in_maps = []
        for c in range(NCORES):
            m = dict(shared)
            m["hT"] = hT[c]
            m["qt"] = qt[c]
            m["kv_own"] = kv[c]
            m["kv_rem"] = kv[c - 1] if c % 2 == 1 else kv[c]
            m["rbias"] = rbias[c]
            m["pT"] = np.ascontiguousarray(p[layer, c // 2, (c % 2) * NT:(c % 2 + 1) * NT, :].T)
            for n_, sh in wsh.items():
                m["w_" + n_] = sh[c]
            in_maps.append(m)
        res = _run(nc, in_maps)
        hT = [res[c]["hT_out"] for c in range(NCORES)]
        _dbg(f"h_ple_{layer}", hT)
        if _KSTOP == 20 + layer:
            return None
        del wsh, in_maps
    out = np.empty((4, 2 * NT, D), np.float32)
    for c in range(NCORES):
        out[c // 2, (c % 2) * NT:(c % 2 + 1) * NT, :] = hT[c].T
    return out
```

```python
import contextlib
import math
import numpy as np
import concourse.bass as bass
import concourse.mybir as mybir
from concourse.bass_utils import run_bass_kernel_spmd

F32 = mybir.dt.float32
BF16 = mybir.dt.bfloat16
AF = mybir.ActivationFunctionType
ALU = mybir.AluOpType

D = 2048
NT = 1024
TH = 512
KT = D // 128
DFF = 5632
FC = DFF // 128
EPS = 1e-6
SEM_LIMIT = 20000


class T:
    __slots__ = ("w", "r", "name")

    def __init__(self, name=""):
        self.w = None
        self.r = {}
        self.name = name


class DmaSem:
    def __init__(self, K, name):
        self.sem = K.newsem(name)
        self.group = "dma_" + name
        self.n = 0


class E:
    def __init__(self, K, name, eng, is_pe=False):
        self.K = K
        self.name = name
        self.e = eng
        self.is_pe = is_pe
        self.epoch = 0
        self.cnt = 0
        self.sem = K.newsem(f"{name}_e0")
        self.seen = {}

    def _need(self, toks):
        for tok in toks:
            if tok is None:
                continue
            group, epoch, val, sem = tok
            if self.is_pe and group == self.name:
                continue
            s = self.seen.get(group)
            if s is not None and s >= (epoch, val):
                continue
            self.e.wait_ge(sem, val)
            self.seen[group] = (epoch, val)

    def deps(self, reads, writes):
        toks = []
        for b in reads:
            toks.append(b.w)
        for b in writes:
            toks.append(b.w)
            toks.extend(b.r.values())
        self._need(toks)

    def _record(self, tok, reads, writes):
        for b in reads:
            old = b.r.get(tok[0])
            if old is None or (old[1], old[2]) < (tok[1], tok[2]):
                b.r[tok[0]] = tok
        for b in writes:
            b.w = tok
            b.r = {}

    def op(self, fn, reads=(), writes=(), inc=True):
        self.deps(reads, writes)
        ins = fn()
        if inc:
            ins.then_inc(self.sem, 1)
            self.cnt += 1
            tok = (self.name, self.epoch, self.cnt, self.sem)
            self._record(tok, reads, writes)
            if self.cnt >= SEM_LIMIT:
                self.epoch += 1
                self.cnt = 0
                self.sem = self.K.newsem(f"{self.name}_e{self.epoch}")
        else:
            tok = (self.name, self.epoch, self.cnt + 1, self.sem)
            self._record(tok, reads, writes)
        return ins

    def dma(self, out, in_, ds, reads=(), writes=(), **kw):
        self.deps(reads, writes)
        ins = self.e.dma_start(out=out, in_=in_, **kw)
        ds.n += 16
        ins.then_inc(ds.sem, 16)
        tok = (ds.group, 0, ds.n, ds.sem)
        self._record(tok, reads, writes)
        return ins


class K:
    def __init__(self):
        self.nc = bass.Bass("TRN2", target_bir_lowering=False)
        self.stack = contextlib.ExitStack()
        self.nsem = 0
        nc = self.nc
        self.pe = E(self, "pe", nc.tensor, is_pe=True)
        self.act = E(self, "act", nc.scalar)
        self.dve = E(self, "dve", nc.vector)
        self.pool = E(self, "pool", nc.gpsimd)
        self.sp = E(self, "sp", nc.sync)
        self.outsems = []

    def newsem(self, name):
        self.nsem += 1
        return self.stack.enter_context(self.nc.semaphore(f"s{self.nsem}_{name}"))

    def sb(self, name, shape, dt):
        return self.stack.enter_context(self.nc.sbuf_tensor(name, shape, dt))

    def ps(self, name, shape, dt=F32):
        return self.stack.enter_context(self.nc.psum_tensor(name, shape, dt))

    def din(self, name, shape, dt=F32):
        return self.nc.dram_tensor(name, list(shape), dt, kind="ExternalInput").ap()

    def dout(self, name, shape, dt=F32):
        return self.nc.dram_tensor(name, list(shape), dt, kind="ExternalOutput").ap()

    def dint(self, name, shape, dt=F32):
        return self.nc.dram_tensor(name, list(shape), dt, kind="Internal").ap()

    def finish(self):
        for ds in self.outsems:
            self.sp.e.wait_ge(ds.sem, ds.n)
        self.stack.close()
        return self.nc


class State:
    def __init__(self, k: K):
        self.k = k
        self.h = k.sb("h", [128, KT, NT], F32)
        self.h_t = [[T(f"h{c}_{t}") for t in range(2)] for c in range(KT)]
        self.bank = [k.ps(f"bank{i}", [128, 512], F32) for i in range(8)]
        self.bank_t = [T(f"bank{i}") for i in range(8)]
        self.ones_bf = k.sb("ones_bf", [128, 128], BF16)
        self.ones_f = k.sb("ones_f", [128, 128], F32)
        self.ones_t = T("ones")
        k.dve.op(lambda: k.nc.vector.memset(self.ones_bf[:], 1.0), writes=[self.ones_t])
        k.dve.op(lambda: k.nc.vector.memset(self.ones_f[:], 1.0), writes=[self.ones_t])
        self.big = k.sb("big", [128, 136 * 1024], mybir.dt.uint8)
        self.sq = [k.sb(f"sq{i}", [128, TH], F32) for i in range(1)]
        self.sq_t = [T(f"sq{i}") for i in range(1)]
        self.sq_i = 0
        self.rstd = k.sb("rstd", [128, TH], F32)
        self.rstd_t = T("rstd")
        self.gains = k.sb("gains", [128, 10, KT], F32)
        self.gains_t = T("gains")
        self.misc_ds = DmaSem(k, "misc")


def rms_stats_accum(k, st, src_ap, src_t, bank_i, first, last, src_is_psum=False):
    i = 0
    sq, sq_t = st.sq[i], st.sq_t[i]
    k.act.op(lambda: k.nc.scalar.activation(out=sq[:], in_=src_ap, func=AF.Square),
             reads=[src_t], writes=[sq_t])
    k.pe.op(lambda: k.nc.tensor.matmul(st.bank[bank_i][:], st.ones_f[:], sq[:], start=first, stop=last),
            reads=[sq_t, st.ones_t], writes=[st.bank_t[bank_i]], inc=True)


def rstd_from_bank(k, st, bank_i, nfeat):
    k.dve.op(lambda: k.nc.vector.tensor_scalar(out=st.rstd[:], in0=st.bank[bank_i][:], scalar1=1.0 / nfeat,
                                              scalar2=EPS, op0=ALU.mult, op1=ALU.add),
             reads=[st.bank_t[bank_i]], writes=[st.rstd_t])
    k.act.op(lambda: k.nc.scalar.activation(out=st.rstd[:], in_=st.rstd[:], func=AF.Sqrt),
             reads=[st.rstd_t], writes=[st.rstd_t])
    k.dve.op(lambda: k.nc.vector.reciprocal(out=st.rstd[:], in_=st.rstd[:]),
             reads=[st.rstd_t], writes=[st.rstd_t])


class WPool:
    def __init__(self, k, st):
        self.k = k
        self.st = st
        self.ds = [DmaSem(k, f"w{i}") for i in range(4)]
        self.ds_x = [DmaSem(k, f"wx{i}") for i in range(4)]
        self.slots = []
        self.ts = []
        self.i = 0
        self.cfg = None

    def config(self, offs_kb, kb):
        cfg = (tuple(offs_kb), kb)
        if cfg == self.cfg:
            return
        st = self.st
        u = {}
        for t_ in self.ts + [st.big_t]:
            for tok in list(t_.r.values()) + ([t_.w] if t_.w is not None else []):
                old = u.get(tok[0])
                if old is None or (old[1], old[2]) < (tok[1], tok[2]):
                    u[tok[0]] = tok
        st.big_t.w = None
        st.big_t.r = dict(u)
        self.slots = [carve(st, o * 1024, [128, kb * 512], BF16) for o in offs_kb]
        self.ts = [T() for _ in offs_kb]
        for t_ in self.ts:
            t_.r = dict(u)
        self.i = 0
        self.cfg = cfg

    def big16(self):
        self.config([104, 120], 16)

    def small4(self):
        self.config([120, 124, 128, 132], 4)

    def next(self):
        i = self.i
        self.i = (self.i + 1) % len(self.slots)
        return self.slots[i], self.ts[i], self.ds[i]


def carve(st, off_bytes, shape, dt):
    esz = 2 if dt == BF16 else 4
    n = 1
    for s_ in shape[1:]:
        n *= s_
    ap = st.big[:, off_bytes:off_bytes + n * esz].bitcast(dt)
    if len(shape) == 3:
        ap = ap.rearrange("p (a b) -> p a b", a=shape[1])
    return ap


def load_gains(k, st, vecs, scales):
    nc = k.nc
    for i, v in enumerate(vecs):
        k.sp.dma(st.gains[:, i, :], v.rearrange("(kt p) -> p kt", p=128), st.misc_ds,
                 writes=[st.gains_t], allow_slow_non_contiguous=True)
    for i, s_ in enumerate(scales):
        if s_ != 1.0:
            k.dve.op(lambda i=i, s_=s_: nc.vector.tensor_scalar(
                out=st.gains[:, i, :], in0=st.gains[:, i, :], scalar1=s_, scalar2=None, op0=ALU.mult),
                reads=[st.gains_t], writes=[st.gains_t])


def all_h(st):
    return [st.h_t[c][t] for c in range(KT) for t in range(2)]


def load_h(k, st, hT):
    v = hT.rearrange("(kt p) t -> p kt t", p=128)
    ds = DmaSem(k, "hload")
    for q in range(4):
        k.sp.dma(st.h[:, q * 4:(q + 1) * 4, :], v[:, q * 4:(q + 1) * 4, :], ds, writes=all_h(st))


def store_h(k, st, hT_out):
    v = hT_out.rearrange("(kt p) t -> p kt t", p=128)
    ds = DmaSem(k, "hstore")
    k.outsems.append(ds)
    for q in range(4):
        k.sp.dma(v[:, q * 4:(q + 1) * 4, :], st.h[:, q * 4:(q + 1) * 4, :], ds, reads=all_h(st))


def prenorm_half(k, st, th, gi, xn, xn_t):
    nc = k.nc
    tsl = slice(th * TH, (th + 1) * TH)
    for kt in range(KT):
        rms_stats_accum(k, st, st.h[:, kt, tsl], st.h_t[kt][th], 6, kt == 0, kt == KT - 1)
    rstd_from_bank(k, st, 6, D)
    for kt in range(KT):
        k.dve.op(lambda kt=kt: nc.vector.scalar_tensor_tensor(
            out=xn[:, kt, :], in0=st.h[:, kt, tsl], scalar=st.gains[:, gi, kt:kt + 1], in1=st.rstd[:],
            op0=ALU.mult, op1=ALU.mult),
            reads=[st.h_t[kt][th], st.rstd_t, st.gains_t], writes=[xn_t[kt]])


def tail_half(k, st, th, gi, y, y_t):
    nc = k.nc
    tsl = slice(th * TH, (th + 1) * TH)
    rstd_from_bank(k, st, 7, D)
    for dc in range(KT):
        k.dve.op(lambda dc=dc: nc.vector.scalar_tensor_tensor(
            out=y[:, dc, :], in0=y[:, dc, :], scalar=st.gains[:, gi, dc:dc + 1], in1=st.rstd[:],
            op0=ALU.mult, op1=ALU.mult),
            reads=[y_t[dc], st.rstd_t, st.gains_t], writes=[y_t[dc]])
        k.dve.op(lambda dc=dc: nc.vector.tensor_tensor(
            out=st.h[:, dc, tsl], in0=st.h[:, dc, tsl], in1=y[:, dc, :], op=ALU.add),
            reads=[y_t[dc], st.h_t[dc][th]], writes=[st.h_t[dc][th]])


def outproj_half(k, st, wp, w_dram, nkc, rhs_of, rhs_t, y, y_t, kblk=None):
    nc = k.nc
    kblk = kblk or nkc
    w_v = w_dram.rearrange("(kc p) n -> p kc n", p=128)
    for dc in range(KT):
        bnk = 4 + (dc % 2)
        for k0 in range(0, nkc, kblk):
            wb, wb_t, wb_ds = wp.next()
            wv = wb[:, 0:kblk * 128].rearrange("p (kc n) -> p kc n", kc=kblk)
            k.pool.dma(wv, w_v[:, k0:k0 + kblk, dc * 128:(dc + 1) * 128], wb_ds, writes=[wb_t])
            for kc in range(k0, k0 + kblk):
                k.pe.op(lambda kc=kc, bnk=bnk, wv=wv, k0=k0: nc.tensor.matmul(
                    st.bank[bnk][:], wv[:, kc - k0, :], rhs_of(kc), start=(kc == 0), stop=(kc == nkc - 1)),
                    reads=[wb_t, rhs_t[kc]], writes=[st.bank_t[bnk]], inc=(kc == k0 + kblk - 1))
        k.act.op(lambda dc=dc, bnk=bnk: nc.scalar.copy(out=y[:, dc, :], in_=st.bank[bnk][:]),
                 reads=[st.bank_t[bnk]], writes=[y_t[dc]])
        rms_stats_accum(k, st, st.bank[bnk][:], st.bank_t[bnk], 7, dc == 0, dc == KT - 1)


def ffn_full(k, st, wp, w_in, w_out, gpre_i, gpost_i):
    nc = k.nc
    wp.small4()
    xn = carve(st, 0, [128, KT, NT], BF16)
    g = carve(st, 32 * 1024, [128, FC, NT], BF16)
    y = carve(st, 0, [128, KT, TH], F32)
    xn_t = [T() for _ in range(KT)]
    g_t = [[T() for _ in range(2)] for _ in range(FC)]
    for t_ in xn_t + [t2 for gg in g_t for t2 in gg]:
        t_.w = st.big_t.w
        t_.r = dict(st.big_t.r)
    for th in range(2):
        prenorm_half(k, st, th, gpre_i, xn[:, :, th * TH:(th + 1) * TH], xn_t)
    w_in_v = w_in.rearrange("(kt p) n -> p kt n", p=128)
    XM0 = 36
    x_slots = [carve(st, 32 * 1024 + (XM0 + 2 * e) * 2048, [128, 2048], BF16) for e in range(4)]
    x_ts = [T() for _ in range(4)]
    for t_ in x_ts:
        t_.w = st.big_t.w
        t_.r = dict(st.big_t.r)
    rot = [0]

    def next_slot(m):
        if m < 32:
            i_ = rot[0] % 8
            rot[0] += 1
            if i_ >= 4:
                return x_slots[i_ - 4], x_ts[i_ - 4], wp.ds_x[i_ - 4]
            return wp.slots[i_], wp.ts[i_], wp.ds[i_]
        i_ = rot[0] % 4
        rot[0] += 1
        return wp.slots[i_], wp.ts[i_], wp.ds[i_]

    for m in range(FC):
        wvs = []
        for half in range(2):
            wb, wb_t, wb_ds = next_slot(m)
            wv = wb[:, 0:KT * 128].rearrange("p (kt n) -> p kt n", kt=KT)
            k.pool.dma(wv, w_in_v[:, :, half * DFF + m * 128:half * DFF + (m + 1) * 128], wb_ds, writes=[wb_t])
            wvs.append((wv, wb_t))
        for th in range(2):
            tsl = slice(th * TH, (th + 1) * TH)
            ba, bb = 2 * th, 2 * th + 1
            for half, bnk in ((0, ba), (1, bb)):
                wv, wb_t = wvs[half]
                for kt in range(KT):
                    k.pe.op(lambda kt=kt, bnk=bnk, wv=wv, tsl=tsl: nc.tensor.matmul(
                        st.bank[bnk][:], wv[:, kt, :], xn[:, kt, tsl], start=(kt == 0), stop=(kt == KT - 1)),
                        reads=[wb_t, xn_t[kt]], writes=[st.bank_t[bnk]], inc=(kt == KT - 1))
            k.act.op(lambda ba=ba, m=m, tsl=tsl: nc.scalar.activation(out=g[:, m, tsl], in_=st.bank[ba][:], func=AF.Silu),
                     reads=[st.bank_t[ba]], writes=[g_t[m][th]] + ([x_ts[(m - XM0) // 2]] if m >= XM0 else []))
            k.dve.op(lambda bb=bb, m=m, tsl=tsl: nc.vector.tensor_tensor(
                out=g[:, m, tsl], in0=st.bank[bb][:], in1=g[:, m, tsl], op=ALU.mult),
                reads=[st.bank_t[bb], g_t[m][th]], writes=[g_t[m][th]])
    y_t = [T() for _ in range(KT)]
    for t_ in y_t:
        for x_ in xn_t:
            for tok in list(x_.r.values()) + ([x_.w] if x_.w is not None else []):
                old = t_.r.get(tok[0])
                if old is None or (old[1], old[2]) < (tok[1], tok[2]):
                    t_.r[tok[0]] = tok
    for th in range(2):
        tsl = slice(th * TH, (th + 1) * TH)
        outproj_half(k, st, wp, w_out, FC, lambda kc, tsl=tsl: g[:, kc, tsl], [g_t[kc][th] for kc in range(FC)], y, y_t, kblk=11)
        tail_half(k, st, th, gpost_i, y, y_t)
    wp.i = 0
    merge_big(st, xn_t + [t2 for gg in g_t for t2 in gg] + y_t + x_ts)


def merge_big(st, ts):
    r = {}
    w = None
    for t_ in ts:
        for tok in list(t_.r.values()) + ([t_.w] if t_.w is not None else []):
            old = r.get(tok[0])
            if old is None or (old[1], old[2]) < (tok[1], tok[2]):
                r[tok[0]] = tok
    st.big_t.w = None
    st.big_t.r = r


def ple_half(k, st, wp, th, w_gate, w_proj, pT_dram, gpre_i, gpost_i):
    nc = k.nc
    xn = carve(st, 0, [128, KT, TH], BF16)
    pt = carve(st, 16 * 1024, [128, 2, TH], BF16)
    sg = carve(st, 20 * 1024, [128, 2, TH], F32)
    y = carve(st, 60 * 1024, [128, KT, TH], F32)
    xn_t = [T() for _ in range(KT)]
    y_t = [T() for _ in range(KT)]
    pt_t = T()
    sg_t = [T(), T()]
    for t_ in xn_t + y_t + [pt_t] + sg_t:
        t_.w = st.big_t.w
        t_.r = dict(st.big_t.r)
    tsl = slice(th * TH, (th + 1) * TH)
    ds = DmaSem(k, f"pt{st.uid()}")
    k.pool.dma(pt, pT_dram.rearrange("(kc p) t -> p kc t", p=128)[:, :, tsl], ds, writes=[pt_t])
    prenorm_half(k, st, th, gpre_i, xn, xn_t)
    wg_v = w_gate.rearrange("(kt p) n -> p kt n", p=128)
    wp_v = w_proj.rearrange("(kt p) n -> p kt n", p=128)
    for blk in range(8):
        wb, wb_t, wb_ds = wp.next()
        wv = wb[:, 0:(KT + 2) * 256].rearrange("p (kt n) -> p kt n", kt=KT + 2)
        k.pool.dma(wv[:, 0:KT, :], wg_v[:, :, blk * 256:(blk + 1) * 256], wb_ds, writes=[wb_t])
        k.pool.dma(wv[:, KT:KT + 2, :], wp_v[:, :, blk * 256:(blk + 1) * 256], wb_ds, writes=[wb_t])
        for c in range(2):
            dc = blk * 2 + c
            bg, bp = (0, 1) if dc % 2 == 0 else (2, 3)
            for kt in range(KT):
                k.pe.op(lambda kt=kt, bg=bg, c=c, wv=wv: nc.tensor.matmul(
                    st.bank[bg][:], wv[:, kt, c * 128:(c + 1) * 128], xn[:, kt, :],
                    start=(kt == 0), stop=(kt == KT - 1)),
                    reads=[wb_t, xn_t[kt]], writes=[st.bank_t[bg]], inc=(kt == KT - 1))
            for kc in range(2):
                k.pe.op(lambda kc=kc, bp=bp, c=c, wv=wv: nc.tensor.matmul(
                    st.bank[bp][:], wv[:, KT + kc, c * 128:(c + 1) * 128], pt[:, kc, :],
                    start=(kc == 0), stop=(kc == 1)),
                    reads=[wb_t, pt_t], writes=[st.bank_t[bp]], inc=(kc == 1))
            s_, s_t = sg[:, dc % 2, :], sg_t[dc % 2]
            k.act.op(lambda bg=bg, s_=s_: nc.scalar.activation(out=s_, in_=st.bank[bg][:], func=AF.Sigmoid),
                     reads=[st.bank_t[bg]], writes=[s_t])
            k.dve.op(lambda bp=bp, s_=s_, dc=dc: nc.vector.tensor_tensor(
                out=y[:, dc, :], in0=st.bank[bp][:], in1=s_, op=ALU.mult),
                reads=[st.bank_t[bp], s_t], writes=[y_t[dc]])
            rms_stats_accum(k, st, y[:, dc, :], y_t[dc], 7, dc == 0, dc == KT - 1)
    tail_half(k, st, th, gpost_i, y, y_t)
    merge_big(st, xn_t + y_t + [pt_t] + sg_t)


class Stager:
    def __init__(self, k, st, n=2):
        self.k = k
        self.tiles = [carve(st, (96 + 2 * i) * 1024, [128, NT], BF16) for i in range(n)]
        self.ts = [T(f"stg{i}") for i in range(n)]
        self.ds = [DmaSem(k, f"stg{i}") for i in range(n)]
        self.i = 0
        self.stores = []

    def next(self):
        i = self.i
        self.i = (self.i + 1) % len(self.tiles)
        return self.tiles[i], self.ts[i], self.ds[i]

    def store(self, dst, src, t_, ds):
        self.k.sp.dma(dst, src, ds, reads=[t_])
        self.stores.append((ds.group, 0, ds.n, ds.sem))

    def sync(self, st):
        for t_ in self.ts:
            t_.w = st.big_t.w
            t_.r = dict(st.big_t.r)

    def barrier(self, eng):
        eng._need(self.stores)
        self.stores = []


def load_w_slot(k, wp, view_ap, nk, ncols, pieces):
    wb, wb_t, wb_ds = wp.next()
    wv = wb[:, 0:nk * ncols].rearrange("p (kt n) -> p kt n", kt=nk)
    for c0, src in pieces:
        w = src.shape[-1]
        k.pool.dma(wv[:, :, c0:c0 + w], src, wb_ds, writes=[wb_t])
    return wv, wb_t


def mm_group(k, st, bank_i, lhs_list, rhs_list, reads, M=128, N=TH):
    nc = k.nc
    n = len(lhs_list)
    for i in range(n):
        k.pe.op(lambda i=i: nc.tensor.matmul(st.bank[bank_i][0:M, 0:N], lhs_list[i], rhs_list[i],
                                            start=(i == 0), stop=(i == n - 1)),
                reads=reads[i], writes=[st.bank_t[bank_i]], inc=(i == n - 1))


def hn_all(k, st, gi):
    hn = carve(st, 0, [128, KT, NT], BF16)
    hn_t = [T() for _ in range(KT)]
    for t_ in hn_t:
        t_.w = st.big_t.w
        t_.r = dict(st.big_t.r)
    for th in range(2):
        prenorm_half(k, st, th, gi, hn[:, :, th * TH:(th + 1) * TH], hn_t)
    return hn, hn_t


def fm_plain(k, st, sg, wv, wb_t, nk, c0, M, rhs_of, rhs_t, dst_rows, pbank):
    nc = k.nc
    stg, stg_t, stg_ds = sg.next()
    for th in range(2):
        b = pbank[0]
        pbank[0] = (pbank[0] + 1) % 4
        mm_group(k, st, b, [wv[:, kt, c0:c0 + M] for kt in range(nk)], [rhs_of(kt, th) for kt in range(nk)],
                 [[wb_t, rhs_t[kt]] for kt in range(nk)], M=M)
        k.act.op(lambda b=b, th=th: nc.scalar.copy(out=stg[0:M, th * TH:(th + 1) * TH], in_=st.bank[b][0:M, :]),
                 reads=[st.bank_t[b]], writes=[stg_t])
    sg.store(dst_rows, stg[0:M, :], stg_t, stg_ds)


def fm_rope(k, st, sg, wv, wb_t, nk, c0, M, wvr, wbr_t, cr0, R, rhs_of, rhs_t, dst_rows, pbank, cc, ss, tmp1, tmp2, tmp_t):
    nc = k.nc
    stg, stg_t, stg_ds = sg.next()
    for th in range(2):
        tsl = slice(th * TH, (th + 1) * TH)
        b = pbank[0]
        b2 = (b + 1) % 4
        pbank[0] = (pbank[0] + 2) % 4
        mm_group(k, st, b, [wv[:, kt, c0:c0 + M] for kt in range(nk)], [rhs_of(kt, th) for kt in range(nk)],
                 [[wb_t, rhs_t[kt]] for kt in range(nk)], M=M)
        mm_group(k, st, b2, [wvr[:, kt, cr0:cr0 + R] for kt in range(nk)], [rhs_of(kt, th) for kt in range(nk)],
                 [[wbr_t, rhs_t[kt]] for kt in range(nk)], M=R)
        k.dve.op(lambda b=b, tsl=tsl: nc.vector.tensor_tensor(out=tmp1[0:R, :], in0=st.bank[b][0:R, :], in1=cc[0:R, tsl], op=ALU.mult),
                 reads=[st.bank_t[b], st.rope_t], writes=[tmp_t[0]])
        k.dve.op(lambda b2=b2, tsl=tsl: nc.vector.tensor_tensor(out=tmp2[0:R, :], in0=st.bank[b2][0:R, :], in1=ss[0:R, tsl], op=ALU.mult),
                 reads=[st.bank_t[b2], st.rope_t], writes=[tmp_t[1]])
        k.dve.op(lambda tsl=tsl: nc.vector.tensor_tensor(out=stg[0:R, tsl], in0=tmp1[0:R, :], in1=tmp2[0:R, :], op=ALU.add),
                 reads=[tmp_t[0], tmp_t[1]], writes=[stg_t])
        if M > R:
            for (p0, p1) in ((32, 64), (64, 128)):
                k.act.op(lambda b=b, tsl=tsl, p0=p0, p1=p1: nc.scalar.copy(out=stg[p0:p1, tsl], in_=st.bank[b][p0:p1, :]),
                         reads=[st.bank_t[b]], writes=[stg_t])
    sg.store(dst_rows, stg[0:M, :], stg_t, stg_ds)


def tm_proj(k, st, sg, wv, wb_t, nk, ncols, lhs_of, lhs_t, dstV, pbank, col_of=None):
    nc = k.nc
    for tt in range(NT // 128):
        b = pbank[0]
        pbank[0] = (pbank[0] + 1) % 4
        stg, stg_t, stg_ds = sg.next()
        if col_of is None:
            mm_group(k, st, b, [lhs_of(kt, tt) for kt in range(nk)], [wv[:, kt, 0:ncols] for kt in range(nk)],
                     [[wb_t, lhs_t[kt]] for kt in range(nk)], M=128, N=ncols)
        else:
            for (o0, c0, w) in col_of:
                k_last = (o0, c0, w) == col_of[-1]
                for kt in range(nk):
                    k.pe.op(lambda kt=kt, o0=o0, c0=c0, w=w: nc.tensor.matmul(
                        st.bank[b][:, o0:o0 + w], lhs_of(kt, tt), wv[:, kt, c0:c0 + w], start=(kt == 0), stop=(kt == nk - 1)),
                        reads=[wb_t, lhs_t[kt]], writes=[st.bank_t[b]], inc=(k_last and kt == nk - 1))
        k.act.op(lambda b=b: nc.scalar.copy(out=stg[:, 0:ncols], in_=st.bank[b][:, 0:ncols]),
                 reads=[st.bank_t[b]], writes=[stg_t])
        sg.store(dstV[tt * 128:(tt + 1) * 128, :], stg[:, 0:ncols], stg_t, stg_ds)


def wview(w2d, nk):
    return w2d.rearrange("(kt p) n -> p kt n", p=128)


def load_rope(k, st, rope_dram, R):
    cc = carve(st, 88 * 1024, [128, NT], F32)
    ss = carve(st, 92 * 1024, [128, NT], F32)
    st.rope_t = T("rope")
    st.rope_t.w = st.big_t.w
    st.rope_t.r = dict(st.big_t.r)
    ds = DmaSem(k, f"rope{st.uid()}")
    k.sp.dma(cc[0:R, :], rope_dram[0], ds, writes=[st.rope_t])
    k.sp.dma(ss[0:R, :], rope_dram[1], ds, writes=[st.rope_t])
    return cc, ss


def proj_even(k, st, wp, sg, W, j, gi, rope_b, QT, KT_, V):
    nc = k.nc
    sg.sync(st)
    hn, hn_t = hn_all(k, st, gi)
    cc, ss = load_rope(k, st, rope_b, 64)
    cq = carve(st, 32 * 1024, [128, 4, NT], F32)
    ckv = carve(st, 48 * 1024, [128, 2, NT], F32)
    cqn = carve(st, 56 * 1024, [128, 4, NT], BF16)
    ckvn = carve(st, 64 * 1024, [128, 2, NT], BF16)
    tmp1 = carve(st, 68 * 1024, [128, TH], F32)
    tmp2 = carve(st, 70 * 1024, [128, TH], F32)
    tmp_t = [T(), T()]
    cq_t = [T() for _ in range(4)]
    ckv_t = [T() for _ in range(2)]
    cqn_t = [T() for _ in range(4)]
    ckvn_t = [T() for _ in range(2)]
    for t_ in tmp_t + cq_t + ckv_t + cqn_t + ckvn_t:
        t_.w = st.big_t.w
        t_.r = dict(st.big_t.r)
    pbank = [0]
    w_in = wview(W("ab_w_in", j), KT)
    rhs_of = lambda kt, th: hn[:, kt, th * TH:(th + 1) * TH]
    for grp, dst in ((0, QT), (1, KT_)):
        for blk in range(2):
            c0 = grp * 1024 + blk * 512
            wv, wb_t = load_w_slot(k, wp, None, KT, 512, [(0, w_in[:, :, c0:c0 + 512])])
            for c in range(4):
                r0 = blk * 512 + c * 128
                fm_plain(k, st, sg, wv, wb_t, KT, c * 128, 128, rhs_of, hn_t, dst[r0:r0 + 128, :], pbank)
    for blk in range(2):
        c0 = 2048 + blk * 512
        wv, wb_t = load_w_slot(k, wp, None, KT, 512, [(0, w_in[:, :, c0:c0 + 512])])
        tm_proj(k, st, sg, wv, wb_t, KT, 512, lambda kt, tt: hn[:, kt, tt * 128:(tt + 1) * 128], hn_t,
                V[:, blk * 512:(blk + 1) * 512], pbank)
    wv, wb_t = load_w_slot(k, wp, None, KT, 512, [(0, w_in[:, :, 3072:3584])])
    for c in range(4):
        for th in range(2):
            b = pbank[0]
            pbank[0] = (pbank[0] + 1) % 4
            mm_group(k, st, b, [wv[:, kt, c * 128:(c + 1) * 128] for kt in range(KT)], [rhs_of(kt, th) for kt in range(KT)],
                     [[wb_t, hn_t[kt]] for kt in range(KT)])
            k.act.op(lambda b=b, c=c, th=th: nc.scalar.copy(out=cq[:, c, th * TH:(th + 1) * TH], in_=st.bank[b][:]),
                     reads=[st.bank_t[b]], writes=[cq_t[c]])
    krp = wview(W("ab_w_in_krp", j), KT)
    wv, wb_t = load_w_slot(k, wp, None, KT, 512, [(0, w_in[:, :, 3584:3904]), (320, krp)])
    for c in range(2):
        for th in range(2):
            b = pbank[0]
            pbank[0] = (pbank[0] + 1) % 4
            mm_group(k, st, b, [wv[:, kt, c * 128:(c + 1) * 128] for kt in range(KT)], [rhs_of(kt, th) for kt in range(KT)],
                     [[wb_t, hn_t[kt]] for kt in range(KT)])
            k.act.op(lambda b=b, c=c, th=th: nc.scalar.copy(out=ckv[:, c, th * TH:(th + 1) * TH], in_=st.bank[b][:]),
                     reads=[st.bank_t[b]], writes=[ckv_t[c]])
    fm_rope(k, st, sg, wv, wb_t, KT, 256, 64, wv, wb_t, 320, 64, rhs_of, hn_t, KT_[2048:2112, :], pbank, cc, ss, tmp1, tmp2, tmp_t)
    for (src, src_t, dstn, dstn_t, n, gidx, nfeat) in ((cq, cq_t, cqn, cqn_t, 4, 8, 512), (ckv, ckv_t, ckvn, ckvn_t, 2, 9, 256)):
        for th in range(2):
            tsl = slice(th * TH, (th + 1) * TH)
            for c in range(n):
                rms_stats_accum(k, st, src[:, c, tsl], src_t[c], 6, c == 0, c == n - 1)
            rstd_from_bank(k, st, 6, nfeat)
            for c in range(n):
                k.dve.op(lambda c=c, tsl=tsl, src=src, dstn=dstn, gidx=gidx: nc.vector.scalar_tensor_tensor(
                    out=dstn[:, c, tsl], in0=src[:, c, tsl], scalar=st.gains[:, gidx, c:c + 1], in1=st.rstd[:],
                    op0=ALU.mult, op1=ALU.mult),
                    reads=[src_t[c], st.rstd_t, st.gains_t], writes=[dstn_t[c]])
    qup = wview(W("b_w_qup", j), 4)
    qrp = wview(W("b_w_qup_rp", j), 4)
    wvq, wbq_t = load_w_slot(k, wp, None, 4, 2048, [(0, qup), (1536, qrp)])
    rhs_q = lambda kt, th: cqn[:, kt, th * TH:(th + 1) * TH]
    for hh in range(8):
        fm_plain(k, st, sg, wvq, wbq_t, 4, hh * 192, 128, rhs_q, cqn_t, QT[1024 + hh * 128:1024 + (hh + 1) * 128, :], pbank)
        fm_rope(k, st, sg, wvq, wbq_t, 4, hh * 192 + 128, 64, wvq, wbq_t, 1536 + hh * 64, 64, rhs_q, cqn_t,
                QT[2048 + hh * 64:2048 + (hh + 1) * 64, :], pbank, cc, ss, tmp1, tmp2, tmp_t)
    kvup = wview(W("b_w_kvup", j), 2)
    wvk, wbk_t = load_w_slot(k, wp, None, 2, 2048, [(0, kvup)])
    rhs_k = lambda kt, th: ckvn[:, kt, th * TH:(th + 1) * TH]
    for hh in range(8):
        fm_plain(k, st, sg, wvk, wbk_t, 2, hh * 256, 128, rhs_k, ckvn_t, KT_[1024 + hh * 128:1024 + (hh + 1) * 128, :], pbank)
    for half in range(2):
        tm_proj(k, st, sg, wvk, wbk_t, 2, 512, lambda kt, tt: ckvn[:, kt, tt * 128:(tt + 1) * 128], ckvn_t,
                V[:, 1024 + half * 512:1024 + (half + 1) * 512], pbank,
                col_of=[(i * 128, (half * 4 + i) * 256 + 128, 128) for i in range(4)])
    merge_big(st, hn_t + tmp_t + cq_t + ckv_t + cqn_t + ckvn_t + [st.rope_t] + sg.ts)


def proj_odd(k, st, wp, sg, W, j, gi, rope_c, QT, KT_, V):
    nc = k.nc
    sg.sync(st)
    hn, hn_t = hn_all(k, st, gi)
    cc, ss = load_rope(k, st, rope_c, 32)
    tmp1 = carve(st, 68 * 1024, [128, TH], F32)
    tmp2 = carve(st, 70 * 1024, [128, TH], F32)
    tmp_t = [T(), T()]
    for t_ in tmp_t:
        t_.w = st.big_t.w
        t_.r = dict(st.big_t.r)
    pbank = [0]
    w_in = wview(W("c_w_in", j), KT)
    w_rp = wview(W("c_w_in_rp", j), KT)
    rhs_of = lambda kt, th: hn[:, kt, th * TH:(th + 1) * TH]
    for hh in range(8):
        wv, wb_t = load_w_slot(k, wp, None, KT, 512, [(0, w_in[:, :, hh * 768:hh * 768 + 512])])
        wvr, wbr_t = load_w_slot(k, wp, None, KT, 512, [(0, w_rp[:, :, hh * 128:(hh + 1) * 128]),
                                                        (128, w_in[:, :, hh * 768 + 512:hh * 768 + 768])])
        for which in range(4):
            dst = (QT if which < 2 else KT_)[hh * 256 + (which % 2) * 128: hh * 256 + (which % 2) * 128 + 128, :]
            fm_rope(k, st, sg, wv, wb_t, KT, which * 128, 128, wvr, wbr_t, which * 32, 32, rhs_of, hn_t, dst, pbank,
                    cc, ss, tmp1, tmp2, tmp_t)
        tm_proj(k, st, sg, wvr, wbr_t, KT, 256, lambda kt, tt: hn[:, kt, tt * 128:(tt + 1) * 128], hn_t,
                V[:, hh * 256:(hh + 1) * 256], pbank, col_of=[(0, 128, 256)])
    merge_big(st, hn_t + tmp_t + [st.rope_t] + sg.ts)


class RowChunks:
    def __init__(self, chunks):
        self.chunks = chunks

    def __getitem__(self, key):
        rs, cs = key
        for (r0, n, ap) in self.chunks:
            if r0 <= rs.start and rs.stop <= r0 + n:
                return ap[rs.start - r0:rs.stop - r0, cs]
        raise IndexError(f"rows {rs} straddle chunks")


class VView:
    def __init__(self, chunk_aps, c0=0, c1=2048):
        self.v = [ap.rearrange("(t a) c -> t (a c)", a=2) for ap in chunk_aps]
        self.c0, self.c1 = c0, c1
        self.raw = chunk_aps

    def __getitem__(self, key):
        rs, cs = key
        if rs == slice(None):
            nv = VView(self.raw, self.c0 + cs.start, self.c0 + cs.stop)
            return nv
        ch = rs.start // 512
        assert (rs.stop - 1) // 512 == ch
        cc0 = self.c0 + (cs.start or 0) if cs != slice(None) else self.c0
        cc1 = self.c0 + cs.stop if cs != slice(None) else self.c1
        return self.v[ch][rs.start - ch * 512:rs.stop - ch * 512, cc0:cc1]

    def half(self, hf):
        return self.v[hf][:, self.c0:self.c1]


class AttnCtx:
    def __init__(self, k, st):
        self.k = k
        self.st = st
        self.ao = carve(st, 0, [128, KT, NT], BF16)
        self.ao_t = [T() for _ in range(KT)]
        self.sets = []
        for s_ in range(2):
            base = 32 * 1024 + s_ * 24 * 1024
            d = dict(
                q=[carve(st, base + i * 2048, [128, NT], BF16) for i in range(2)],
                ko=[carve(st, base + 4096 + i * 2048, [128, NT], BF16) for i in range(2)],
                kr=[carve(st, base + 8192 + i * 2048, [128, NT], BF16) for i in range(2)],
                vo=carve(st, base + 12288, [128, 8, 256], BF16),
                vr=carve(st, base + 16384, [128, 8, 256], BF16),
                t=T(), ds=DmaSem(k, f"hs{st.uid()}"))
            self.sets.append(d)
        self.pt = [carve(st, 80 * 1024 + i * 256, [128, 128], BF16) for i in range(8)]
        self.pt_t = [T() for _ in range(8)]
        self.pt_i = 0
        self.dg_i = 0
        self.bias = [carve(st, 82 * 1024 + i * 1536, [128, 3, 128], F32) for i in range(2)]
        self.bias_t = [T(), T()]
        self.bias_ds = [DmaSem(k, f"bs{st.uid()}") for _ in range(2)]
        self.rinv = [carve(st, 86 * 1024 + i * 512, [128, 128], F32) for i in range(2)]
        self.rinv_t = [T(), T()]
        self.o1n = carve(st, 87 * 1024, [128, 256], F32)
        self.o2n = carve(st, 88 * 1024, [128, 256], F32)
        self.dd = carve(st, 89 * 1024, [128, 256], F32)
        self.sq2 = carve(st, 90 * 1024, [128, 256], F32)
        self.stmp = carve(st, 91 * 1024, [128, 128], F32)
        self.tmp_t = [T() for _ in range(5)]
        self.sslot_t = [T() for _ in range(8)]
        self.ss_i = 0
        every = self.ao_t + [d["t"] for d in self.sets] + self.pt_t + self.bias_t + self.rinv_t + self.tmp_t
        for t_ in every:
            t_.w = st.big_t.w
            t_.r = dict(st.big_t.r)
        self.every = every
        for i in (6, 7):
            k.dve.op(lambda i=i: k.nc.vector.memset(self.pt[i][:], 0.0), writes=[self.pt_t[i]])

    def done(self):
        merge_big(self.st, self.every)


def attn_tile(k, st, ax, hs, i, streams, blocks, dv, finalize, lookahead=1):
    nc = k.nc
    qsl = slice(i * 128, (i + 1) * 128)
    nb = len(blocks)

    def stage_a(bi, blk, si, stream):
        is_rem, j, kind, bias_ap, bt = blk
        parts, scale = stream
        ksl = slice(j * 128, (j + 1) * 128)
        s_i = ax.ss_i
        ax.ss_i = (ax.ss_i + 1) % 2
        S = st.bank[s_i][:, 0:128]
        S_t = ax.sslot_t[s_i]
        for pi, (qi, ki, Kp) in enumerate(parts):
            kt_ = (hs["kr"] if is_rem else hs["ko"])[ki]
            k.pe.op(lambda kt_=kt_, qi=qi, Kp=Kp, pi=pi: nc.tensor.matmul(
                S, kt_[0:Kp, ksl], hs["q"][qi][0:Kp, qsl], start=(pi == 0), stop=(pi == len(parts) - 1)),
                reads=[hs["t"]], writes=[S_t], inc=(pi == len(parts) - 1))
        if kind == "diag":
            p_i = 6 + ax.dg_i
            ax.dg_i ^= 1
        else:
            p_i = ax.pt_i
            ax.pt_i = (ax.pt_i + 1) % 6
        PT, PT_t = ax.pt[p_i], ax.pt_t[p_i]
        if kind == "tile":
            k.dve.op(lambda: nc.vector.scalar_tensor_tensor(
                out=ax.stmp[:], in0=S, scalar=scale, in1=ax.cur_bias[:, bt, :], op0=ALU.mult, op1=ALU.add),
                reads=[S_t, ax.cur_bias_t], writes=[ax.tmp_t[4]])
            k.act.op(lambda: nc.scalar.activation(out=PT[:], in_=ax.stmp[:], func=AF.Exp, bias=bias_ap, scale=1.0),
                     reads=[ax.tmp_t[4], st.cst_t], writes=[PT_t])
        elif kind == "diag":
            k.act.op(lambda: nc.scalar.activation(
                out=PT[0:64, :], in_=S[0:64, :], func=AF.Exp, bias=bias_ap[0:64, :], scale=scale),
                reads=[S_t, st.cst_t], writes=[PT_t])
            k.act.op(lambda: nc.scalar.activation(
                out=PT[64:128, 64:128], in_=S[64:128, 64:128], func=AF.Exp, bias=bias_ap[64:128, :], scale=scale),
                reads=[S_t, st.cst_t], writes=[PT_t])
        else:
            k.act.op(lambda: nc.scalar.activation(out=PT[:], in_=S, func=AF.Exp, bias=bias_ap, scale=scale),
                     reads=[S_t, st.cst_t], writes=[PT_t])
        return (bi, is_rem, j, si, PT, PT_t)

    def stage_b(ctx):
        bi, is_rem, j, si, PT, PT_t = ctx
        vt = hs["vr"] if is_rem else hs["vo"]
        ob, sb2 = 2 + 2 * si, 3 + 2 * si
        for c in range(dv // 128):
            k.pe.op(lambda c=c: nc.tensor.matmul(
                st.bank[ob][:, c * 128:(c + 1) * 128], vt[:, j, c * 128:(c + 1) * 128], PT[:],
                start=(bi == 0 and c == 0), stop=(bi == nb - 1)),
                reads=[hs["t"], PT_t], writes=[st.bank_t[ob]], inc=False)
        k.pe.op(lambda: nc.tensor.matmul(
            st.bank[sb2][:, 0:128], st.ones_bf[:], PT[:], start=(bi == 0), stop=(bi == nb - 1)),
            reads=[PT_t, st.ones_t], writes=[st.bank_t[sb2]], inc=True)

    pend = []
    for bi, blk in enumerate(blocks):
        for si, stream in enumerate(streams):
            pend.append(stage_a(bi, blk, si, stream))
            if len(pend) > lookahead:
                stage_b(pend.pop(0))
    while pend:
        stage_b(pend.pop(0))
    finalize(i)


def load_head(k, ax, hs, qsrc, kosrc, krsrc, vosrc, vrsrc, dv):
    for idx, (ap, R) in enumerate(qsrc):
        k.sp.dma(hs["q"][idx][0:R, :], ap, hs["ds"], writes=[hs["t"]])
    for idx, (ap, R) in enumerate(kosrc):
        k.sp.dma(hs["ko"][idx][0:R, :], ap, hs["ds"], writes=[hs["t"]])
    for idx, (ap, R) in enumerate(krsrc):
        k.sp.dma(hs["kr"][idx][0:R, :], ap, hs["ds"], writes=[hs["t"]])
    for hf in range(2):
        k.sp.dma(hs["vo"][:, hf * 4:(hf + 1) * 4, 0:dv], vosrc.half(hf).rearrange("(j p) d -> p j d", p=128), hs["ds"], writes=[hs["t"]])
        k.sp.dma(hs["vr"][:, hf * 4:(hf + 1) * 4, 0:dv], vrsrc.half(hf).rearrange("(j p) d -> p j d", p=128), hs["ds"], writes=[hs["t"]])


def fin_simple(k, st, ax, chunk):
    nc = k.nc

    def f(i):
        qsl = slice(i * 128, (i + 1) * 128)
        r, r_t = ax.rinv[0], ax.rinv_t[0]
        k.dve.op(lambda: nc.vector.reciprocal(out=r[:], in_=st.bank[3][:, 0:128]), reads=[st.bank_t[3]], writes=[r_t])
        k.dve.op(lambda: nc.vector.tensor_tensor(out=ax.ao[:, chunk, qsl], in0=st.bank[2][:, 0:128], in1=r[:], op=ALU.mult),
                 reads=[st.bank_t[2], r_t], writes=[ax.ao_t[chunk]])
    return f


def attn_even(k, st, sg, j, QT, KTo, Vo, KTr, Vr, abias, acv):
    nc = k.nc
    ax = AttnCtx(k, st)
    cb = st.cb
    ds = DmaSem(k, f"cb{st.uid()}")
    k.sp.dma(cb[:, 0:8], acv[j].partition_broadcast(128), ds, writes=[st.cst_t])
    k.dve.op(lambda: nc.vector.tensor_scalar(out=cb[:, 8:16], in0=cb[:, 0:8], scalar1=st.rb[:, 0:1], scalar2=None, op0=ALU.add),
             reads=[st.cst_t], writes=[st.cst_t])
    sA = 128 ** -0.5
    for hh in range(8):
        hs = ax.sets[hh % 2]
        load_head(k, ax, hs, [(QT[hh * 128:(hh + 1) * 128, :], 128)], [(KTo[hh * 128:(hh + 1) * 128, :], 128)],
                  [(KTr[hh * 128:(hh + 1) * 128, :], 128)], Vo[:, hh * 128:(hh + 1) * 128], Vr[:, hh * 128:(hh + 1) * 128], 128)
        bt, bt_t, bt_ds = ax.bias[hh % 2], ax.bias_t[hh % 2], ax.bias_ds[hh % 2]
        k.sp.dma(bt, abias[j, hh].rearrange("t k q -> k t q"), bt_ds, writes=[bt_t])
        ax.cur_bias, ax.cur_bias_t = bt, bt_t
        for i in range(8):
            blocks = []
            for d_ in range(4, -1, -1):
                jg = i - d_
                rem = jg < 0
                jj = jg + 8 if rem else jg
                if d_ in (2, 3):
                    blocks.append((rem, jj, "plain", cb[:, (8 if rem else 0) + hh:(8 if rem else 0) + hh + 1], None))
                else:
                    tix = {4: 0, 1: 1, 0: 2}[d_]
                    blocks.append((rem, jj, "tile", (st.rb if rem else st.zb)[:, 0:1], tix))
            attn_tile(k, st, ax, hs, i, [([(0, 0, 128)], sA)], blocks, 128, fin_simple(k, st, ax, hh))
    sB = 192 ** -0.5
    for hh in range(8):
        hs = ax.sets[hh % 2]
        load_head(k, ax, hs,
                  [(QT[1024 + hh * 128:1024 + (hh + 1) * 128, :], 128), (QT[2048 + hh * 64:2048 + (hh + 1) * 64, :], 64)],
                  [(KTo[1024 + hh * 128:1024 + (hh + 1) * 128, :], 128), (KTo[2048:2112, :], 64)],
                  [(KTr[1024 + hh * 128:1024 + (hh + 1) * 128, :], 128), (KTr[2048:2112, :], 64)],
                  Vo[:, 1024 + hh * 128:1024 + (hh + 1) * 128], Vr[:, 1024 + hh * 128:1024 + (hh + 1) * 128], 128)
        for i in range(8):
            blocks = [(True, jj, "plain", st.rb[:, 0:1], None) for jj in range(8)]
            blocks += [(False, jj, "diag" if jj == i else "plain", st.zb[:, 0:1], None) for jj in range(i + 1)]
            attn_tile(k, st, ax, hs, i, [([(0, 0, 128), (1, 1, 64)], sB)], blocks, 128, fin_simple(k, st, ax, 8 + hh))
    return ax


def attn_odd(k, st, sg, j, layer, QT, KTo, Vo, KTr, Vr, lvec, gsub):
    nc = k.nc
    ax = AttnCtx(k, st)
    lam_init = 0.8 - 0.6 * math.exp(-0.3 * layer)
    lv = st.lv
    ds = DmaSem(k, f"lv{st.uid()}")
    for q in range(4):
        k.sp.dma(lv[:, q:q + 1], lvec[q][j].rearrange("(p o) -> p o", o=1), ds, writes=[st.cst_t])
    k.sp.dma(lv[:, 8:10], gsub[j].rearrange("(c p) -> p c", p=128), ds, writes=[st.cst_t], allow_slow_non_contiguous=True)
    k.dve.op(lambda: nc.vector.tensor_tensor(out=lv[:, 4:5], in0=lv[:, 0:1], in1=lv[:, 1:2], op=ALU.mult), reads=[st.cst_t], writes=[st.cst_t])
    k.dve.op(lambda: nc.vector.tensor_tensor(out=lv[:, 5:6], in0=lv[:, 2:3], in1=lv[:, 3:4], op=ALU.mult), reads=[st.cst_t], writes=[st.cst_t])
    k.pe.op(lambda: nc.tensor.matmul(st.bank[6][:, 0:2], st.ones_f[:], lv[:, 4:6], start=True, stop=True),
            reads=[st.cst_t, st.ones_t], writes=[st.bank_t[6]], inc=True)
    k.act.op(lambda: nc.scalar.activation(out=lv[:, 6:8], in_=st.bank[6][:, 0:2], func=AF.Exp), reads=[st.bank_t[6]], writes=[st.cst_t])
    k.dve.op(lambda: nc.vector.tensor_tensor(out=lv[:, 4:5], in0=lv[:, 7:8], in1=lv[:, 6:7], op=ALU.subtract), reads=[st.cst_t], writes=[st.cst_t])
    k.dve.op(lambda: nc.vector.tensor_scalar(out=lv[:, 4:5], in0=lv[:, 4:5], scalar1=-lam_init, scalar2=None, op0=ALU.add),
             reads=[st.cst_t], writes=[st.cst_t])
    k.dve.op(lambda: nc.vector.tensor_scalar(out=lv[:, 8:10], in0=lv[:, 8:10], scalar1=1.0 - lam_init, scalar2=None, op0=ALU.mult),
             reads=[st.cst_t], writes=[st.cst_t])
    sC = 128 ** -0.5

    def fin(hh):
        def f(i):
            qsl = slice(i * 128, (i + 1) * 128)
            for si, (dst, dst_t) in enumerate(((ax.o1n, ax.tmp_t[0]), (ax.o2n, ax.tmp_t[1]))):
                r, r_t = ax.rinv[si], ax.rinv_t[si]
                k.dve.op(lambda r=r, si=si: nc.vector.reciprocal(out=r[:], in_=st.bank[3 + 2 * si][:, 0:128]),
                         reads=[st.bank_t[3 + 2 * si]], writes=[r_t])
                for c in range(2):
                    k.dve.op(lambda r=r, si=si, c=c, dst=dst: nc.vector.tensor_tensor(
                        out=dst[:, c * 128:(c + 1) * 128], in0=st.bank[2 + 2 * si][:, c * 128:(c + 1) * 128], in1=r[:], op=ALU.mult),
                        reads=[st.bank_t[2 + 2 * si], r_t], writes=[dst_t])
            k.dve.op(lambda: nc.vector.scalar_tensor_tensor(out=ax.dd[:], in0=ax.o2n[:], scalar=lv[:, 4:5], in1=ax.o1n[:],
                                                           op0=ALU.mult, op1=ALU.add),
                     reads=[ax.tmp_t[0], ax.tmp_t[1], st.cst_t], writes=[ax.tmp_t[2]])
            k.act.op(lambda: nc.scalar.activation(out=ax.sq2[:], in_=ax.dd[:], func=AF.Square), reads=[ax.tmp_t[2]], writes=[ax.tmp_t[3]])
            for c in range(2):
                k.pe.op(lambda c=c: nc.tensor.matmul(st.bank[6][:, 0:128], st.ones_f[:], ax.sq2[:, c * 128:(c + 1) * 128],
                                                     start=(c == 0), stop=(c == 1)),
                        reads=[ax.tmp_t[3], st.ones_t], writes=[st.bank_t[6]], inc=(c == 1))
            k.dve.op(lambda: nc.vector.tensor_scalar(out=ax.stmp[:], in0=st.bank[6][:, 0:128], scalar1=1.0 / 256, scalar2=EPS,
                                                    op0=ALU.mult, op1=ALU.add), reads=[st.bank_t[6]], writes=[ax.tmp_t[4]])
            k.act.op(lambda: nc.scalar.activation(out=ax.stmp[:], in_=ax.stmp[:], func=AF.Sqrt), reads=[ax.tmp_t[4]], writes=[ax.tmp_t[4]])
            k.dve.op(lambda: nc.vector.reciprocal(out=ax.stmp[:], in_=ax.stmp[:]), reads=[ax.tmp_t[4]], writes=[ax.tmp_t[4]])
            for c in range(2):
                k.dve.op(lambda c=c: nc.vector.scalar_tensor_tensor(
                    out=ax.ao[:, hh * 2 + c, qsl], in0=ax.dd[:, c * 128:(c + 1) * 128], scalar=lv[:, 8 + c:9 + c], in1=ax.stmp[:],
                    op0=ALU.mult, op1=ALU.mult),
                    reads=[ax.tmp_t[2], ax.tmp_t[4], st.cst_t], writes=[ax.ao_t[hh * 2 + c]])
        return f

    for hh in range(8):
        hs = ax.sets[hh % 2]
        r0 = hh * 256
        load_head(k, ax, hs, [(QT[r0:r0 + 128, :], 128), (QT[r0 + 128:r0 + 256, :], 128)],
                  [(KTo[r0:r0 + 128, :], 128), (KTo[r0 + 128:r0 + 256, :], 128)],
                  [(KTr[r0:r0 + 128, :], 128), (KTr[r0 + 128:r0 + 256, :], 128)],
                  Vo[:, r0:r0 + 256], Vr[:, r0:r0 + 256], 256)
        for i in range(8):
            blocks = [(True, jj, "plain", st.rb[:, 0:1], None) for jj in range(8)]
            blocks += [(False, jj, "diag" if jj == i else "plain", st.zb[:, 0:1], None) for jj in range(i + 1)]
            attn_tile(k, st, ax, hs, i, [([(0, 0, 128)], sC), ([(1, 1, 128)], sC)], blocks, 256, fin(hh))
    return ax


def mix_out(k, st, wp, ax, w_out, gi):
    for th in range(2):
        y = carve(st, 60 * 1024, [128, KT, TH], F32)
        y_t = [T() for _ in range(KT)]
        for t_ in y_t:
            t_.w = st.big_t.w
            t_.r = dict(st.big_t.r)
            for e_ in ax.every:
                for tok in list(e_.r.values()) + ([e_.w] if e_.w else []):
                    old = t_.r.get(tok[0])
                    if old is None or (old[1], old[2]) < (tok[1], tok[2]):
                        t_.r[tok[0]] = tok
        outproj_half(k, st, wp, w_out, KT, lambda kc, th=th: ax.ao[:, kc, th * TH:(th + 1) * TH], ax.ao_t, y, y_t)
        tail_half(k, st, th, gi, y, y_t)
        ax.every = ax.every + y_t
    ax.done()


NCORES = 8
QROWS = 2560
KROWS = 2112
KVROWS = KROWS + 2048


def make_state(k):
    st = State(k)
    st.big_t = T("big")
    st._uid = [0]

    def uid():
        st._uid[0] += 1
        return st._uid[0]
    st.uid = uid
    st.cb = k.sb("cb", [128, 16], F32)
    st.rb = k.sb("rb", [128, 1], F32)
    st.zb = k.sb("zb", [128, 1], F32)
    st.lv = k.sb("lv", [128, 16], F32)
    st.cst_t = T("cst")
    k.dve.op(lambda: k.nc.vector.memset(st.zb[:], 0.0), writes=[st.cst_t])
    return st


class Weights:
    def __init__(self, k, specs, gather=True):
        self.k = k
        self.full = {}
        self.t = {}
        nc = k.nc
        for name, (R, C) in specs.items():
            if not gather:
                self.full[name] = k.din("w_" + name, [R, C])
                self.t[name] = T()
                continue
            sh = k.din("w_" + name, [R // NCORES, C])
            bounce = nc.dram_tensor("wb_" + name, [R // NCORES, C], F32)
            full = nc.dram_tensor("wf_" + name, [R, C], F32)
            ds = DmaSem(k, "wb_" + name)
            rows = R // NCORES
            step = max(1, (1 << 18) // C)
            for r0 in range(0, rows, step):
                r1 = min(rows, r0 + step)
                k.pool.dma(bounce.ap()[r0:r1, :], sh[r0:r1, :], ds)
            k.pool.e.wait_ge(ds.sem, ds.n)
            sem = k.newsem("cc_" + name)
            nc.gpsimd.collective_compute("AllGather", ALU.bypass, replica_groups=[list(range(NCORES))],
                                         ins=[bounce.ap().opt()], outs=[full.ap().opt()]).then_inc(sem)
            t_ = T()
            t_.w = ("cc_" + name, 0, 1, sem)
            self.full[name] = full.ap()
            self.t[name] = t_

    def get(self, name):
        self.k.pool._need([self.t[name].w])
        return self.full[name]


def build_part1(layer, gather=False):
    even = layer % 2 == 0
    k = K()
    hT = k.din("hT", [D, NT])
    gv = k.din("gvec", [3, D])
    out = k.dout("hT_out", [D, NT])
    qt = k.dout("qt_out", [QROWS, NT], BF16)
    kv = k.dout("kv_out", [KVROWS, NT], BF16)
    specs = {"ffn_w_in": (D, 2 * DFF), "ffn_w_out": (DFF, D)}
    if even:
        specs.update({"ab_w_in": (D, 3904), "ab_w_in_krp": (D, 64), "b_w_qup": (512, 1536), "b_w_qup_rp": (512, 512),
                      "b_w_kvup": (256, 2048)})
        rope = k.din("rope", [2, 64, NT])
        gq = k.din("g_q", [512])
        gkv = k.din("g_kv", [256])
    else:
        specs.update({"c_w_in": (D, 6144), "c_w_in_rp": (D, 1024)})
        rope = k.din("rope", [2, 32, NT])
    st = make_state(k)
    W = Weights(k, specs, gather)
    wp = WPool(k, st)
    wp.big16()
    sg = Stager(k, st)
    load_gains(k, st, [gv[0], gv[1], gv[2]], [1.0, 0.5, 1.0])
    if even:
        k.sp.dma(st.gains[:, 8, 0:4], gq.rearrange("(c p) -> p c", p=128), st.misc_ds, writes=[st.gains_t], allow_slow_non_contiguous=True)
        k.sp.dma(st.gains[:, 9, 0:2], gkv.rearrange("(c p) -> p c", p=128), st.misc_ds, writes=[st.gains_t], allow_slow_non_contiguous=True)
    load_h(k, st, hT)
    ffn_full(k, st, wp, W.get("ffn_w_in"), W.get("ffn_w_out"), 0, 1)
    wp.big16()
    KT_ = RowChunks([(0, KROWS, kv[0:KROWS, :])])
    V = VView([kv[KROWS:KROWS + 1024, :], kv[KROWS + 1024:KVROWS, :]])
    Wf = lambda name, j: W.get(name)
    if even:
        proj_even(k, st, wp, sg, Wf, 0, 2, rope, qt, KT_, V)
    else:
        proj_odd(k, st, wp, sg, Wf, 0, 2, rope, qt, KT_, V)
    store_h(k, st, out)
    for ds in sg.ds:
        k.outsems.append(ds)
    return k.finish()


def build_part2(layer, gather=False):
    even = layer % 2 == 0
    k = K()
    hT = k.din("hT", [D, NT])
    gv = k.din("gvec", [5, D])
    qt = k.din("qt", [QROWS, NT], BF16)
    kvo = k.din("kv_own", [KVROWS, NT], BF16)
    kvr = k.din("kv_rem", [KVROWS, NT], BF16)
    rbias = k.din("rbias", [128, 1])
    pT = k.din("pT", [256, NT])
    out = k.dout("hT_out", [D, NT])
    specs = {"mix_w_out": (D, D), "ffn_w_in": (D, 2 * DFF), "ffn_w_out": (DFF, D), "ple_w_gate": (D, D), "ple_w_proj": (256, D)}
    if even:
        abias = k.din("abias", [1, 8, 3, 128, 128])
        acv = k.din("acv", [1, 8])
    else:
        lvec = [k.din(f"lvec{q}", [1, 128]) for q in range(4)]
        gsub = k.din("gsub", [1, 256])
    st = make_state(k)
    W = Weights(k, specs, gather)
    wp = WPool(k, st)
    wp.big16()
    k.sp.dma(st.rb[:], rbias, st.misc_ds, writes=[st.cst_t])
    load_gains(k, st, [gv[0], gv[1], gv[2], gv[3], gv[4]], [1.0, 1.0, 0.5, 1.0, 1.0])
    load_h(k, st, hT)
    KTo, KTr = RowChunks([(0, KROWS, kvo[0:KROWS, :])]), RowChunks([(0, KROWS, kvr[0:KROWS, :])])
    Vo = VView([kvo[KROWS:KROWS + 1024, :], kvo[KROWS + 1024:KVROWS, :]])
    Vr = VView([kvr[KROWS:KROWS + 1024, :], kvr[KROWS + 1024:KVROWS, :]])
    if even:
        ax = attn_even(k, st, None, 0, qt, KTo, Vo, KTr, Vr, abias, acv)
    else:
        ax = attn_odd(k, st, None, 0, layer, qt, KTo, Vo, KTr, Vr, lvec, gsub)
    mix_out(k, st, wp, ax, W.get("mix_w_out"), 0)
    ffn_full(k, st, wp, W.get("ffn_w_in"), W.get("ffn_w_out"), 1, 2)
    wp.big16()
    for th in range(2):
        ple_half(k, st, wp, th, W.get("ple_w_gate"), W.get("ple_w_proj"), pT, 3, 4)
    store_h(k, st, out)
    return k.finish()


W_SHAPES = {
    "ffn1_w_in": (4, D, 2 * DFF), "ffn1_w_out": (4, DFF, D), "ffn2_w_in": (4, D, 2 * DFF), "ffn2_w_out": (4, DFF, D),
    "ab_w_in": (2, D, 3904), "ab_w_in_krp": (2, D, 64), "b_w_qup": (2, 512, 1536), "b_w_qup_rp": (2, 512, 512),
    "b_w_kvup": (2, 256, 2048), "ab_w_out": (2, D, D), "c_w_in": (2, D, 6144), "c_w_in_rp": (2, D, 1024),
    "c_w_out": (2, D, D), "ple_w_gate": (4, D, D), "ple_w_proj": (4, 256, D),
}


def build_fused(nlayers=4, ncores=NCORES):
    k = K()
    nc = k.nc
    hT = k.din("hT", [D, NT])
    out = k.dout("hT_out", [D, NT])
    gv = k.din("gvec", [4, 8, D])
    gq = k.din("g_q", [2, 512])
    gkv = k.din("g_kv", [2, 256])
    rope_b = k.din("rope_b", [2, 64, NT])
    rope_c = k.din("rope_c", [2, 32, NT])
    rbias = k.din("rbias", [128, 1])
    pT = k.din("pT", [4, 256, NT])
    abias = k.din("abias", [2, 8, 3, 128, 128])
    acv = k.din("acv", [2, 8])
    lvec = [k.din(f"lvec{q}", [2, 128]) for q in range(4)]
    gsub = k.din("gsub", [2, 256])
    Wd = {n: k.din("w_" + n, list(shp)) for n, shp in W_SHAPES.items()}
    W = lambda name, j: Wd[name][j]
    st = make_state(k)
    wp = WPool(k, st)
    wp.big16()
    sg = Stager(k, st)
    k.sp.dma(st.rb[:], rbias, DmaSem(k, "rb"), writes=[st.cst_t])
    load_h(k, st, hT)
    for layer in range(nlayers):
        even = layer % 2 == 0
        j = layer // 2
        gds = DmaSem(k, f"g{layer}")
        for i in range(8):
            k.sp.dma(st.gains[:, i, :], gv[layer, i].rearrange("(kt p) -> p kt", p=128), gds,
                     writes=[st.gains_t], allow_slow_non_contiguous=True)
        if even:
            k.sp.dma(st.gains[:, 8, 0:4], gq[j].rearrange("(c p) -> p c", p=128), gds, writes=[st.gains_t], allow_slow_non_contiguous=True)
            k.sp.dma(st.gains[:, 9, 0:2], gkv[j].rearrange("(c p) -> p c", p=128), gds, writes=[st.gains_t], allow_slow_non_contiguous=True)
        for i in (1, 5):
            k.dve.op(lambda i=i: nc.vector.tensor_scalar(out=st.gains[:, i, :], in0=st.gains[:, i, :], scalar1=0.5, scalar2=None,
                                                         op0=ALU.mult), reads=[st.gains_t], writes=[st.gains_t])
        ffn_full(k, st, wp, W("ffn1_w_in", layer), W("ffn1_w_out", layer), 0, 1)
        wp.big16()
        qts = k.dint(f"qt{layer}", [QROWS, NT], BF16)
        csz = [1024, 1024, 1024, 1024] + ([64] if even else [])
        own_c = [k.dint(f"kvown{layer}_{ci}", [n_, NT], BF16) for ci, n_ in enumerate(csz)]
        pair_c = [k.dint(f"kvpair{layer}_{ci}", [2 * n_, NT], BF16) for ci, n_ in enumerate(csz)]
        kt_chunks = [(0, 1024, own_c[0]), (1024, 1024, own_c[1])] + ([(2048, 64, own_c[4])] if even else [])
        KTo = RowChunks(kt_chunks)
        Vo = VView([own_c[2], own_c[3]])
        if even:
            proj_even(k, st, wp, sg, W, j, 2, rope_b, qts, KTo, Vo)
        else:
            proj_odd(k, st, wp, sg, W, j, 2, rope_c, qts, KTo, Vo)
        toks = list(sg.stores)
        sg.stores = []
        k.pool._need(toks)
        k.sp._need(toks)
        cctoks = []
        for ci in range(len(csz)):
            ccsem = k.newsem(f"cc{layer}_{ci}")
            nc.gpsimd.collective_compute("AllGather", ALU.bypass, replica_groups=[[2 * i_, 2 * i_ + 1] for i_ in range(ncores // 2)],
                                         ins=[own_c[ci].opt()], outs=[pair_c[ci].opt()]).then_inc(ccsem)
            cctoks.append((f"cc{layer}_{ci}", 0, 1, ccsem))
        k.sp._need(cctoks)
        KTr = RowChunks([(0, 1024, pair_c[0][0:1024, :]), (1024, 1024, pair_c[1][0:1024, :])]
                        + ([(2048, 64, pair_c[4][0:64, :])] if even else []))
        Vr = VView([pair_c[2][0:1024, :], pair_c[3][0:1024, :]])
        if even:
            ax = attn_even(k, st, None, j, qts, KTo, Vo, KTr, Vr, abias, acv)
            mix_out(k, st, wp, ax, W("ab_w_out", j), 3)
        else:
            ax = attn_odd(k, st, None, j, layer, qts, KTo, Vo, KTr, Vr, lvec, gsub)
            mix_out(k, st, wp, ax, W("c_w_out", j), 3)
        ffn_full(k, st, wp, W("ffn2_w_in", layer), W("ffn2_w_out", layer), 4, 5)
        wp.big16()
        for th in range(2):
            ple_half(k, st, wp, th, W("ple_w_gate", layer), W("ple_w_proj", layer), pT[layer], 6, 7)
    store_h(k, st, out)
    return k.finish()


def kernel_fused(I, nlayers=4):
    x, p = I["x"], I["p"]
    shared = {"gvec": np.ascontiguousarray(np.stack([
        np.stack([I["ffn1_g_pre"][l], I["ffn1_g_post"][l], I["mix_g_pre"][l], I["mix_g_post"][l],
                  I["ffn2_g_pre"][l], I["ffn2_g_post"][l], I["ple_g_pre"][l], I["ple_g_post"][l]]) for l in range(4)])),
        "g_q": I["b_g_q"], "g_kv": I["b_g_kv"], "gsub": I["c_g_sub"]}
    tiles = [_abias_tiles(I["a_rel_bias"][j]) for j in range(2)]
    shared["abias"] = np.ascontiguousarray(np.stack([t[0] for t in tiles]))
    shared["acv"] = np.ascontiguousarray(np.concatenate([t[1] for t in tiles], 0))
    for q_, n_ in enumerate(("c_lq1", "c_lk1", "c_lq2", "c_lk2")):
        shared[f"lvec{q_}"] = I[n_]
    for n_ in ("ffn1_w_in", "ffn1_w_out", "ffn2_w_in", "ffn2_w_out", "ab_w_in", "b_w_qup", "b_w_kvup", "ab_w_out",
               "c_w_in", "c_w_out", "ple_w_gate", "ple_w_proj"):
        shared["w_" + n_] = I[n_]
    ab = I["ab_w_in"]
    shared["w_ab_w_in_krp"] = np.ascontiguousarray(ab[:, :, 3840 + _swap_halves(64)])
    idx = np.concatenate([h_ * 192 + 128 + _swap_halves(64) for h_ in range(8)])
    shared["w_b_w_qup_rp"] = np.ascontiguousarray(I["b_w_qup"][:, :, idx])
    idx = np.concatenate([h_ * 768 + w_ * 128 + _swap_halves(32) for h_ in range(8) for w_ in range(4)])
    shared["w_c_w_in_rp"] = np.ascontiguousarray(I["c_w_in"][:, :, idx])
    in_maps = []
    for c in range(NCORES):
        m = dict(shared)
        b, hf = c // 2, c % 2
        m["hT"] = np.ascontiguousarray(x[b, hf * NT:(hf + 1) * NT, :].T)
        m["rope_b"] = _rope_tab(64, hf * NT)
        m["rope_c"] = _rope_tab(32, hf * NT)
        m["rbias"] = np.full((128, 1), 0.0 if hf == 1 else -30000.0, np.float32)
        m["pT"] = np.ascontiguousarray(np.transpose(p[:, b, hf * NT:(hf + 1) * NT, :], (0, 2, 1)))
        in_maps.append(m)
    nc = build_fused(nlayers, _NRUN)
    res = _run(nc, in_maps)
    hT = [res[c]["hT_out"] for c in range(NCORES)]
    _dbg(f"h_ple_{nlayers - 1}", hT)
    out = np.empty((4, 2 * NT, D), np.float32)
    for c in range(NCORES):
        out[c // 2, (c % 2) * NT:(c % 2 + 1) * NT, :] = hT[c].T
    return out


def _shard_rows(w):
    w = np.ascontiguousarray(w)
    return [w] * NCORES


def _rope_tab(rot, pos0):
    inv = (500000.0 ** (-np.arange(0, rot, 2, dtype=np.float32) / np.float32(rot))).astype(np.float32)
    ang = (np.arange(pos0, pos0 + NT, dtype=np.float32)[:, None] * inv[None, :]).astype(np.float32)
    c = np.cos(ang).astype(np.float32).T
    s = np.sin(ang).astype(np.float32).T
    return np.ascontiguousarray(np.stack([np.concatenate([c, c], 0), np.concatenate([-s, s], 0)], 0))


def _swap_halves(n):
    return np.concatenate([np.arange(n // 2, n), np.arange(0, n // 2)])


def _abias_tiles(table):
    kk = np.arange(128)[:, None]
    qq = np.arange(128)[None, :]
    out = np.empty((8, 3, 128, 128), np.float32)
    far = table[:, 191]
    t4 = np.broadcast_to(far[:, None, None], (8, 128, 128)).copy()
    t4[:, (kk < 64) & (qq >= 64)] = -30000.0
    out[:, 0] = t4
    d1 = np.clip(128 + qq - kk, -63, 128) + 63
    out[:, 1] = table[:, d1]
    d0 = np.clip(qq - kk, -63, 128) + 63
    t0 = table[:, d0].copy()
    t0[:, (kk >= 64) & (qq < 64)] = -30000.0
    out[:, 2] = t0
    return out, np.ascontiguousarray(far[None, :])


_NRUN = NCORES


def _run(nc, in_maps):
    res = run_bass_kernel_spmd(nc, in_maps[:_NRUN], core_ids=list(range(_NRUN)))
    r = list(res.results)
    while len(r) < NCORES:
        r.append(r[len(r) % _NRUN])
    return r


_DEBUG = None
_KSTOP = 0


def _dbg(name, hT):
    if _DEBUG is None:
        return
    ref = _DEBUG[name][0]
    for c in range(2):
        o = hT[c].T.astype(np.float32)
        r = ref[c * NT:(c + 1) * NT]
        print("DBG", name, "core", c, "relerr", float(np.sqrt(((o - r) ** 2).mean() / (r ** 2).mean())),
              "finite", bool(np.isfinite(o).all()), flush=True)
        print("   per-tile", [round(float(np.sqrt(((o[t * 128:(t + 1) * 128] - r[t * 128:(t + 1) * 128]) ** 2).mean() / (r ** 2).mean())), 4) for t in range(8)], flush=True)


_FUSED = True
_NLAYERS = 4


def kernel(**I):
    import ml_dtypes
    I = {k_: np.asarray(v) for k_, v in I.items()}
    if _FUSED:
        return kernel_fused(I, _NLAYERS)
    x, p = I["x"], I["p"]
    hT = [np.ascontiguousarray(x[c // 2, (c % 2) * NT:(c % 2 + 1) * NT, :].T) for c in range(NCORES)]
    rbias = [np.full((128, 1), 0.0 if c % 2 == 1 else -30000.0, np.float32) for c in range(NCORES)]
    for layer in range(4):
        even = layer % 2 == 0
        j = layer // 2
        shared = {"gvec": np.stack([I["ffn1_g_pre"][layer], I["ffn1_g_post"][layer], I["mix_g_pre"][layer]])}
        wsh = {"ffn_w_in": _shard_rows(I["ffn1_w_in"][layer]), "ffn_w_out": _shard_rows(I["ffn1_w_out"][layer])}
        if even:
            w_in = I["ab_w_in"][j]
            wsh["ab_w_in"] = _shard_rows(w_in)
            wsh["ab_w_in_krp"] = _shard_rows(w_in[:, 3840 + _swap_halves(64)])
            qup = I["b_w_qup"][j]
            wsh["b_w_qup"] = _shard_rows(qup)
            idx = np.concatenate([h_ * 192 + 128 + _swap_halves(64) for h_ in range(8)])
            wsh["b_w_qup_rp"] = _shard_rows(qup[:, idx])
            wsh["b_w_kvup"] = _shard_rows(I["b_w_kvup"][j])
            shared["g_q"] = I["b_g_q"][j]
            shared["g_kv"] = I["b_g_kv"][j]
            rope = [_rope_tab(64, (c % 2) * NT) for c in range(NCORES)]
        else:
            w_in = I["c_w_in"][j]
            wsh["c_w_in"] = _shard_rows(w_in)
            idx = np.concatenate([h_ * 768 + w_ * 128 + _swap_halves(32) for h_ in range(8) for w_ in range(4)])
            wsh["c_w_in_rp"] = _shard_rows(w_in[:, idx])
            rope = [_rope_tab(32, (c % 2) * NT) for c in range(NCORES)]
        nc = build_part1(layer)
        in_maps = []
        for c in range(NCORES):
            m = dict(shared)
            m["hT"] = hT[c]
            m["rope"] = rope[c]
            for n_, sh in wsh.items():
                m["w_" + n_] = sh[c]
            in_maps.append(m)
        res = _run(nc, in_maps)
        hT = [res[c]["hT_out"] for c in range(NCORES)]
        _dbg(f"h_ffn1_{layer}", hT)
        qt = [res[c]["qt_out"] for c in range(NCORES)]
        kv = [res[c]["kv_out"] for c in range(NCORES)]
        if _KSTOP == 10 + layer:
            return None
        del wsh, in_maps
        shared = {"gvec": np.stack([I["mix_g_post"][layer], I["ffn2_g_pre"][layer], I["ffn2_g_post"][layer],
                                    I["ple_g_pre"][layer], I["ple_g_post"][layer]])}
        wsh = {"mix_w_out": _shard_rows((I["ab_w_out"] if even else I["c_w_out"])[j]),
               "ffn_w_in": _shard_rows(I["ffn2_w_in"][layer]), "ffn_w_out": _shard_rows(I["ffn2_w_out"][layer]),
               "ple_w_gate": _shard_rows(I["ple_w_gate"][layer]), "ple_w_proj": _shard_rows(I["ple_w_proj"][layer])}
        if even:
            tiles, far = _abias_tiles(I["a_rel_bias"][j])
            shared["abias"] = tiles[None]
            shared["acv"] = far
        else:
            for q_, n_ in enumerate(("c_lq1", "c_lk1", "c_lq2", "c_lk2")):
                shared[f"lvec{q_}"] = I[n_][j][None]
            shared["gsub"] = I["c_g_sub"][j][None]
        nc = build_part2(layer)
        in_maps = []
        for c in range(NCORES):
            m = dict(shared)
            m["hT"] = hT[c]
            m["qt"] = qt[c]
            m["kv_own"] = kv[c]
            m["kv_rem"] = kv[c - 1] if c % 2 == 1 else kv[c]
            m["rbias"] = rbias[c]
            m["pT"] = np.ascontiguousarray(p[layer, c // 2, (c % 2) * NT:(c % 2 + 1) * NT, :].T)
            for n_, sh in wsh.items():
                m["w_" + n_] = sh[c]
            in_maps.append(m)
        res = _run(nc, in_maps)
        hT = [res[c]["hT_out"] for c in range(NCORES)]
        _dbg(f"h_ple_{layer}", hT)
        if _KSTOP == 20 + layer:
            return None
        del wsh, in_maps
    out = np.empty((4, 2 * NT, D), np.float32)
    for c in range(NCORES):
        out[c // 2, (c % 2) * NT:(c % 2 + 1) * NT, :] = hT[c].T
    return out
```

```python
import contextlib
import math
import numpy as np
import concourse.bass as bass
import concourse.mybir as mybir
from concourse.bass_utils import run_bass_kernel_spmd

F32 = mybir.dt.float32
BF16 = mybir.dt.bfloat16
AF = mybir.ActivationFunctionType
ALU = mybir.AluOpType

D = 2048
NT = 1024
TH = 512
KT = D // 128
DFF = 5632
FC = DFF // 128
EPS = 1e-6
SEM_LIMIT = 20000


class T:
    __slots__ = ("w", "r", "name")

    def __init__(self, name=""):
        self.w = None
        self.r = {}
        self.name = name


class DmaSem:
    def __init__(self, K, name):
        self.sem = K.newsem(name)
        self.group = "dma_" + name
        self.n = 0


class E:
    def __init__(self, K, name, eng, is_pe=False):
        self.K = K
        self.name = name
        self.e = eng
        self.is_pe = is_pe
        self.epoch = 0
        self.cnt = 0
        self.sem = K.newsem(f"{name}_e0")
        self.seen = {}

    def _need(self, toks):
        for tok in toks:
            if tok is None:
                continue
            group, epoch, val, sem = tok
            if self.is_pe and group == self.name:
                continue
            s = self.seen.get(group)
            if s is not None and s >= (epoch, val):
                continue
            self.e.wait_ge(sem, val)
            self.seen[group] = (epoch, val)

    def deps(self, reads, writes):
        toks = []
        for b in reads:
            toks.append(b.w)
        for b in writes:
            toks.append(b.w)
            toks.extend(b.r.values())
        self._need(toks)

    def _record(self, tok, reads, writes):
        for b in reads:
            old = b.r.get(tok[0])
            if old is None or (old[1], old[2]) < (tok[1], tok[2]):
                b.r[tok[0]] = tok
        for b in writes:
            b.w = tok
            b.r = {}

    def op(self, fn, reads=(), writes=(), inc=True):
        self.deps(reads, writes)
        ins = fn()
        if inc:
            ins.then_inc(self.sem, 1)
            self.cnt += 1
            tok = (self.name, self.epoch, self.cnt, self.sem)
            self._record(tok, reads, writes)
            if self.cnt >= SEM_LIMIT:
                self.epoch += 1
                self.cnt = 0
                self.sem = self.K.newsem(f"{self.name}_e{self.epoch}")
        else:
            tok = (self.name, self.epoch, self.cnt + 1, self.sem)
            self._record(tok, reads, writes)
        return ins

    def dma(self, out, in_, ds, reads=(), writes=(), **kw):
        self.deps(reads, writes)
        ins = self.e.dma_start(out=out, in_=in_, **kw)
        ds.n += 16
        ins.then_inc(ds.sem, 16)
        tok = (ds.group, 0, ds.n, ds.sem)
        self._record(tok, reads, writes)
        return ins


class K:
    def __init__(self):
        self.nc = bass.Bass("TRN2", target_bir_lowering=False)
        self.stack = contextlib.ExitStack()
        self.nsem = 0
        nc = self.nc
        self.pe = E(self, "pe", nc.tensor, is_pe=True)
        self.act = E(self, "act", nc.scalar)
        self.dve = E(self, "dve", nc.vector)
        self.pool = E(self, "pool", nc.gpsimd)
        self.sp = E(self, "sp", nc.sync)
        self.outsems = []

    def newsem(self, name):
        self.nsem += 1
        return self.stack.enter_context(self.nc.semaphore(f"s{self.nsem}_{name}"))

    def sb(self, name, shape, dt):
        return self.stack.enter_context(self.nc.sbuf_tensor(name, shape, dt))

    def ps(self, name, shape, dt=F32):
        return self.stack.enter_context(self.nc.psum_tensor(name, shape, dt))

    def din(self, name, shape, dt=F32):
        return self.nc.dram_tensor(name, list(shape), dt, kind="ExternalInput").ap()

    def dout(self, name, shape, dt=F32):
        return self.nc.dram_tensor(name, list(shape), dt, kind="ExternalOutput").ap()

    def dint(self, name, shape, dt=F32):
        return self.nc.dram_tensor(name, list(shape), dt, kind="Internal").ap()

    def finish(self):
        for ds in self.outsems:
            self.sp.e.wait_ge(ds.sem, ds.n)
        self.stack.close()
        return self.nc


class State:
    def __init__(self, k: K):
        self.k = k
        self.h = k.sb("h", [128, KT, NT], F32)
        self.h_t = [[T(f"h{c}_{t}") for t in range(2)] for c in range(KT)]
        self.bank = [k.ps(f"bank{i}", [128, 512], F32) for i in range(8)]
        self.bank_t = [T(f"bank{i}") for i in range(8)]
        self.ones_bf = k.sb("ones_bf", [128, 128], BF16)
        self.ones_f = k.sb("ones_f", [128, 128], F32)
        self.ones_t = T("ones")
        k.dve.op(lambda: k.nc.vector.memset(self.ones_bf[:], 1.0), writes=[self.ones_t])
        k.dve.op(lambda: k.nc.vector.memset(self.ones_f[:], 1.0), writes=[self.ones_t])
        self.big = k.sb("big", [128, 136 * 1024], mybir.dt.uint8)
        self.sq = [k.sb(f"sq{i}", [128, TH], F32) for i in range(1)]
        self.sq_t = [T(f"sq{i}") for i in range(1)]
        self.sq_i = 0
        self.rstd = k.sb("rstd", [128, TH], F32)
        self.rstd_t = T("rstd")
        self.gains = k.sb("gains", [128, 10, KT], F32)
        self.gains_t = T("gains")
        self.misc_ds = DmaSem(k, "misc")


def rms_stats_accum(k, st, src_ap, src_t, bank_i, first, last, src_is_psum=False):
    i = 0
    sq, sq_t = st.sq[i], st.sq_t[i]
    k.act.op(lambda: k.nc.scalar.activation(out=sq[:], in_=src_ap, func=AF.Square),
             reads=[src_t], writes=[sq_t])
    k.pe.op(lambda: k.nc.tensor.matmul(st.bank[bank_i][:], st.ones_f[:], sq[:], start=first, stop=last),
            reads=[sq_t, st.ones_t], writes=[st.bank_t[bank_i]], inc=True)


def rstd_from_bank(k, st, bank_i, nfeat):
    k.dve.op(lambda: k.nc.vector.tensor_scalar(out=st.rstd[:], in0=st.bank[bank_i][:], scalar1=1.0 / nfeat,
                                              scalar2=EPS, op0=ALU.mult, op1=ALU.add),
             reads=[st.bank_t[bank_i]], writes=[st.rstd_t])
    k.act.op(lambda: k.nc.scalar.activation(out=st.rstd[:], in_=st.rstd[:], func=AF.Sqrt),
             reads=[st.rstd_t], writes=[st.rstd_t])
    k.dve.op(lambda: k.nc.vector.reciprocal(out=st.rstd[:], in_=st.rstd[:]),
             reads=[st.rstd_t], writes=[st.rstd_t])


class WPool:
    def __init__(self, k, st):
        self.k = k
        self.st = st
        self.ds = [DmaSem(k, f"w{i}") for i in range(4)]
        self.ds_x = [DmaSem(k, f"wx{i}") for i in range(4)]
        self.slots = []
        self.ts = []
        self.i = 0
        self.cfg = None

    def config(self, offs_kb, kb):
        cfg = (tuple(offs_kb), kb)
        if cfg == self.cfg:
            return
        st = self.st
        u = {}
        for t_ in self.ts + [st.big_t]:
            for tok in list(t_.r.values()) + ([t_.w] if t_.w is not None else []):
                old = u.get(tok[0])
                if old is None or (old[1], old[2]) < (tok[1], tok[2]):
                    u[tok[0]] = tok
        st.big_t.w = None
        st.big_t.r = dict(u)
        self.slots = [carve(st, o * 1024, [128, kb * 512], BF16) for o in offs_kb]
        self.ts = [T() for _ in offs_kb]
        for t_ in self.ts:
            t_.r = dict(u)
        self.i = 0
        self.cfg = cfg

    def big16(self):
        self.config([104, 120], 16)

    def small4(self):
        self.config([120, 124, 128, 132], 4)

    def next(self):
        i = self.i
        self.i = (self.i + 1) % len(self.slots)
        return self.slots[i], self.ts[i], self.ds[i]


def carve(st, off_bytes, shape, dt):
    esz = 2 if dt == BF16 else 4
    n = 1
    for s_ in shape[1:]:
        n *= s_
    ap = st.big[:, off_bytes:off_bytes + n * esz].bitcast(dt)
    if len(shape) == 3:
        ap = ap.rearrange("p (a b) -> p a b", a=shape[1])
    return ap


def load_gains(k, st, vecs, scales):
    nc = k.nc
    for i, v in enumerate(vecs):
        k.sp.dma(st.gains[:, i, :], v.rearrange("(kt p) -> p kt", p=128), st.misc_ds,
                 writes=[st.gains_t], allow_slow_non_contiguous=True)
    for i, s_ in enumerate(scales):
        if s_ != 1.0:
            k.dve.op(lambda i=i, s_=s_: nc.vector.tensor_scalar(
                out=st.gains[:, i, :], in0=st.gains[:, i, :], scalar1=s_, scalar2=None, op0=ALU.mult),
                reads=[st.gains_t], writes=[st.gains_t])


def all_h(st):
    return [st.h_t[c][t] for c in range(KT) for t in range(2)]


def load_h(k, st, hT):
    v = hT.rearrange("(kt p) t -> p kt t", p=128)
    ds = DmaSem(k, "hload")
    for q in range(4):
        k.sp.dma(st.h[:, q * 4:(q + 1) * 4, :], v[:, q * 4:(q + 1) * 4, :], ds, writes=all_h(st))


def store_h(k, st, hT_out):
    v = hT_out.rearrange("(kt p) t -> p kt t", p=128)
    ds = DmaSem(k, "hstore")
    k.outsems.append(ds)
    for q in range(4):
        k.sp.dma(v[:, q * 4:(q + 1) * 4, :], st.h[:, q * 4:(q + 1) * 4, :], ds, reads=all_h(st))


def prenorm_half(k, st, th, gi, xn, xn_t):
    nc = k.nc
    tsl = slice(th * TH, (th + 1) * TH)
    for kt in range(KT):
        rms_stats_accum(k, st, st.h[:, kt, tsl], st.h_t[kt][th], 6, kt == 0, kt == KT - 1)
    rstd_from_bank(k, st, 6, D)
    for kt in range(KT):
        k.dve.op(lambda kt=kt: nc.vector.scalar_tensor_tensor(
            out=xn[:, kt, :], in0=st.h[:, kt, tsl], scalar=st.gains[:, gi, kt:kt + 1], in1=st.rstd[:],
            op0=ALU.mult, op1=ALU.mult),
            reads=[st.h_t[kt][th], st.rstd_t, st.gains_t], writes=[xn_t[kt]])


def tail_half(k, st, th, gi, y, y_t):
    nc = k.nc
    tsl = slice(th * TH, (th + 1) * TH)
    rstd_from_bank(k, st, 7, D)
    for dc in range(KT):
        k.dve.op(lambda dc=dc: nc.vector.scalar_tensor_tensor(
            out=y[:, dc, :], in0=y[:, dc, :], scalar=st.gains[:, gi, dc:dc + 1], in1=st.rstd[:],
            op0=ALU.mult, op1=ALU.mult),
            reads=[y_t[dc], st.rstd_t, st.gains_t], writes=[y_t[dc]])
        k.dve.op(lambda dc=dc: nc.vector.tensor_tensor(
            out=st.h[:, dc, tsl], in0=st.h[:, dc, tsl], in1=y[:, dc, :], op=ALU.add),
            reads=[y_t[dc], st.h_t[dc][th]], writes=[st.h_t[dc][th]])


def outproj_half(k, st, wp, w_dram, nkc, rhs_of, rhs_t, y, y_t, kblk=None):
    nc = k.nc
    kblk = kblk or nkc
    w_v = w_dram.rearrange("(kc p) n -> p kc n", p=128)
    for dc in range(KT):
        bnk = 4 + (dc % 2)
        for k0 in range(0, nkc, kblk):
            wb, wb_t, wb_ds = wp.next()
            wv = wb[:, 0:kblk * 128].rearrange("p (kc n) -> p kc n", kc=kblk)
            k.pool.dma(wv, w_v[:, k0:k0 + kblk, dc * 128:(dc + 1) * 128], wb_ds, writes=[wb_t])
            for kc in range(k0, k0 + kblk):
                k.pe.op(lambda kc=kc, bnk=bnk, wv=wv, k0=k0: nc.tensor.matmul(
                    st.bank[bnk][:], wv[:, kc - k0, :], rhs_of(kc), start=(kc == 0), stop=(kc == nkc - 1)),
                    reads=[wb_t, rhs_t[kc]], writes=[st.bank_t[bnk]], inc=(kc == k0 + kblk - 1))
        k.act.op(lambda dc=dc, bnk=bnk: nc.scalar.copy(out=y[:, dc, :], in_=st.bank[bnk][:]),
                 reads=[st.bank_t[bnk]], writes=[y_t[dc]])
        rms_stats_accum(k, st, st.bank[bnk][:], st.bank_t[bnk], 7, dc == 0, dc == KT - 1)


def ffn_full(k, st, wp, w_in, w_out, gpre_i, gpost_i):
    nc = k.nc
    wp.small4()
    xn = carve(st, 0, [128, KT, NT], BF16)
    g = carve(st, 32 * 1024, [128, FC, NT], BF16)
    y = carve(st, 0, [128, KT, TH], F32)
    xn_t = [T() for _ in range(KT)]
    g_t = [[T() for _ in range(2)] for _ in range(FC)]
    for t_ in xn_t + [t2 for gg in g_t for t2 in gg]:
        t_.w = st.big_t.w
        t_.r = dict(st.big_t.r)
    for th in range(2):
        prenorm_half(k, st, th, gpre_i, xn[:, :, th * TH:(th + 1) * TH], xn_t)
    w_in_v = w_in.rearrange("(kt p) n -> p kt n", p=128)
    XM0 = 36
    x_slots = [carve(st, 32 * 1024 + (XM0 + 2 * e) * 2048, [128, 2048], BF16) for e in range(4)]
    x_ts = [T() for _ in range(4)]
    for t_ in x_ts:
        t_.w = st.big_t.w
        t_.r = dict(st.big_t.r)
    rot = [0]

    def next_slot(m):
        if m < 32:
            i_ = rot[0] % 8
            rot[0] += 1
            if i_ >= 4:
                return x_slots[i_ - 4], x_ts[i_ - 4], wp.ds_x[i_ - 4]
            return wp.slots[i_], wp.ts[i_], wp.ds[i_]
        i_ = rot[0] % 4
        rot[0] += 1
        return wp.slots[i_], wp.ts[i_], wp.ds[i_]

    for m in range(FC):
        wvs = []
        for half in range(2):
            wb, wb_t, wb_ds = next_slot(m)
            wv = wb[:, 0:KT * 128].rearrange("p (kt n) -> p kt n", kt=KT)
            k.pool.dma(wv, w_in_v[:, :, half * DFF + m * 128:half * DFF + (m + 1) * 128], wb_ds, writes=[wb_t])
            wvs.append((wv, wb_t))
        for th in range(2):
            tsl = slice(th * TH, (th + 1) * TH)
            ba, bb = 2 * th, 2 * th + 1
            for half, bnk in ((0, ba), (1, bb)):
                wv, wb_t = wvs[half]
                for kt in range(KT):
                    k.pe.op(lambda kt=kt, bnk=bnk, wv=wv, tsl=tsl: nc.tensor.matmul(
                        st.bank[bnk][:], wv[:, kt, :], xn[:, kt, tsl], start=(kt == 0), stop=(kt == KT - 1)),
                        reads=[wb_t, xn_t[kt]], writes=[st.bank_t[bnk]], inc=(kt == KT - 1))
            k.act.op(lambda ba=ba, m=m, tsl=tsl: nc.scalar.activation(out=g[:, m, tsl], in_=st.bank[ba][:], func=AF.Silu),
                     reads=[st.bank_t[ba]], writes=[g_t[m][th]] + ([x_ts[(m - XM0) // 2]] if m >= XM0 else []))
            k.dve.op(lambda bb=bb, m=m, tsl=tsl: nc.vector.tensor_tensor(
                out=g[:, m, tsl], in0=st.bank[bb][:], in1=g[:, m, tsl], op=ALU.mult),
                reads=[st.bank_t[bb], g_t[m][th]], writes=[g_t[m][th]])
    y_t = [T() for _ in range(KT)]
    for t_ in y_t:
        for x_ in xn_t:
            for tok in list(x_.r.values()) + ([x_.w] if x_.w is not None else []):
                old = t_.r.get(tok[0])
                if old is None or (old[1], old[2]) < (tok[1], tok[2]):
                    t_.r[tok[0]] = tok
    for th in range(2):
        tsl = slice(th * TH, (th + 1) * TH)
        outproj_half(k, st, wp, w_out, FC, lambda kc, tsl=tsl: g[:, kc, tsl], [g_t[kc][th] for kc in range(FC)], y, y_t, kblk=11)
        tail_half(k, st, th, gpost_i, y, y_t)
    wp.i = 0
    merge_big(st, xn_t + [t2 for gg in g_t for t2 in gg] + y_t + x_ts)


def merge_big(st, ts):
    r = {}
    w = None
    for t_ in ts:
        for tok in list(t_.r.values()) + ([t_.w] if t_.w is not None else []):
            old = r.get(tok[0])
            if old is None or (old[1], old[2]) < (tok[1], tok[2]):
                r[tok[0]] = tok
    st.big_t.w = None
    st.big_t.r = r


def ple_half(k, st, wp, th, w_gate, w_proj, pT_dram, gpre_i, gpost_i):
    nc = k.nc
    xn = carve(st, 0, [128, KT, TH], BF16)
    pt = carve(st, 16 * 1024, [128, 2, TH], BF16)
    sg = carve(st, 20 * 1024, [128, 2, TH], F32)
    y = carve(st, 60 * 1024, [128, KT, TH], F32)
    xn_t = [T() for _ in range(KT)]
    y_t = [T() for _ in range(KT)]
    pt_t = T()
    sg_t = [T(), T()]
    for t_ in xn_t + y_t + [pt_t] + sg_t:
        t_.w = st.big_t.w
        t_.r = dict(st.big_t.r)
    tsl = slice(th * TH, (th + 1) * TH)
    ds = DmaSem(k, f"pt{st.uid()}")
    k.pool.dma(pt, pT_dram.rearrange("(kc p) t -> p kc t", p=128)[:, :, tsl], ds, writes=[pt_t])
    prenorm_half(k, st, th, gpre_i, xn, xn_t)
    wg_v = w_gate.rearrange("(kt p) n -> p kt n", p=128)
    wp_v = w_proj.rearrange("(kt p) n -> p kt n", p=128)
    for blk in range(8):
        wb, wb_t, wb_ds = wp.next()
        wv = wb[:, 0:(KT + 2) * 256].rearrange("p (kt n) -> p kt n", kt=KT + 2)
        k.pool.dma(wv[:, 0:KT, :], wg_v[:, :, blk * 256:(blk + 1) * 256], wb_ds, writes=[wb_t])
        k.pool.dma(wv[:, KT:KT + 2, :], wp_v[:, :, blk * 256:(blk + 1) * 256], wb_ds, writes=[wb_t])
        for c in range(2):
            dc = blk * 2 + c
            bg, bp = (0, 1) if dc % 2 == 0 else (2, 3)
            for kt in range(KT):
                k.pe.op(lambda kt=kt, bg=bg, c=c, wv=wv: nc.tensor.matmul(
                    st.bank[bg][:], wv[:, kt, c * 128:(c + 1) * 128], xn[:, kt, :],
                    start=(kt == 0), stop=(kt == KT - 1)),
                    reads=[wb_t, xn_t[kt]], writes=[st.bank_t[bg]], inc=(kt == KT - 1))
            for kc in range(2):
                k.pe.op(lambda kc=kc, bp=bp, c=c, wv=wv: nc.tensor.matmul(
                    st.bank[bp][:], wv[:, KT + kc, c * 128:(c + 1) * 128], pt[:, kc, :],
                    start=(kc == 0), stop=(kc == 1)),
                    reads=[wb_t, pt_t], writes=[st.bank_t[bp]], inc=(kc == 1))
            s_, s_t = sg[:, dc % 2, :], sg_t[dc % 2]
            k.act.op(lambda bg=bg, s_=s_: nc.scalar.activation(out=s_, in_=st.bank[bg][:], func=AF.Sigmoid),
                     reads=[st.bank_t[bg]], writes=[s_t])
            k.dve.op(lambda bp=bp, s_=s_, dc=dc: nc.vector.tensor_tensor(
                out=y[:, dc, :], in0=st.bank[bp][:], in1=s_, op=ALU.mult),
                reads=[st.bank_t[bp], s_t], writes=[y_t[dc]])
            rms_stats_accum(k, st, y[:, dc, :], y_t[dc], 7, dc == 0, dc == KT - 1)
    tail_half(k, st, th, gpost_i, y, y_t)
    merge_big(st, xn_t + y_t + [pt_t] + sg_t)


class Stager:
    def __init__(self, k, st, n=2):
        self.k = k
        self.tiles = [carve(st, (96 + 2 * i) * 1024, [128, NT], BF16) for i in range(n)]
        self.ts = [T(f"stg{i}") for i in range(n)]
        self.ds = [DmaSem(k, f"stg{i}") for i in range(n)]
        self.i = 0
        self.stores = []

    def next(self):
        i = self.i
        self.i = (self.i + 1) % len(self.tiles)
        return self.tiles[i], self.ts[i], self.ds[i]

    def store(self, dst, src, t_, ds):
        self.k.sp.dma(dst, src, ds, reads=[t_])
        self.stores.append((ds.group, 0, ds.n, ds.sem))

    def sync(self, st):
        for t_ in self.ts:
            t_.w = st.big_t.w
            t_.r = dict(st.big_t.r)

    def barrier(self, eng):
        eng._need(self.stores)
        self.stores = []


def load_w_slot(k, wp, view_ap, nk, ncols, pieces):
    wb, wb_t, wb_ds = wp.next()
    wv = wb[:, 0:nk * ncols].rearrange("p (kt n) -> p kt n", kt=nk)
    for c0, src in pieces:
        w = src.shape[-1]
        k.pool.dma(wv[:, :, c0:c0 + w], src, wb_ds, writes=[wb_t])
    return wv, wb_t


def mm_group(k, st, bank_i, lhs_list, rhs_list, reads, M=128, N=TH):
    nc = k.nc
    n = len(lhs_list)
    for i in range(n):
        k.pe.op(lambda i=i: nc.tensor.matmul(st.bank[bank_i][0:M, 0:N], lhs_list[i], rhs_list[i],
                                            start=(i == 0), stop=(i == n - 1)),
                reads=reads[i], writes=[st.bank_t[bank_i]], inc=(i == n - 1))


def hn_all(k, st, gi):
    hn = carve(st, 0, [128, KT, NT], BF16)
    hn_t = [T() for _ in range(KT)]
    for t_ in hn_t:
        t_.w = st.big_t.w
        t_.r = dict(st.big_t.r)
    for th in range(2):
        prenorm_half(k, st, th, gi, hn[:, :, th * TH:(th + 1) * TH], hn_t)
    return hn, hn_t


def fm_plain(k, st, sg, wv, wb_t, nk, c0, M, rhs_of, rhs_t, dst_rows, pbank):
    nc = k.nc
    stg, stg_t, stg_ds = sg.next()
    for th in range(2):
        b = pbank[0]
        pbank[0] = (pbank[0] + 1) % 4
        mm_group(k, st, b, [wv[:, kt, c0:c0 + M] for kt in range(nk)], [rhs_of(kt, th) for kt in range(nk)],
                 [[wb_t, rhs_t[kt]] for kt in range(nk)], M=M)
        k.act.op(lambda b=b, th=th: nc.scalar.copy(out=stg[0:M, th * TH:(th + 1) * TH], in_=st.bank[b][0:M, :]),
                 reads=[st.bank_t[b]], writes=[stg_t])
    sg.store(dst_rows, stg[0:M, :], stg_t, stg_ds)


def fm_rope(k, st, sg, wv, wb_t, nk, c0, M, wvr, wbr_t, cr0, R, rhs_of, rhs_t, dst_rows, pbank, cc, ss, tmp1, tmp2, tmp_t):
    nc = k.nc
    stg, stg_t, stg_ds = sg.next()
    for th in range(2):
        tsl = slice(th * TH, (th + 1) * TH)
        b = pbank[0]
        b2 = (b + 1) % 4
        pbank[0] = (pbank[0] + 2) % 4
        mm_group(k, st, b, [wv[:, kt, c0:c0 + M] for kt in range(nk)], [rhs_of(kt, th) for kt in range(nk)],
                 [[wb_t, rhs_t[kt]] for kt in range(nk)], M=M)
        mm_group(k, st, b2, [wvr[:, kt, cr0:cr0 + R] for kt in range(nk)], [rhs_of(kt, th) for kt in range(nk)],
                 [[wbr_t, rhs_t[kt]] for kt in range(nk)], M=R)
        k.dve.op(lambda b=b, tsl=tsl: nc.vector.tensor_tensor(out=tmp1[0:R, :], in0=st.bank[b][0:R, :], in1=cc[0:R, tsl], op=ALU.mult),
                 reads=[st.bank_t[b], st.rope_t], writes=[tmp_t[0]])
        k.dve.op(lambda b2=b2, tsl=tsl: nc.vector.tensor_tensor(out=tmp2[0:R, :], in0=st.bank[b2][0:R, :], in1=ss[0:R, tsl], op=ALU.mult),
                 reads=[st.bank_t[b2], st.rope_t], writes=[tmp_t[1]])
        k.dve.op(lambda tsl=tsl: nc.vector.tensor_tensor(out=stg[0:R, tsl], in0=tmp1[0:R, :], in1=tmp2[0:R, :], op=ALU.add),
                 reads=[tmp_t[0], tmp_t[1]], writes=[stg_t])
        if M > R:
            for (p0, p1) in ((32, 64), (64, 128)):
                k.act.op(lambda b=b, tsl=tsl, p0=p0, p1=p1: nc.scalar.copy(out=stg[p0:p1, tsl], in_=st.bank[b][p0:p1, :]),
                         reads=[st.bank_t[b]], writes=[stg_t])
    sg.store(dst_rows, stg[0:M, :], stg_t, stg_ds)


def tm_proj(k, st, sg, wv, wb_t, nk, ncols, lhs_of, lhs_t, dstV, pbank, col_of=None):
    nc = k.nc
    for tt in range(NT // 128):
        b = pbank[0]
        pbank[0] = (pbank[0] + 1) % 4
        stg, stg_t, stg_ds = sg.next()
        if col_of is None:
            mm_group(k, st, b, [lhs_of(kt, tt) for kt in range(nk)], [wv[:, kt, 0:ncols] for kt in range(nk)],
                     [[wb_t, lhs_t[kt]] for kt in range(nk)], M=128, N=ncols)
        else:
            for (o0, c0, w) in col_of:
                k_last = (o0, c0, w) == col_of[-1]
                for kt in range(nk):
                    k.pe.op(lambda kt=kt, o0=o0, c0=c0, w=w: nc.tensor.matmul(
                        st.bank[b][:, o0:o0 + w], lhs_of(kt, tt), wv[:, kt, c0:c0 + w], start=(kt == 0), stop=(kt == nk - 1)),
                        reads=[wb_t, lhs_t[kt]], writes=[st.bank_t[b]], inc=(k_last and kt == nk - 1))
        k.act.op(lambda b=b: nc.scalar.copy(out=stg[:, 0:ncols], in_=st.bank[b][:, 0:ncols]),
                 reads=[st.bank_t[b]], writes=[stg_t])
        sg.store(dstV[tt * 128:(tt + 1) * 128, :], stg[:, 0:ncols], stg_t, stg_ds)


def wview(w2d, nk):
    return w2d.rearrange("(kt p) n -> p kt n", p=128)


def load_rope(k, st, rope_dram, R):
    cc = carve(st, 88 * 1024, [128, NT], F32)
    ss = carve(st, 92 * 1024, [128, NT], F32)
    st.rope_t = T("rope")
    st.rope_t.w = st.big_t.w
    st.rope_t.r = dict(st.big_t.r)
    ds = DmaSem(k, f"rope{st.uid()}")
    k.sp.dma(cc[0:R, :], rope_dram[0], ds, writes=[st.rope_t])
    k.sp.dma(ss[0:R, :], rope_dram[1], ds, writes=[st.rope_t])
    return cc, ss


def proj_even(k, st, wp, sg, W, j, gi, rope_b, QT, KT_, V):
    nc = k.nc
    sg.sync(st)
    hn, hn_t = hn_all(k, st, gi)
    cc, ss = load_rope(k, st, rope_b, 64)
    cq = carve(st, 32 * 1024, [128, 4, NT], F32)
    ckv = carve(st, 48 * 1024, [128, 2, NT], F32)
    cqn = carve(st, 56 * 1024, [128, 4, NT], BF16)
    ckvn = carve(st, 64 * 1024, [128, 2, NT], BF16)
    tmp1 = carve(st, 68 * 1024, [128, TH], F32)
    tmp2 = carve(st, 70 * 1024, [128, TH], F32)
    tmp_t = [T(), T()]
    cq_t = [T() for _ in range(4)]
    ckv_t = [T() for _ in range(2)]
    cqn_t = [T() for _ in range(4)]
    ckvn_t = [T() for _ in range(2)]
    for t_ in tmp_t + cq_t + ckv_t + cqn_t + ckvn_t:
        t_.w = st.big_t.w
        t_.r = dict(st.big_t.r)
    pbank = [0]
    w_in = wview(W("ab_w_in", j), KT)
    rhs_of = lambda kt, th: hn[:, kt, th * TH:(th + 1) * TH]
    for grp, dst in ((0, QT), (1, KT_)):
        for blk in range(2):
            c0 = grp * 1024 + blk * 512
            wv, wb_t = load_w_slot(k, wp, None, KT, 512, [(0, w_in[:, :, c0:c0 + 512])])
            for c in range(4):
                r0 = blk * 512 + c * 128
                fm_plain(k, st, sg, wv, wb_t, KT, c * 128, 128, rhs_of, hn_t, dst[r0:r0 + 128, :], pbank)
    for blk in range(2):
        c0 = 2048 + blk * 512
        wv, wb_t = load_w_slot(k, wp, None, KT, 512, [(0, w_in[:, :, c0:c0 + 512])])
        tm_proj(k, st, sg, wv, wb_t, KT, 512, lambda kt, tt: hn[:, kt, tt * 128:(tt + 1) * 128], hn_t,
                V[:, blk * 512:(blk + 1) * 512], pbank)
    wv, wb_t = load_w_slot(k, wp, None, KT, 512, [(0, w_in[:, :, 3072:3584])])
    for c in range(4):
        for th in range(2):
            b = pbank[0]
            pbank[0] = (pbank[0] + 1) % 4
            mm_group(k, st, b, [wv[:, kt, c * 128:(c + 1) * 128] for kt in range(KT)], [rhs_of(kt, th) for kt in range(KT)],
                     [[wb_t, hn_t[kt]] for kt in range(KT)])
            k.act.op(lambda b=b, c=c, th=th: nc.scalar.copy(out=cq[:, c, th * TH:(th + 1) * TH], in_=st.bank[b][:]),
                     reads=[st.bank_t[b]], writes=[cq_t[c]])
    krp = wview(W("ab_w_in_krp", j), KT)
    wv, wb_t = load_w_slot(k, wp, None, KT, 512, [(0, w_in[:, :, 3584:3904]), (320, krp)])
    for c in range(2):
        for th in range(2):
            b = pbank[0]
            pbank[0] = (pbank[0] + 1) % 4
            mm_group(k, st, b, [wv[:, kt, c * 128:(c + 1) * 128] for kt in range(KT)], [rhs_of(kt, th) for kt in range(KT)],
                     [[wb_t, hn_t[kt]] for kt in range(KT)])
            k.act.op(lambda b=b, c=c, th=th: nc.scalar.copy(out=ckv[:, c, th * TH:(th + 1) * TH], in_=st.bank[b][:]),
                     reads=[st.bank_t[b]], writes=[ckv_t[c]])
    fm_rope(k, st, sg, wv, wb_t, KT, 256, 64, wv, wb_t, 320, 64, rhs_of, hn_t, KT_[2048:2112, :], pbank, cc, ss, tmp1, tmp2, tmp_t)
    for (src, src_t, dstn, dstn_t, n, gidx, nfeat) in ((cq, cq_t, cqn, cqn_t, 4, 8, 512), (ckv, ckv_t, ckvn, ckvn_t, 2, 9, 256)):
        for th in range(2):
            tsl = slice(th * TH, (th + 1) * TH)
            for c in range(n):
                rms_stats_accum(k, st, src[:, c, tsl], src_t[c], 6, c == 0, c == n - 1)
            rstd_from_bank(k, st, 6, nfeat)
            for c in range(n):
                k.dve.op(lambda c=c, tsl=tsl, src=src, dstn=dstn, gidx=gidx: nc.vector.scalar_tensor_tensor(
                    out=dstn[:, c, tsl], in0=src[:, c, tsl], scalar=st.gains[:, gidx, c:c + 1], in1=st.rstd[:],
                    op0=ALU.mult, op1=ALU.mult),
                    reads=[src_t[c], st.rstd_t, st.gains_t], writes=[dstn_t[c]])
    qup = wview(W("b_w_qup", j), 4)
    qrp = wview(W("b_w_qup_rp", j), 4)
    wvq, wbq_t = load_w_slot(k, wp, None, 4, 2048, [(0, qup), (1536, qrp)])
    rhs_q = lambda kt, th: cqn[:, kt, th * TH:(th + 1) * TH]
    for hh in range(8):
        fm_plain(k, st, sg, wvq, wbq_t, 4, hh * 192, 128, rhs_q, cqn_t, QT[1024 + hh * 128:1024 + (hh + 1) * 128, :], pbank)
        fm_rope(k, st, sg, wvq, wbq_t, 4, hh * 192 + 128, 64, wvq, wbq_t, 1536 + hh * 64, 64, rhs_q, cqn_t,
                QT[2048 + hh * 64:2048 + (hh + 1) * 64, :], pbank, cc, ss, tmp1, tmp2, tmp_t)
    kvup = wview(W("b_w_kvup", j), 2)
    wvk, wbk_t = load_w_slot(k, wp, None, 2, 2048, [(0, kvup)])
    rhs_k = lambda kt, th: ckvn[:, kt, th * TH:(th + 1) * TH]
    for hh in range(8):
        fm_plain(k, st, sg, wvk, wbk_t, 2, hh * 256, 128, rhs_k, ckvn_t, KT_[1024 + hh * 128:1024 + (hh + 1) * 128, :], pbank)
    for half in range(2):
        tm_proj(k, st, sg, wvk, wbk_t, 2, 512, lambda kt, tt: ckvn[:, kt, tt * 128:(tt + 1) * 128], ckvn_t,
                V[:, 1024 + half * 512:1024 + (half + 1) * 512], pbank,
                col_of=[(i * 128, (half * 4 + i) * 256 + 128, 128) for i in range(4)])
    merge_big(st, hn_t + tmp_t + cq_t + ckv_t + cqn_t + ckvn_t + [st.rope_t] + sg.ts)


def proj_odd(k, st, wp, sg, W, j, gi, rope_c, QT, KT_, V):
    nc = k.nc
    sg.sync(st)
    hn, hn_t = hn_all(k, st, gi)
    cc, ss = load_rope(k, st, rope_c, 32)
    tmp1 = carve(st, 68 * 1024, [128, TH], F32)
    tmp2 = carve(st, 70 * 1024, [128, TH], F32)
    tmp_t = [T(), T()]
    for t_ in tmp_t:
        t_.w = st.big_t.w
        t_.r = dict(st.big_t.r)
    pbank = [0]
    w_in = wview(W("c_w_in", j), KT)
    w_rp = wview(W("c_w_in_rp", j), KT)
    rhs_of = lambda kt, th: hn[:, kt, th * TH:(th + 1) * TH]
    for hh in range(8):
        wv, wb_t = load_w_slot(k, wp, None, KT, 512, [(0, w_in[:, :, hh * 768:hh * 768 + 512])])
        wvr, wbr_t = load_w_slot(k, wp, None, KT, 512, [(0, w_rp[:, :, hh * 128:(hh + 1) * 128]),
                                                        (128, w_in[:, :, hh * 768 + 512:hh * 768 + 768])])
        for which in range(4):
            dst = (QT if which < 2 else KT_)[hh * 256 + (which % 2) * 128: hh * 256 + (which % 2) * 128 + 128, :]
            fm_rope(k, st, sg, wv, wb_t, KT, which * 128, 128, wvr, wbr_t, which * 32, 32, rhs_of, hn_t, dst, pbank,
                    cc, ss, tmp1, tmp2, tmp_t)
        tm_proj(k, st, sg, wvr, wbr_t, KT, 256, lambda kt, tt: hn[:, kt, tt * 128:(tt + 1) * 128], hn_t,
                V[:, hh * 256:(hh + 1) * 256], pbank, col_of=[(0, 128, 256)])
    merge_big(st, hn_t + tmp_t + [st.rope_t] + sg.ts)


class RowChunks:
    def __init__(self, chunks):
        self.chunks = chunks

    def __getitem__(self, key):
        rs, cs = key
        for (r0, n, ap) in self.chunks:
            if r0 <= rs.start and rs.stop <= r0 + n:
                return ap[rs.start - r0:rs.stop - r0, cs]
        raise IndexError(f"rows {rs} straddle chunks")


class VView:
    def __init__(self, chunk_aps, c0=0, c1=2048):
        self.v = [ap.rearrange("(t a) c -> t (a c)", a=2) for ap in chunk_aps]
        self.c0, self.c1 = c0, c1
        self.raw = chunk_aps

    def __getitem__(self, key):
        rs, cs = key
        if rs == slice(None):
            nv = VView(self.raw, self.c0 + cs.start, self.c0 + cs.stop)
            return nv
        ch = rs.start // 512
        assert (rs.stop - 1) // 512 == ch
        cc0 = self.c0 + (cs.start or 0) if cs != slice(None) else self.c0
        cc1 = self.c0 + cs.stop if cs != slice(None) else self.c1
        return self.v[ch][rs.start - ch * 512:rs.stop - ch * 512, cc0:cc1]

    def half(self, hf):
        return self.v[hf][:, self.c0:self.c1]


class AttnCtx:
    def __init__(self, k, st):
        self.k = k
        self.st = st
        self.ao = carve(st, 0, [128, KT, NT], BF16)
        self.ao_t = [T() for _ in range(KT)]
        self.sets = []
        for s_ in range(2):
            base = 32 * 1024 + s_ * 24 * 1024
            d = dict(
                q=[carve(st, base + i * 2048, [128, NT], BF16) for i in range(2)],
                ko=[carve(st, base + 4096 + i * 2048, [128, NT], BF16) for i in range(2)],
                kr=[carve(st, base + 8192 + i * 2048, [128, NT], BF16) for i in range(2)],
                vo=carve(st, base + 12288, [128, 8, 256], BF16),
                vr=carve(st, base + 16384, [128, 8, 256], BF16),
                t=T(), ds=DmaSem(k, f"hs{st.uid()}"))
            self.sets.append(d)
        self.pt = [carve(st, 80 * 1024 + i * 256, [128, 128], BF16) for i in range(8)]
        self.pt_t = [T() for _ in range(8)]
        self.pt_i = 0
        self.dg_i = 0
        self.bias = [carve(st, 82 * 1024 + i * 1536, [128, 3, 128], F32) for i in range(2)]
        self.bias_t = [T(), T()]
        self.bias_ds = [DmaSem(k, f"bs{st.uid()}") for _ in range(2)]
        self.rinv = [carve(st, 86 * 1024 + i * 512, [128, 128], F32) for i in range(2)]
        self.rinv_t = [T(), T()]
        self.o1n = carve(st, 87 * 1024, [128, 256], F32)
        self.o2n = carve(st, 88 * 1024, [128, 256], F32)
        self.dd = carve(st, 89 * 1024, [128, 256], F32)
        self.sq2 = carve(st, 90 * 1024, [128, 256], F32)
        self.stmp = carve(st, 91 * 1024, [128, 128], F32)
        self.tmp_t = [T() for _ in range(5)]
        self.sslot_t = [T() for _ in range(8)]
        self.ss_i = 0
        every = self.ao_t + [d["t"] for d in self.sets] + self.pt_t + self.bias_t + self.rinv_t + self.tmp_t
        for t_ in every:
            t_.w = st.big_t.w
            t_.r = dict(st.big_t.r)
        self.every = every
        for i in (6, 7):
            k.dve.op(lambda i=i: k.nc.vector.memset(self.pt[i][:], 0.0), writes=[self.pt_t[i]])

    def done(self):
        merge_big(self.st, self.every)


def attn_tile(k, st, ax, hs, i, streams, blocks, dv, finalize, lookahead=2):
    nc = k.nc
    qsl = slice(i * 128, (i + 1) * 128)
    nb = len(blocks)

    def stage_a(bi, blk, si, stream):
        is_rem, j, kind, bias_ap, bt = blk
        parts, scale = stream
        ksl = slice(j * 128, (j + 1) * 128)
        s_i = (0, 1, 7)[ax.ss_i]
        ax.ss_i = (ax.ss_i + 1) % 3
        S = st.bank[s_i][:, 0:128]
        S_t = st.bank_t[s_i]
        for pi, (qi, ki, Kp) in enumerate(parts):
            kt_ = (hs["kr"] if is_rem else hs["ko"])[ki]
            k.pe.op(lambda kt_=kt_, qi=qi, Kp=Kp, pi=pi: nc.tensor.matmul(
                S, kt_[0:Kp, ksl], hs["q"][qi][0:Kp, qsl], start=(pi == 0), stop=(pi == len(parts) - 1)),
                reads=[hs["t"]], writes=[S_t], inc=(pi == len(parts) - 1))
        if kind == "diag":
            p_i = 6 + ax.dg_i
            ax.dg_i ^= 1
        else:
            p_i = ax.pt_i
            ax.pt_i = (ax.pt_i + 1) % 6
        PT, PT_t = ax.pt[p_i], ax.pt_t[p_i]
        if kind == "tile":
            k.dve.op(lambda: nc.vector.scalar_tensor_tensor(
                out=ax.stmp[:], in0=S, scalar=scale, in1=ax.cur_bias[:, bt, :], op0=ALU.mult, op1=ALU.add),
                reads=[S_t, ax.cur_bias_t], writes=[ax.tmp_t[4]])
            k.act.op(lambda: nc.scalar.activation(out=PT[:], in_=ax.stmp[:], func=AF.Exp, bias=bias_ap, scale=1.0),
                     reads=[ax.tmp_t[4], st.cst_t], writes=[PT_t])
        elif kind == "diag":
            k.act.op(lambda: nc.scalar.activation(
                out=PT[0:64, :], in_=S[0:64, :], func=AF.Exp, bias=bias_ap[0:64, :], scale=scale),
                reads=[S_t, st.cst_t], writes=[PT_t])
            k.act.op(lambda: nc.scalar.activation(
                out=PT[64:128, 64:128], in_=S[64:128, 64:128], func=AF.Exp, bias=bias_ap[64:128, :], scale=scale),
                reads=[S_t, st.cst_t], writes=[PT_t])
        else:
            k.act.op(lambda: nc.scalar.activation(out=PT[:], in_=S, func=AF.Exp, bias=bias_ap, scale=scale),
                     reads=[S_t, st.cst_t], writes=[PT_t])
        return (bi, is_rem, j, si, PT, PT_t)

    def stage_b(ctx):
        bi, is_rem, j, si, PT, PT_t = ctx
        vt = hs["vr"] if is_rem else hs["vo"]
        ob, sb2 = 2 + 2 * si, 3 + 2 * si
        for c in range(dv // 128):
            k.pe.op(lambda c=c: nc.tensor.matmul(
                st.bank[ob][:, c * 128:(c + 1) * 128], vt[:, j, c * 128:(c + 1) * 128], PT[:],
                start=(bi == 0 and c == 0), stop=(bi == nb - 1)),
                reads=[hs["t"], PT_t], writes=[st.bank_t[ob]], inc=False)
        k.pe.op(lambda: nc.tensor.matmul(
            st.bank[sb2][:, 0:128], st.ones_bf[:], PT[:], start=(bi == 0), stop=(bi == nb - 1)),
            reads=[PT_t, st.ones_t], writes=[st.bank_t[sb2]], inc=True)

    pend = []
    for bi, blk in enumerate(blocks):
        for si, stream in enumerate(streams):
            pend.append(stage_a(bi, blk, si, stream))
            if len(pend) > lookahead:
                stage_b(pend.pop(0))
    while pend:
        stage_b(pend.pop(0))
    finalize(i)


def load_head(k, ax, hs, qsrc, kosrc, krsrc, vosrc, vrsrc, dv):
    for idx, (ap, R) in enumerate(qsrc):
        k.sp.dma(hs["q"][idx][0:R, :], ap, hs["ds"], writes=[hs["t"]])
    for idx, (ap, R) in enumerate(kosrc):
        k.sp.dma(hs["ko"][idx][0:R, :], ap, hs["ds"], writes=[hs["t"]])
    for idx, (ap, R) in enumerate(krsrc):
        k.sp.dma(hs["kr"][idx][0:R, :], ap, hs["ds"], writes=[hs["t"]])
    for hf in range(2):
        k.sp.dma(hs["vo"][:, hf * 4:(hf + 1) * 4, 0:dv], vosrc.half(hf).rearrange("(j p) d -> p j d", p=128), hs["ds"], writes=[hs["t"]])
        k.sp.dma(hs["vr"][:, hf * 4:(hf + 1) * 4, 0:dv], vrsrc.half(hf).rearrange("(j p) d -> p j d", p=128), hs["ds"], writes=[hs["t"]])


def fin_simple(k, st, ax, chunk):
    nc = k.nc

    def f(i):
        qsl = slice(i * 128, (i + 1) * 128)
        r, r_t = ax.rinv[0], ax.rinv_t[0]
        k.dve.op(lambda: nc.vector.reciprocal(out=r[:], in_=st.bank[3][:, 0:128]), reads=[st.bank_t[3]], writes=[r_t])
        k.dve.op(lambda: nc.vector.tensor_tensor(out=ax.ao[:, chunk, qsl], in0=st.bank[2][:, 0:128], in1=r[:], op=ALU.mult),
                 reads=[st.bank_t[2], r_t], writes=[ax.ao_t[chunk]])
    return f


def attn_even(k, st, sg, j, QT, KTo, Vo, KTr, Vr, abias, acv):
    nc = k.nc
    ax = AttnCtx(k, st)
    cb = st.cb
    ds = DmaSem(k, f"cb{st.uid()}")
    k.sp.dma(cb[:, 0:8], acv[j].partition_broadcast(128), ds, writes=[st.cst_t])
    k.dve.op(lambda: nc.vector.tensor_scalar(out=cb[:, 8:16], in0=cb[:, 0:8], scalar1=st.rb[:, 0:1], scalar2=None, op0=ALU.add),
             reads=[st.cst_t], writes=[st.cst_t])
    sA = 128 ** -0.5
    for hh in range(8):
        hs = ax.sets[hh % 2]
        load_head(k, ax, hs, [(QT[hh * 128:(hh + 1) * 128, :], 128)], [(KTo[hh * 128:(hh + 1) * 128, :], 128)],
                  [(KTr[hh * 128:(hh + 1) * 128, :], 128)], Vo[:, hh * 128:(hh + 1) * 128], Vr[:, hh * 128:(hh + 1) * 128], 128)
        bt, bt_t, bt_ds = ax.bias[hh % 2], ax.bias_t[hh % 2], ax.bias_ds[hh % 2]
        k.sp.dma(bt, abias[j, hh].rearrange("t k q -> k t q"), bt_ds, writes=[bt_t])
        ax.cur_bias, ax.cur_bias_t = bt, bt_t
        for i in range(8):
            blocks = []
            for d_ in range(4, -1, -1):
                jg = i - d_
                rem = jg < 0
                jj = jg + 8 if rem else jg
                if d_ in (2, 3):
                    blocks.append((rem, jj, "plain", cb[:, (8 if rem else 0) + hh:(8 if rem else 0) + hh + 1], None))
                else:
                    tix = {4: 0, 1: 1, 0: 2}[d_]
                    blocks.append((rem, jj, "tile", (st.rb if rem else st.zb)[:, 0:1], tix))
            attn_tile(k, st, ax, hs, i, [([(0, 0, 128)], sA)], blocks, 128, fin_simple(k, st, ax, hh))
    sB = 192 ** -0.5
    for hh in range(8):
        hs = ax.sets[hh % 2]
        load_head(k, ax, hs,
                  [(QT[1024 + hh * 128:1024 + (hh + 1) * 128, :], 128), (QT[2048 + hh * 64:2048 + (hh + 1) * 64, :], 64)],
                  [(KTo[1024 + hh * 128:1024 + (hh + 1) * 128, :], 128), (KTo[2048:2112, :], 64)],
                  [(KTr[1024 + hh * 128:1024 + (hh + 1) * 128, :], 128), (KTr[2048:2112, :], 64)],
                  Vo[:, 1024 + hh * 128:1024 + (hh + 1) * 128], Vr[:, 1024 + hh * 128:1024 + (hh + 1) * 128], 128)
        for i in range(8):
            blocks = [(True, jj, "plain", st.rb[:, 0:1], None) for jj in range(8)]
            blocks += [(False, jj, "diag" if jj == i else "plain", st.zb[:, 0:1], None) for jj in range(i + 1)]
            attn_tile(k, st, ax, hs, i, [([(0, 0, 128), (1, 1, 64)], sB)], blocks, 128, fin_simple(k, st, ax, 8 + hh))
    return ax


def attn_odd(k, st, sg, j, layer, QT, KTo, Vo, KTr, Vr, lvec, gsub):
    nc = k.nc
    ax = AttnCtx(k, st)
    lam_init = 0.8 - 0.6 * math.exp(-0.3 * layer)
    lv = st.lv
    ds = DmaSem(k, f"lv{st.uid()}")
    for q in range(4):
        k.sp.dma(lv[:, q:q + 1], lvec[q][j].rearrange("(p o) -> p o", o=1), ds, writes=[st.cst_t])
    k.sp.dma(lv[:, 8:10], gsub[j].rearrange("(c p) -> p c", p=128), ds, writes=[st.cst_t], allow_slow_non_contiguous=True)
    k.dve.op(lambda: nc.vector.tensor_tensor(out=lv[:, 4:5], in0=lv[:, 0:1], in1=lv[:, 1:2], op=ALU.mult), reads=[st.cst_t], writes=[st.cst_t])
    k.dve.op(lambda: nc.vector.tensor_tensor(out=lv[:, 5:6], in0=lv[:, 2:3], in1=lv[:, 3:4], op=ALU.mult), reads=[st.cst_t], writes=[st.cst_t])
    k.pe.op(lambda: nc.tensor.matmul(st.bank[6][:, 0:2], st.ones_f[:], lv[:, 4:6], start=True, stop=True),
            reads=[st.cst_t, st.ones_t], writes=[st.bank_t[6]], inc=True)
    k.act.op(lambda: nc.scalar.activation(out=lv[:, 6:8], in_=st.bank[6][:, 0:2], func=AF.Exp), reads=[st.bank_t[6]], writes=[st.cst_t])
    k.dve.op(lambda: nc.vector.tensor_tensor(out=lv[:, 4:5], in0=lv[:, 7:8], in1=lv[:, 6:7], op=ALU.subtract), reads=[st.cst_t], writes=[st.cst_t])
    k.dve.op(lambda: nc.vector.tensor_scalar(out=lv[:, 4:5], in0=lv[:, 4:5], scalar1=-lam_init, scalar2=None, op0=ALU.add),
             reads=[st.cst_t], writes=[st.cst_t])
    k.dve.op(lambda: nc.vector.tensor_scalar(out=lv[:, 8:10], in0=lv[:, 8:10], scalar1=1.0 - lam_init, scalar2=None, op0=ALU.mult),
             reads=[st.cst_t], writes=[st.cst_t])
    sC = 128 ** -0.5

    def fin(hh):
        def f(i):
            qsl = slice(i * 128, (i + 1) * 128)
            for si, (dst, dst_t) in enumerate(((ax.o1n, ax.tmp_t[0]), (ax.o2n, ax.tmp_t[1]))):
                r, r_t = ax.rinv[si], ax.rinv_t[si]
                k.dve.op(lambda r=r, si=si: nc.vector.reciprocal(out=r[:], in_=st.bank[3 + 2 * si][:, 0:128]),
                         reads=[st.bank_t[3 + 2 * si]], writes=[r_t])
                for c in range(2):
                    k.dve.op(lambda r=r, si=si, c=c, dst=dst: nc.vector.tensor_tensor(
                        out=dst[:, c * 128:(c + 1) * 128], in0=st.bank[2 + 2 * si][:, c * 128:(c + 1) * 128], in1=r[:], op=ALU.mult),
                        reads=[st.bank_t[2 + 2 * si], r_t], writes=[dst_t])
            k.dve.op(lambda: nc.vector.scalar_tensor_tensor(out=ax.dd[:], in0=ax.o2n[:], scalar=lv[:, 4:5], in1=ax.o1n[:],
                                                           op0=ALU.mult, op1=ALU.add),
                     reads=[ax.tmp_t[0], ax.tmp_t[1], st.cst_t], writes=[ax.tmp_t[2]])
            k.act.op(lambda: nc.scalar.activation(out=ax.sq2[:], in_=ax.dd[:], func=AF.Square), reads=[ax.tmp_t[2]], writes=[ax.tmp_t[3]])
            for c in range(2):
                k.pe.op(lambda c=c: nc.tensor.matmul(st.bank[6][:, 0:128], st.ones_f[:], ax.sq2[:, c * 128:(c + 1) * 128],
                                                     start=(c == 0), stop=(c == 1)),
                        reads=[ax.tmp_t[3], st.ones_t], writes=[st.bank_t[6]], inc=(c == 1))
            k.dve.op(lambda: nc.vector.tensor_scalar(out=ax.stmp[:], in0=st.bank[6][:, 0:128], scalar1=1.0 / 256, scalar2=EPS,
                                                    op0=ALU.mult, op1=ALU.add), reads=[st.bank_t[6]], writes=[ax.tmp_t[4]])
            k.act.op(lambda: nc.scalar.activation(out=ax.stmp[:], in_=ax.stmp[:], func=AF.Sqrt), reads=[ax.tmp_t[4]], writes=[ax.tmp_t[4]])
            k.dve.op(lambda: nc.vector.reciprocal(out=ax.stmp[:], in_=ax.stmp[:]), reads=[ax.tmp_t[4]], writes=[ax.tmp_t[4]])
            for c in range(2):
                k.dve.op(lambda c=c: nc.vector.scalar_tensor_tensor(
                    out=ax.ao[:, hh * 2 + c, qsl], in0=ax.dd[:, c * 128:(c + 1) * 128], scalar=lv[:, 8 + c:9 + c], in1=ax.stmp[:],
                    op0=ALU.mult, op1=ALU.mult),
                    reads=[ax.tmp_t[2], ax.tmp_t[4], st.cst_t], writes=[ax.ao_t[hh * 2 + c]])
        return f

    for hh in range(8):
        hs = ax.sets[hh % 2]
        r0 = hh * 256
        load_head(k, ax, hs, [(QT[r0:r0 + 128, :], 128), (QT[r0 + 128:r0 + 256, :], 128)],
                  [(KTo[r0:r0 + 128, :], 128), (KTo[r0 + 128:r0 + 256, :], 128)],
                  [(KTr[r0:r0 + 128, :], 128), (KTr[r0 + 128:r0 + 256, :], 128)],
                  Vo[:, r0:r0 + 256], Vr[:, r0:r0 + 256], 256)
        for i in range(8):
            blocks = [(True, jj, "plain", st.rb[:, 0:1], None) for jj in range(8)]
            blocks += [(False, jj, "diag" if jj == i else "plain", st.zb[:, 0:1], None) for jj in range(i + 1)]
            attn_tile(k, st, ax, hs, i, [([(0, 0, 128)], sC), ([(1, 1, 128)], sC)], blocks, 256, fin(hh))
    return ax


def mix_out(k, st, wp, ax, w_out, gi):
    for th in range(2):
        y = carve(st, 60 * 1024, [128, KT, TH], F32)
        y_t = [T() for _ in range(KT)]
        for t_ in y_t:
            t_.w = st.big_t.w
            t_.r = dict(st.big_t.r)
            for e_ in ax.every:
                for tok in list(e_.r.values()) + ([e_.w] if e_.w else []):
                    old = t_.r.get(tok[0])
                    if old is None or (old[1], old[2]) < (tok[1], tok[2]):
                        t_.r[tok[0]] = tok
        outproj_half(k, st, wp, w_out, KT, lambda kc, th=th: ax.ao[:, kc, th * TH:(th + 1) * TH], ax.ao_t, y, y_t)
        tail_half(k, st, th, gi, y, y_t)
        ax.every = ax.every + y_t
    ax.done()


NCORES = 8
QROWS = 2560
KROWS = 2112
KVROWS = KROWS + 2048


def make_state(k):
    st = State(k)
    st.big_t = T("big")
    st._uid = [0]

    def uid():
        st._uid[0] += 1
        return st._uid[0]
    st.uid = uid
    st.cb = k.sb("cb", [128, 16], F32)
    st.rb = k.sb("rb", [128, 1], F32)
    st.zb = k.sb("zb", [128, 1], F32)
    st.lv = k.sb("lv", [128, 16], F32)
    st.cst_t = T("cst")
    k.dve.op(lambda: k.nc.vector.memset(st.zb[:], 0.0), writes=[st.cst_t])
    return st


class Weights:
    def __init__(self, k, specs, gather=True):
        self.k = k
        self.full = {}
        self.t = {}
        nc = k.nc
        for name, (R, C) in specs.items():
            if not gather:
                self.full[name] = k.din("w_" + name, [R, C])
                self.t[name] = T()
                continue
            sh = k.din("w_" + name, [R // NCORES, C])
            bounce = nc.dram_tensor("wb_" + name, [R // NCORES, C], F32)
            full = nc.dram_tensor("wf_" + name, [R, C], F32)
            ds = DmaSem(k, "wb_" + name)
            rows = R // NCORES
            step = max(1, (1 << 18) // C)
            for r0 in range(0, rows, step):
                r1 = min(rows, r0 + step)
                k.pool.dma(bounce.ap()[r0:r1, :], sh[r0:r1, :], ds)
            k.pool.e.wait_ge(ds.sem, ds.n)
            sem = k.newsem("cc_" + name)
            nc.gpsimd.collective_compute("AllGather", ALU.bypass, replica_groups=[list(range(NCORES))],
                                         ins=[bounce.ap().opt()], outs=[full.ap().opt()]).then_inc(sem)
            t_ = T()
            t_.w = ("cc_" + name, 0, 1, sem)
            self.full[name] = full.ap()
            self.t[name] = t_

    def get(self, name):
        self.k.pool._need([self.t[name].w])
        return self.full[name]


def build_part1(layer, gather=False):
    even = layer % 2 == 0
    k = K()
    hT = k.din("hT", [D, NT])
    gv = k.din("gvec", [3, D])
    out = k.dout("hT_out", [D, NT])
    qt = k.dout("qt_out", [QROWS, NT], BF16)
    kv = k.dout("kv_out", [KVROWS, NT], BF16)
    specs = {"ffn_w_in": (D, 2 * DFF), "ffn_w_out": (DFF, D)}
    if even:
        specs.update({"ab_w_in": (D, 3904), "ab_w_in_krp": (D, 64), "b_w_qup": (512, 1536), "b_w_qup_rp": (512, 512),
                      "b_w_kvup": (256, 2048)})
        rope = k.din("rope", [2, 64, NT])
        gq = k.din("g_q", [512])
        gkv = k.din("g_kv", [256])
    else:
        specs.update({"c_w_in": (D, 6144), "c_w_in_rp": (D, 1024)})
        rope = k.din("rope", [2, 32, NT])
    st = make_state(k)
    W = Weights(k, specs, gather)
    wp = WPool(k, st)
    wp.big16()
    sg = Stager(k, st)
    load_gains(k, st, [gv[0], gv[1], gv[2]], [1.0, 0.5, 1.0])
    if even:
        k.sp.dma(st.gains[:, 8, 0:4], gq.rearrange("(c p) -> p c", p=128), st.misc_ds, writes=[st.gains_t], allow_slow_non_contiguous=True)
        k.sp.dma(st.gains[:, 9, 0:2], gkv.rearrange("(c p) -> p c", p=128), st.misc_ds, writes=[st.gains_t], allow_slow_non_contiguous=True)
    load_h(k, st, hT)
    ffn_full(k, st, wp, W.get("ffn_w_in"), W.get("ffn_w_out"), 0, 1)
    wp.big16()
    KT_ = RowChunks([(0, KROWS, kv[0:KROWS, :])])
    V = VView([kv[KROWS:KROWS + 1024, :], kv[KROWS + 1024:KVROWS, :]])
    Wf = lambda name, j: W.get(name)
    if even:
        proj_even(k, st, wp, sg, Wf, 0, 2, rope, qt, KT_, V)
    else:
        proj_odd(k, st, wp, sg, Wf, 0, 2, rope, qt, KT_, V)
    store_h(k, st, out)
    for ds in sg.ds:
        k.outsems.append(ds)
    return k.finish()


def build_part2(layer, gather=False):
    even = layer % 2 == 0
    k = K()
    hT = k.din("hT", [D, NT])
    gv = k.din("gvec", [5, D])
    qt = k.din("qt", [QROWS, NT], BF16)
    kvo = k.din("kv_own", [KVROWS, NT], BF16)
    kvr = k.din("kv_rem", [KVROWS, NT], BF16)
    rbias = k.din("rbias", [128, 1])
    pT = k.din("pT", [256, NT])
    out = k.dout("hT_out", [D, NT])
    specs = {"mix_w_out": (D, D), "ffn_w_in": (D, 2 * DFF), "ffn_w_out": (DFF, D), "ple_w_gate": (D, D), "ple_w_proj": (256, D)}
    if even:
        abias = k.din("abias", [1, 8, 3, 128, 128])
        acv = k.din("acv", [1, 8])
    else:
        lvec = [k.din(f"lvec{q}", [1, 128]) for q in range(4)]
        gsub = k.din("gsub", [1, 256])
    st = make_state(k)
    W = Weights(k, specs, gather)
    wp = WPool(k, st)
    wp.big16()
    k.sp.dma(st.rb[:], rbias, st.misc_ds, writes=[st.cst_t])
    load_gains(k, st, [gv[0], gv[1], gv[2], gv[3], gv[4]], [1.0, 1.0, 0.5, 1.0, 1.0])
    load_h(k, st, hT)
    KTo, KTr = RowChunks([(0, KROWS, kvo[0:KROWS, :])]), RowChunks([(0, KROWS, kvr[0:KROWS, :])])
    Vo = VView([kvo[KROWS:KROWS + 1024, :], kvo[KROWS + 1024:KVROWS, :]])
    Vr = VView([kvr[KROWS:KROWS + 1024, :], kvr[KROWS + 1024:KVROWS, :]])
    if even:
        ax = attn_even(k, st, None, 0, qt, KTo, Vo, KTr, Vr, abias, acv)
    else:
        ax = attn_odd(k, st, None, 0, layer, qt, KTo, Vo, KTr, Vr, lvec, gsub)
    mix_out(k, st, wp, ax, W.get("mix_w_out"), 0)
    ffn_full(k, st, wp, W.get("ffn_w_in"), W.get("ffn_w_out"), 1, 2)
    wp.big16()
    for th in range(2):
        ple_half(k, st, wp, th, W.get("ple_w_gate"), W.get("ple_w_proj"), pT, 3, 4)
    store_h(k, st, out)
    return k.finish()


W_SHAPES = {
    "ffn1_w_in": (4, D, 2 * DFF), "ffn1_w_out": (4, DFF, D), "ffn2_w_in": (4, D, 2 * DFF), "ffn2_w_out": (4, DFF, D),
    "ab_w_in": (2, D, 3904), "ab_w_in_krp": (2, D, 64), "b_w_qup": (2, 512, 1536), "b_w_qup_rp": (2, 512, 512),
    "b_w_kvup": (2, 256, 2048), "ab_w_out": (2, D, D), "c_w_in": (2, D, 6144), "c_w_in_rp": (2, D, 1024),
    "c_w_out": (2, D, D), "ple_w_gate": (4, D, D), "ple_w_proj": (4, 256, D),
}


def build_fused(nlayers=4, ncores=NCORES):
    k = K()
    nc = k.nc
    hT = k.din("hT", [D, NT])
    out = k.dout("hT_out", [D, NT])
    gv = k.din("gvec", [4, 8, D])
    gq = k.din("g_q", [2, 512])
    gkv = k.din("g_kv", [2, 256])
    rope_b = k.din("rope_b", [2, 64, NT])
    rope_c = k.din("rope_c", [2, 32, NT])
    rbias = k.din("rbias", [128, 1])
    pT = k.din("pT", [4, 256, NT])
    abias = k.din("abias", [2, 8, 3, 128, 128])
    acv = k.din("acv", [2, 8])
    lvec = [k.din(f"lvec{q}", [2, 128]) for q in range(4)]
    gsub = k.din("gsub", [2, 256])
    Wd = {n: k.din("w_" + n, list(shp)) for n, shp in W_SHAPES.items()}
    W = lambda name, j: Wd[name][j]
    st = make_state(k)
    wp = WPool(k, st)
    wp.big16()
    sg = Stager(k, st)
    k.sp.dma(st.rb[:], rbias, DmaSem(k, "rb"), writes=[st.cst_t])
    load_h(k, st, hT)
    for layer in range(nlayers):
        even = layer % 2 == 0
        j = layer // 2
        gds = DmaSem(k, f"g{layer}")
        for i in range(8):
            k.sp.dma(st.gains[:, i, :], gv[layer, i].rearrange("(kt p) -> p kt", p=128), gds,
                     writes=[st.gains_t], allow_slow_non_contiguous=True)
        if even:
            k.sp.dma(st.gains[:, 8, 0:4], gq[j].rearrange("(c p) -> p c", p=128), gds, writes=[st.gains_t], allow_slow_non_contiguous=True)
            k.sp.dma(st.gains[:, 9, 0:2], gkv[j].rearrange("(c p) -> p c", p=128), gds, writes=[st.gains_t], allow_slow_non_contiguous=True)
        for i in (1, 5):
            k.dve.op(lambda i=i: nc.vector.tensor_scalar(out=st.gains[:, i, :], in0=st.gains[:, i, :], scalar1=0.5, scalar2=None,
                                                         op0=ALU.mult), reads=[st.gains_t], writes=[st.gains_t])
        ffn_full(k, st, wp, W("ffn1_w_in", layer), W("ffn1_w_out", layer), 0, 1)
        wp.big16()
        qts = k.dint(f"qt{layer}", [QROWS, NT], BF16)
        csz = [1024, 1024, 1024, 1024] + ([64] if even else [])
        own_c = [k.dint(f"kvown{layer}_{ci}", [n_, NT], BF16) for ci, n_ in enumerate(csz)]
        pair_c = [k.dint(f"kvpair{layer}_{ci}", [2 * n_, NT], BF16) for ci, n_ in enumerate(csz)]
        kt_chunks = [(0, 1024, own_c[0]), (1024, 1024, own_c[1])] + ([(2048, 64, own_c[4])] if even else [])
        KTo = RowChunks(kt_chunks)
        Vo = VView([own_c[2], own_c[3]])
        if even:
            proj_even(k, st, wp, sg, W, j, 2, rope_b, qts, KTo, Vo)
        else:
            proj_odd(k, st, wp, sg, W, j, 2, rope_c, qts, KTo, Vo)
        toks = list(sg.stores)
        sg.stores = []
        k.pool._need(toks)
        k.sp._need(toks)
        cctoks = []
        for ci in range(len(csz)):
            ccsem = k.newsem(f"cc{layer}_{ci}")
            nc.gpsimd.collective_compute("AllGather", ALU.bypass, replica_groups=[[2 * i_, 2 * i_ + 1] for i_ in range(ncores // 2)],
                                         ins=[own_c[ci].opt()], outs=[pair_c[ci].opt()]).then_inc(ccsem)
            cctoks.append((f"cc{layer}_{ci}", 0, 1, ccsem))
        k.sp._need(cctoks)
        KTr = RowChunks([(0, 1024, pair_c[0][0:1024, :]), (1024, 1024, pair_c[1][0:1024, :])]
                        + ([(2048, 64, pair_c[4][0:64, :])] if even else []))
        Vr = VView([pair_c[2][0:1024, :], pair_c[3][0:1024, :]])
        if even:
            ax = attn_even(k, st, None, j, qts, KTo, Vo, KTr, Vr, abias, acv)
            mix_out(k, st, wp, ax, W("ab_w_out", j), 3)
        else:
            ax = attn_odd(k, st, None, j, layer, qts, KTo, Vo, KTr, Vr, lvec, gsub)
            mix_out(k, st, wp, ax, W("c_w_out", j), 3)
        ffn_full(k, st, wp, W("ffn2_w_in", layer), W("ffn2_w_out", layer), 4, 5)
        wp.big16()
        for th in range(2):
            ple_half(k, st, wp, th, W("ple_w_gate", layer), W("ple_w_proj", layer), pT[layer], 6, 7)
    store_h(k, st, out)
    return k.finish()


def kernel_fused(I, nlayers=4):
    x, p = I["x"], I["p"]
    shared = {"gvec": np.ascontiguousarray(np.stack([
        np.stack([I["ffn1_g_pre"][l], I["ffn1_g_post"][l], I["mix_g_pre"][l], I["mix_g_post"][l],
                  I["ffn2_g_pre"][l], I["ffn2_g_post"][l], I["ple_g_pre"][l], I["ple_g_post"][l]]) for l in range(4)])),
        "g_q": I["b_g_q"], "g_kv": I["b_g_kv"], "gsub": I["c_g_sub"]}
    tiles = [_abias_tiles(I["a_rel_bias"][j]) for j in range(2)]
    shared["abias"] = np.ascontiguousarray(np.stack([t[0] for t in tiles]))
    shared["acv"] = np.ascontiguousarray(np.concatenate([t[1] for t in tiles], 0))
    for q_, n_ in enumerate(("c_lq1", "c_lk1", "c_lq2", "c_lk2")):
        shared[f"lvec{q_}"] = I[n_]
    for n_ in ("ffn1_w_in", "ffn1_w_out", "ffn2_w_in", "ffn2_w_out", "ab_w_in", "b_w_qup", "b_w_kvup", "ab_w_out",
               "c_w_in", "c_w_out", "ple_w_gate", "ple_w_proj"):
        shared["w_" + n_] = I[n_]
    ab = I["ab_w_in"]
    shared["w_ab_w_in_krp"] = np.ascontiguousarray(ab[:, :, 3840 + _swap_halves(64)])
    idx = np.concatenate([h_ * 192 + 128 + _swap_halves(64) for h_ in range(8)])
    shared["w_b_w_qup_rp"] = np.ascontiguousarray(I["b_w_qup"][:, :, idx])
    idx = np.concatenate([h_ * 768 + w_ * 128 + _swap_halves(32) for h_ in range(8) for w_ in range(4)])
    shared["w_c_w_in_rp"] = np.ascontiguousarray(I["c_w_in"][:, :, idx])
    in_maps = []
    for c in range(NCORES):
        m = dict(shared)
        b, hf = c // 2, c % 2
        m["hT"] = np.ascontiguousarray(x[b, hf * NT:(hf + 1) * NT, :].T)
        m["rope_b"] = _rope_tab(64, hf * NT)
        m["rope_c"] = _rope_tab(32, hf * NT)
        m["rbias"] = np.full((128, 1), 0.0 if hf == 1 else -30000.0, np.float32)
        m["pT"] = np.ascontiguousarray(np.transpose(p[:, b, hf * NT:(hf + 1) * NT, :], (0, 2, 1)))
        in_maps.append(m)
    nc = build_fused(nlayers, _NRUN)
    res = _run(nc, in_maps)
    hT = [res[c]["hT_out"] for c in range(NCORES)]
    _dbg(f"h_ple_{nlayers - 1}", hT)
    out = np.empty((4, 2 * NT, D), np.float32)
    for c in range(NCORES):
        out[c // 2, (c % 2) * NT:(c % 2 + 1) * NT, :] = hT[c].T
    return out


def _shard_rows(w):
    w = np.ascontiguousarray(w)
    return [w] * NCORES


def _rope_tab(rot, pos0):
    inv = (500000.0 ** (-np.arange(0, rot, 2, dtype=np.float32) / np.float32(rot))).astype(np.float32)
    ang = (np.arange(pos0, pos0 + NT, dtype=np.float32)[:, None] * inv[None, :]).astype(np.float32)
    c = np.cos(ang).astype(np.float32).T
    s = np.sin(ang).astype(np.float32).T
    return np.ascontiguousarray(np.stack([np.concatenate([c, c], 0), np.concatenate([-s, s], 0)], 0))


def _swap_halves(n):
    return np.concatenate([np.arange(n // 2, n), np.arange(0, n // 2)])


def _abias_tiles(table):
    kk = np.arange(128)[:, None]
    qq = np.arange(128)[None, :]
    out = np.empty((8, 3, 128, 128), np.float32)
    far = table[:, 191]
    t4 = np.broadcast_to(far[:, None, None], (8, 128, 128)).copy()
    t4[:, (kk < 64) & (qq >= 64)] = -30000.0
    out[:, 0] = t4
    d1 = np.clip(128 + qq - kk, -63, 128) + 63
    out[:, 1] = table[:, d1]
    d0 = np.clip(qq - kk, -63, 128) + 63
    t0 = table[:, d0].copy()
    t0[:, (kk >= 64) & (qq < 64)] = -30000.0
    out[:, 2] = t0
    return out, np.ascontiguousarray(far[None, :])


_NRUN = NCORES


def _run(nc, in_maps):
    res = run_bass_kernel_spmd(nc, in_maps[:_NRUN], core_ids=list(range(_NRUN)))
    r = list(res.results)
    while len(r) < NCORES:
        r.append(r[len(r) % _NRUN])
    return r


_DEBUG = None
_KSTOP = 0


def _dbg(name, hT):
    if _DEBUG is None:
        return
    ref = _DEBUG[name][0]
    for c in range(2):
        o = hT[c].T.astype(np.float32)
        r = ref[c * NT:(c + 1) * NT]
        print("DBG", name, "core", c, "relerr", float(np.sqrt(((o - r) ** 2).mean() / (r ** 2).mean())),
              "finite", bool(np.isfinite(o).all()), flush=True)
        print("   per-tile", [round(float(np.sqrt(((o[t * 128:(t + 1) * 128] - r[t * 128:(t + 1) * 128]) ** 2).mean() / (r ** 2).mean())), 4) for t in range(8)], flush=True)


_FUSED = True
_NLAYERS = 4


def kernel(**I):
    import ml_dtypes
    I = {k_: np.asarray(v) for k_, v in I.items()}
    if _FUSED:
        return kernel_fused(I, _NLAYERS)
    x, p = I["x"], I["p"]
    hT = [np.ascontiguousarray(x[c // 2, (c % 2) * NT:(c % 2 + 1) * NT, :].T) for c in range(NCORES)]
    rbias = [np.full((128, 1), 0.0 if c % 2 == 1 else -30000.0, np.float32) for c in range(NCORES)]
    for layer in range(4):
        even = layer % 2 == 0
        j = layer // 2
        shared = {"gvec": np.stack([I["ffn1_g_pre"][layer], I["ffn1_g_post"][layer], I["mix_g_pre"][layer]])}
        wsh = {"ffn_w_in": _shard_rows(I["ffn1_w_in"][layer]), "ffn_w_out": _shard_rows(I["ffn1_w_out"][layer])}
        if even:
            w_in = I["ab_w_in"][j]
            wsh["ab_w_in"] = _shard_rows(w_in)
            wsh["ab_w_in_krp"] = _shard_rows(w_in[:, 3840 + _swap_halves(64)])
            qup = I["b_w_qup"][j]
            wsh["b_w_qup"] = _shard_rows(qup)
            idx = np.concatenate([h_ * 192 + 128 + _swap_halves(64) for h_ in range(8)])
            wsh["b_w_qup_rp"] = _shard_rows(qup[:, idx])
            wsh["b_w_kvup"] = _shard_rows(I["b_w_kvup"][j])
            shared["g_q"] = I["b_g_q"][j]
            shared["g_kv"] = I["b_g_kv"][j]
            rope = [_rope_tab(64, (c % 2) * NT) for c in range(NCORES)]
        else:
            w_in = I["c_w_in"][j]
            wsh["c_w_in"] = _shard_rows(w_in)
            idx = np.concatenate([h_ * 768 + w_ * 128 + _swap_halves(32) for h_ in range(8) for w_ in range(4)])
            wsh["c_w_in_rp"] = _shard_rows(w_in[:, idx])
            rope = [_rope_tab(32, (c % 2) * NT) for c in range(NCORES)]
        nc = build_part1(layer)
        in_maps = []
        for c in range(NCORES):
            m = dict(shared)
            m["hT"] = hT[c]
            m["rope"] = rope[c]
            for n_, sh in wsh.items():
                m["w_" + n_] = sh[c]
            in_maps.append(m)
        res = _run(nc, in_maps)
        hT = [res[c]["hT_out"] for c in range(NCORES)]
        _dbg(f"h_ffn1_{layer}", hT)
        qt = [res[c]["qt_out"] for c in range(NCORES)]
        kv = [res[c]["kv_out"] for c in range(NCORES)]
        if _KSTOP == 10 + layer:
            return None
        del wsh, in_maps
        shared = {"gvec": np.stack([I["mix_g_post"][layer], I["ffn2_g_pre"][layer], I["ffn2_g_post"][layer],
                                    I["ple_g_pre"][layer], I["ple_g_post"][layer]])}
        wsh = {"mix_w_out": _shard_rows((I["ab_w_out"] if even else I["c_w_out"])[j]),
               "ffn_w_in": _shard_rows(I["ffn2_w_in"][layer]), "ffn_w_out": _shard_rows(I["ffn2_w_out"][layer]),
               "ple_w_gate": _shard_rows(I["ple_w_gate"][layer]), "ple_w_proj": _shard_rows(I["ple_w_proj"][layer])}
        if even:
            tiles, far = _abias_tiles(I["a_rel_bias"][j])
            shared["abias"] = tiles[None]
            shared["acv"] = far
        else:
            for q_, n_ in enumerate(("c_lq1", "c_lk1", "c_lq2", "c_lk2")):
                shared[f"lvec{q_}"] = I[n_][j][None]
            shared["gsub"] = I["c_g_sub"][j][None]
        nc = build_part2(layer)
        in_maps = []
        for c in range(NCORES):
            m = dict(shared)
            m["hT"] = hT[c]
            m["qt"] = qt[c]
            m["kv_own"] = kv[c]
            m["kv_rem"] = kv[c - 1] if c % 2 == 1 else kv[c]
            m["rbias"] = rbias[c]
            m["pT"] = np.ascontiguousarray(p[layer, c // 2, (c % 2) * NT:(c % 2 + 1) * NT, :].T)
            for n_, sh in wsh.items():
                m["w_" + n_] = sh[c]
            in_maps.append(m)
        res = _run(nc, in_maps)
        hT = [res[c]["hT_out"] for c in range(NCORES)]
        _dbg(f"h_ple_{layer}", hT)
        if _KSTOP == 20 + layer:
            return None
        del wsh, in_maps
    out = np.empty((4, 2 * NT, D), np.float32)
    for c in range(NCORES):
        out[c // 2, (c % 2) * NT:(c % 2 + 1) * NT, :] = hT[c].T
    return out
```
